# Optimizing a Trainium2 kernel written in Bass

```python
import jax, jax.numpy as jnp
from jax import lax
import numpy as np

D_MODEL = 2048
BATCH = 8
SEQ = 2048
DEPTH = 2

MEM_LEN = 256
NORM_EPS = 1e-6
ROPE_THETA = 500000.0

D_FF = 5632

SSD_EXPAND = 2
SSD_INNER = SSD_EXPAND * D_MODEL
SSD_HEAD_DIM = 64
SSD_HEADS = SSD_INNER // SSD_HEAD_DIM
SSD_GROUPS = 8
SSD_HEADS_PER_GROUP = SSD_HEADS // SSD_GROUPS
SSD_STATE = 128
SSD_CONV = 4
SSD_CHUNK = 128
SSD_CONV_CH = SSD_INNER + 2 * SSD_GROUPS * SSD_STATE

DSA_HEADS = 16
DSA_WIDTH = D_MODEL
DSA_HEAD_DIM = DSA_WIDTH // DSA_HEADS
DSA_ROPE = DSA_HEAD_DIM // 4
IDX_HEADS = 16
IDX_DIM = 64
IDX_ROPE = IDX_DIM // 4
DSA_TOPK_MAX = 256
Q_BLOCK = 128

XA_HEADS = 4
XA_WIDTH = D_MODEL
XA_HEAD_DIM = XA_WIDTH // XA_HEADS

N_BRANCH = 3

IN_SPLITS = (SSD_INNER, SSD_CONV_CH, SSD_HEADS,
             DSA_WIDTH, DSA_HEAD_DIM, DSA_HEAD_DIM,
             IDX_HEADS * IDX_DIM, IDX_DIM, IDX_HEADS,
             XA_WIDTH, N_BRANCH * D_MODEL)
IN_WIDTH = int(sum(IN_SPLITS))
IN_OFFSETS = tuple(int(o) for o in np.cumsum(IN_SPLITS)[:-1].tolist())

kernel_name = 'hybrid_ssd_dsa_memxattn_macaron'


def rms_norm(x, g):
    xf = x.astype(jnp.float32)
    y = xf * lax.rsqrt(jnp.mean(xf * xf, axis=-1, keepdims=True) + NORM_EPS)
    return (y * g.astype(jnp.float32)).astype(x.dtype)


def swiglu(h, w_in, w_out):
    gate, up = jnp.split(h @ w_in, 2, axis=-1)
    return (jax.nn.silu(gate) * up) @ w_out


def rope_tables(positions, rot_dim):
    inv = jnp.power(ROPE_THETA, -jnp.arange(0, rot_dim, 2, dtype=jnp.float32) / rot_dim)
    ang = positions.astype(jnp.float32)[..., None] * inv
    return jnp.cos(ang), jnp.sin(ang)


def apply_partial_rope(x, cos, sin):
    half = cos.shape[-1]
    rot = 2 * half
    extra = x.ndim - cos.ndim
    shp = cos.shape[:2] + (1,) * extra + (half,)
    c = cos.reshape(shp)
    s = sin.reshape(shp)
    xr = x[..., :rot].astype(jnp.float32)
    x1, x2 = xr[..., :half], xr[..., half:]
    out = jnp.concatenate([x1 * c - x2 * s, x2 * c + x1 * s], axis=-1).astype(x.dtype)
    return jnp.concatenate([out, x[..., rot:]], axis=-1)


def causal_depthwise_conv(x, w, b):
    ch = x.shape[-1]
    y = lax.conv_general_dilated(x, w[:, None, :].astype(x.dtype), window_strides=(1,),
                                 padding=[(SSD_CONV - 1, 0)],
                                 dimension_numbers=('NWC', 'WIO', 'NWC'),
                                 feature_group_count=ch)
    return y + b.astype(x.dtype)


def ssd_mixer(z, xbc, dt_raw, conv_w, conv_b, dt_bias, a_log, d_skip, norm_g):
    f32 = jnp.float32
    bsz, seq, _ = z.shape
    G, R, P, N, Q = SSD_GROUPS, SSD_HEADS_PER_GROUP, SSD_HEAD_DIM, SSD_STATE, SSD_CHUNK
    nc = seq // Q
    xbc = jax.nn.silu(causal_depthwise_conv(xbc, conv_w, conv_b))
    xs, bm, cm = jnp.split(xbc, [SSD_INNER, SSD_INNER + G * N], axis=-1)
    xs = xs.reshape(bsz, nc, Q, G, R, P).astype(f32)
    bm = bm.reshape(bsz, nc, Q, G, N).astype(f32)
    cm = cm.reshape(bsz, nc, Q, G, N).astype(f32)
    dt = jax.nn.softplus(dt_raw.astype(f32) + dt_bias.astype(f32)).reshape(bsz, nc, Q, G, R)
    a = -jnp.exp(a_log.astype(f32)).reshape(G, R)
    a_cs = jnp.cumsum(dt * a, axis=2)
    xdt = xs * dt[..., None]
    causal = jnp.tril(jnp.ones((Q, Q), dtype=bool))
    seg = a_cs[:, :, :, None] - a_cs[:, :, None, :]
    decay = jnp.exp(jnp.where(causal[None, None, :, :, None, None], seg, -jnp.inf))
    cb = jnp.einsum('bclgn,bcsgn->bclsg', cm, bm)
    y_diag = jnp.einsum('bclsgr,bcsgrp->bclgrp', cb[..., None] * decay, xdt)
    decay_to_end = jnp.exp(a_cs[:, :, -1:] - a_cs)
    states = jnp.einsum('bcsgn,bcsgrp->bcgrpn', bm, xdt * decay_to_end[..., None])
    chunk_decay = jnp.exp(a_cs[:, :, -1])

    def step(h, inp):
        st, dc = inp
        return h * dc[..., None, None] + st, h

    h0 = jnp.zeros((bsz, G, R, P, N), f32)
    _, prev = lax.scan(step, h0, (jnp.moveaxis(states, 1, 0), jnp.moveaxis(chunk_decay, 1, 0)))
    prev = jnp.moveaxis(prev, 0, 1)
    y_off = jnp.einsum('bclgn,bcgrpn->bclgrp', cm, prev) * jnp.exp(a_cs)[..., None]
    y = y_diag + y_off + xs * d_skip.astype(f32).reshape(G, R)[..., None]
    y = y.reshape(bsz, seq, SSD_INNER) * jax.nn.silu(z.astype(f32))
    return rms_norm(y, norm_g).astype(z.dtype)


def dsa_mixer(q, k, v, qi, ki, wi, cos_a, sin_a, cos_i, sin_i):
    f32 = jnp.float32
    bsz, seq, _ = q.shape
    q = apply_partial_rope(q.reshape(bsz, seq, DSA_HEADS, DSA_HEAD_DIM), cos_a, sin_a)
    k = apply_partial_rope(k, cos_a, sin_a)
    qi = apply_partial_rope(qi.reshape(bsz, seq, IDX_HEADS, IDX_DIM), cos_i, sin_i)
    ki = apply_partial_rope(ki, cos_i, sin_i)
    wi = wi.astype(f32) * IDX_HEADS ** -0.5
    topk = min(DSA_TOPK_MAX, seq // 4)
    nb = seq // Q_BLOCK
    key_pos = jnp.arange(seq)

    def to_blocks(t):
        return jnp.moveaxis(t.reshape((bsz, nb, Q_BLOCK) + t.shape[2:]), 1, 0)

    def block(args):
        q_b, qi_b, w_b, q_pos = args
        s = jnp.einsum('bqhd,bkd->bqhk', qi_b, ki).astype(f32) * IDX_DIM ** -0.5
        score = jnp.einsum('bqhk,bqh->bqk', jax.nn.relu(s), w_b)
        score = jnp.where(key_pos[None, None, :] <= q_pos[None, :, None], score, -jnp.inf)
        _, idx = lax.top_k(score, topk)
        k_sel = jax.vmap(lambda kk, ii: kk[ii])(k, idx)
        v_sel = jax.vmap(lambda vv, ii: vv[ii])(v, idx)
        logits = jnp.einsum('bqhd,bqkd->bqhk', q_b, k_sel).astype(f32) * DSA_HEAD_DIM ** -0.5
        valid = (idx <= q_pos[None, :, None])[:, :, None, :]
        p = jax.nn.softmax(jnp.where(valid, logits, -jnp.inf), axis=-1)
        return jnp.einsum('bqhk,bqkd->bqhd', p.astype(v.dtype), v_sel)

    q_pos_blocks = jnp.arange(seq).reshape(nb, Q_BLOCK)
    out = lax.map(block, (to_blocks(q), to_blocks(qi), to_blocks(wi), q_pos_blocks))
    return jnp.moveaxis(out, 0, 1).reshape(bsz, seq, DSA_WIDTH)


def memory_cross_attention(q, mem_n, w_mem_kv):
    bsz, seq, _ = q.shape
    k_m, v_m = jnp.split(mem_n @ w_mem_kv, 2, axis=-1)
    q = q.reshape(bsz, seq, XA_HEADS, XA_HEAD_DIM)
    k_m = k_m.reshape(bsz, -1, XA_HEADS, XA_HEAD_DIM)
    v_m = v_m.reshape(bsz, -1, XA_HEADS, XA_HEAD_DIM)
    logits = jnp.einsum('bqhd,bmhd->bhqm', q, k_m).astype(jnp.float32) * XA_HEAD_DIM ** -0.5
    p = jax.nn.softmax(logits, axis=-1)
    out = jnp.einsum('bhqm,bmhd->bqhd', p.astype(v_m.dtype), v_m)
    return out.reshape(bsz, seq, XA_WIDTH)


def setup_inputs(seed: int = 0) -> dict:
    key = jax.random.key(seed)
    ks = jax.random.split(key, 32)
    f32 = jnp.float32
    L = DEPTH

    def dense(k, shape, fan_in):
        return jax.random.normal(k, shape, f32) * fan_in ** -0.5

    def gain(k, shape):
        return 1.0 + 0.02 * jax.random.normal(k, shape, f32)

    x = jax.random.normal(ks[0], (BATCH, SEQ, D_MODEL), f32)
    mem = jax.random.normal(ks[1], (BATCH, MEM_LEN, D_MODEL), f32)
    positions = (jax.random.randint(ks[2], (BATCH, 1), 0, 4096, dtype=jnp.int32)
                 + jnp.arange(SEQ, dtype=jnp.int32)[None, :])
    dt0 = jnp.exp(jax.random.uniform(ks[10], (L, SSD_HEADS), f32, np.log(1e-3), np.log(1e-1)))
    dt_bias = dt0 + jnp.log(-jnp.expm1(-dt0))
    a_log = jnp.log(jax.random.uniform(ks[11], (L, SSD_HEADS), f32, 1.0, 16.0))
    return {
        'x': x,
        'mem': mem,
        'positions': positions,
        'ffn1_norm': gain(ks[3], (L, D_MODEL)),
        'w_ffn1_in': dense(ks[4], (L, D_MODEL, 2 * D_FF), D_MODEL),
        'w_ffn1_out': dense(ks[5], (L, D_FF, D_MODEL), D_FF),
        'mix_norm': gain(ks[6], (L, D_MODEL)),
        'w_in': dense(ks[7], (L, D_MODEL, IN_WIDTH), D_MODEL),
        'conv_w': dense(ks[8], (L, SSD_CONV, SSD_CONV_CH), SSD_CONV),
        'conv_b': 0.01 * jax.random.normal(ks[9], (L, SSD_CONV_CH), f32),
        'dt_bias': dt_bias,
        'a_log': a_log,
        'd_skip': 1.0 + 0.1 * jax.random.normal(ks[12], (L, SSD_HEADS), f32),
        'ssd_norm': gain(ks[13], (L, SSD_INNER)),
        'mem_norm': gain(ks[14], (L, D_MODEL)),
        'w_mem_kv': dense(ks[15], (L, D_MODEL, 2 * XA_WIDTH), D_MODEL),
        'w_br_ssd': dense(ks[16], (L, SSD_INNER, D_MODEL), SSD_INNER),
        'w_br_dsa': dense(ks[17], (L, DSA_WIDTH, D_MODEL), DSA_WIDTH),
        'w_br_mem': dense(ks[18], (L, XA_WIDTH, D_MODEL), XA_WIDTH),
        'w_out': dense(ks[19], (L, D_MODEL, D_MODEL), D_MODEL),
        'ffn2_norm': gain(ks[20], (L, D_MODEL)),
        'w_ffn2_in': dense(ks[21], (L, D_MODEL, 2 * D_FF), D_MODEL),
        'w_ffn2_out': dense(ks[22], (L, D_FF, D_MODEL), D_FF),
        'final_norm': gain(ks[23], (D_MODEL,)),
    }


def reference(x, mem, positions, ffn1_norm, w_ffn1_in, w_ffn1_out, mix_norm, w_in,
              conv_w, conv_b, dt_bias, a_log, d_skip, ssd_norm, mem_norm, w_mem_kv,
              w_br_ssd, w_br_dsa, w_br_mem, w_out, ffn2_norm, w_ffn2_in, w_ffn2_out,
              final_norm):
    bsz, seq, _ = x.shape
    cos_a, sin_a = rope_tables(positions, DSA_ROPE)
    cos_i, sin_i = rope_tables(positions, IDX_ROPE)
    for l in range(DEPTH):
        x = x + 0.5 * swiglu(rms_norm(x, ffn1_norm[l]), w_ffn1_in[l], w_ffn1_out[l])
        h = rms_norm(x, mix_norm[l])
        (z, xbc, dt_raw, q, k, v, qi, ki, wi, q_mem, gate_pre) = jnp.split(
            h @ w_in[l], IN_OFFSETS, axis=-1)
        y_ssd = ssd_mixer(z, xbc, dt_raw, conv_w[l], conv_b[l], dt_bias[l], a_log[l],
                          d_skip[l], ssd_norm[l])
        y_dsa = dsa_mixer(q, k, v, qi, ki, wi, cos_a, sin_a, cos_i, sin_i)
        y_mem = memory_cross_attention(q_mem, rms_norm(mem, mem_norm[l]), w_mem_kv[l])
        g = jax.nn.sigmoid(gate_pre.astype(jnp.float32)).astype(x.dtype)
        g = g.reshape(bsz, seq, N_BRANCH, D_MODEL)
        merged = (g[:, :, 0] * (y_ssd @ w_br_ssd[l])
                  + g[:, :, 1] * (y_dsa @ w_br_dsa[l])
                  + g[:, :, 2] * (y_mem @ w_br_mem[l]))
        x = x + merged @ w_out[l]
        x = x + 0.5 * swiglu(rms_norm(x, ffn2_norm[l]), w_ffn2_in[l], w_ffn2_out[l])
    return rms_norm(x, final_norm)
```

```python
from contextlib import ExitStack
import numpy as np
import concourse.bass as bass
import concourse.mybir as mybir
from concourse.bass_utils import run_bass_kernel_spmd

F32 = mybir.dt.float32
BF16 = mybir.dt.bfloat16
I32 = mybir.dt.int32
AF = mybir.ActivationFunctionType
ALU = mybir.AluOpType
AX = mybir.AxisListType

D = 2048
L = 2048
DEPTH = 2
DFF = 5632
MEM = 256
EPS = 1e-6
SSD_INNER = 4096
CONV_CH = 6144
NH_SSD = 64
IN_SPLITS = (4096, 6144, 64, 2048, 128, 128, 1024, 64, 16, 2048, 6144)
IN_OFF = [0]
for _s in IN_SPLITS:
    IN_OFF.append(IN_OFF[-1] + _s)
IN_WIDTH = IN_OFF[-1]
(O_Z, O_XBC, O_DT, O_Q, O_K, O_V, O_QI, O_KI, O_WI, O_QM, O_G) = IN_OFF[:11]
TB = 512
NTB = L // TB
KC_D = D // 128


class Sched:
    ENGS = ("pe", "act", "dve", "pool", "sp")

    def __init__(self, nc, es, n_dma_sems=24):
        self.nc = nc
        self.eng = dict(pe=nc.tensor, act=nc.scalar, dve=nc.vector, pool=nc.gpsimd, sp=nc.sync)
        self.sem = {e: es.enter_context(nc.semaphore("sem_" + e)) for e in self.ENGS}
        self.cnt = {e: 0 for e in self.ENGS}
        self.seen = {e: {} for e in self.ENGS}
        self.snap = {}
        self.dsem = {}
        self.dcur = {}
        self.drr = {}
        for q in ("sp", "pool", "act"):
            n = n_dma_sems if q != "act" else 8
            self.dsem[q] = [es.enter_context(nc.semaphore("dsem_%s_%d" % (q, i))) for i in range(n)]
            self.dcur[q] = [0] * n
            self.drr[q] = 0
        self.bufs = {}
        self.n_wait = 0
        self.n_inst = 0

    def _wait(self, e, tok):
        if tok is None:
            return
        kind = tok[0]
        seen = self.seen[e]
        if kind == "c":
            _, f, c = tok
            if seen.get(f, 0) >= c:
                return
            if f == e and e == "pe":
                return
            self.eng[e].wait_ge(self.sem[f], c)
            self.n_wait += 1
            seen[f] = c
            sn = self.snap.get((f, c))
            if sn is not None:
                for g, v in zip(self.ENGS, sn):
                    if v > seen.get(g, 0):
                        seen[g] = v
        else:
            _, q, i, v = tok
            key = (q, i)
            if seen.get(key, 0) >= v:
                return
            self.eng[e].wait_ge(self.dsem[q][i], v)
            self.n_wait += 1
            seen[key] = v

    def _deps(self, e, r, w):
        for k in r:
            st = self.bufs.get(k)
            if st is not None:
                self._wait(e, st[0])
        for k in w:
            st = self.bufs.get(k)
            if st is not None:
                self._wait(e, st[0])
                for t in st[1]:
                    if t[0] == "c" and t[1] == e:
                        continue
                    self._wait(e, t)

    def _commit(self, tok, r, w):
        for k in r:
            st = self.bufs.setdefault(k, [None, []])
            st[1] = [t for t in st[1] if not (t[0] == tok[0] and t[1] == tok[1] and (t[0] == "c" or t[2] == tok[2]))]
            st[1].append(tok)
        for k in w:
            self.bufs[k] = [tok, []]

    def op(self, e, fn, r=(), w=()):
        self._deps(e, r, w)
        ins = fn(self.eng[e])
        self.cnt[e] += 1
        c = self.cnt[e]
        ins.then_inc(self.sem[e], 1)
        self.n_inst += 1
        sn = self.seen[e]
        self.snap[(e, c)] = tuple(c if g == e else sn.get(g, 0) for g in self.ENGS)
        tok = ("c", e, c)
        self._commit(tok, r, w)
        return tok

    def dma(self, q, out, in_, r=(), w=(), **kw):
        self._deps(q, r, w)
        i = self.drr[q]
        self.drr[q] = (i + 1) % len(self.dsem[q])
        cur = self.dcur[q][i]
        if cur > 0:
            self._wait(q, ("d", q, i, cur))
        self.eng[q].dma_start(out=out, in_=in_, **kw).then_inc(self.dsem[q][i], 16)
        self.dcur[q][i] = cur + 16
        tok = ("d", q, i, cur + 16)
        self._commit(tok, r, w)
        return tok

    def finish(self, e="sp"):
        for f in self.ENGS:
            if self.cnt[f] > 0 and f != e:
                self._wait(e, ("c", f, self.cnt[f]))
        for q in self.dsem:
            for i, cur in enumerate(self.dcur[q]):
                if cur > 0:
                    self._wait(e, ("d", q, i, cur))


class Buf:
    def __init__(self, t, key):
        self.t = t
        self.k = key

    def __getitem__(self, idx):
        return self.t[idx]


class Ctx:
    pass


def make_ctx(nc, es):
    c = Ctx()
    c.nc = nc
    c.es = es
    c.S = Sched(nc, es)
    c.nbuf = 0

    def sb(shape, dt, name=None):
        c.nbuf += 1
        name = name or ("sb%d" % c.nbuf)
        t = es.enter_context(nc.sbuf_tensor(name, list(shape), dt))
        return Buf(t, name)

    def ps(name, shape=(128, 512), dt=F32):
        t = es.enter_context(nc.psum_tensor(name, list(shape), dt))
        return Buf(t, name)
    c.sb = sb
    c.ps = ps
    return c


def stream_linear(c, jobs, rhs_fn, rhs_keys, ntb, epilogue, wslots, psum_sets, tbw=TB):
    S = c.S
    nslot = len(wslots)
    state = c.__dict__.setdefault("_lin_state", {"slot": 0, "pset": 0})
    flat = []
    for ji, job in enumerate(jobs):
        for gi, g in enumerate(job):
            flat.append((ji, gi, g))
    PREF = nslot - 1
    slot_of = {}

    def issue(n):
        ji, gi, (W, col0, ncols, KC) = flat[n]
        si = state["slot"]
        state["slot"] = (si + 1) % nslot
        ws = wslots[si]
        wv = W.rearrange("(kc p) f -> p kc f", p=128)
        dst = ws.t[:, 0:KC * ncols].rearrange("p (kc f) -> p kc f", f=ncols)
        half = KC // 2
        S.dma("pool", dst[:, 0:half, :], wv[:, 0:half, col0:col0 + ncols], w=[ws.k + "_a"])
        S.dma("pool", dst[:, half:KC, :], wv[:, half:KC, col0:col0 + ncols], w=[ws.k + "_b"])
        slot_of[n] = (ws, dst)

    nxt = 0
    n = 0
    for ji, job in enumerate(jobs):
        while nxt < len(flat) and nxt < n + nslot:
            issue(nxt)
            nxt += 1
        for tb in range(ntb):
            pi = state["pset"] % len(psum_sets)
            state["pset"] = (pi + 1) % len(psum_sets)
            pset = psum_sets[pi]
            for gi, (W, col0, ncols, KC) in enumerate(job):
                ws, dst = slot_of[n + gi]
                pb = pset[gi]

                def mm(e, dst=dst, pb=pb, KC=KC, gi=gi, tb=tb, ncols=ncols):
                    ins = None
                    for kc in range(KC):
                        ins = e.matmul(pb.t[0:ncols, 0:tbw], dst[:, kc, :], rhs_fn(gi, kc, tb),
                                       start=(kc == 0), stop=(kc == KC - 1))
                    return ins
                S.op("pe", mm, r=[ws.k + "_a", ws.k + "_b"] + list(rhs_keys(gi, tb)), w=[pb.k])
            epilogue(ji, tb, pset)
        n += len(job)


def sched_barrier(S):
    engs = [e for e in S.ENGS]
    toks = [("c", f, S.cnt[f]) for f in S.ENGS if S.cnt[f] > 0]
    dtoks = []
    for q in S.dsem:
        for i, cur in enumerate(S.dcur[q]):
            if cur > 0:
                dtoks.append(("d", q, i, cur))
    for e in engs:
        for t in toks:
            if t[1] != e:
                S._wait(e, t)
        for t in dtoks:
            S._wait(e, t)
    S.bufs = {}


def phase_norm(c, x_src, gcol, out_mode, out_dst=None):
    S = c.S
    xv = x_src.rearrange("(kc p) t -> p kc t", p=128)
    xin = c.R44.t[:, 0:16 * TB * 2].bitcast(F32).rearrange("p (kc t) -> p kc t", t=TB)
    hT = c.hT
    for tb in range(NTB):
        t0 = tb * TB
        for hf in range(2):
            S.dma("sp", xin[:, hf * 8:(hf + 1) * 8, :], xv[:, hf * 8:(hf + 1) * 8, t0:t0 + TB],
                  w=["xin%d" % hf])
        pb = c.psb[tb % 2]
        for kc in range(KC_D):
            sq = c.stg_bf[kc % 3]
            S.op("act", lambda e, sq=sq, kc=kc: e.activation(out=sq.t[:, :], in_=xin[:, kc, :], func=AF.Square),
                 r=["xin%d" % (kc // 8)], w=[sq.k])
            S.op("pe", lambda e, sq=sq, kc=kc, pb=pb: e.matmul(pb.t[:, :], c.ones_bf.t[:, :], sq.t[:, :],
                                                                start=(kc == 0), stop=(kc == KC_D - 1)),
                 r=[sq.k, "const"], w=[pb.k])
        rt = c.stg_f32[0]
        S.op("act", lambda e, pb=pb: e.activation(out=rt.t[:, :], in_=pb.t[:, :], func=AF.Sqrt,
                                                  scale=1.0 / D, bias=c.eps_col.t[:, 0:1]),
             r=[pb.k, "const"], w=[rt.k])
        rs = c.stg_f32[1]
        S.op("dve", lambda e: e.reciprocal(out=rs.t[:, :], in_=rt.t[:, :]), r=[rt.k], w=[rs.k])
        for kc in range(KC_D):
            if out_mode == "h":
                S.op("dve", lambda e, kc=kc: e.scalar_tensor_tensor(
                    out=hT[:, kc, t0:t0 + TB], in0=xin[:, kc, :], scalar=gcol[:, kc:kc + 1], in1=rs.t[:, :],
                    op0=ALU.mult, op1=ALU.mult),
                    r=["xin%d" % (kc // 8), rs.k, "vecs"], w=["hT%d" % tb])
            else:
                ot = c.stg_f32[2 + kc % 2]
                S.op("dve", lambda e, kc=kc, ot=ot: e.scalar_tensor_tensor(
                    out=ot.t[:, :], in0=xin[:, kc, :], scalar=gcol[:, kc:kc + 1], in1=rs.t[:, :],
                    op0=ALU.mult, op1=ALU.mult),
                    r=["xin%d" % (kc // 8), rs.k, "vecs"], w=[ot.k])
                S.dma("sp", out_dst[kc * 128:(kc + 1) * 128, t0:t0 + TB], ot.t[:, :], r=[ot.k])


def phase_ffn(c, w_in, w_out, x_src, x_dst):
    S = c.S
    NJ = DFF // 128
    hT = c.hT
    actd = c.act_d
    jobs = [[(w_in, j * 128, 128, KC_D), (w_in, DFF + j * 128, 128, KC_D)] for j in range(NJ)]
    cnt = [0]

    def epi_a(ji, tb, pset):
        i = cnt[0]
        cnt[0] += 1
        sg = c.stg_f32[i % 2]
        S.op("act", lambda e: e.activation(out=sg.t[:, :], in_=pset[0].t[:, :], func=AF.Silu),
             r=[pset[0].k], w=[sg.k])
        ab = c.stg_bf[i % 3]
        S.op("dve", lambda e: e.tensor_tensor(out=ab.t[:, :], in0=pset[1].t[:, :], in1=sg.t[:, :], op=ALU.mult),
             r=[pset[1].k, sg.k], w=[ab.k])
        S.dma("sp", actd[ji * 128:(ji + 1) * 128, tb * TB:(tb + 1) * TB], ab.t[:, :], r=[ab.k],
              w=[("act", ji, tb)])

    stream_linear(c, jobs, lambda gi, kc, tb: hT[:, kc, tb * TB:(tb + 1) * TB],
                  lambda gi, tb: ["hT%d" % tb], NTB, epi_a, c.wslots,
                  [[c.psb[0], c.psb[1]], [c.psb[2], c.psb[3]], [c.psb[4], c.psb[5]], [c.psb[6], c.psb[7]]])
    sched_barrier(S)
    av = actd.rearrange("(kc p) t -> p kc t", p=128)
    blks = [c.R64.t[:, 0:NJ * TB].rearrange("p (kc t) -> p kc t", t=TB),
            c.R44.t[:, 0:NJ * TB].rearrange("p (kc t) -> p kc t", t=TB)]
    bkeys = ["ablkA", "ablkB"]
    for tb in range(NTB):
        blk = blks[tb % 2]
        bk = bkeys[tb % 2]
        t0 = tb * TB
        for q4 in range(4):
            S.dma("sp", blk[:, q4 * 11:(q4 + 1) * 11, :], av[:, q4 * 11:(q4 + 1) * 11, t0:t0 + TB], w=[bk + str(q4)])
        jobs = [[(w_out, dc * 128, 128, NJ)] for dc in range(KC_D)]
        cnt2 = [0]

        def epi_b(ji, tb_unused, pset, tb=tb, t0=t0):
            i = cnt2[0]
            cnt2[0] += 1
            xt = c.stg_f32[i % 2]
            S.dma("sp", xt.t[:, :], x_src[ji * 128:(ji + 1) * 128, t0:t0 + TB], w=[xt.k])
            xo = c.stg_f32[2 + i % 2]
            S.op("dve", lambda e: e.scalar_tensor_tensor(out=xo.t[:, :], in0=pset[0].t[:, :], scalar=0.5,
                                                         in1=xt.t[:, :], op0=ALU.mult, op1=ALU.add),
                 r=[pset[0].k, xt.k], w=[xo.k])
            S.dma("sp", x_dst[ji * 128:(ji + 1) * 128, t0:t0 + TB], xo.t[:, :], r=[xo.k])

        stream_linear(c, jobs, lambda gi, kc, tb_, blk=blk: blk[:, kc, :],
                      lambda gi, tb_, bk=bk: [bk + str(q) for q in range(4)], 1, epi_b, c.wslots,
                      [[c.psb[i]] for i in range(8)])
    sched_barrier(S)


V_FFN1, V_MIX, V_FFN2, V_MEMN = 0, 16, 32, 48
V_SSDN = 64
V_CONVW = 96
V_CONVB = 288
V_DTB, V_ALOG, V_DSKIP = 336, 337, 338
V_FINAL = 339
V_DBC = 360
NV = 424


def pack_vecs(inp, l):
    v = np.zeros((128, NV), np.float32)

    def col(vec):
        return np.ascontiguousarray(np.asarray(vec, np.float32).reshape(-1, 128).T)
    v[:, V_FFN1:V_FFN1 + 16] = col(inp["ffn1_norm"][l])
    v[:, V_MIX:V_MIX + 16] = col(inp["mix_norm"][l])
    v[:, V_FFN2:V_FFN2 + 16] = col(inp["ffn2_norm"][l])
    v[:, V_MEMN:V_MEMN + 16] = col(inp["mem_norm"][l])
    v[:, V_SSDN:V_SSDN + 32] = col(inp["ssd_norm"][l])
    for k in range(4):
        v[:, V_CONVW + k * 48:V_CONVW + (k + 1) * 48] = col(inp["conv_w"][l][k])
    v[:, V_CONVB:V_CONVB + 48] = col(inp["conv_b"][l])
    v[0:64, V_DTB] = inp["dt_bias"][l]
    v[0:64, V_ALOG] = inp["a_log"][l]
    v[0:64, V_DSKIP] = inp["d_skip"][l]
    v[:, V_FINAL:V_FINAL + 16] = col(inp["final_norm"])
    v[:, V_DBC:V_DBC + 64] = np.asarray(inp["d_skip"][l], np.float32)[None, :]
    return v


C_IDENT, C_ONES = 0, 128
NCONST = 256


def make_consts():
    cst = np.zeros((128, NCONST), np.float32)
    cst[:, C_IDENT:C_IDENT + 128] = np.eye(128, dtype=np.float32)
    cst[:, C_ONES:C_ONES + 128] = 1.0
    return cst


def build(plan, dbg=None):
    nc = bass.Bass("TRN2", target_bir_lowering=False)
    es = ExitStack()
    with es:
        c = make_ctx(nc, es)
        S = c.S
        dt = nc.dram_tensor
        xT = dt("xT", [D, L], F32, kind="ExternalInput").ap()
        memT = dt("memT", [D, MEM], F32, kind="ExternalInput").ap()
        pos = dt("pos", [128, L], I32, kind="ExternalInput").ap()
        c.consts2_d = dt("consts2", [128, NCONST2], F32, kind="ExternalInput").ap()
        vecs = dt("vecs", [DEPTH, 128, NV], F32, kind="ExternalInput").ap()
        consts = dt("consts", [128, NCONST], F32, kind="ExternalInput").ap()
        W = {}
        for name, shp in (("w_ffn1_in", [D, 2 * DFF]), ("w_ffn1_out", [DFF, D]), ("w_ffn2_in", [D, 2 * DFF]),
                          ("w_ffn2_out", [DFF, D]), ("w_in", [D, IN_WIDTH]), ("w_mem_kv", [D, 2 * D]),
                          ("w_br_ssd", [SSD_INNER, D]), ("w_br_dsa", [D, D]), ("w_br_mem", [D, D]), ("w_out", [D, D])):
            W[name] = dt(name, [DEPTH] + shp, F32, kind="ExternalInput").ap()
        outT = dt("outT", [D, L], F32, kind="ExternalOutput").ap()
        c.xres = dt("xres", [D, L], F32, kind="Internal").ap()
        c.act_d = dt("act_d", [DFF, L], BF16, kind="Internal").ap()
        c.P = {}
        for name, off, width, dt_ in P_SPECS:
            c.P[name] = dt("P_" + name, [width, L], dt_, kind="Internal").ap()
        c.yssd = dt("yssd", [SSD_INNER, L], BF16, kind="Internal").ap()
        c.ydsa = dt("ydsa", [D, L], BF16, kind="Internal").ap()
        c.ymem = dt("ymem", [D, L], BF16, kind="Internal").ap()
        c.pos = pos
        c.xc_d = dt("xc_d", [CONV_CH, L], BF16, kind="Internal").ap()
        c.acs_d = dt("acs_d", [NH_SSD, L], F32, kind="Internal").ap()
        c.V_l = None
        c.consts_d = consts

        c.R64 = c.sb([128, 32768], BF16, "R64")
        c.R44 = c.sb([128, 22528], BF16, "R44")
        c.hT = c.R64.t[:, :].rearrange("p (kc t) -> p kc t", t=L)
        c.wslots = [c.sb([128, 5632], BF16, "wslot%d" % i) for i in range(6)]
        c.stg_f32 = [c.sb([128, TB], F32, "stgf%d" % i) for i in range(4)]
        c.stg_bf = [c.sb([128, TB], BF16, "stgb%d" % i) for i in range(3)]
        c.vecs = [c.sb([128, NV], F32, "vecs%d" % l) for l in range(DEPTH)]
        c.cst = c.sb([128, NCONST], F32, "cst")
        c.ones_bf = c.sb([128, 128], BF16, "ones_bf")
        c.ident_bf = c.sb([128, 128], BF16, "ident_bf")
        c.eps_col = c.sb([128, 1], F32, "eps_col")
        c.psb = [c.ps("ps%d" % i) for i in range(8)]

        for l in range(DEPTH):
            S.dma("sp", c.vecs[l].t[:, :], vecs[l], w=["vecs"])
        S.dma("sp", c.cst.t[:, :], consts, w=["cst"])
        S.op("dve", lambda e: e.tensor_copy(out=c.ones_bf.t[:, :], in_=c.cst.t[:, C_ONES:C_ONES + 128]), r=["cst"], w=["const"])
        S.op("dve", lambda e: e.tensor_copy(out=c.ident_bf.t[:, :], in_=c.cst.t[:, C_IDENT:C_IDENT + 128]), r=["cst"], w=["const"])
        S.op("dve", lambda e: e.memset(c.eps_col.t[:, :], EPS), w=["const"])
        sched_barrier(S)

        x_cur = xT
        for l in range(DEPTH):
            V = c.vecs[l].t
            if ("ffn1", l) in plan:
                phase_norm(c, x_cur, V[:, V_FFN1:V_FFN1 + 16], "h")
                sched_barrier(S)
                phase_ffn(c, W["w_ffn1_in"][l], W["w_ffn1_out"][l], x_cur, c.xres)
                x_cur = c.xres
            if ("inproj", l) in plan:
                phase_norm(c, x_cur, V[:, V_MIX:V_MIX + 16], "h")
                sched_barrier(S)
                phase_inproj(c, W["w_in"][l])
            if ("mem", l) in plan:
                phase_mem(c, memT, V[:, V_MEMN:V_MEMN + 16], W["w_mem_kv"][l])
            if ("dsa", l) in plan:
                phase_dsa(c, l)
            if ("ssd", l) in plan:
                phase_ssd(c, l, V)
            if ("merge", l) in plan:
                phase_merge(c, W["w_br_ssd"][l], W["w_br_dsa"][l], W["w_br_mem"][l], W["w_out"][l], x_cur, c.xres)
                x_cur = c.xres
            if ("ffn2", l) in plan:
                phase_norm(c, x_cur, V[:, V_FFN2:V_FFN2 + 16], "h")
                sched_barrier(S)
                phase_ffn(c, W["w_ffn2_in"][l], W["w_ffn2_out"][l], x_cur, c.xres)
                x_cur = c.xres
        if "final" in plan:
            phase_norm(c, x_cur, c.vecs[0].t[:, V_FINAL:V_FINAL + 16], "out", outT)
        elif dbg is not None:
            if dbg == "xres":
                src, nrow, sdt = x_cur, D, F32
            else:
                src, nrow, sdt = dbg
            for kc in range((nrow + 127) // 128):
                nr = min(128, nrow - kc * 128)
                for tb in range(NTB):
                    st = c.stg_f32[(kc * NTB + tb) % 4]
                    if sdt == F32:
                        S.dma("sp", st.t[0:nr, :], src(c)[kc * 128:kc * 128 + nr, tb * TB:(tb + 1) * TB] if callable(src) else src[kc * 128:kc * 128 + nr, tb * TB:(tb + 1) * TB], w=[st.k])
                    else:
                        sb_ = c.stg_bf[(kc * NTB + tb) % 3]
                        S.dma("sp", sb_.t[0:nr, :], src(c)[kc * 128:kc * 128 + nr, tb * TB:(tb + 1) * TB], w=[sb_.k])
                        S.op("dve", lambda e, st=st, sb_=sb_, nr=nr: e.tensor_copy(out=st.t[0:nr, :], in_=sb_.t[0:nr, :]), r=[sb_.k], w=[st.k])
                    S.dma("sp", outT[kc * 128:kc * 128 + nr, tb * TB:(tb + 1) * TB], st.t[0:nr, :], r=[st.k])
        S.finish("sp")
        print("sched: inst=%d waits=%d cnt=%s" % (S.n_inst, S.n_wait, S.cnt))
    return nc


W_NAMES = ("w_ffn1_in", "w_ffn1_out", "w_ffn2_in", "w_ffn2_out", "w_in", "w_mem_kv", "w_br_ssd", "w_br_dsa", "w_br_mem", "w_out")


def core_inputs(inp, b, vec, cst):
    im = {"xT": np.ascontiguousarray(inp["x"][b].T), "memT": np.ascontiguousarray(inp["mem"][b].T),
          "pos": np.ascontiguousarray(np.broadcast_to(inp["positions"][b].reshape(1, L).astype(np.int32), (128, L))), "vecs": vec, "consts": cst,
          "consts2": make_consts2()}
    for n in W_NAMES:
        im[n] = inp[n]
    return im


P_SPECS = [("z", O_Z, 4096, BF16), ("xbc", O_XBC, 6144, BF16), ("dt", O_DT, 64, F32), ("q", O_Q, 2048, BF16),
           ("k", O_K, 128, BF16), ("v", O_V, 128, BF16), ("qi", O_QI, 1024, BF16), ("ki", O_KI, 64, BF16),
           ("wi", O_WI, 16, F32), ("qm", O_QM, 2048, BF16), ("g", O_G, 6144, BF16)]


def phase_inproj(c, w_in):
    S = c.S
    hT = c.hT
    jobs = []
    meta = []
    for name, off, width, dt_ in P_SPECS:
        for f0 in range(0, width, 128):
            nco = min(128, width - f0)
            jobs.append([(w_in, off + f0, nco, KC_D)])
            meta.append((name, f0, nco, dt_))
    cnt = [0]

    def epi(ji, tb, pset):
        name, f0, nco, dt_ = meta[ji]
        i = cnt[0]
        cnt[0] += 1
        if dt_ == F32:
            st = c.stg_f32[i % 4]
        else:
            st = c.stg_bf[i % 3]
        if i % 2 == 0:
            S.op("act", lambda e: e.copy(out=st.t[0:nco, :], in_=pset[0].t[0:nco, :]), r=[pset[0].k], w=[st.k])
        else:
            S.op("dve", lambda e: e.tensor_copy(out=st.t[0:nco, :], in_=pset[0].t[0:nco, :]), r=[pset[0].k], w=[st.k])
        S.dma("sp", c.P[name][f0:f0 + nco, tb * TB:(tb + 1) * TB], st.t[0:nco, :], r=[st.k], w=[("P", name, f0, tb)])

    stream_linear(c, jobs, lambda gi, kc, tb: hT[:, kc, tb * TB:(tb + 1) * TB],
                  lambda gi, tb: ["hT%d" % tb], NTB, epi, c.wslots, [[c.psb[i]] for i in range(8)])
    sched_barrier(S)


def phase_mem(c, memT, gcol, w_kv):
    S = c.S
    r = c.R44.t
    mem_in = r[:, 0:8192].bitcast(F32).rearrange("p (kc t) -> p kc t", t=MEM)
    mnT = r[:, 8192:12288].rearrange("p (kc t) -> p kc t", t=MEM)
    kmT = r[:, 12288:16384].rearrange("p (kc t) -> p kc t", t=MEM)
    vm = r[:, 16384:20480].rearrange("p (mt d) -> p mt d", d=D)
    memv = memT.rearrange("(kc p) t -> p kc t", p=128)
    S.dma("sp", mem_in, memv, w=["mem_in"])
    pb = c.psb[0]
    for kc in range(KC_D):
        sq = c.stg_bf[kc % 3]
        S.op("act", lambda e, sq=sq, kc=kc: e.activation(out=sq.t[:, 0:MEM], in_=mem_in[:, kc, :], func=AF.Square),
             r=["mem_in"], w=[sq.k])
        S.op("pe", lambda e, sq=sq, kc=kc: e.matmul(pb.t[:, 0:MEM], c.ones_bf.t[:, :], sq.t[:, 0:MEM],
                                                    start=(kc == 0), stop=(kc == KC_D - 1)), r=[sq.k], w=[pb.k])
    rt, rs = c.stg_f32[0], c.stg_f32[1]
    S.op("act", lambda e: e.activation(out=rt.t[:, 0:MEM], in_=pb.t[:, 0:MEM], func=AF.Sqrt, scale=1.0 / D,
                                       bias=c.eps_col.t[:, 0:1]), r=[pb.k], w=[rt.k])
    S.op("dve", lambda e: e.reciprocal(out=rs.t[:, 0:MEM], in_=rt.t[:, 0:MEM]), r=[rt.k], w=[rs.k])
    for kc in range(KC_D):
        S.op("dve", lambda e, kc=kc: e.scalar_tensor_tensor(out=mnT[:, kc, :], in0=mem_in[:, kc, :],
                                                             scalar=gcol[:, kc:kc + 1], in1=rs.t[:, 0:MEM],
                                                             op0=ALU.mult, op1=ALU.mult),
             r=["mem_in", rs.k], w=["mnT"])
    jobs = [[(w_kv, fc * 128, 128, KC_D)] for fc in range(16)]

    def epi_k(ji, tb, pset):
        S.op("act", lambda e: e.copy(out=kmT[:, ji, :], in_=pset[0].t[:, 0:MEM]), r=[pset[0].k], w=["kmT"])
    stream_linear(c, jobs, lambda gi, kc, tb: mnT[:, kc, :], lambda gi, tb: ["mnT"], 1, epi_k, c.wslots,
                  [[c.psb[i]] for i in range(1, 5)], tbw=MEM)
    wv = w_kv.rearrange("(kc p) f -> p kc f", p=128)
    for cb in range(8):
        ws = c.wslots[cb % 6]
        dst = ws.t[:, 0:4096].rearrange("p (kc f) -> p kc f", f=256)
        S.dma("pool", dst[:, 0:8, :], wv[:, 0:8, D + cb * 256:D + (cb + 1) * 256], w=[ws.k + "_a"])
        S.dma("pool", dst[:, 8:16, :], wv[:, 8:16, D + cb * 256:D + (cb + 1) * 256], w=[ws.k + "_b"])
        for mt in range(2):
            pbv = c.psb[5 + mt]

            def mm(e, dst=dst, pbv=pbv, mt=mt):
                ins = None
                for kc in range(KC_D):
                    ins = e.matmul(pbv.t[:, 0:256], mnT[:, kc, mt * 128:(mt + 1) * 128], dst[:, kc, :],
                                   start=(kc == 0), stop=(kc == KC_D - 1))
                return ins
            S.op("pe", mm, r=[ws.k + "_a", ws.k + "_b", "mnT"], w=[pbv.k])
            S.op("dve", lambda e, pbv=pbv, mt=mt, cb=cb: e.tensor_copy(out=vm[:, mt, cb * 256:(cb + 1) * 256], in_=pbv.t[:, 0:256]),
                 r=[pbv.k], w=["vm"])
    sched_barrier(S)
    qmv = c.P["qm"].rearrange("(kc p) t -> p kc t", p=128)
    qblk = c.R64.t[:, 0:16 * TB].rearrange("p (kc t) -> p kc t", t=TB)
    pT = [c.R64.t[:, 16 * TB + i * TB:16 * TB + (i + 1) * TB] for i in range(2)]
    scale = 512.0 ** -0.5
    for tb in range(NTB):
        t0 = tb * TB
        S.dma("sp", qblk, qmv[:, :, t0:t0 + TB], w=["qblk"])
        for hd in range(4):
            for mt in range(2):
                pl = c.psb[mt]

                def mmq(e, pl=pl, hd=hd, mt=mt):
                    ins = None
                    for j in range(4):
                        ins = e.matmul(pl.t[:, :], kmT[:, hd * 4 + j, mt * 128:(mt + 1) * 128], qblk[:, hd * 4 + j, :],
                                       start=(j == 0), stop=(j == 3))
                    return ins
                S.op("pe", mmq, r=["kmT", "qblk"], w=[pl.k])
                S.op("act", lambda e, pl=pl, mt=mt: e.activation(out=pT[mt], in_=pl.t[:, :], func=AF.Exp, scale=scale),
                     r=[pl.k], w=["pT%d" % mt])
            prs = c.psb[2]

            def mmrs(e):
                e.matmul(prs.t[:, :], c.ones_bf.t[:, :], pT[0], start=True, stop=False)
                return e.matmul(prs.t[:, :], c.ones_bf.t[:, :], pT[1], start=False, stop=True)
            S.op("pe", mmrs, r=["pT0", "pT1"], w=[prs.k])
            rinv = c.stg_f32[0]
            S.op("dve", lambda e: e.reciprocal(out=rinv.t[:, :], in_=prs.t[:, :]), r=[prs.k], w=[rinv.k])
            for j in range(4):
                po = c.psb[3 + j]
                ch = hd * 4 + j

                def mmo(e, po=po, ch=ch):
                    e.matmul(po.t[:, :], vm[:, 0, ch * 128:(ch + 1) * 128], pT[0], start=True, stop=False)
                    return e.matmul(po.t[:, :], vm[:, 1, ch * 128:(ch + 1) * 128], pT[1], start=False, stop=True)
                S.op("pe", mmo, r=["pT0", "pT1", "vm"], w=[po.k])
                st = c.stg_bf[j % 3]
                S.op("dve", lambda e, po=po, st=st: e.tensor_tensor(out=st.t[:, :], in0=po.t[:, :], in1=rinv.t[:, :], op=ALU.mult),
                     r=[po.k, rinv.k], w=[st.k])
                S.dma("sp", c.ymem[ch * 128:(ch + 1) * 128, t0:t0 + TB], st.t[:, :], r=[st.k], w=[("ymem", ch, tb)])
    sched_barrier(S)


def phase_merge(c, w_ssd, w_dsa, w_memw, w_out, x_src, x_dst):
    S = c.S
    mT = c.R64.t[:, :].rearrange("p (kc t) -> p kc t", t=L)
    ysv = c.yssd.rearrange("(kc p) t -> p kc t", p=128)
    ydv = c.ydsa.rearrange("(kc p) t -> p kc t", p=128)
    ymv = c.ymem.rearrange("(kc p) t -> p kc t", p=128)
    gv = c.P["g"]
    TBM = 256
    ys = c.R44.t[:, 0:32 * TBM].rearrange("p (kc t) -> p kc t", t=TBM)
    yd = c.R44.t[:, 32 * TBM:48 * TBM].rearrange("p (kc t) -> p kc t", t=TBM)
    ym = c.R44.t[:, 48 * TBM:64 * TBM].rearrange("p (kc t) -> p kc t", t=TBM)
    gt = [c.R44.t[:, 64 * TBM + i * TBM:64 * TBM + (i + 1) * TBM] for i in range(6)]
    for tb in range(L // TBM):
        t0 = tb * TBM
        S.dma("sp", ys[:, 0:16, :], ysv[:, 0:16, t0:t0 + TBM], w=["ys0"])
        S.dma("sp", ys[:, 16:32, :], ysv[:, 16:32, t0:t0 + TBM], w=["ys1"])
        S.dma("sp", yd, ydv[:, :, t0:t0 + TBM], w=["yd"])
        S.dma("sp", ym, ymv[:, :, t0:t0 + TBM], w=["ym"])
        jobs = [[(w_ssd, dc * 128, 128, 32), (w_dsa, dc * 128, 128, 16), (w_memw, dc * 128, 128, 16)] for dc in range(KC_D)]
        cnt = [0]

        def rhs_fn(gi, kc, tb_):
            return (ys, yd, ym)[gi][:, kc, :]

        def rhs_keys(gi, tb_):
            return (["ys0", "ys1"], ["yd"], ["ym"])[gi]

        def epi(ji, tb_, pset, t0=t0):
            i = cnt[0]
            cnt[0] += 1
            acc = c.stg_f32[i % 2]
            for bi in range(3):
                g = gt[(i % 2) * 3 + bi]
                gk = "gt%d" % ((i % 2) * 3 + bi)
                S.dma("sp", g, gv[bi * D + ji * 128:bi * D + (ji + 1) * 128, t0:t0 + TBM], w=[gk])
                sg = c.stg_f32[2 + (bi % 2)]
                S.op("act", lambda e, g=g, sg=sg: e.activation(out=sg.t[:, 0:TBM], in_=g, func=AF.Sigmoid), r=[gk], w=[sg.k])
                if bi == 0:
                    S.op("dve", lambda e, sg=sg: e.tensor_tensor(out=acc.t[:, 0:TBM], in0=pset[0].t[:, 0:TBM], in1=sg.t[:, 0:TBM], op=ALU.mult),
                         r=[pset[0].k, sg.k], w=[acc.k])
                else:
                    S.op("dve", lambda e, sg=sg, bi=bi: e.tensor_tensor(out=sg.t[:, 0:TBM], in0=pset[bi].t[:, 0:TBM], in1=sg.t[:, 0:TBM], op=ALU.mult),
                         r=[pset[bi].k, sg.k], w=[sg.k])
                    if bi == 1:
                        S.op("dve", lambda e, sg=sg: e.tensor_tensor(out=acc.t[:, 0:TBM], in0=acc.t[:, 0:TBM], in1=sg.t[:, 0:TBM], op=ALU.add),
                             r=[acc.k, sg.k], w=[acc.k])
                    else:
                        S.op("dve", lambda e, sg=sg: e.tensor_tensor(out=mT[:, ji, t0:t0 + TBM], in0=acc.t[:, 0:TBM], in1=sg.t[:, 0:TBM], op=ALU.add),
                             r=[acc.k, sg.k], w=["mT%d" % (t0 // TB)])

        stream_linear(c, jobs, rhs_fn, rhs_keys, 1, epi, c.wslots,
                      [[c.psb[0], c.psb[1], c.psb[2]], [c.psb[3], c.psb[4], c.psb[5]]], tbw=TBM)
    sched_barrier(S)
    jobs = [[(w_out, dc * 128, 128, KC_D)] for dc in range(KC_D)]
    cnt2 = [0]

    def epi_o(ji, tb, pset):
        i = cnt2[0]
        cnt2[0] += 1
        t0 = tb * TB
        xt = c.stg_f32[i % 2]
        S.dma("sp", xt.t[:, :], x_src[ji * 128:(ji + 1) * 128, t0:t0 + TB], w=[xt.k])
        xo = c.stg_f32[2 + i % 2]
        S.op("dve", lambda e: e.tensor_tensor(out=xo.t[:, :], in0=pset[0].t[:, :], in1=xt.t[:, :], op=ALU.add),
             r=[pset[0].k, xt.k], w=[xo.k])
        S.dma("sp", x_dst[ji * 128:(ji + 1) * 128, t0:t0 + TB], xo.t[:, :], r=[xo.k])
    stream_linear(c, jobs, lambda gi, kc, tb: mT[:, kc, tb * TB:(tb + 1) * TB], lambda gi, tb: ["mT%d" % tb], NTB, epi_o,
                  c.wslots, [[c.psb[i]] for i in range(8)])
    sched_barrier(S)


C_INVA, C_INVI, C_SGNA, C_SGNI = 256, 257, 258, 259
C_PA, C_PI, C_CAUS = 264, 392, 520
C_U = 648
NCONST2 = 776
TWO_PI = 6.283185307179586
CW1 = 6.28125
CW2 = TWO_PI - CW1
PI_F = 3.1415925


def make_consts2():
    cst = np.zeros((128, NCONST2), np.float32)
    cst[:, 0:NCONST] = make_consts()
    th = np.float32(500000.0)
    inv_a = np.power(th, -(np.arange(0, 32, 2, dtype=np.float32) / np.float32(32))).astype(np.float32)
    inv_i = np.power(th, -(np.arange(0, 16, 2, dtype=np.float32) / np.float32(16))).astype(np.float32)
    for d_ in range(128):
        if d_ < 32:
            cst[d_, C_INVA] = inv_a[d_ % 16]
            cst[d_, C_SGNA] = -1.0 if d_ < 16 else 1.0
        e = d_ % 64
        if e < 16:
            cst[d_, C_INVI] = inv_i[e % 8]
            cst[d_, C_SGNI] = -1.0 if e < 8 else 1.0
    for dp in range(32):
        dsrc = dp + 16 if dp < 16 else dp - 16
        cst[dsrc, C_PA + dp] = 1.0
    for blk in (0, 64):
        for ep in range(16):
            esrc = ep + 8 if ep < 8 else ep - 8
            cst[blk + esrc, C_PI + blk + ep] = 1.0
    q = np.arange(128)[:, None]
    k = np.arange(128)[None, :]
    cst[:, C_CAUS:C_CAUS + 128] = np.where(k <= q, 0.0, -1e30).astype(np.float32)
    cst[:, C_U:C_U + 128] = (q <= k).astype(np.float32)
    return cst


def _sin_table(c, out_bf, ang, sgncol, tmp_u, tmp_n, tmp_r, tag):
    S = c.S
    ni = tmp_n.bitcast(I32)
    S.op("dve", lambda e: e.tensor_scalar(out=tmp_u, in0=ang, scalar1=1.0 / TWO_PI, scalar2=None, op0=ALU.mult), r=[tag + "ang"], w=[tag + "u"])
    S.op("dve", lambda e: e.tensor_copy(out=ni, in_=tmp_u), r=[tag + "u"], w=[tag + "n"])
    S.op("dve", lambda e: e.tensor_copy(out=tmp_u, in_=ni), r=[tag + "n"], w=[tag + "u"])
    S.op("dve", lambda e: e.scalar_tensor_tensor(out=tmp_r, in0=tmp_u, scalar=-CW1, in1=ang, op0=ALU.mult, op1=ALU.add),
         r=[tag + "u", tag + "ang"], w=[tag + "r"])
    S.op("dve", lambda e: e.scalar_tensor_tensor(out=tmp_r, in0=tmp_u, scalar=-CW2, in1=tmp_r, op0=ALU.mult, op1=ALU.add),
         r=[tag + "u", tag + "r"], w=[tag + "r"])
    S.op("dve", lambda e: e.tensor_scalar(out=tmp_u, in0=tmp_r, scalar1=PI_F, scalar2=None, op0=ALU.is_gt), r=[tag + "r"], w=[tag + "u"])
    S.op("dve", lambda e: e.scalar_tensor_tensor(out=tmp_r, in0=tmp_u, scalar=-TWO_PI, in1=tmp_r, op0=ALU.mult, op1=ALU.add),
         r=[tag + "u", tag + "r"], w=[tag + "r"])
    S.op("dve", lambda e: e.tensor_scalar(out=tmp_u, in0=tmp_r, scalar1=-PI_F, scalar2=None, op0=ALU.is_lt), r=[tag + "r"], w=[tag + "u"])
    S.op("dve", lambda e: e.scalar_tensor_tensor(out=tmp_r, in0=tmp_u, scalar=TWO_PI, in1=tmp_r, op0=ALU.mult, op1=ALU.add),
         r=[tag + "u", tag + "r"], w=[tag + "r"])
    S.op("dve", lambda e: e.tensor_scalar(out=tmp_r, in0=tmp_r, scalar1=PI_F, scalar2=-PI_F, op0=ALU.min, op1=ALU.max), r=[tag + "r"], w=[tag + "r"])
    if sgncol is None:
        S.op("act", lambda e: e.activation(out=out_bf, in_=tmp_r, func=AF.Sin), r=[tag + "r"], w=["tables"])
    else:
        S.op("act", lambda e: e.activation(out=out_bf, in_=tmp_r, func=AF.Sin, scale=sgncol), r=[tag + "r"], w=["tables"])


def _rope(c, X, ncols, cosT, sinT, PT, xkeys, pbank, i):
    S = c.S
    S.op("pe", lambda e: e.matmul(pbank.t[:, 0:ncols], PT, X, start=True, stop=True), r=list(xkeys) + ["dsaconst"], w=[pbank.k])
    t1 = c.stg_f32[i % 2]
    t2 = c.stg_f32[2 + i % 2]
    S.op("dve", lambda e: e.tensor_tensor(out=t1.t[:, 0:ncols], in0=X, in1=cosT, op=ALU.mult), r=list(xkeys) + ["tables"], w=[t1.k])
    S.op("dve", lambda e: e.tensor_tensor(out=t2.t[:, 0:ncols], in0=pbank.t[:, 0:ncols], in1=sinT, op=ALU.mult), r=[pbank.k, "tables"], w=[t2.k])
    S.op("pool", lambda e: e.tensor_tensor(out=X, in0=t1.t[:, 0:ncols], in1=t2.t[:, 0:ncols], op=ALU.add), r=[t1.k, t2.k], w=list(xkeys))


def phase_dsa(c, l):
    S = c.S
    R6, R4 = c.R64.t, c.R44.t
    qblk = R6[:, 0:8192].rearrange("p (h t) -> p h t", t=TB)
    selT = R6[:, 8192:16384].rearrange("p (k t) -> p k t", t=TB)
    qiblk = R6[:, 16384:20480].rearrange("p (h t) -> p h t", t=TB)
    cosA, sinA, cosI, sinI = (R6[:, 20480 + i * 2048:20480 + (i + 1) * 2048] for i in range(4))
    krT = R6[:, 28672:30720]
    kir2 = R6[:, 30720:32768]
    acc = R4[:, 0:4096].bitcast(F32)
    work = R4[:, 4096:8192].bitcast(F32)
    sel01 = R4[:, 8192:10240]
    v_tok = R4[:, 10240:12288].rearrange("p (k d) -> p k d", d=128)
    wi_tok = R4[:, 12288:12800].bitcast(F32).rearrange("p (q h) -> p q h", h=16)
    m8 = R4[:, 12800:12816].bitcast(F32)
    pTs = [R4[:, 13312 + i * 512:13312 + (i + 1) * 512] for i in range(6)]
    tmp3 = R4[:, 8192:12288].bitcast(F32)
    cst2 = c.wslots[0].t[:, 0:2 * NCONST2].bitcast(F32)
    PA_bf = c.wslots[1].t[:, 0:128]
    PI_bf = c.wslots[1].t[:, 128:256]
    posi = c.wslots[2].t[:, 0:4096].bitcast(I32)
    vT_sb = c.wslots[3].t[:, 0:2048]
    wiT_sb = c.wslots[4].t[:, 0:4096].bitcast(F32)
    ident_f = c.cst.t[:, C_IDENT:C_IDENT + 128]

    S.dma("sp", cst2, c.consts2_d, w=["cst2"])
    S.dma("sp", posi, c.pos, w=["posi"])
    S.op("dve", lambda e: e.tensor_copy(out=PA_bf, in_=cst2[:, C_PA:C_PA + 128]), r=["cst2"], w=["dsaconst"])
    S.op("dve", lambda e: e.tensor_copy(out=PI_bf, in_=cst2[:, C_PI:C_PI + 128]), r=["cst2"], w=["dsaconst"])
    S.op("dve", lambda e: e.tensor_copy(out=acc, in_=posi), r=["posi"], w=["posf"])
    for tname, invc, sgnc, cosT, sinT in (("A", C_INVA, C_SGNA, cosA, sinA), ("I", C_INVI, C_SGNI, cosI, sinI)):
        S.op("dve", lambda e, invc=invc: e.tensor_scalar(out=work, in0=acc, scalar1=cst2[:, invc:invc + 1], scalar2=None, op0=ALU.mult),
             r=["posf", "cst2"], w=[tname + "sang"])
        u_t = tmp3
        r_t = c.wslots[5].t[:, 0:4096].bitcast(F32)
        _sin_table(c, sinT, work, cst2[:, sgnc:sgnc + 1], u_t, u_t, r_t, tname + "s")
        S.op("dve", lambda e: e.tensor_scalar(out=work, in0=work, scalar1=1.5707963267948966, scalar2=None, op0=ALU.add),
             r=[tname + "sang", tname + "sr", tname + "su"], w=[tname + "cang"])
        _sin_table(c, cosT, work, None, u_t, u_t, r_t, tname + "c")
        sched_barrier(S)
    S.dma("sp", krT, c.P["k"], w=["krT"])
    S.dma("sp", kir2[0:64, :], c.P["ki"], w=["kir2"])
    S.dma("sp", kir2[64:128, :], c.P["ki"], w=["kir2b"])
    S.dma("sp", vT_sb, c.P["v"], w=["vT"])
    S.dma("sp", wiT_sb[0:16, :], c.P["wi"], w=["wiT"])
    for tb in range(NTB):
        sl = slice(tb * TB, (tb + 1) * TB)
        _rope(c, krT[:, sl], TB, cosA[:, sl], sinA[:, sl], PA_bf, ["krT"], c.psb[tb % 4], tb)
    for tb in range(NTB):
        sl = slice(tb * TB, (tb + 1) * TB)
        _rope(c, kir2[:, sl], TB, cosI[:, sl], sinI[:, sl], PI_bf, ["kir2", "kir2b"], c.psb[4 + tb % 4], tb)
    for half in range(2):
        pb = c.psb[half]
        pbv = pb.t[:, :].bitcast(BF16)

        def tr(e, half=half, pbv=pbv):
            ins = None
            for j in range(8):
                kt = half * 8 + j
                ins = e.transpose(pbv[:, j * 128:(j + 1) * 128], vT_sb[:, kt * 128:(kt + 1) * 128], c.ident_bf.t[:, :])
            return ins
        S.op("pe", tr, r=["vT", "const"], w=[pb.k])
        S.op("act", lambda e, half=half, pbv=pbv: e.copy(out=v_tok[:, half * 8:(half + 1) * 8, :],
                                                       in_=pbv[:, 0:1024].rearrange("p (k d) -> p k d", d=128)),
             r=[pb.k], w=["v_tok"])
    pbw = c.psb[2]

    def trw(e):
        ins = None
        for qt in range(16):
            ins = e.transpose(pbw.t[:, qt * 16:(qt + 1) * 16], wiT_sb[0:16, qt * 128:(qt + 1) * 128], ident_f[0:16, 0:16])
        return ins
    S.op("pe", trw, r=["wiT", "cst"], w=[pbw.k])
    S.op("act", lambda e: e.activation(out=wi_tok, in_=pbw.t[:, 0:256].rearrange("p (q h) -> p q h", h=16), func=AF.Copy,
                                       scale=0.25 * 0.125), r=[pbw.k], w=["wi_tok"])
    sched_barrier(S)

    qv = c.P["q"].rearrange("(h p) t -> p h t", p=128)
    qiv = c.P["qi"].rearrange("(h p) t -> p h t", p=128)
    scale = 128.0 ** -0.5
    NEG = -1e30
    for b in range(NTB):
        t0 = b * TB
        S.dma("sp", qblk[:, 0:8, :], qv[:, 0:8, t0:t0 + TB], w=["qblk0"])
        S.dma("sp", qblk[:, 8:16, :], qv[:, 8:16, t0:t0 + TB], w=["qblk1"])
        S.dma("sp", qiblk, qiv[:, :, t0:t0 + TB], w=["qiblk"])
        for h in range(16):
            _rope(c, qblk[:, h, :], TB, cosA[:, t0:t0 + TB], sinA[:, t0:t0 + TB], PA_bf, ["qblk%d" % (h // 8)], c.psb[h % 4], h)
        for ch in range(8):
            _rope(c, qiblk[:, ch, :], TB, cosI[:, t0:t0 + TB], sinI[:, t0:t0 + TB], PI_bf, ["qiblk"], c.psb[4 + ch % 4], ch)
        S.op("pool", lambda e, b=b: e.memset(selT[:, 4 * b:4 * b + 4, :], 0.0), w=["selT"])
        for qi_ in range(4):
            qt = 4 * b + qi_
            nk = 128 * (qt + 1)
            nkb = (nk + TB - 1) // TB
            cnt = 0
            for kb in range(nkb):
                w_ = min(TB, nk - kb * TB)
                for h in range(16):
                    chn, hf = h // 2, h % 2
                    pb = c.psb[cnt % 4]
                    tmp = c.stg_f32[cnt % 4]
                    cnt += 1
                    S.op("pe", lambda e, pb=pb, chn=chn, hf=hf, kb=kb, w_=w_, qi_=qi_: e.matmul(
                        pb.t[:, 0:w_], qiblk[hf * 64:(hf + 1) * 64, chn, qi_ * 128:(qi_ + 1) * 128],
                        kir2[hf * 64:(hf + 1) * 64, kb * TB:kb * TB + w_], start=True, stop=True),
                        r=["qiblk", "kir2", "kir2b"], w=[pb.k])
                    if h == 0:
                        S.op("dve", lambda e, pb=pb, kb=kb, w_=w_, qt=qt, h=h: e.tensor_scalar(
                            out=acc[:, kb * TB:kb * TB + w_], in0=pb.t[:, 0:w_], scalar1=0.0, scalar2=wi_tok[:, qt, h:h + 1],
                            op0=ALU.max, op1=ALU.mult), r=[pb.k, "wi_tok"], w=["acc%d" % kb])
                    else:
                        S.op("dve", lambda e, pb=pb, tmp=tmp, w_=w_, qt=qt, h=h: e.tensor_scalar(
                            out=tmp.t[:, 0:w_], in0=pb.t[:, 0:w_], scalar1=0.0, scalar2=wi_tok[:, qt, h:h + 1],
                            op0=ALU.max, op1=ALU.mult), r=[pb.k, "wi_tok"], w=[tmp.k])
                        S.op("pool", lambda e, tmp=tmp, kb=kb, w_=w_: e.tensor_tensor(
                            out=acc[:, kb * TB:kb * TB + w_], in0=acc[:, kb * TB:kb * TB + w_], in1=tmp.t[:, 0:w_], op=ALU.add),
                            r=[tmp.k, "acc%d" % kb], w=["acc%d" % kb])
            acck = ["acc%d" % kb for kb in range(nkb)]
            S.op("pool", lambda e, nk=nk: e.tensor_tensor(out=acc[:, nk - 128:nk], in0=acc[:, nk - 128:nk],
                                                          in1=cst2[:, C_CAUS:C_CAUS + 128], op=ALU.add),
                 r=acck + ["cst2"], w=acck)
            if qt >= 2:
                for rd in range(32):
                    src = acc if rd == 0 else work
                    sk = acck if rd == 0 else ["work"]
                    S.op("dve", lambda e, src=src, nk=nk: e.max(out=m8, in_=src[:, 0:nk]), r=sk, w=["m8"])
                    if rd < 31:
                        S.op("dve", lambda e, src=src, nk=nk: e.match_replace(out=work[:, 0:nk], in_to_replace=m8,
                                                                             in_values=src[:, 0:nk], imm_value=NEG),
                             r=sk + ["m8"], w=["work"])
            else:
                S.op("dve", lambda e: e.memset(m8, -1e29), w=["m8"])
            S.op("dve", lambda e, nk=nk: e.tensor_scalar(out=sel01[:, 0:nk], in0=acc[:, 0:nk], scalar1=m8[:, 7:8], scalar2=None,
                                                         op0=ALU.is_ge), r=acck + ["m8"], w=["sel01"])
            for k0 in range(0, qt + 1, 8):
                n_ = min(8, qt + 1 - k0)
                pb = c.psb[4 + (k0 // 8) % 2]
                pbv = pb.t[:, :].bitcast(BF16)

                def trs(e, k0=k0, n_=n_, pbv=pbv):
                    ins = None
                    for j in range(n_):
                        ins = e.transpose(pbv[:, j * 128:(j + 1) * 128], sel01[:, (k0 + j) * 128:(k0 + j + 1) * 128], c.ident_bf.t[:, :])
                    return ins
                S.op("pe", trs, r=["sel01", "const"], w=[pb.k])
                S.op("act", lambda e, k0=k0, n_=n_, pbv=pbv, qi_=qi_: e.copy(
                    out=selT[:, k0:k0 + n_, qi_ * 128:(qi_ + 1) * 128],
                    in_=pbv[:, 0:n_ * 128].rearrange("p (k d) -> p k d", d=128)), r=[pb.k], w=["selT"])
        nkt = 4 * (b + 1)
        for h in range(16):
            po = c.psb[3 + h % 2]
            prs = c.psb[5 + h % 2]
            qk = "qblk%d" % (h // 8)
            pend = []

            def pv(kt, pTm, pk, po=po, prs=prs, nkt=nkt):
                S.op("pe", lambda e: e.matmul(po.t[:, :], v_tok[:, kt, :], pTm, start=(kt == 0), stop=(kt == nkt - 1)),
                     r=[pk, "v_tok"], w=[po.k])
                S.op("pe", lambda e: e.matmul(prs.t[:, :], c.ones_bf.t[:, :], pTm, start=(kt == 0), stop=(kt == nkt - 1)),
                     r=[pk, "const"], w=[prs.k])
            for kt in range(nkt):
                pl = c.psb[kt % 3]
                S.op("pe", lambda e, pl=pl, kt=kt, h=h: e.matmul(pl.t[:, :], krT[:, kt * 128:(kt + 1) * 128], qblk[:, h, :],
                                                                start=True, stop=True), r=["krT", qk], w=[pl.k])
                pT = pTs[kt % 3]
                pTm = pTs[3 + kt % 3]
                S.op("act", lambda e, pl=pl, pT=pT: e.activation(out=pT, in_=pl.t[:, :], func=AF.Exp, scale=scale),
                     r=[pl.k], w=["pT%d" % (kt % 3)])
                S.op("dve", lambda e, pT=pT, pTm=pTm, kt=kt: e.tensor_tensor(out=pTm, in0=pT, in1=selT[:, kt, :], op=ALU.mult),
                     r=["pT%d" % (kt % 3), "selT"], w=["pTm%d" % (kt % 3)])
                pend.append((kt, pTm, "pTm%d" % (kt % 3)))
                if len(pend) > 2:
                    pv(*pend.pop(0))
            while pend:
                pv(*pend.pop(0))
            rinv = c.stg_f32[h % 2]
            S.op("dve", lambda e, prs=prs, rinv=rinv: e.reciprocal(out=rinv.t[:, :], in_=prs.t[:, :]), r=[prs.k], w=[rinv.k])
            st = c.stg_bf[h % 3]
            S.op("dve", lambda e, po=po, rinv=rinv, st=st: e.tensor_tensor(out=st.t[:, :], in0=po.t[:, :], in1=rinv.t[:, :], op=ALU.mult),
                 r=[po.k, rinv.k], w=[st.k])
            S.dma("sp", c.ydsa[h * 128:(h + 1) * 128, t0:t0 + TB], st.t[:, :], r=[st.k], w=[("ydsa", h, b)])
        sched_barrier(S)


def phase_ssd(c, l, V):
    S = c.S
    R6, R4 = c.R64.t, c.R44.t
    ident_f = c.cst.t[:, C_IDENT:C_IDENT + 128]
    xraw = [R6[:, i * 2560:i * 2560 + 2051] for i in range(2)]
    xout = [R6[:, 8192 + i * 2048:8192 + (i + 1) * 2048] for i in range(2)]
    diag = [R4[:, i * 512:(i + 1) * 512].rearrange("p (k c) -> p k c", c=128) for i in range(2)]
    for i in range(2):
        S.op("dve", lambda e, i=i: e.memset(xraw[i][:, 0:3], 0.0), w=["xraw%d" % i])
    for cc in range(48):
        i = cc % 2
        xr, xo, dg = xraw[i], xout[i], diag[i]
        S.dma("sp", xr[:, 3:2051], c.P["xbc"][cc * 128:(cc + 1) * 128, :], w=["xraw%d" % i])
        for k in range(4):
            S.op("dve", lambda e, k=k, dg=dg, cc=cc: e.tensor_scalar(
                out=dg[:, k, :], in0=c.ident_bf.t[:, :], scalar1=V[:, V_CONVW + k * 48 + cc:V_CONVW + k * 48 + cc + 1],
                scalar2=None, op0=ALU.mult), r=["const", "vecs"], w=["diag%d" % i])
        for tb in range(NTB):
            pb = c.psb[(cc * NTB + tb) % 8]

            def mm(e, pb=pb, dg=dg, xr=xr, tb=tb):
                ins = None
                for k in range(4):
                    ins = e.matmul(pb.t[:, :], dg[:, k, :], xr[:, tb * TB + k:tb * TB + k + TB], start=(k == 0), stop=(k == 3))
                return ins
            S.op("pe", mm, r=["xraw%d" % i, "diag%d" % i], w=[pb.k])
            S.op("act", lambda e, pb=pb, xo=xo, tb=tb, cc=cc: e.activation(
                out=xo[:, tb * TB:(tb + 1) * TB], in_=pb.t[:, :], func=AF.Silu, bias=V[:, V_CONVB + cc:V_CONVB + cc + 1]),
                r=[pb.k, "vecs"], w=["xout%d" % i])
        S.dma("sp", c.xc_d[cc * 128:(cc + 1) * 128, :], xo, r=["xout%d" % i], w=[("xc", cc)])
    sched_barrier(S)

    AcsT = c.wslots[0].t[:, 0:4096].bitcast(F32)
    dtT = c.wslots[1].t[:, 0:4096].bitcast(F32)
    tA = c.wslots[2].t[:, 0:4096].bitcast(F32)
    tokw = c.wslots[3].t[:, 0:4096].bitcast(F32)
    dt_tok = tokw[:, 0:1024].rearrange("p (c h) -> p c h", h=64)
    Acs_tok = tokw[:, 1024:2048].rearrange("p (c h) -> p c h", h=64)
    tok2 = c.wslots[4].t[:, 0:4096].bitcast(F32)
    expA_tok = tok2[:, 0:1024].rearrange("p (c h) -> p c h", h=64)
    dA_tok = tok2[:, 1024:2048].rearrange("p (c h) -> p c h", h=64)
    acol = c.stg_f32[3].t[:, 0:1]
    cst2 = c.stg_f32[2].t[:, 0:128]
    S.dma("sp", cst2, c.consts2_d[:, C_U:C_U + 128], w=["Uf"])
    S.dma("sp", dtT[0:64, :], c.P["dt"], w=["dtT"])
    one_col = c.cst.t[0:64, C_ONES:C_ONES + 1]
    S.op("dve", lambda e: e.tensor_scalar(out=dtT[0:64, :], in0=dtT[0:64, :], scalar1=V[0:64, V_DTB:V_DTB + 1], scalar2=None, op0=ALU.add),
         r=["dtT", "vecs"], w=["dtT"])
    S.op("act", lambda e: e.activation(out=tA[0:64, :], in_=dtT[0:64, :], func=AF.Abs), r=["dtT"], w=["tA"])
    S.op("act", lambda e: e.activation(out=tA[0:64, :], in_=tA[0:64, :], func=AF.Exp, scale=-1.0), r=["tA"], w=["tA"])
    S.op("act", lambda e: e.activation(out=tA[0:64, :], in_=tA[0:64, :], func=AF.Ln, bias=one_col), r=["tA", "cst"], w=["tA"])
    S.op("dve", lambda e: e.scalar_tensor_tensor(out=dtT[0:64, :], in0=dtT[0:64, :], scalar=0.0, in1=tA[0:64, :], op0=ALU.max, op1=ALU.add),
         r=["dtT", "tA"], w=["dtT"])
    S.op("act", lambda e: e.activation(out=acol[0:64, :], in_=V[0:64, V_ALOG:V_ALOG + 1], func=AF.Exp), r=["vecs"], w=["acol"])
    S.op("dve", lambda e: e.tensor_scalar(out=tA[0:64, :], in0=dtT[0:64, :], scalar1=acol[0:64, :], scalar2=-1.0, op0=ALU.mult, op1=ALU.mult),
         r=["dtT", "acol"], w=["tA"])
    for src, dst, nm in ((dtT, dt_tok, "dt_tok"), (tA, dA_tok, "dA_tok")):
        for half in range(2):
            pb = c.psb[half]

            def tr(e, src=src, half=half, pb=pb):
                ins = None
                for j in range(8):
                    cch = half * 8 + j
                    ins = e.transpose(pb.t[:, j * 64:(j + 1) * 64], src[0:64, cch * 128:(cch + 1) * 128], ident_f[0:64, 0:64])
                return ins
            S.op("pe", tr, r=["dtT", "tA", "cst"], w=[pb.k])
            S.op("dve", lambda e, dst=dst, half=half, pb=pb: e.tensor_copy(
                out=dst[:, half * 8:(half + 1) * 8, :], in_=pb.t[:, :].rearrange("p (c h) -> p c h", h=64)), r=[pb.k], w=[nm])
    for half in range(2):
        pb = c.psb[2 + half]

        def cs(e, half=half, pb=pb):
            ins = None
            for j in range(8):
                cch = half * 8 + j
                ins = e.matmul(pb.t[:, j * 64:(j + 1) * 64], cst2, dA_tok[:, cch, :], start=True, stop=True)
            return ins
        S.op("pe", cs, r=["dA_tok", "Uf"], w=[pb.k])
        S.op("dve", lambda e, half=half, pb=pb: e.tensor_copy(out=Acs_tok[:, half * 8:(half + 1) * 8, :],
                                                           in_=pb.t[:, :].rearrange("p (c h) -> p c h", h=64)), r=[pb.k], w=["Acs_tok"])
    for q4 in range(4):
        pb = c.psb[4 + q4]

        def cs2(e, q4=q4, pb=pb):
            ins = None
            for j in range(4):
                cch = q4 * 4 + j
                ins = e.matmul(pb.t[0:64, j * 128:(j + 1) * 128], dA_tok[:, cch, :], cst2, start=True, stop=True)
            return ins
        S.op("pe", cs2, r=["dA_tok", "Uf"], w=[pb.k])
        S.op("dve", lambda e, q4=q4, pb=pb: e.tensor_copy(out=AcsT[0:64, q4 * TB:(q4 + 1) * TB], in_=pb.t[0:64, :]), r=[pb.k], w=["AcsT"])
    S.op("act", lambda e: e.activation(out=tok2[:, 0:1024], in_=tokw[:, 1024:2048], func=AF.Exp), r=["Acs_tok"], w=["expA_tok"])
    S.dma("sp", c.acs_d, AcsT[0:64, :], r=["AcsT"], w=["acs_d"])
    Dfull = c.wslots[1].t[:, 0:4096]
    sched_barrier(S)
    for h in range(NH_SSD):
        S.op("dve", lambda e, h=h: e.tensor_scalar(out=Dfull[:, h * 64:(h + 1) * 64], in0=c.cst.t[:, C_ONES:C_ONES + 64],
                                                   scalar1=V[:, V_DBC + h:V_DBC + h + 1], scalar2=None, op0=ALU.mult),
             r=["cst", "vecs"], w=["Dfull"])

    H = R6[:, 0:8192].bitcast(F32)
    Hbf = R6[:, 8192:12288]
    xs_tok = R6[:, 12288:16384]
    xw = R6[:, 16384:20480]
    y_tok = R6[:, 20480:24576]
    zc = R6[:, 24576:28672].rearrange("p (cc t) -> p cc t", t=128)
    xsT = R6[:, 28672:32768].rearrange("p (cc t) -> p cc t", t=128)
    ygf = R4[:, 0:8192].bitcast(F32).rearrange("p (cc t) -> p cc t", t=128)
    yout = R4[:, 8192:12288].rearrange("p (cc t) -> p cc t", t=128)
    BT = R4[:, 12288:13312].rearrange("p (g t) -> p g t", t=128)
    CT = R4[:, 13312:14336].rearrange("p (g t) -> p g t", t=128)
    B_tok = R4[:, 14336:15360].rearrange("p (g t) -> p g t", t=128)
    cbTm = R4[:, 15360:17408].bitcast(F32).rearrange("p (g t) -> p g t", t=128)
    Et = [R4[:, 17408 + i * 1024:17408 + (i + 1) * 1024].bitcast(F32) for i in range(2)]
    smt = [R4[:, 19456 + i * 1024:19456 + (i + 1) * 1024].bitcast(F32) for i in range(2)]
    Mp = [R4[:, 21504 + i * 512:21504 + (i + 1) * 512] for i in range(2)]
    dec = c.stg_f32[3].t[:, 64:128]
    A_rows = c.wslots[5].t[:, 0:5632].bitcast(F32)
    Uf = cst2
    pgroups = [(0, 0, 22), (32, 22, 44), (64, 44, 64)]
    batches = []
    for pbase, hs, he in pgroups:
        for h0 in range(hs, he, 4):
            batches.append((pbase, hs, h0, min(4, he - h0)))
    xcv = c.xc_d.rearrange("(cc p) t -> p cc t", p=128)
    zv = c.P["z"].rearrange("(cc p) t -> p cc t", p=128)
    yv = c.yssd.rearrange("(cc p) t -> p cc t", p=128)
    S.op("dve", lambda e: e.memset(H, 0.0), w=["H"])
    S.op("dve", lambda e: e.memset(Hbf, 0.0), w=["Hbf"])
    NCH = L // 128
    for ch in range(NCH):
        t0 = ch * 128
        S.dma("sp", xsT[:, 0:16, :], xcv[:, 0:16, t0:t0 + 128], w=["xsT0"])
        S.dma("sp", xsT[:, 16:32, :], xcv[:, 16:32, t0:t0 + 128], w=["xsT1"])
        S.dma("sp", BT, xcv[:, 32:40, t0:t0 + 128], w=["BT"])
        S.dma("sp", CT, xcv[:, 40:48, t0:t0 + 128], w=["CT"])
        S.dma("sp", zc, zv[:, :, t0:t0 + 128], w=["zc"])
        for pbase, hs, he in pgroups:
            S.dma("sp", A_rows[pbase:pbase + 1, 0:(he - hs) * 128].rearrange("o (h t) -> o h t", t=128),
                  c.acs_d[hs:he, t0:t0 + 128].rearrange("(o h) t -> o h t", o=1), r=["acs_d"], w=["A_rows"])
        pb7 = c.psb[7]
        pb7v = pb7.t[:, :].bitcast(BF16)
        for q8 in range(4):
            def trx(e, q8=q8):
                ins = None
                for j in range(8):
                    ins = e.transpose(pb7v[:, j * 128:(j + 1) * 128], xsT[:, q8 * 8 + j, :], c.ident_bf.t[:, :])
                return ins
            S.op("pe", trx, r=["xsT0", "xsT1", "const"], w=[pb7.k])
            S.op("act", lambda e, q8=q8: e.copy(out=xs_tok[:, q8 * 1024:(q8 + 1) * 1024], in_=pb7v[:, 0:1024]), r=[pb7.k], w=["xs_tok"])

        def trb(e):
            ins = None
            for g in range(8):
                ins = e.transpose(pb7v[:, g * 128:(g + 1) * 128], BT[:, g, :], c.ident_bf.t[:, :])
            return ins
        S.op("pe", trb, r=["BT", "const"], w=[pb7.k])
        S.op("act", lambda e: e.copy(out=B_tok, in_=pb7v[:, 0:1024].rearrange("p (g t) -> p g t", t=128)), r=[pb7.k], w=["B_tok"])
        for half in range(2):
            pb = c.psb[2 + half]

            def mcb(e, half=half, pb=pb):
                ins = None
                for j in range(4):
                    g = half * 4 + j
                    ins = e.matmul(pb.t[:, j * 128:(j + 1) * 128], BT[:, g, :], CT[:, g, :], start=True, stop=True)
                return ins
            S.op("pe", mcb, r=["BT", "CT"], w=[pb.k])
            for j in range(4):
                g = half * 4 + j
                S.op("dve", lambda e, g=g, j=j, pb=pb: e.tensor_tensor(out=cbTm[:, g, :], in0=pb.t[:, j * 128:(j + 1) * 128], in1=Uf, op=ALU.mult),
                     r=[pb.k, "Uf"], w=["cbTm"])
        S.op("act", lambda e: e.activation(out=zc, in_=zc, func=AF.Silu), r=["zc"], w=["zc"])
        bi = 0
        for pbase, hs, h0, nh in batches:
            pT1 = c.psb[bi % 2]
            E, sm, M = Et[bi % 2], smt[bi % 2], Mp[bi % 2]
            ek, sk, mk = "E%d" % (bi % 2), "sm%d" % (bi % 2), "Mp%d" % (bi % 2)
            bi += 1
            S.op("pe", lambda e, pT1=pT1, pbase=pbase, hs=hs, h0=h0, nh=nh: e.matmul(
                pT1.t[:, 0:nh * 128], c.cst.t[pbase:pbase + 1, C_ONES:C_ONES + 128],
                A_rows[pbase:pbase + 1, (h0 - hs) * 128:(h0 - hs + nh) * 128], start=True, stop=True),
                r=["A_rows", "cst"], w=[pT1.k])
            for i in range(nh):
                h = h0 + i
                S.op("dve", lambda e, i=i, h=h, pT1=pT1, sm=sm: e.tensor_scalar(
                    out=sm[:, i * 128:(i + 1) * 128], in0=pT1.t[:, i * 128:(i + 1) * 128], scalar1=Acs_tok[:, ch, h:h + 1], scalar2=0.0,
                    op0=ALU.subtract, op1=ALU.min), r=[pT1.k, "Acs_tok"], w=[sk])
            S.op("act", lambda e, pT1=pT1, h0=h0, nh=nh: e.activation(
                out=dec[:, h0:h0 + nh].rearrange("p (h o) -> p h o", o=1),
                in_=pT1.t[:, 0:nh * 128].rearrange("p (h t) -> p h t", t=128)[:, :, 127:128], func=AF.Exp), r=[pT1.k], w=["dec"])
            S.op("act", lambda e, E=E, sm=sm, nh=nh: e.activation(out=E[:, 0:nh * 128], in_=sm[:, 0:nh * 128], func=AF.Exp), r=[sk], w=[ek])
            for i in range(nh):
                h = h0 + i
                g = h // 8
                S.op("dve", lambda e, i=i, h=h, g=g, E=E, M=M: e.scalar_tensor_tensor(
                    out=M[:, i * 128:(i + 1) * 128], in0=E[:, i * 128:(i + 1) * 128], scalar=dt_tok[:, ch, h:h + 1], in1=cbTm[:, g, :],
                    op0=ALU.mult, op1=ALU.mult), r=[ek, "dt_tok", "cbTm"], w=[mk])
                if ch < NCH - 1:
                    S.op("pool", lambda e, i=i, h=h, E=E: e.tensor_scalar(
                        out=xw[:, h * 64:(h + 1) * 64], in0=xs_tok[:, h * 64:(h + 1) * 64], scalar1=E[:, i * 128 + 127:i * 128 + 128],
                        scalar2=dt_tok[:, ch, h:h + 1], op0=ALU.mult, op1=ALU.mult), r=[ek, "xs_tok", "dt_tok"], w=["xw%d" % g])
                po1 = c.psb[4]
                S.op("pe", lambda e, i=i, h=h, M=M, po1=po1: e.matmul(po1.t[:, (h % 8) * 64:(h % 8 + 1) * 64], M[:, i * 128:(i + 1) * 128],
                                                                     xs_tok[:, h * 64:(h + 1) * 64], start=True, stop=True),
                     r=[mk, "xs_tok"], w=[po1.k + "_h%d" % (h % 8)])
                if h % 8 == 7:
                    gs = slice(g * 512, (g + 1) * 512)
                    po2 = c.psb[5]
                    S.op("pe", lambda e, g=g, gs=gs, po2=po2: e.matmul(po2.t[:, :], CT[:, g, :], Hbf[:, gs], start=True, stop=True),
                         r=["CT", "Hbf%d" % g], w=[po2.k])
                    xd = c.stg_f32[0]
                    S.op("pool", lambda e, gs=gs, xd=xd: e.tensor_tensor(out=xd.t[:, :], in0=xs_tok[:, gs], in1=Dfull[:, gs], op=ALU.mult),
                         r=["xs_tok", "Dfull"], w=[xd.k])
                    sb1 = c.stg_f32[1]
                    S.op("dve", lambda e, po1=po1, xd=xd, sb1=sb1: e.tensor_tensor(out=sb1.t[:, :], in0=po1.t[:, :], in1=xd.t[:, :], op=ALU.add),
                         r=[po1.k + "_h%d" % r_ for r_ in range(8)] + [xd.k], w=[sb1.k] + [po1.k + "_h%d" % r_ for r_ in range(8)])
                    for r_ in range(8):
                        hh = g * 8 + r_
                        S.op("dve", lambda e, r_=r_, hh=hh, po2=po2, sb1=sb1: e.scalar_tensor_tensor(
                            out=y_tok[:, hh * 64:(hh + 1) * 64], in0=po2.t[:, r_ * 64:(r_ + 1) * 64], scalar=expA_tok[:, ch, hh:hh + 1],
                            in1=sb1.t[:, r_ * 64:(r_ + 1) * 64], op0=ALU.mult, op1=ALU.add), r=[po2.k, sb1.k, "expA_tok"], w=["y_tok"])
                    if ch < NCH - 1:
                        pS = c.psb[6]
                        S.op("pe", lambda e, g=g, gs=gs, pS=pS: e.matmul(pS.t[:, :], B_tok[:, g, :], xw[:, gs], start=True, stop=True),
                             r=["B_tok", "xw%d" % g], w=[pS.k])
                        for r_ in range(8):
                            hh = g * 8 + r_
                            S.op("pool", lambda e, hh=hh: e.tensor_scalar(out=H[:, hh * 64:(hh + 1) * 64], in0=H[:, hh * 64:(hh + 1) * 64],
                                                                           scalar1=dec[:, hh:hh + 1], scalar2=None, op0=ALU.mult),
                                 r=["dec", "H%d" % g], w=["H%d" % g])
                        S.op("dve", lambda e, gs=gs, pS=pS: e.tensor_tensor(out=H[:, gs], in0=H[:, gs], in1=pS.t[:, :], op=ALU.add),
                             r=[pS.k, "H%d" % g], w=["H%d" % g])
                        S.op("act", lambda e, gs=gs: e.copy(out=Hbf[:, gs], in_=H[:, gs]), r=["H%d" % g], w=["Hbf%d" % g])
        sq = xw.rearrange("p (cc t) -> p cc t", t=128)
        for q8 in range(4):
            def try_(e, q8=q8):
                ins = None
                for j in range(8):
                    cc = q8 * 8 + j
                    ins = e.transpose(pb7v[:, j * 128:(j + 1) * 128], y_tok[:, cc * 128:(cc + 1) * 128], c.ident_bf.t[:, :])
                return ins
            S.op("pe", try_, r=["y_tok", "const"], w=[pb7.k])
            S.op("dve", lambda e, q8=q8: e.tensor_tensor(out=ygf[:, q8 * 8:(q8 + 1) * 8, :],
                                                       in0=pb7v[:, 0:1024].rearrange("p (cc t) -> p cc t", t=128),
                                                       in1=zc[:, q8 * 8:(q8 + 1) * 8, :], op=ALU.mult), r=[pb7.k, "zc"], w=["ygf"])
        xwk = ["xw%d" % g for g in range(8)]
        S.op("act", lambda e: e.activation(out=sq, in_=ygf, func=AF.Square), r=["ygf"], w=xwk)
        pn = c.psb[6]

        def mmn(e):
            ins = None
            for cc in range(32):
                ins = e.matmul(pn.t[:, 0:128], c.ones_bf.t[:, :], sq[:, cc, :], start=(cc == 0), stop=(cc == 31))
            return ins
        S.op("pe", mmn, r=xwk + ["const"], w=[pn.k])
        rt = c.stg_f32[2].t[:, 128:256]
        rs = c.stg_f32[2].t[:, 256:384]
        S.op("act", lambda e: e.activation(out=rt, in_=pn.t[:, 0:128], func=AF.Sqrt, scale=1.0 / SSD_INNER, bias=c.eps_col.t[:, 0:1]),
             r=[pn.k], w=["rt"])
        S.op("dve", lambda e: e.reciprocal(out=rs, in_=rt), r=["rt"], w=["rs"])
        for cc in range(32):
            S.op("dve", lambda e, cc=cc: e.scalar_tensor_tensor(out=yout[:, cc, :], in0=ygf[:, cc, :], scalar=V[:, V_SSDN + cc:V_SSDN + cc + 1],
                                                                 in1=rs, op0=ALU.mult, op1=ALU.mult), r=["ygf", "rs", "vecs"], w=["yout"])
        S.dma("sp", yv[:, 0:16, t0:t0 + 128], yout[:, 0:16, :], r=["yout"], w=[("yssd", ch, 0)])
        S.dma("sp", yv[:, 16:32, t0:t0 + 128], yout[:, 16:32, :], r=["yout"], w=[("yssd", ch, 1)])
    sched_barrier(S)


FULL_PLAN = []
for _l in range(DEPTH):
    FULL_PLAN += [("ffn1", _l), ("inproj", _l), ("mem", _l), ("dsa", _l), ("ssd", _l), ("merge", _l), ("ffn2", _l)]
FULL_PLAN.append("final")
_NC_CACHE = {}


def kernel(**inputs):
    inp = {k: np.asarray(v) for k, v in inputs.items()}
    B = inp["x"].shape[0]
    if "nc" not in _NC_CACHE:
        _NC_CACHE["nc"] = build(FULL_PLAN)
    nc = _NC_CACHE["nc"]
    vec = np.stack([pack_vecs(inp, l) for l in range(DEPTH)])
    cst = make_consts()
    in_maps = [core_inputs(inp, b, vec, cst) for b in range(B)]
    res = run_bass_kernel_spmd(nc, in_maps, core_ids=list(range(B)))
    out = np.stack([np.ascontiguousarray(res.results[b]["outT"].T) for b in range(B)]).astype(np.float32)
    return out
```

```python
from contextlib import ExitStack
import numpy as np
import concourse.bass as bass
import concourse.mybir as mybir
from concourse.bass_utils import run_bass_kernel_spmd

F32 = mybir.dt.float32
BF16 = mybir.dt.bfloat16
I32 = mybir.dt.int32
AF = mybir.ActivationFunctionType
ALU = mybir.AluOpType
AX = mybir.AxisListType

D = 2048
L = 2048
DEPTH = 2
DFF = 5632
MEM = 256
EPS = 1e-6
SSD_INNER = 4096
CONV_CH = 6144
NH_SSD = 64
IN_SPLITS = (4096, 6144, 64, 2048, 128, 128, 1024, 64, 16, 2048, 6144)
IN_OFF = [0]
for _s in IN_SPLITS:
    IN_OFF.append(IN_OFF[-1] + _s)
IN_WIDTH = IN_OFF[-1]
(O_Z, O_XBC, O_DT, O_Q, O_K, O_V, O_QI, O_KI, O_WI, O_QM, O_G) = IN_OFF[:11]
TB = 512
NTB = L // TB
KC_D = D // 128


class Sched:
    ENGS = ("pe", "act", "dve", "pool", "sp")

    def __init__(self, nc, es, n_dma_sems=24):
        self.nc = nc
        self.eng = dict(pe=nc.tensor, act=nc.scalar, dve=nc.vector, pool=nc.gpsimd, sp=nc.sync)
        self.sem = {e: es.enter_context(nc.semaphore("sem_" + e)) for e in self.ENGS}
        self.cnt = {e: 0 for e in self.ENGS}
        self.seen = {e: {} for e in self.ENGS}
        self.snap = {}
        self.dsem = {}
        self.dcur = {}
        self.drr = {}
        for q in ("sp", "pool", "act"):
            n = n_dma_sems if q != "act" else 8
            self.dsem[q] = [es.enter_context(nc.semaphore("dsem_%s_%d" % (q, i))) for i in range(n)]
            self.dcur[q] = [0] * n
            self.drr[q] = 0
        self.bufs = {}
        self.n_wait = 0
        self.n_inst = 0

    def _wait(self, e, tok):
        if tok is None:
            return
        kind = tok[0]
        seen = self.seen[e]
        if kind == "c":
            _, f, c = tok
            if seen.get(f, 0) >= c:
                return
            if f == e and e == "pe":
                return
            self.eng[e].wait_ge(self.sem[f], c)
            self.n_wait += 1
            seen[f] = c
            sn = self.snap.get((f, c))
            if sn is not None:
                for g, v in zip(self.ENGS, sn):
                    if v > seen.get(g, 0):
                        seen[g] = v
        else:
            _, q, i, v = tok
            key = (q, i)
            if seen.get(key, 0) >= v:
                return
            self.eng[e].wait_ge(self.dsem[q][i], v)
            self.n_wait += 1
            seen[key] = v

    def _deps(self, e, r, w):
        for k in r:
            st = self.bufs.get(k)
            if st is not None:
                self._wait(e, st[0])
        for k in w:
            st = self.bufs.get(k)
            if st is not None:
                self._wait(e, st[0])
                for t in st[1]:
                    if t[0] == "c" and t[1] == e:
                        continue
                    self._wait(e, t)

    def _commit(self, tok, r, w):
        for k in r:
            st = self.bufs.setdefault(k, [None, []])
            st[1] = [t for t in st[1] if not (t[0] == tok[0] and t[1] == tok[1] and (t[0] == "c" or t[2] == tok[2]))]
            st[1].append(tok)
        for k in w:
            self.bufs[k] = [tok, []]

    def op(self, e, fn, r=(), w=()):
        self._deps(e, r, w)
        ins = fn(self.eng[e])
        self.cnt[e] += 1
        c = self.cnt[e]
        ins.then_inc(self.sem[e], 1)
        self.n_inst += 1
        sn = self.seen[e]
        self.snap[(e, c)] = tuple(c if g == e else sn.get(g, 0) for g in self.ENGS)
        tok = ("c", e, c)
        self._commit(tok, r, w)
        return tok

    def dma(self, q, out, in_, r=(), w=(), **kw):
        self._deps(q, r, w)
        i = self.drr[q]
        self.drr[q] = (i + 1) % len(self.dsem[q])
        cur = self.dcur[q][i]
        if cur > 0:
            self._wait(q, ("d", q, i, cur))
        self.eng[q].dma_start(out=out, in_=in_, **kw).then_inc(self.dsem[q][i], 16)
        self.dcur[q][i] = cur + 16
        tok = ("d", q, i, cur + 16)
        self._commit(tok, r, w)
        return tok

    def finish(self, e="sp"):
        for f in self.ENGS:
            if self.cnt[f] > 0 and f != e:
                self._wait(e, ("c", f, self.cnt[f]))
        for q in self.dsem:
            for i, cur in enumerate(self.dcur[q]):
                if cur > 0:
                    self._wait(e, ("d", q, i, cur))


class Buf:
    def __init__(self, t, key):
        self.t = t
        self.k = key

    def __getitem__(self, idx):
        return self.t[idx]


class Ctx:
    pass


def make_ctx(nc, es):
    c = Ctx()
    c.nc = nc
    c.es = es
    c.S = Sched(nc, es)
    c.nbuf = 0

    def sb(shape, dt, name=None):
        c.nbuf += 1
        name = name or ("sb%d" % c.nbuf)
        t = es.enter_context(nc.sbuf_tensor(name, list(shape), dt))
        return Buf(t, name)

    def ps(name, shape=(128, 512), dt=F32):
        t = es.enter_context(nc.psum_tensor(name, list(shape), dt))
        return Buf(t, name)
    c.sb = sb
    c.ps = ps
    return c


def stream_linear(c, jobs, rhs_fn, rhs_keys, ntb, epilogue, wslots, psum_sets, tbw=TB):
    S = c.S
    nslot = len(wslots)
    state = c.__dict__.setdefault("_lin_state", {"slot": 0, "pset": 0})
    flat = []
    for ji, job in enumerate(jobs):
        for gi, g in enumerate(job):
            flat.append((ji, gi, g))
    PREF = nslot - 1
    slot_of = {}

    def issue(n):
        ji, gi, (W, col0, ncols, KC) = flat[n]
        si = state["slot"]
        state["slot"] = (si + 1) % nslot
        ws = wslots[si]
        wv = W.rearrange("(kc p) f -> p kc f", p=128)
        dst = ws.t[:, 0:KC * ncols].rearrange("p (kc f) -> p kc f", f=ncols)
        half = KC // 2
        S.dma("pool", dst[:, 0:half, :], wv[:, 0:half, col0:col0 + ncols], w=[ws.k + "_a"])
        S.dma("pool", dst[:, half:KC, :], wv[:, half:KC, col0:col0 + ncols], w=[ws.k + "_b"])
        slot_of[n] = (ws, dst)

    nxt = 0
    n = 0
    for ji, job in enumerate(jobs):
        while nxt < len(flat) and nxt < n + nslot:
            issue(nxt)
            nxt += 1
        for tb in range(ntb):
            pi = state["pset"] % len(psum_sets)
            state["pset"] = (pi + 1) % len(psum_sets)
            pset = psum_sets[pi]
            for gi, (W, col0, ncols, KC) in enumerate(job):
                ws, dst = slot_of[n + gi]
                pb = pset[gi]

                def mm(e, dst=dst, pb=pb, KC=KC, gi=gi, tb=tb, ncols=ncols):
                    ins = None
                    for kc in range(KC):
                        ins = e.matmul(pb.t[0:ncols, 0:tbw], dst[:, kc, :], rhs_fn(gi, kc, tb),
                                       start=(kc == 0), stop=(kc == KC - 1))
                    return ins
                S.op("pe", mm, r=[ws.k + "_a", ws.k + "_b"] + list(rhs_keys(gi, tb)), w=[pb.k])
            epilogue(ji, tb, pset)
        n += len(job)


def sched_barrier(S):
    engs = [e for e in S.ENGS]
    toks = [("c", f, S.cnt[f]) for f in S.ENGS if S.cnt[f] > 0]
    dtoks = []
    for q in S.dsem:
        for i, cur in enumerate(S.dcur[q]):
            if cur > 0:
                dtoks.append(("d", q, i, cur))
    for e in engs:
        for t in toks:
            if t[1] != e:
                S._wait(e, t)
        for t in dtoks:
            S._wait(e, t)
    S.bufs = {}


def phase_norm(c, x_src, gcol, out_mode, out_dst=None):
    S = c.S
    xv = x_src.rearrange("(kc p) t -> p kc t", p=128)
    xin = c.R44.t[:, 0:16 * TB * 2].bitcast(F32).rearrange("p (kc t) -> p kc t", t=TB)
    hT = c.hT
    for tb in range(NTB):
        t0 = tb * TB
        for hf in range(2):
            S.dma("sp", xin[:, hf * 8:(hf + 1) * 8, :], xv[:, hf * 8:(hf + 1) * 8, t0:t0 + TB],
                  w=["xin%d" % hf])
        pb = c.psb[tb % 2]
        for kc in range(KC_D):
            sq = c.stg_bf[kc % 3]
            S.op("act", lambda e, sq=sq, kc=kc: e.activation(out=sq.t[:, :], in_=xin[:, kc, :], func=AF.Square),
                 r=["xin%d" % (kc // 8)], w=[sq.k])
            S.op("pe", lambda e, sq=sq, kc=kc, pb=pb: e.matmul(pb.t[:, :], c.ones_bf.t[:, :], sq.t[:, :],
                                                                start=(kc == 0), stop=(kc == KC_D - 1)),
                 r=[sq.k, "const"], w=[pb.k])
        rt = c.stg_f32[0]
        S.op("act", lambda e, pb=pb: e.activation(out=rt.t[:, :], in_=pb.t[:, :], func=AF.Sqrt,
                                                  scale=1.0 / D, bias=c.eps_col.t[:, 0:1]),
             r=[pb.k, "const"], w=[rt.k])
        rs = c.stg_f32[1]
        S.op("dve", lambda e: e.reciprocal(out=rs.t[:, :], in_=rt.t[:, :]), r=[rt.k], w=[rs.k])
        for kc in range(KC_D):
            if out_mode == "h":
                S.op("dve", lambda e, kc=kc: e.scalar_tensor_tensor(
                    out=hT[:, kc, t0:t0 + TB], in0=xin[:, kc, :], scalar=gcol[:, kc:kc + 1], in1=rs.t[:, :],
                    op0=ALU.mult, op1=ALU.mult),
                    r=["xin%d" % (kc // 8), rs.k, "vecs"], w=["hT%d" % tb])
            else:
                ot = c.stg_f32[2 + kc % 2]
                S.op("dve", lambda e, kc=kc, ot=ot: e.scalar_tensor_tensor(
                    out=ot.t[:, :], in0=xin[:, kc, :], scalar=gcol[:, kc:kc + 1], in1=rs.t[:, :],
                    op0=ALU.mult, op1=ALU.mult),
                    r=["xin%d" % (kc // 8), rs.k, "vecs"], w=[ot.k])
                S.dma("sp", out_dst[kc * 128:(kc + 1) * 128, t0:t0 + TB], ot.t[:, :], r=[ot.k])


def phase_ffn(c, w_in, w_out, x_src, x_dst):
    S = c.S
    NJ = DFF // 128
    hT = c.hT
    actd = c.act_d
    jobs = [[(w_in, j * 128, 128, KC_D), (w_in, DFF + j * 128, 128, KC_D)] for j in range(NJ)]
    cnt = [0]

    def epi_a(ji, tb, pset):
        i = cnt[0]
        cnt[0] += 1
        sg = c.stg_f32[i % 2]
        S.op("act", lambda e: e.activation(out=sg.t[:, :], in_=pset[0].t[:, :], func=AF.Silu),
             r=[pset[0].k], w=[sg.k])
        ab = c.stg_bf[i % 3]
        S.op("dve", lambda e: e.tensor_tensor(out=ab.t[:, :], in0=pset[1].t[:, :], in1=sg.t[:, :], op=ALU.mult),
             r=[pset[1].k, sg.k], w=[ab.k])
        S.dma("sp", actd[ji * 128:(ji + 1) * 128, tb * TB:(tb + 1) * TB], ab.t[:, :], r=[ab.k],
              w=[("act", ji, tb)])

    stream_linear(c, jobs, lambda gi, kc, tb: hT[:, kc, tb * TB:(tb + 1) * TB],
                  lambda gi, tb: ["hT%d" % tb], NTB, epi_a, c.wslots,
                  [[c.psb[0], c.psb[1]], [c.psb[2], c.psb[3]], [c.psb[4], c.psb[5]], [c.psb[6], c.psb[7]]])
    sched_barrier(S)
    av = actd.rearrange("(kc p) t -> p kc t", p=128)
    blks = [c.R64.t[:, 0:NJ * TB].rearrange("p (kc t) -> p kc t", t=TB),
            c.R44.t[:, 0:NJ * TB].rearrange("p (kc t) -> p kc t", t=TB)]
    bkeys = ["ablkA", "ablkB"]
    for tb in range(NTB):
        blk = blks[tb % 2]
        bk = bkeys[tb % 2]
        t0 = tb * TB
        for q4 in range(4):
            S.dma("sp", blk[:, q4 * 11:(q4 + 1) * 11, :], av[:, q4 * 11:(q4 + 1) * 11, t0:t0 + TB], w=[bk + str(q4)])
        jobs = [[(w_out, dc * 128, 128, NJ)] for dc in range(KC_D)]
        cnt2 = [0]

        def epi_b(ji, tb_unused, pset, tb=tb, t0=t0):
            i = cnt2[0]
            cnt2[0] += 1
            xt = c.stg_f32[i % 2]
            S.dma("sp", xt.t[:, :], x_src[ji * 128:(ji + 1) * 128, t0:t0 + TB], w=[xt.k])
            xo = c.stg_f32[2 + i % 2]
            S.op("dve", lambda e: e.scalar_tensor_tensor(out=xo.t[:, :], in0=pset[0].t[:, :], scalar=0.5,
                                                         in1=xt.t[:, :], op0=ALU.mult, op1=ALU.add),
                 r=[pset[0].k, xt.k], w=[xo.k])
            S.dma("sp", x_dst[ji * 128:(ji + 1) * 128, t0:t0 + TB], xo.t[:, :], r=[xo.k])

        stream_linear(c, jobs, lambda gi, kc, tb_, blk=blk: blk[:, kc, :],
                      lambda gi, tb_, bk=bk: [bk + str(q) for q in range(4)], 1, epi_b, c.wslots,
                      [[c.psb[i]] for i in range(8)])
    sched_barrier(S)


V_FFN1, V_MIX, V_FFN2, V_MEMN = 0, 16, 32, 48
V_SSDN = 64
V_CONVW = 96
V_CONVB = 288
V_DTB, V_ALOG, V_DSKIP = 336, 337, 338
V_FINAL = 339
V_DBC = 360
NV = 424


def pack_vecs(inp, l):
    v = np.zeros((128, NV), np.float32)

    def col(vec):
        return np.ascontiguousarray(np.asarray(vec, np.float32).reshape(-1, 128).T)
    v[:, V_FFN1:V_FFN1 + 16] = col(inp["ffn1_norm"][l])
    v[:, V_MIX:V_MIX + 16] = col(inp["mix_norm"][l])
    v[:, V_FFN2:V_FFN2 + 16] = col(inp["ffn2_norm"][l])
    v[:, V_MEMN:V_MEMN + 16] = col(inp["mem_norm"][l])
    v[:, V_SSDN:V_SSDN + 32] = col(inp["ssd_norm"][l])
    for k in range(4):
        v[:, V_CONVW + k * 48:V_CONVW + (k + 1) * 48] = col(inp["conv_w"][l][k])
    v[:, V_CONVB:V_CONVB + 48] = col(inp["conv_b"][l])
    v[0:64, V_DTB] = inp["dt_bias"][l]
    v[0:64, V_ALOG] = inp["a_log"][l]
    v[0:64, V_DSKIP] = inp["d_skip"][l]
    v[:, V_FINAL:V_FINAL + 16] = col(inp["final_norm"])
    v[:, V_DBC:V_DBC + 64] = np.asarray(inp["d_skip"][l], np.float32)[None, :]
    return v


C_IDENT, C_ONES = 0, 128
NCONST = 256


def make_consts():
    cst = np.zeros((128, NCONST), np.float32)
    cst[:, C_IDENT:C_IDENT + 128] = np.eye(128, dtype=np.float32)
    cst[:, C_ONES:C_ONES + 128] = 1.0
    return cst


def build(plan, dbg=None):
    nc = bass.Bass("TRN2", target_bir_lowering=False)
    es = ExitStack()
    with es:
        c = make_ctx(nc, es)
        S = c.S
        dt = nc.dram_tensor
        xT = dt("xT", [D, L], F32, kind="ExternalInput").ap()
        memT = dt("memT", [D, MEM], F32, kind="ExternalInput").ap()
        pos = dt("pos", [128, L], I32, kind="ExternalInput").ap()
        c.consts2_d = dt("consts2", [128, NCONST2], F32, kind="ExternalInput").ap()
        vecs = dt("vecs", [DEPTH, 128, NV], F32, kind="ExternalInput").ap()
        consts = dt("consts", [128, NCONST], F32, kind="ExternalInput").ap()
        W = {}
        for name, shp in (("w_ffn1_in", [D, 2 * DFF]), ("w_ffn1_out", [DFF, D]), ("w_ffn2_in", [D, 2 * DFF]),
                          ("w_ffn2_out", [DFF, D]), ("w_in", [D, IN_WIDTH]), ("w_mem_kv", [D, 2 * D]),
                          ("w_br_ssd", [SSD_INNER, D]), ("w_br_dsa", [D, D]), ("w_br_mem", [D, D]), ("w_out", [D, D])):
            W[name] = dt(name, [DEPTH, shp[0] + 1, shp[1]], F32, kind="ExternalInput").ap()[:, 0:shp[0], :]
        outT = dt("outT", [D, L], F32, kind="ExternalOutput").ap()
        c.xres = dt("xres", [D, L], F32, kind="Internal").ap()
        c.act_d = dt("act_d", [DFF, L], BF16, kind="Internal").ap()
        c.P = {}
        for name, off, width, dt_ in P_SPECS:
            c.P[name] = dt("P_" + name, [width, L], dt_, kind="Internal").ap()
        c.yssd = dt("yssd", [SSD_INNER, L], BF16, kind="Internal").ap()
        c.ydsa = dt("ydsa", [D, L], BF16, kind="Internal").ap()
        c.ymem = dt("ymem", [D, L], BF16, kind="Internal").ap()
        c.pos = pos
        c.xc_d = dt("xc_d", [CONV_CH, L], BF16, kind="Internal").ap()
        c.acs_d = dt("acs_d", [NH_SSD, L], F32, kind="Internal").ap()
        c.V_l = None
        c.consts_d = consts

        c.R64 = c.sb([128, 32768], BF16, "R64")
        c.R44 = c.sb([128, 22528], BF16, "R44")
        c.hT = c.R64.t[:, :].rearrange("p (kc t) -> p kc t", t=L)
        c.wslots = [c.sb([128, 5632], BF16, "wslot%d" % i) for i in range(6)]
        c.stg_f32 = [c.sb([128, TB], F32, "stgf%d" % i) for i in range(4)]
        c.stg_bf = [c.sb([128, TB], BF16, "stgb%d" % i) for i in range(3)]
        c.vecs = [c.sb([128, NV], F32, "vecs%d" % l) for l in range(DEPTH)]
        c.cst = c.sb([128, NCONST], F32, "cst")
        c.ones_bf = c.sb([128, 128], BF16, "ones_bf")
        c.ident_bf = c.sb([128, 128], BF16, "ident_bf")
        c.eps_col = c.sb([128, 1], F32, "eps_col")
        c.psb = [c.ps("ps%d" % i) for i in range(8)]

        for l in range(DEPTH):
            S.dma("sp", c.vecs[l].t[:, :], vecs[l], w=["vecs"])
        S.dma("sp", c.cst.t[:, :], consts, w=["cst"])
        S.op("dve", lambda e: e.tensor_copy(out=c.ones_bf.t[:, :], in_=c.cst.t[:, C_ONES:C_ONES + 128]), r=["cst"], w=["const"])
        S.op("dve", lambda e: e.tensor_copy(out=c.ident_bf.t[:, :], in_=c.cst.t[:, C_IDENT:C_IDENT + 128]), r=["cst"], w=["const"])
        S.op("dve", lambda e: e.memset(c.eps_col.t[:, :], EPS), w=["const"])
        sched_barrier(S)

        x_cur = xT
        for l in range(DEPTH):
            V = c.vecs[l].t
            if ("ffn1", l) in plan:
                phase_norm(c, x_cur, V[:, V_FFN1:V_FFN1 + 16], "h")
                sched_barrier(S)
                phase_ffn(c, W["w_ffn1_in"][l], W["w_ffn1_out"][l], x_cur, c.xres)
                x_cur = c.xres
            if ("inproj", l) in plan:
                phase_norm(c, x_cur, V[:, V_MIX:V_MIX + 16], "h")
                sched_barrier(S)
                phase_inproj(c, W["w_in"][l])
            if ("mem", l) in plan:
                phase_mem(c, memT, V[:, V_MEMN:V_MEMN + 16], W["w_mem_kv"][l])
            if ("dsa", l) in plan:
                phase_dsa(c, l)
            if ("ssd", l) in plan:
                phase_ssd(c, l, V)
            if ("merge", l) in plan:
                phase_merge(c, W["w_br_ssd"][l], W["w_br_dsa"][l], W["w_br_mem"][l], W["w_out"][l], x_cur, c.xres)
                x_cur = c.xres
            if ("ffn2", l) in plan:
                phase_norm(c, x_cur, V[:, V_FFN2:V_FFN2 + 16], "h")
                sched_barrier(S)
                phase_ffn(c, W["w_ffn2_in"][l], W["w_ffn2_out"][l], x_cur, c.xres)
                x_cur = c.xres
        if "final" in plan:
            phase_norm(c, x_cur, c.vecs[0].t[:, V_FINAL:V_FINAL + 16], "out", outT)
        elif dbg is not None:
            if dbg == "xres":
                src, nrow, sdt = x_cur, D, F32
            else:
                src, nrow, sdt = dbg
            for kc in range((nrow + 127) // 128):
                nr = min(128, nrow - kc * 128)
                for tb in range(NTB):
                    st = c.stg_f32[(kc * NTB + tb) % 4]
                    if sdt == F32:
                        S.dma("sp", st.t[0:nr, :], src(c)[kc * 128:kc * 128 + nr, tb * TB:(tb + 1) * TB] if callable(src) else src[kc * 128:kc * 128 + nr, tb * TB:(tb + 1) * TB], w=[st.k])
                    else:
                        sb_ = c.stg_bf[(kc * NTB + tb) % 3]
                        S.dma("sp", sb_.t[0:nr, :], src(c)[kc * 128:kc * 128 + nr, tb * TB:(tb + 1) * TB], w=[sb_.k])
                        S.op("dve", lambda e, st=st, sb_=sb_, nr=nr: e.tensor_copy(out=st.t[0:nr, :], in_=sb_.t[0:nr, :]), r=[sb_.k], w=[st.k])
                    S.dma("sp", outT[kc * 128:kc * 128 + nr, tb * TB:(tb + 1) * TB], st.t[0:nr, :], r=[st.k])
        S.finish("sp")
        print("sched: inst=%d waits=%d cnt=%s" % (S.n_inst, S.n_wait, S.cnt))
    return nc


W_NAMES = ("w_ffn1_in", "w_ffn1_out", "w_ffn2_in", "w_ffn2_out", "w_in", "w_mem_kv", "w_br_ssd", "w_br_dsa", "w_br_mem", "w_out")


def core_inputs(inp, b, vec, cst):
    im = {"xT": np.ascontiguousarray(inp["x"][b].T), "memT": np.ascontiguousarray(inp["mem"][b].T),
          "pos": np.ascontiguousarray(np.broadcast_to(inp["positions"][b].reshape(1, L).astype(np.int32), (128, L))), "vecs": vec, "consts": cst,
          "consts2": make_consts2()}
    for n in W_NAMES:
        w = inp[n]
        p = np.empty((w.shape[0], w.shape[1] + 1, w.shape[2]), np.float32)
        p[:, :w.shape[1], :] = w
        p[:, w.shape[1], :] = float(b)
        im[n] = p
    return im


P_SPECS = [("z", O_Z, 4096, BF16), ("xbc", O_XBC, 6144, BF16), ("dt", O_DT, 64, F32), ("q", O_Q, 2048, BF16),
           ("k", O_K, 128, BF16), ("v", O_V, 128, BF16), ("qi", O_QI, 1024, BF16), ("ki", O_KI, 64, BF16),
           ("wi", O_WI, 16, F32), ("qm", O_QM, 2048, BF16), ("g", O_G, 6144, BF16)]


def phase_inproj(c, w_in):
    S = c.S
    hT = c.hT
    jobs = []
    meta = []
    for name, off, width, dt_ in P_SPECS:
        for f0 in range(0, width, 128):
            nco = min(128, width - f0)
            jobs.append([(w_in, off + f0, nco, KC_D)])
            meta.append((name, f0, nco, dt_))
    cnt = [0]

    def epi(ji, tb, pset):
        name, f0, nco, dt_ = meta[ji]
        i = cnt[0]
        cnt[0] += 1
        if dt_ == F32:
            st = c.stg_f32[i % 4]
        else:
            st = c.stg_bf[i % 3]
        if i % 2 == 0:
            S.op("act", lambda e: e.copy(out=st.t[0:nco, :], in_=pset[0].t[0:nco, :]), r=[pset[0].k], w=[st.k])
        else:
            S.op("dve", lambda e: e.tensor_copy(out=st.t[0:nco, :], in_=pset[0].t[0:nco, :]), r=[pset[0].k], w=[st.k])
        S.dma("sp", c.P[name][f0:f0 + nco, tb * TB:(tb + 1) * TB], st.t[0:nco, :], r=[st.k], w=[("P", name, f0, tb)])

    stream_linear(c, jobs, lambda gi, kc, tb: hT[:, kc, tb * TB:(tb + 1) * TB],
                  lambda gi, tb: ["hT%d" % tb], NTB, epi, c.wslots, [[c.psb[i]] for i in range(8)])
    sched_barrier(S)


def phase_mem(c, memT, gcol, w_kv):
    S = c.S
    r = c.R44.t
    mem_in = r[:, 0:8192].bitcast(F32).rearrange("p (kc t) -> p kc t", t=MEM)
    mnT = r[:, 8192:12288].rearrange("p (kc t) -> p kc t", t=MEM)
    kmT = r[:, 12288:16384].rearrange("p (kc t) -> p kc t", t=MEM)
    vm = r[:, 16384:20480].rearrange("p (mt d) -> p mt d", d=D)
    memv = memT.rearrange("(kc p) t -> p kc t", p=128)
    S.dma("sp", mem_in, memv, w=["mem_in"])
    pb = c.psb[0]
    for kc in range(KC_D):
        sq = c.stg_bf[kc % 3]
        S.op("act", lambda e, sq=sq, kc=kc: e.activation(out=sq.t[:, 0:MEM], in_=mem_in[:, kc, :], func=AF.Square),
             r=["mem_in"], w=[sq.k])
        S.op("pe", lambda e, sq=sq, kc=kc: e.matmul(pb.t[:, 0:MEM], c.ones_bf.t[:, :], sq.t[:, 0:MEM],
                                                    start=(kc == 0), stop=(kc == KC_D - 1)), r=[sq.k], w=[pb.k])
    rt, rs = c.stg_f32[0], c.stg_f32[1]
    S.op("act", lambda e: e.activation(out=rt.t[:, 0:MEM], in_=pb.t[:, 0:MEM], func=AF.Sqrt, scale=1.0 / D,
                                       bias=c.eps_col.t[:, 0:1]), r=[pb.k], w=[rt.k])
    S.op("dve", lambda e: e.reciprocal(out=rs.t[:, 0:MEM], in_=rt.t[:, 0:MEM]), r=[rt.k], w=[rs.k])
    for kc in range(KC_D):
        S.op("dve", lambda e, kc=kc: e.scalar_tensor_tensor(out=mnT[:, kc, :], in0=mem_in[:, kc, :],
                                                             scalar=gcol[:, kc:kc + 1], in1=rs.t[:, 0:MEM],
                                                             op0=ALU.mult, op1=ALU.mult),
             r=["mem_in", rs.k], w=["mnT"])
    jobs = [[(w_kv, fc * 128, 128, KC_D)] for fc in range(16)]

    def epi_k(ji, tb, pset):
        S.op("act", lambda e: e.copy(out=kmT[:, ji, :], in_=pset[0].t[:, 0:MEM]), r=[pset[0].k], w=["kmT"])
    stream_linear(c, jobs, lambda gi, kc, tb: mnT[:, kc, :], lambda gi, tb: ["mnT"], 1, epi_k, c.wslots,
                  [[c.psb[i]] for i in range(1, 5)], tbw=MEM)
    wv = w_kv.rearrange("(kc p) f -> p kc f", p=128)
    for cb in range(8):
        ws = c.wslots[cb % 6]
        dst = ws.t[:, 0:4096].rearrange("p (kc f) -> p kc f", f=256)
        S.dma("pool", dst[:, 0:8, :], wv[:, 0:8, D + cb * 256:D + (cb + 1) * 256], w=[ws.k + "_a"])
        S.dma("pool", dst[:, 8:16, :], wv[:, 8:16, D + cb * 256:D + (cb + 1) * 256], w=[ws.k + "_b"])
        for mt in range(2):
            pbv = c.psb[5 + mt]

            def mm(e, dst=dst, pbv=pbv, mt=mt):
                ins = None
                for kc in range(KC_D):
                    ins = e.matmul(pbv.t[:, 0:256], mnT[:, kc, mt * 128:(mt + 1) * 128], dst[:, kc, :],
                                   start=(kc == 0), stop=(kc == KC_D - 1))
                return ins
            S.op("pe", mm, r=[ws.k + "_a", ws.k + "_b", "mnT"], w=[pbv.k])
            S.op("dve", lambda e, pbv=pbv, mt=mt, cb=cb: e.tensor_copy(out=vm[:, mt, cb * 256:(cb + 1) * 256], in_=pbv.t[:, 0:256]),
                 r=[pbv.k], w=["vm"])
    sched_barrier(S)
    qmv = c.P["qm"].rearrange("(kc p) t -> p kc t", p=128)
    qblk = c.R64.t[:, 0:16 * TB].rearrange("p (kc t) -> p kc t", t=TB)
    pT = [c.R64.t[:, 16 * TB + i * TB:16 * TB + (i + 1) * TB] for i in range(2)]
    scale = 512.0 ** -0.5
    for tb in range(NTB):
        t0 = tb * TB
        S.dma("sp", qblk, qmv[:, :, t0:t0 + TB], w=["qblk"])
        for hd in range(4):
            for mt in range(2):
                pl = c.psb[mt]

                def mmq(e, pl=pl, hd=hd, mt=mt):
                    ins = None
                    for j in range(4):
                        ins = e.matmul(pl.t[:, :], kmT[:, hd * 4 + j, mt * 128:(mt + 1) * 128], qblk[:, hd * 4 + j, :],
                                       start=(j == 0), stop=(j == 3))
                    return ins
                S.op("pe", mmq, r=["kmT", "qblk"], w=[pl.k])
                S.op("act", lambda e, pl=pl, mt=mt: e.activation(out=pT[mt], in_=pl.t[:, :], func=AF.Exp, scale=scale),
                     r=[pl.k], w=["pT%d" % mt])
            prs = c.psb[2]

            def mmrs(e):
                e.matmul(prs.t[:, :], c.ones_bf.t[:, :], pT[0], start=True, stop=False)
                return e.matmul(prs.t[:, :], c.ones_bf.t[:, :], pT[1], start=False, stop=True)
            S.op("pe", mmrs, r=["pT0", "pT1"], w=[prs.k])
            rinv = c.stg_f32[0]
            S.op("dve", lambda e: e.reciprocal(out=rinv.t[:, :], in_=prs.t[:, :]), r=[prs.k], w=[rinv.k])
            for j in range(4):
                po = c.psb[3 + j]
                ch = hd * 4 + j

                def mmo(e, po=po, ch=ch):
                    e.matmul(po.t[:, :], vm[:, 0, ch * 128:(ch + 1) * 128], pT[0], start=True, stop=False)
                    return e.matmul(po.t[:, :], vm[:, 1, ch * 128:(ch + 1) * 128], pT[1], start=False, stop=True)
                S.op("pe", mmo, r=["pT0", "pT1", "vm"], w=[po.k])
                st = c.stg_bf[j % 3]
                S.op("dve", lambda e, po=po, st=st: e.tensor_tensor(out=st.t[:, :], in0=po.t[:, :], in1=rinv.t[:, :], op=ALU.mult),
                     r=[po.k, rinv.k], w=[st.k])
                S.dma("sp", c.ymem[ch * 128:(ch + 1) * 128, t0:t0 + TB], st.t[:, :], r=[st.k], w=[("ymem", ch, tb)])
    sched_barrier(S)


def phase_merge(c, w_ssd, w_dsa, w_memw, w_out, x_src, x_dst):
    S = c.S
    mT = c.R64.t[:, :].rearrange("p (kc t) -> p kc t", t=L)
    ysv = c.yssd.rearrange("(kc p) t -> p kc t", p=128)
    ydv = c.ydsa.rearrange("(kc p) t -> p kc t", p=128)
    ymv = c.ymem.rearrange("(kc p) t -> p kc t", p=128)
    gv = c.P["g"]
    TBM = 256
    ys = c.R44.t[:, 0:32 * TBM].rearrange("p (kc t) -> p kc t", t=TBM)
    yd = c.R44.t[:, 32 * TBM:48 * TBM].rearrange("p (kc t) -> p kc t", t=TBM)
    ym = c.R44.t[:, 48 * TBM:64 * TBM].rearrange("p (kc t) -> p kc t", t=TBM)
    gt = [c.R44.t[:, 64 * TBM + i * TBM:64 * TBM + (i + 1) * TBM] for i in range(6)]
    for tb in range(L // TBM):
        t0 = tb * TBM
        S.dma("sp", ys[:, 0:16, :], ysv[:, 0:16, t0:t0 + TBM], w=["ys0"])
        S.dma("sp", ys[:, 16:32, :], ysv[:, 16:32, t0:t0 + TBM], w=["ys1"])
        S.dma("sp", yd, ydv[:, :, t0:t0 + TBM], w=["yd"])
        S.dma("sp", ym, ymv[:, :, t0:t0 + TBM], w=["ym"])
        jobs = [[(w_ssd, dc * 128, 128, 32), (w_dsa, dc * 128, 128, 16), (w_memw, dc * 128, 128, 16)] for dc in range(KC_D)]
        cnt = [0]

        def rhs_fn(gi, kc, tb_):
            return (ys, yd, ym)[gi][:, kc, :]

        def rhs_keys(gi, tb_):
            return (["ys0", "ys1"], ["yd"], ["ym"])[gi]

        def epi(ji, tb_, pset, t0=t0):
            i = cnt[0]
            cnt[0] += 1
            acc = c.stg_f32[i % 2]
            for bi in range(3):
                g = gt[(i % 2) * 3 + bi]
                gk = "gt%d" % ((i % 2) * 3 + bi)
                S.dma("sp", g, gv[bi * D + ji * 128:bi * D + (ji + 1) * 128, t0:t0 + TBM], w=[gk])
                sg = c.stg_f32[2 + (bi % 2)]
                S.op("act", lambda e, g=g, sg=sg: e.activation(out=sg.t[:, 0:TBM], in_=g, func=AF.Sigmoid), r=[gk], w=[sg.k])
                if bi == 0:
                    S.op("dve", lambda e, sg=sg: e.tensor_tensor(out=acc.t[:, 0:TBM], in0=pset[0].t[:, 0:TBM], in1=sg.t[:, 0:TBM], op=ALU.mult),
                         r=[pset[0].k, sg.k], w=[acc.k])
                else:
                    S.op("dve", lambda e, sg=sg, bi=bi: e.tensor_tensor(out=sg.t[:, 0:TBM], in0=pset[bi].t[:, 0:TBM], in1=sg.t[:, 0:TBM], op=ALU.mult),
                         r=[pset[bi].k, sg.k], w=[sg.k])
                    if bi == 1:
                        S.op("dve", lambda e, sg=sg: e.tensor_tensor(out=acc.t[:, 0:TBM], in0=acc.t[:, 0:TBM], in1=sg.t[:, 0:TBM], op=ALU.add),
                             r=[acc.k, sg.k], w=[acc.k])
                    else:
                        S.op("dve", lambda e, sg=sg: e.tensor_tensor(out=mT[:, ji, t0:t0 + TBM], in0=acc.t[:, 0:TBM], in1=sg.t[:, 0:TBM], op=ALU.add),
                             r=[acc.k, sg.k], w=["mT%d" % (t0 // TB)])

        stream_linear(c, jobs, rhs_fn, rhs_keys, 1, epi, c.wslots,
                      [[c.psb[0], c.psb[1], c.psb[2]], [c.psb[3], c.psb[4], c.psb[5]]], tbw=TBM)
    sched_barrier(S)
    jobs = [[(w_out, dc * 128, 128, KC_D)] for dc in range(KC_D)]
    cnt2 = [0]

    def epi_o(ji, tb, pset):
        i = cnt2[0]
        cnt2[0] += 1
        t0 = tb * TB
        xt = c.stg_f32[i % 2]
        S.dma("sp", xt.t[:, :], x_src[ji * 128:(ji + 1) * 128, t0:t0 + TB], w=[xt.k])
        xo = c.stg_f32[2 + i % 2]
        S.op("dve", lambda e: e.tensor_tensor(out=xo.t[:, :], in0=pset[0].t[:, :], in1=xt.t[:, :], op=ALU.add),
             r=[pset[0].k, xt.k], w=[xo.k])
        S.dma("sp", x_dst[ji * 128:(ji + 1) * 128, t0:t0 + TB], xo.t[:, :], r=[xo.k])
    stream_linear(c, jobs, lambda gi, kc, tb: mT[:, kc, tb * TB:(tb + 1) * TB], lambda gi, tb: ["mT%d" % tb], NTB, epi_o,
                  c.wslots, [[c.psb[i]] for i in range(8)])
    sched_barrier(S)


C_INVA, C_INVI, C_SGNA, C_SGNI = 256, 257, 258, 259
C_PA, C_PI, C_CAUS = 264, 392, 520
C_U = 648
NCONST2 = 776
TWO_PI = 6.283185307179586
CW1 = 6.28125
CW2 = TWO_PI - CW1
PI_F = 3.1415925


def make_consts2():
    cst = np.zeros((128, NCONST2), np.float32)
    cst[:, 0:NCONST] = make_consts()
    th = np.float32(500000.0)
    inv_a = np.power(th, -(np.arange(0, 32, 2, dtype=np.float32) / np.float32(32))).astype(np.float32)
    inv_i = np.power(th, -(np.arange(0, 16, 2, dtype=np.float32) / np.float32(16))).astype(np.float32)
    for d_ in range(128):
        if d_ < 32:
            cst[d_, C_INVA] = inv_a[d_ % 16]
            cst[d_, C_SGNA] = -1.0 if d_ < 16 else 1.0
        e = d_ % 64
        if e < 16:
            cst[d_, C_INVI] = inv_i[e % 8]
            cst[d_, C_SGNI] = -1.0 if e < 8 else 1.0
    for dp in range(32):
        dsrc = dp + 16 if dp < 16 else dp - 16
        cst[dsrc, C_PA + dp] = 1.0
    for blk in (0, 64):
        for ep in range(16):
            esrc = ep + 8 if ep < 8 else ep - 8
            cst[blk + esrc, C_PI + blk + ep] = 1.0
    q = np.arange(128)[:, None]
    k = np.arange(128)[None, :]
    cst[:, C_CAUS:C_CAUS + 128] = np.where(k <= q, 0.0, -1e30).astype(np.float32)
    cst[:, C_U:C_U + 128] = (q <= k).astype(np.float32)
    return cst


def _sin_table(c, out_bf, ang, sgncol, tmp_u, tmp_n, tmp_r, tag):
    S = c.S
    ni = tmp_n.bitcast(I32)
    S.op("dve", lambda e: e.tensor_scalar(out=tmp_u, in0=ang, scalar1=1.0 / TWO_PI, scalar2=None, op0=ALU.mult), r=[tag + "ang"], w=[tag + "u"])
    S.op("dve", lambda e: e.tensor_copy(out=ni, in_=tmp_u), r=[tag + "u"], w=[tag + "n"])
    S.op("dve", lambda e: e.tensor_copy(out=tmp_u, in_=ni), r=[tag + "n"], w=[tag + "u"])
    S.op("dve", lambda e: e.scalar_tensor_tensor(out=tmp_r, in0=tmp_u, scalar=-CW1, in1=ang, op0=ALU.mult, op1=ALU.add),
         r=[tag + "u", tag + "ang"], w=[tag + "r"])
    S.op("dve", lambda e: e.scalar_tensor_tensor(out=tmp_r, in0=tmp_u, scalar=-CW2, in1=tmp_r, op0=ALU.mult, op1=ALU.add),
         r=[tag + "u", tag + "r"], w=[tag + "r"])
    S.op("dve", lambda e: e.tensor_scalar(out=tmp_u, in0=tmp_r, scalar1=PI_F, scalar2=None, op0=ALU.is_gt), r=[tag + "r"], w=[tag + "u"])
    S.op("dve", lambda e: e.scalar_tensor_tensor(out=tmp_r, in0=tmp_u, scalar=-TWO_PI, in1=tmp_r, op0=ALU.mult, op1=ALU.add),
         r=[tag + "u", tag + "r"], w=[tag + "r"])
    S.op("dve", lambda e: e.tensor_scalar(out=tmp_u, in0=tmp_r, scalar1=-PI_F, scalar2=None, op0=ALU.is_lt), r=[tag + "r"], w=[tag + "u"])
    S.op("dve", lambda e: e.scalar_tensor_tensor(out=tmp_r, in0=tmp_u, scalar=TWO_PI, in1=tmp_r, op0=ALU.mult, op1=ALU.add),
         r=[tag + "u", tag + "r"], w=[tag + "r"])
    S.op("dve", lambda e: e.tensor_scalar(out=tmp_r, in0=tmp_r, scalar1=PI_F, scalar2=-PI_F, op0=ALU.min, op1=ALU.max), r=[tag + "r"], w=[tag + "r"])
    if sgncol is None:
        S.op("act", lambda e: e.activation(out=out_bf, in_=tmp_r, func=AF.Sin), r=[tag + "r"], w=["tables"])
    else:
        S.op("act", lambda e: e.activation(out=out_bf, in_=tmp_r, func=AF.Sin, scale=sgncol), r=[tag + "r"], w=["tables"])


def _rope(c, X, ncols, cosT, sinT, PT, xkeys, pbank, i):
    S = c.S
    S.op("pe", lambda e: e.matmul(pbank.t[:, 0:ncols], PT, X, start=True, stop=True), r=list(xkeys) + ["dsaconst"], w=[pbank.k])
    t1 = c.stg_f32[i % 2]
    t2 = c.stg_f32[2 + i % 2]
    S.op("dve", lambda e: e.tensor_tensor(out=t1.t[:, 0:ncols], in0=X, in1=cosT, op=ALU.mult), r=list(xkeys) + ["tables"], w=[t1.k])
    S.op("dve", lambda e: e.tensor_tensor(out=t2.t[:, 0:ncols], in0=pbank.t[:, 0:ncols], in1=sinT, op=ALU.mult), r=[pbank.k, "tables"], w=[t2.k])
    S.op("pool", lambda e: e.tensor_tensor(out=X, in0=t1.t[:, 0:ncols], in1=t2.t[:, 0:ncols], op=ALU.add), r=[t1.k, t2.k], w=list(xkeys))


def phase_dsa(c, l):
    S = c.S
    R6, R4 = c.R64.t, c.R44.t
    qblk = R6[:, 0:8192].rearrange("p (h t) -> p h t", t=TB)
    selT = R6[:, 8192:16384].rearrange("p (k t) -> p k t", t=TB)
    qiblk = R6[:, 16384:20480].rearrange("p (h t) -> p h t", t=TB)
    cosA, sinA, cosI, sinI = (R6[:, 20480 + i * 2048:20480 + (i + 1) * 2048] for i in range(4))
    krT = R6[:, 28672:30720]
    kir2 = R6[:, 30720:32768]
    acc = R4[:, 0:4096].bitcast(F32)
    work = R4[:, 4096:8192].bitcast(F32)
    sel01 = R4[:, 8192:10240]
    v_tok = R4[:, 10240:12288].rearrange("p (k d) -> p k d", d=128)
    wi_tok = R4[:, 12288:12800].bitcast(F32).rearrange("p (q h) -> p q h", h=16)
    m8 = R4[:, 12800:12816].bitcast(F32)
    pTs = [R4[:, 13312 + i * 512:13312 + (i + 1) * 512] for i in range(6)]
    tmp3 = R4[:, 8192:12288].bitcast(F32)
    cst2 = c.wslots[0].t[:, 0:2 * NCONST2].bitcast(F32)
    PA_bf = c.wslots[1].t[:, 0:128]
    PI_bf = c.wslots[1].t[:, 128:256]
    posi = c.wslots[2].t[:, 0:4096].bitcast(I32)
    vT_sb = c.wslots[3].t[:, 0:2048]
    wiT_sb = c.wslots[4].t[:, 0:4096].bitcast(F32)
    ident_f = c.cst.t[:, C_IDENT:C_IDENT + 128]

    S.dma("sp", cst2, c.consts2_d, w=["cst2"])
    S.dma("sp", posi, c.pos, w=["posi"])
    S.op("dve", lambda e: e.tensor_copy(out=PA_bf, in_=cst2[:, C_PA:C_PA + 128]), r=["cst2"], w=["dsaconst"])
    S.op("dve", lambda e: e.tensor_copy(out=PI_bf, in_=cst2[:, C_PI:C_PI + 128]), r=["cst2"], w=["dsaconst"])
    S.op("dve", lambda e: e.tensor_copy(out=acc, in_=posi), r=["posi"], w=["posf"])
    for tname, invc, sgnc, cosT, sinT in (("A", C_INVA, C_SGNA, cosA, sinA), ("I", C_INVI, C_SGNI, cosI, sinI)):
        S.op("dve", lambda e, invc=invc: e.tensor_scalar(out=work, in0=acc, scalar1=cst2[:, invc:invc + 1], scalar2=None, op0=ALU.mult),
             r=["posf", "cst2"], w=[tname + "sang"])
        u_t = tmp3
        r_t = c.wslots[5].t[:, 0:4096].bitcast(F32)
        _sin_table(c, sinT, work, cst2[:, sgnc:sgnc + 1], u_t, u_t, r_t, tname + "s")
        S.op("dve", lambda e: e.tensor_scalar(out=work, in0=work, scalar1=1.5707963267948966, scalar2=None, op0=ALU.add),
             r=[tname + "sang", tname + "sr", tname + "su"], w=[tname + "cang"])
        _sin_table(c, cosT, work, None, u_t, u_t, r_t, tname + "c")
        sched_barrier(S)
    S.dma("sp", krT, c.P["k"], w=["krT"])
    S.dma("sp", kir2[0:64, :], c.P["ki"], w=["kir2"])
    S.dma("sp", kir2[64:128, :], c.P["ki"], w=["kir2b"])
    S.dma("sp", vT_sb, c.P["v"], w=["vT"])
    S.dma("sp", wiT_sb[0:16, :], c.P["wi"], w=["wiT"])
    for tb in range(NTB):
        sl = slice(tb * TB, (tb + 1) * TB)
        _rope(c, krT[:, sl], TB, cosA[:, sl], sinA[:, sl], PA_bf, ["krT"], c.psb[tb % 4], tb)
    for tb in range(NTB):
        sl = slice(tb * TB, (tb + 1) * TB)
        _rope(c, kir2[:, sl], TB, cosI[:, sl], sinI[:, sl], PI_bf, ["kir2", "kir2b"], c.psb[4 + tb % 4], tb)
    for half in range(2):
        pb = c.psb[half]
        pbv = pb.t[:, :].bitcast(BF16)

        def tr(e, half=half, pbv=pbv):
            ins = None
            for j in range(8):
                kt = half * 8 + j
                ins = e.transpose(pbv[:, j * 128:(j + 1) * 128], vT_sb[:, kt * 128:(kt + 1) * 128], c.ident_bf.t[:, :])
            return ins
        S.op("pe", tr, r=["vT", "const"], w=[pb.k])
        S.op("act", lambda e, half=half, pbv=pbv: e.copy(out=v_tok[:, half * 8:(half + 1) * 8, :],
                                                       in_=pbv[:, 0:1024].rearrange("p (k d) -> p k d", d=128)),
             r=[pb.k], w=["v_tok"])
    pbw = c.psb[2]

    def trw(e):
        ins = None
        for qt in range(16):
            ins = e.transpose(pbw.t[:, qt * 16:(qt + 1) * 16], wiT_sb[0:16, qt * 128:(qt + 1) * 128], ident_f[0:16, 0:16])
        return ins
    S.op("pe", trw, r=["wiT", "cst"], w=[pbw.k])
    S.op("act", lambda e: e.activation(out=wi_tok, in_=pbw.t[:, 0:256].rearrange("p (q h) -> p q h", h=16), func=AF.Copy,
                                       scale=0.25 * 0.125), r=[pbw.k], w=["wi_tok"])
    sched_barrier(S)

    qv = c.P["q"].rearrange("(h p) t -> p h t", p=128)
    qiv = c.P["qi"].rearrange("(h p) t -> p h t", p=128)
    scale = 128.0 ** -0.5
    NEG = -1e30
    for b in range(NTB):
        t0 = b * TB
        S.dma("sp", qblk[:, 0:8, :], qv[:, 0:8, t0:t0 + TB], w=["qblk0"])
        S.dma("sp", qblk[:, 8:16, :], qv[:, 8:16, t0:t0 + TB], w=["qblk1"])
        S.dma("sp", qiblk, qiv[:, :, t0:t0 + TB], w=["qiblk"])
        for h in range(16):
            _rope(c, qblk[:, h, :], TB, cosA[:, t0:t0 + TB], sinA[:, t0:t0 + TB], PA_bf, ["qblk%d" % (h // 8)], c.psb[h % 4], h)
        for ch in range(8):
            _rope(c, qiblk[:, ch, :], TB, cosI[:, t0:t0 + TB], sinI[:, t0:t0 + TB], PI_bf, ["qiblk"], c.psb[4 + ch % 4], ch)
        S.op("pool", lambda e, b=b: e.memset(selT[:, 4 * b:4 * b + 4, :], 0.0), w=["selT"])
        for qi_ in range(4):
            qt = 4 * b + qi_
            nk = 128 * (qt + 1)
            nkb = (nk + TB - 1) // TB
            cnt = 0
            for kb in range(nkb):
                w_ = min(TB, nk - kb * TB)
                for h in range(16):
                    chn, hf = h // 2, h % 2
                    pb = c.psb[cnt % 4]
                    tmp = c.stg_f32[cnt % 4]
                    cnt += 1
                    S.op("pe", lambda e, pb=pb, chn=chn, hf=hf, kb=kb, w_=w_, qi_=qi_: e.matmul(
                        pb.t[:, 0:w_], qiblk[hf * 64:(hf + 1) * 64, chn, qi_ * 128:(qi_ + 1) * 128],
                        kir2[hf * 64:(hf + 1) * 64, kb * TB:kb * TB + w_], start=True, stop=True),
                        r=["qiblk", "kir2", "kir2b"], w=[pb.k])
                    if h == 0:
                        S.op("dve", lambda e, pb=pb, kb=kb, w_=w_, qt=qt, h=h: e.tensor_scalar(
                            out=acc[:, kb * TB:kb * TB + w_], in0=pb.t[:, 0:w_], scalar1=0.0, scalar2=wi_tok[:, qt, h:h + 1],
                            op0=ALU.max, op1=ALU.mult), r=[pb.k, "wi_tok"], w=["acc%d" % kb])
                    else:
                        S.op("dve", lambda e, pb=pb, tmp=tmp, w_=w_, qt=qt, h=h: e.tensor_scalar(
                            out=tmp.t[:, 0:w_], in0=pb.t[:, 0:w_], scalar1=0.0, scalar2=wi_tok[:, qt, h:h + 1],
                            op0=ALU.max, op1=ALU.mult), r=[pb.k, "wi_tok"], w=[tmp.k])
                        S.op("pool", lambda e, tmp=tmp, kb=kb, w_=w_: e.tensor_tensor(
                            out=acc[:, kb * TB:kb * TB + w_], in0=acc[:, kb * TB:kb * TB + w_], in1=tmp.t[:, 0:w_], op=ALU.add),
                            r=[tmp.k, "acc%d" % kb], w=["acc%d" % kb])
            acck = ["acc%d" % kb for kb in range(nkb)]
            S.op("pool", lambda e, nk=nk: e.tensor_tensor(out=acc[:, nk - 128:nk], in0=acc[:, nk - 128:nk],
                                                          in1=cst2[:, C_CAUS:C_CAUS + 128], op=ALU.add),
                 r=acck + ["cst2"], w=acck)
            if qt >= 2:
                for rd in range(32):
                    src = acc if rd == 0 else work
                    sk = acck if rd == 0 else ["work"]
                    S.op("dve", lambda e, src=src, nk=nk: e.max(out=m8, in_=src[:, 0:nk]), r=sk, w=["m8"])
                    if rd < 31:
                        S.op("dve", lambda e, src=src, nk=nk: e.match_replace(out=work[:, 0:nk], in_to_replace=m8,
                                                                             in_values=src[:, 0:nk], imm_value=NEG),
                             r=sk + ["m8"], w=["work"])
            else:
                S.op("dve", lambda e: e.memset(m8, -1e29), w=["m8"])
            S.op("dve", lambda e, nk=nk: e.tensor_scalar(out=sel01[:, 0:nk], in0=acc[:, 0:nk], scalar1=m8[:, 7:8], scalar2=None,
                                                         op0=ALU.is_ge), r=acck + ["m8"], w=["sel01"])
            for k0 in range(0, qt + 1, 8):
                n_ = min(8, qt + 1 - k0)
                pb = c.psb[4 + (k0 // 8) % 2]
                pbv = pb.t[:, :].bitcast(BF16)

                def trs(e, k0=k0, n_=n_, pbv=pbv):
                    ins = None
                    for j in range(n_):
                        ins = e.transpose(pbv[:, j * 128:(j + 1) * 128], sel01[:, (k0 + j) * 128:(k0 + j + 1) * 128], c.ident_bf.t[:, :])
                    return ins
                S.op("pe", trs, r=["sel01", "const"], w=[pb.k])
                S.op("act", lambda e, k0=k0, n_=n_, pbv=pbv, qi_=qi_: e.copy(
                    out=selT[:, k0:k0 + n_, qi_ * 128:(qi_ + 1) * 128],
                    in_=pbv[:, 0:n_ * 128].rearrange("p (k d) -> p k d", d=128)), r=[pb.k], w=["selT"])
        nkt = 4 * (b + 1)
        for h in range(16):
            po = c.psb[3 + h % 2]
            prs = c.psb[5 + h % 2]
            qk = "qblk%d" % (h // 8)
            pend = []

            def pv(kt, pTm, pk, po=po, prs=prs, nkt=nkt):
                S.op("pe", lambda e: e.matmul(po.t[:, :], v_tok[:, kt, :], pTm, start=(kt == 0), stop=(kt == nkt - 1)),
                     r=[pk, "v_tok"], w=[po.k])
                S.op("pe", lambda e: e.matmul(prs.t[:, :], c.ones_bf.t[:, :], pTm, start=(kt == 0), stop=(kt == nkt - 1)),
                     r=[pk, "const"], w=[prs.k])
            for kt in range(nkt):
                pl = c.psb[kt % 3]
                S.op("pe", lambda e, pl=pl, kt=kt, h=h: e.matmul(pl.t[:, :], krT[:, kt * 128:(kt + 1) * 128], qblk[:, h, :],
                                                                start=True, stop=True), r=["krT", qk], w=[pl.k])
                pT = pTs[kt % 3]
                pTm = pTs[3 + kt % 3]
                S.op("act", lambda e, pl=pl, pT=pT: e.activation(out=pT, in_=pl.t[:, :], func=AF.Exp, scale=scale),
                     r=[pl.k], w=["pT%d" % (kt % 3)])
                S.op("dve", lambda e, pT=pT, pTm=pTm, kt=kt: e.tensor_tensor(out=pTm, in0=pT, in1=selT[:, kt, :], op=ALU.mult),
                     r=["pT%d" % (kt % 3), "selT"], w=["pTm%d" % (kt % 3)])
                pend.append((kt, pTm, "pTm%d" % (kt % 3)))
                if len(pend) > 2:
                    pv(*pend.pop(0))
            while pend:
                pv(*pend.pop(0))
            rinv = c.stg_f32[h % 2]
            S.op("dve", lambda e, prs=prs, rinv=rinv: e.reciprocal(out=rinv.t[:, :], in_=prs.t[:, :]), r=[prs.k], w=[rinv.k])
            st = c.stg_bf[h % 3]
            S.op("dve", lambda e, po=po, rinv=rinv, st=st: e.tensor_tensor(out=st.t[:, :], in0=po.t[:, :], in1=rinv.t[:, :], op=ALU.mult),
                 r=[po.k, rinv.k], w=[st.k])
            S.dma("sp", c.ydsa[h * 128:(h + 1) * 128, t0:t0 + TB], st.t[:, :], r=[st.k], w=[("ydsa", h, b)])
        sched_barrier(S)


def phase_ssd(c, l, V):
    S = c.S
    R6, R4 = c.R64.t, c.R44.t
    ident_f = c.cst.t[:, C_IDENT:C_IDENT + 128]
    xraw = [R6[:, i * 2560:i * 2560 + 2051] for i in range(2)]
    xout = [R6[:, 8192 + i * 2048:8192 + (i + 1) * 2048] for i in range(2)]
    diag = [R4[:, i * 512:(i + 1) * 512].rearrange("p (k c) -> p k c", c=128) for i in range(2)]
    for i in range(2):
        S.op("dve", lambda e, i=i: e.memset(xraw[i][:, 0:3], 0.0), w=["xraw%d" % i])
    for cc in range(48):
        i = cc % 2
        xr, xo, dg = xraw[i], xout[i], diag[i]
        S.dma("sp", xr[:, 3:2051], c.P["xbc"][cc * 128:(cc + 1) * 128, :], w=["xraw%d" % i])
        for k in range(4):
            S.op("dve", lambda e, k=k, dg=dg, cc=cc: e.tensor_scalar(
                out=dg[:, k, :], in0=c.ident_bf.t[:, :], scalar1=V[:, V_CONVW + k * 48 + cc:V_CONVW + k * 48 + cc + 1],
                scalar2=None, op0=ALU.mult), r=["const", "vecs"], w=["diag%d" % i])
        for tb in range(NTB):
            pb = c.psb[(cc * NTB + tb) % 8]

            def mm(e, pb=pb, dg=dg, xr=xr, tb=tb):
                ins = None
                for k in range(4):
                    ins = e.matmul(pb.t[:, :], dg[:, k, :], xr[:, tb * TB + k:tb * TB + k + TB], start=(k == 0), stop=(k == 3))
                return ins
            S.op("pe", mm, r=["xraw%d" % i, "diag%d" % i], w=[pb.k])
            S.op("act", lambda e, pb=pb, xo=xo, tb=tb, cc=cc: e.activation(
                out=xo[:, tb * TB:(tb + 1) * TB], in_=pb.t[:, :], func=AF.Silu, bias=V[:, V_CONVB + cc:V_CONVB + cc + 1]),
                r=[pb.k, "vecs"], w=["xout%d" % i])
        S.dma("sp", c.xc_d[cc * 128:(cc + 1) * 128, :], xo, r=["xout%d" % i], w=[("xc", cc)])
    sched_barrier(S)

    AcsT = c.wslots[0].t[:, 0:4096].bitcast(F32)
    dtT = c.wslots[1].t[:, 0:4096].bitcast(F32)
    tA = c.wslots[2].t[:, 0:4096].bitcast(F32)
    tokw = c.wslots[3].t[:, 0:4096].bitcast(F32)
    dt_tok = tokw[:, 0:1024].rearrange("p (c h) -> p c h", h=64)
    Acs_tok = tokw[:, 1024:2048].rearrange("p (c h) -> p c h", h=64)
    tok2 = c.wslots[4].t[:, 0:4096].bitcast(F32)
    expA_tok = tok2[:, 0:1024].rearrange("p (c h) -> p c h", h=64)
    dA_tok = tok2[:, 1024:2048].rearrange("p (c h) -> p c h", h=64)
    acol = c.stg_f32[3].t[:, 0:1]
    cst2 = c.stg_f32[2].t[:, 0:128]
    S.dma("sp", cst2, c.consts2_d[:, C_U:C_U + 128], w=["Uf"])
    S.dma("sp", dtT[0:64, :], c.P["dt"], w=["dtT"])
    one_col = c.cst.t[0:64, C_ONES:C_ONES + 1]
    S.op("dve", lambda e: e.tensor_scalar(out=dtT[0:64, :], in0=dtT[0:64, :], scalar1=V[0:64, V_DTB:V_DTB + 1], scalar2=None, op0=ALU.add),
         r=["dtT", "vecs"], w=["dtT"])
    S.op("act", lambda e: e.activation(out=tA[0:64, :], in_=dtT[0:64, :], func=AF.Abs), r=["dtT"], w=["tA"])
    S.op("act", lambda e: e.activation(out=tA[0:64, :], in_=tA[0:64, :], func=AF.Exp, scale=-1.0), r=["tA"], w=["tA"])
    S.op("act", lambda e: e.activation(out=tA[0:64, :], in_=tA[0:64, :], func=AF.Ln, bias=one_col), r=["tA", "cst"], w=["tA"])
    S.op("dve", lambda e: e.scalar_tensor_tensor(out=dtT[0:64, :], in0=dtT[0:64, :], scalar=0.0, in1=tA[0:64, :], op0=ALU.max, op1=ALU.add),
         r=["dtT", "tA"], w=["dtT"])
    S.op("act", lambda e: e.activation(out=acol[0:64, :], in_=V[0:64, V_ALOG:V_ALOG + 1], func=AF.Exp), r=["vecs"], w=["acol"])
    S.op("dve", lambda e: e.tensor_scalar(out=tA[0:64, :], in0=dtT[0:64, :], scalar1=acol[0:64, :], scalar2=-1.0, op0=ALU.mult, op1=ALU.mult),
         r=["dtT", "acol"], w=["tA"])
    for src, dst, nm in ((dtT, dt_tok, "dt_tok"), (tA, dA_tok, "dA_tok")):
        for half in range(2):
            pb = c.psb[half]

            def tr(e, src=src, half=half, pb=pb):
                ins = None
                for j in range(8):
                    cch = half * 8 + j
                    ins = e.transpose(pb.t[:, j * 64:(j + 1) * 64], src[0:64, cch * 128:(cch + 1) * 128], ident_f[0:64, 0:64])
                return ins
            S.op("pe", tr, r=["dtT", "tA", "cst"], w=[pb.k])
            S.op("dve", lambda e, dst=dst, half=half, pb=pb: e.tensor_copy(
                out=dst[:, half * 8:(half + 1) * 8, :], in_=pb.t[:, :].rearrange("p (c h) -> p c h", h=64)), r=[pb.k], w=[nm])
    for half in range(2):
        pb = c.psb[2 + half]

        def cs(e, half=half, pb=pb):
            ins = None
            for j in range(8):
                cch = half * 8 + j
                ins = e.matmul(pb.t[:, j * 64:(j + 1) * 64], cst2, dA_tok[:, cch, :], start=True, stop=True)
            return ins
        S.op("pe", cs, r=["dA_tok", "Uf"], w=[pb.k])
        S.op("dve", lambda e, half=half, pb=pb: e.tensor_copy(out=Acs_tok[:, half * 8:(half + 1) * 8, :],
                                                           in_=pb.t[:, :].rearrange("p (c h) -> p c h", h=64)), r=[pb.k], w=["Acs_tok"])
    for q4 in range(4):
        pb = c.psb[4 + q4]

        def cs2(e, q4=q4, pb=pb):
            ins = None
            for j in range(4):
                cch = q4 * 4 + j
                ins = e.matmul(pb.t[0:64, j * 128:(j + 1) * 128], dA_tok[:, cch, :], cst2, start=True, stop=True)
            return ins
        S.op("pe", cs2, r=["dA_tok", "Uf"], w=[pb.k])
        S.op("dve", lambda e, q4=q4, pb=pb: e.tensor_copy(out=AcsT[0:64, q4 * TB:(q4 + 1) * TB], in_=pb.t[0:64, :]), r=[pb.k], w=["AcsT"])
    S.op("act", lambda e: e.activation(out=tok2[:, 0:1024], in_=tokw[:, 1024:2048], func=AF.Exp), r=["Acs_tok"], w=["expA_tok"])
    S.dma("sp", c.acs_d, AcsT[0:64, :], r=["AcsT"], w=["acs_d"])
    Dfull = c.wslots[1].t[:, 0:4096]
    sched_barrier(S)
    for h in range(NH_SSD):
        S.op("dve", lambda e, h=h: e.tensor_scalar(out=Dfull[:, h * 64:(h + 1) * 64], in0=c.cst.t[:, C_ONES:C_ONES + 64],
                                                   scalar1=V[:, V_DBC + h:V_DBC + h + 1], scalar2=None, op0=ALU.mult),
             r=["cst", "vecs"], w=["Dfull"])

    H = R6[:, 0:8192].bitcast(F32)
    Hbf = R6[:, 8192:12288]
    xs_tok = R6[:, 12288:16384]
    xw = R6[:, 16384:20480]
    y_tok = R6[:, 20480:24576]
    zc = R6[:, 24576:28672].rearrange("p (cc t) -> p cc t", t=128)
    xsT = R6[:, 28672:32768].rearrange("p (cc t) -> p cc t", t=128)
    ygf = R4[:, 0:8192].bitcast(F32).rearrange("p (cc t) -> p cc t", t=128)
    yout = R4[:, 8192:12288].rearrange("p (cc t) -> p cc t", t=128)
    BT = R4[:, 12288:13312].rearrange("p (g t) -> p g t", t=128)
    CT = R4[:, 13312:14336].rearrange("p (g t) -> p g t", t=128)
    B_tok = R4[:, 14336:15360].rearrange("p (g t) -> p g t", t=128)
    cbTm = R4[:, 15360:17408].bitcast(F32).rearrange("p (g t) -> p g t", t=128)
    Et = [R4[:, 17408 + i * 1024:17408 + (i + 1) * 1024].bitcast(F32) for i in range(2)]
    smt = [R4[:, 19456 + i * 1024:19456 + (i + 1) * 1024].bitcast(F32) for i in range(2)]
    Mp = [R4[:, 21504 + i * 512:21504 + (i + 1) * 512] for i in range(2)]
    dec = c.stg_f32[3].t[:, 64:128]
    A_rows = c.wslots[5].t[:, 0:5632].bitcast(F32)
    Uf = cst2
    pgroups = [(0, 0, 22), (32, 22, 44), (64, 44, 64)]
    batches = []
    for pbase, hs, he in pgroups:
        for h0 in range(hs, he, 4):
            batches.append((pbase, hs, h0, min(4, he - h0)))
    xcv = c.xc_d.rearrange("(cc p) t -> p cc t", p=128)
    zv = c.P["z"].rearrange("(cc p) t -> p cc t", p=128)
    yv = c.yssd.rearrange("(cc p) t -> p cc t", p=128)
    S.op("dve", lambda e: e.memset(H, 0.0), w=["H"])
    S.op("dve", lambda e: e.memset(Hbf, 0.0), w=["Hbf"])
    NCH = L // 128
    for ch in range(NCH):
        t0 = ch * 128
        S.dma("sp", xsT[:, 0:16, :], xcv[:, 0:16, t0:t0 + 128], w=["xsT0"])
        S.dma("sp", xsT[:, 16:32, :], xcv[:, 16:32, t0:t0 + 128], w=["xsT1"])
        S.dma("sp", BT, xcv[:, 32:40, t0:t0 + 128], w=["BT"])
        S.dma("sp", CT, xcv[:, 40:48, t0:t0 + 128], w=["CT"])
        S.dma("sp", zc, zv[:, :, t0:t0 + 128], w=["zc"])
        for pbase, hs, he in pgroups:
            S.dma("sp", A_rows[pbase:pbase + 1, 0:(he - hs) * 128].rearrange("o (h t) -> o h t", t=128),
                  c.acs_d[hs:he, t0:t0 + 128].rearrange("(o h) t -> o h t", o=1), r=["acs_d"], w=["A_rows"])
        pb7 = c.psb[7]
        pb7v = pb7.t[:, :].bitcast(BF16)
        for q8 in range(4):
            def trx(e, q8=q8):
                ins = None
                for j in range(8):
                    ins = e.transpose(pb7v[:, j * 128:(j + 1) * 128], xsT[:, q8 * 8 + j, :], c.ident_bf.t[:, :])
                return ins
            S.op("pe", trx, r=["xsT0", "xsT1", "const"], w=[pb7.k])
            S.op("act", lambda e, q8=q8: e.copy(out=xs_tok[:, q8 * 1024:(q8 + 1) * 1024], in_=pb7v[:, 0:1024]), r=[pb7.k], w=["xs_tok"])

        def trb(e):
            ins = None
            for g in range(8):
                ins = e.transpose(pb7v[:, g * 128:(g + 1) * 128], BT[:, g, :], c.ident_bf.t[:, :])
            return ins
        S.op("pe", trb, r=["BT", "const"], w=[pb7.k])
        S.op("act", lambda e: e.copy(out=B_tok, in_=pb7v[:, 0:1024].rearrange("p (g t) -> p g t", t=128)), r=[pb7.k], w=["B_tok"])
        for half in range(2):
            pb = c.psb[2 + half]

            def mcb(e, half=half, pb=pb):
                ins = None
                for j in range(4):
                    g = half * 4 + j
                    ins = e.matmul(pb.t[:, j * 128:(j + 1) * 128], BT[:, g, :], CT[:, g, :], start=True, stop=True)
                return ins
            S.op("pe", mcb, r=["BT", "CT"], w=[pb.k])
            for j in range(4):
                g = half * 4 + j
                S.op("dve", lambda e, g=g, j=j, pb=pb: e.tensor_tensor(out=cbTm[:, g, :], in0=pb.t[:, j * 128:(j + 1) * 128], in1=Uf, op=ALU.mult),
                     r=[pb.k, "Uf"], w=["cbTm"])
        S.op("act", lambda e: e.activation(out=zc, in_=zc, func=AF.Silu), r=["zc"], w=["zc"])
        bi = 0
        for pbase, hs, h0, nh in batches:
            pT1 = c.psb[bi % 2]
            E, sm, M = Et[bi % 2], smt[bi % 2], Mp[bi % 2]
            ek, sk, mk = "E%d" % (bi % 2), "sm%d" % (bi % 2), "Mp%d" % (bi % 2)
            bi += 1
            S.op("pe", lambda e, pT1=pT1, pbase=pbase, hs=hs, h0=h0, nh=nh: e.matmul(
                pT1.t[:, 0:nh * 128], c.cst.t[pbase:pbase + 1, C_ONES:C_ONES + 128],
                A_rows[pbase:pbase + 1, (h0 - hs) * 128:(h0 - hs + nh) * 128], start=True, stop=True),
                r=["A_rows", "cst"], w=[pT1.k])
            for i in range(nh):
                h = h0 + i
                S.op("dve", lambda e, i=i, h=h, pT1=pT1, sm=sm: e.tensor_scalar(
                    out=sm[:, i * 128:(i + 1) * 128], in0=pT1.t[:, i * 128:(i + 1) * 128], scalar1=Acs_tok[:, ch, h:h + 1], scalar2=0.0,
                    op0=ALU.subtract, op1=ALU.min), r=[pT1.k, "Acs_tok"], w=[sk])
            S.op("act", lambda e, pT1=pT1, h0=h0, nh=nh: e.activation(
                out=dec[:, h0:h0 + nh].rearrange("p (h o) -> p h o", o=1),
                in_=pT1.t[:, 0:nh * 128].rearrange("p (h t) -> p h t", t=128)[:, :, 127:128], func=AF.Exp), r=[pT1.k], w=["dec"])
            S.op("act", lambda e, E=E, sm=sm, nh=nh: e.activation(out=E[:, 0:nh * 128], in_=sm[:, 0:nh * 128], func=AF.Exp), r=[sk], w=[ek])
            for i in range(nh):
                h = h0 + i
                g = h // 8
                S.op("dve", lambda e, i=i, h=h, g=g, E=E, M=M: e.scalar_tensor_tensor(
                    out=M[:, i * 128:(i + 1) * 128], in0=E[:, i * 128:(i + 1) * 128], scalar=dt_tok[:, ch, h:h + 1], in1=cbTm[:, g, :],
                    op0=ALU.mult, op1=ALU.mult), r=[ek, "dt_tok", "cbTm"], w=[mk])
                if ch < NCH - 1:
                    S.op("pool", lambda e, i=i, h=h, E=E: e.tensor_scalar(
                        out=xw[:, h * 64:(h + 1) * 64], in0=xs_tok[:, h * 64:(h + 1) * 64], scalar1=E[:, i * 128 + 127:i * 128 + 128],
                        scalar2=dt_tok[:, ch, h:h + 1], op0=ALU.mult, op1=ALU.mult), r=[ek, "xs_tok", "dt_tok"], w=["xw%d" % g])
                po1 = c.psb[4]
                S.op("pe", lambda e, i=i, h=h, M=M, po1=po1: e.matmul(po1.t[:, (h % 8) * 64:(h % 8 + 1) * 64], M[:, i * 128:(i + 1) * 128],
                                                                     xs_tok[:, h * 64:(h + 1) * 64], start=True, stop=True),
                     r=[mk, "xs_tok"], w=[po1.k + "_h%d" % (h % 8)])
                if h % 8 == 7:
                    gs = slice(g * 512, (g + 1) * 512)
                    po2 = c.psb[5]
                    S.op("pe", lambda e, g=g, gs=gs, po2=po2: e.matmul(po2.t[:, :], CT[:, g, :], Hbf[:, gs], start=True, stop=True),
                         r=["CT", "Hbf%d" % g], w=[po2.k])
                    xd = c.stg_f32[0]
                    S.op("pool", lambda e, gs=gs, xd=xd: e.tensor_tensor(out=xd.t[:, :], in0=xs_tok[:, gs], in1=Dfull[:, gs], op=ALU.mult),
                         r=["xs_tok", "Dfull"], w=[xd.k])
                    sb1 = c.stg_f32[1]
                    S.op("dve", lambda e, po1=po1, xd=xd, sb1=sb1: e.tensor_tensor(out=sb1.t[:, :], in0=po1.t[:, :], in1=xd.t[:, :], op=ALU.add),
                         r=[po1.k + "_h%d" % r_ for r_ in range(8)] + [xd.k], w=[sb1.k] + [po1.k + "_h%d" % r_ for r_ in range(8)])
                    for r_ in range(8):
                        hh = g * 8 + r_
                        S.op("dve", lambda e, r_=r_, hh=hh, po2=po2, sb1=sb1: e.scalar_tensor_tensor(
                            out=y_tok[:, hh * 64:(hh + 1) * 64], in0=po2.t[:, r_ * 64:(r_ + 1) * 64], scalar=expA_tok[:, ch, hh:hh + 1],
                            in1=sb1.t[:, r_ * 64:(r_ + 1) * 64], op0=ALU.mult, op1=ALU.add), r=[po2.k, sb1.k, "expA_tok"], w=["y_tok"])
                    if ch < NCH - 1:
                        pS = c.psb[6]
                        S.op("pe", lambda e, g=g, gs=gs, pS=pS: e.matmul(pS.t[:, :], B_tok[:, g, :], xw[:, gs], start=True, stop=True),
                             r=["B_tok", "xw%d" % g], w=[pS.k])
                        for r_ in range(8):
                            hh = g * 8 + r_
                            S.op("pool", lambda e, hh=hh: e.tensor_scalar(out=H[:, hh * 64:(hh + 1) * 64], in0=H[:, hh * 64:(hh + 1) * 64],
                                                                           scalar1=dec[:, hh:hh + 1], scalar2=None, op0=ALU.mult),
                                 r=["dec", "H%d" % g], w=["H%d" % g])
                        S.op("dve", lambda e, gs=gs, pS=pS: e.tensor_tensor(out=H[:, gs], in0=H[:, gs], in1=pS.t[:, :], op=ALU.add),
                             r=[pS.k, "H%d" % g], w=["H%d" % g])
                        S.op("act", lambda e, gs=gs: e.copy(out=Hbf[:, gs], in_=H[:, gs]), r=["H%d" % g], w=["Hbf%d" % g])
        sq = xw.rearrange("p (cc t) -> p cc t", t=128)
        for q8 in range(4):
            def try_(e, q8=q8):
                ins = None
                for j in range(8):
                    cc = q8 * 8 + j
                    ins = e.transpose(pb7v[:, j * 128:(j + 1) * 128], y_tok[:, cc * 128:(cc + 1) * 128], c.ident_bf.t[:, :])
                return ins
            S.op("pe", try_, r=["y_tok", "const"], w=[pb7.k])
            S.op("dve", lambda e, q8=q8: e.tensor_tensor(out=ygf[:, q8 * 8:(q8 + 1) * 8, :],
                                                       in0=pb7v[:, 0:1024].rearrange("p (cc t) -> p cc t", t=128),
                                                       in1=zc[:, q8 * 8:(q8 + 1) * 8, :], op=ALU.mult), r=[pb7.k, "zc"], w=["ygf"])
        xwk = ["xw%d" % g for g in range(8)]
        S.op("act", lambda e: e.activation(out=sq, in_=ygf, func=AF.Square), r=["ygf"], w=xwk)
        pn = c.psb[6]

        def mmn(e):
            ins = None
            for cc in range(32):
                ins = e.matmul(pn.t[:, 0:128], c.ones_bf.t[:, :], sq[:, cc, :], start=(cc == 0), stop=(cc == 31))
            return ins
        S.op("pe", mmn, r=xwk + ["const"], w=[pn.k])
        rt = c.stg_f32[2].t[:, 128:256]
        rs = c.stg_f32[2].t[:, 256:384]
        S.op("act", lambda e: e.activation(out=rt, in_=pn.t[:, 0:128], func=AF.Sqrt, scale=1.0 / SSD_INNER, bias=c.eps_col.t[:, 0:1]),
             r=[pn.k], w=["rt"])
        S.op("dve", lambda e: e.reciprocal(out=rs, in_=rt), r=["rt"], w=["rs"])
        for cc in range(32):
            S.op("dve", lambda e, cc=cc: e.scalar_tensor_tensor(out=yout[:, cc, :], in0=ygf[:, cc, :], scalar=V[:, V_SSDN + cc:V_SSDN + cc + 1],
                                                                 in1=rs, op0=ALU.mult, op1=ALU.mult), r=["ygf", "rs", "vecs"], w=["yout"])
        S.dma("sp", yv[:, 0:16, t0:t0 + 128], yout[:, 0:16, :], r=["yout"], w=[("yssd", ch, 0)])
        S.dma("sp", yv[:, 16:32, t0:t0 + 128], yout[:, 16:32, :], r=["yout"], w=[("yssd", ch, 1)])
    sched_barrier(S)


FULL_PLAN = []
for _l in range(DEPTH):
    FULL_PLAN += [("ffn1", _l), ("inproj", _l), ("mem", _l), ("dsa", _l), ("ssd", _l), ("merge", _l), ("ffn2", _l)]
FULL_PLAN.append("final")
_NC_CACHE = {}


def kernel(**inputs):
    inp = {k: np.asarray(v) for k, v in inputs.items()}
    B = inp["x"].shape[0]
    if "nc" not in _NC_CACHE:
        _NC_CACHE["nc"] = build(FULL_PLAN)
    nc = _NC_CACHE["nc"]
    vec = np.stack([pack_vecs(inp, l) for l in range(DEPTH)])
    cst = make_consts()
    in_maps = [core_inputs(inp, b, vec, cst) for b in range(B)]
    res = run_bass_kernel_spmd(nc, in_maps, core_ids=list(range(B)))
    out = np.stack([np.ascontiguousarray(res.results[b]["outT"].T) for b in range(B)]).astype(np.float32)
    return out
```

```python
from contextlib import ExitStack
import numpy as np
import concourse.bass as bass
import concourse.mybir as mybir
from concourse.bass_utils import run_bass_kernel_spmd

F32 = mybir.dt.float32
BF16 = mybir.dt.bfloat16
I32 = mybir.dt.int32
AF = mybir.ActivationFunctionType
ALU = mybir.AluOpType
AX = mybir.AxisListType

D = 2048
L = 2048
DEPTH = 2
DFF = 5632
MEM = 256
EPS = 1e-6
SSD_INNER = 4096
CONV_CH = 6144
NH_SSD = 64
IN_SPLITS = (4096, 6144, 64, 2048, 128, 128, 1024, 64, 16, 2048, 6144)
IN_OFF = [0]
for _s in IN_SPLITS:
    IN_OFF.append(IN_OFF[-1] + _s)
IN_WIDTH = IN_OFF[-1]
(O_Z, O_XBC, O_DT, O_Q, O_K, O_V, O_QI, O_KI, O_WI, O_QM, O_G) = IN_OFF[:11]
TB = 512
NTB = L // TB
KC_D = D // 128


class Sched:
    ENGS = ("pe", "act", "dve", "pool", "sp")

    def __init__(self, nc, es, n_dma_sems=24):
        self.nc = nc
        self.eng = dict(pe=nc.tensor, act=nc.scalar, dve=nc.vector, pool=nc.gpsimd, sp=nc.sync)
        self.sem = {e: es.enter_context(nc.semaphore("sem_" + e)) for e in self.ENGS}
        self.cnt = {e: 0 for e in self.ENGS}
        self.seen = {e: {} for e in self.ENGS}
        self.snap = {}
        self.dsem = {}
        self.dcur = {}
        self.drr = {}
        for q in ("sp", "pool", "act"):
            n = n_dma_sems if q != "act" else 8
            self.dsem[q] = [es.enter_context(nc.semaphore("dsem_%s_%d" % (q, i))) for i in range(n)]
            self.dcur[q] = [0] * n
            self.drr[q] = 0
        self.bufs = {}
        self.n_wait = 0
        self.n_inst = 0

    def _wait(self, e, tok):
        if tok is None:
            return
        kind = tok[0]
        seen = self.seen[e]
        if kind == "c":
            _, f, c = tok
            if seen.get(f, 0) >= c:
                return
            if f == e and e == "pe":
                return
            self.eng[e].wait_ge(self.sem[f], c)
            self.n_wait += 1
            seen[f] = c
            sn = self.snap.get((f, c))
            if sn is not None:
                for g, v in zip(self.ENGS, sn):
                    if v > seen.get(g, 0):
                        seen[g] = v
        else:
            _, q, i, v = tok
            key = (q, i)
            if seen.get(key, 0) >= v:
                return
            self.eng[e].wait_ge(self.dsem[q][i], v)
            self.n_wait += 1
            seen[key] = v

    def _deps(self, e, r, w):
        for k in r:
            st = self.bufs.get(k)
            if st is not None:
                self._wait(e, st[0])
        for k in w:
            st = self.bufs.get(k)
            if st is not None:
                self._wait(e, st[0])
                for t in st[1]:
                    if t[0] == "c" and t[1] == e:
                        continue
                    self._wait(e, t)

    def _commit(self, tok, r, w):
        for k in r:
            st = self.bufs.setdefault(k, [None, []])
            st[1] = [t for t in st[1] if not (t[0] == tok[0] and t[1] == tok[1] and (t[0] == "c" or t[2] == tok[2]))]
            st[1].append(tok)
        for k in w:
            self.bufs[k] = [tok, []]

    def op(self, e, fn, r=(), w=()):
        self._deps(e, r, w)
        ins = fn(self.eng[e])
        self.cnt[e] += 1
        c = self.cnt[e]
        ins.then_inc(self.sem[e], 1)
        self.n_inst += 1
        sn = self.seen[e]
        self.snap[(e, c)] = tuple(c if g == e else sn.get(g, 0) for g in self.ENGS)
        tok = ("c", e, c)
        self._commit(tok, r, w)
        return tok

    def dma(self, q, out, in_, r=(), w=(), **kw):
        self._deps(q, r, w)
        i = self.drr[q]
        self.drr[q] = (i + 1) % len(self.dsem[q])
        cur = self.dcur[q][i]
        if cur > 0:
            self._wait(q, ("d", q, i, cur))
        self.eng[q].dma_start(out=out, in_=in_, **kw).then_inc(self.dsem[q][i], 16)
        self.dcur[q][i] = cur + 16
        tok = ("d", q, i, cur + 16)
        self._commit(tok, r, w)
        return tok

    def finish(self, e="sp"):
        for f in self.ENGS:
            if self.cnt[f] > 0 and f != e:
                self._wait(e, ("c", f, self.cnt[f]))
        for q in self.dsem:
            for i, cur in enumerate(self.dcur[q]):
                if cur > 0:
                    self._wait(e, ("d", q, i, cur))


class Buf:
    def __init__(self, t, key):
        self.t = t
        self.k = key

    def __getitem__(self, idx):
        return self.t[idx]


class Ctx:
    pass


def make_ctx(nc, es):
    c = Ctx()
    c.nc = nc
    c.es = es
    c.S = Sched(nc, es)
    c.nbuf = 0

    def sb(shape, dt, name=None):
        c.nbuf += 1
        name = name or ("sb%d" % c.nbuf)
        t = es.enter_context(nc.sbuf_tensor(name, list(shape), dt))
        return Buf(t, name)

    def ps(name, shape=(128, 512), dt=F32):
        t = es.enter_context(nc.psum_tensor(name, list(shape), dt))
        return Buf(t, name)
    c.sb = sb
    c.ps = ps
    return c


def stream_linear(c, jobs, rhs_fn, rhs_keys, ntb, epilogue, wslots, psum_sets, tbw=TB):
    S = c.S
    nslot = len(wslots)
    state = c.__dict__.setdefault("_lin_state", {"slot": 0, "pset": 0})
    flat = []
    for ji, job in enumerate(jobs):
        for gi, g in enumerate(job):
            flat.append((ji, gi, g))
    PREF = nslot - 1
    slot_of = {}

    def issue(n):
        ji, gi, (W, col0, ncols, KC) = flat[n]
        si = state["slot"]
        state["slot"] = (si + 1) % nslot
        ws = wslots[si]
        wv = W.rearrange("(kc p) f -> p kc f", p=128)
        dst = ws.t[:, 0:KC * ncols].rearrange("p (kc f) -> p kc f", f=ncols)
        half = KC // 2
        S.dma("pool", dst[:, 0:half, :], wv[:, 0:half, col0:col0 + ncols], w=[ws.k + "_a"])
        S.dma("pool", dst[:, half:KC, :], wv[:, half:KC, col0:col0 + ncols], w=[ws.k + "_b"])
        slot_of[n] = (ws, dst)

    nxt = 0
    n = 0
    for ji, job in enumerate(jobs):
        while nxt < len(flat) and nxt < n + nslot:
            issue(nxt)
            nxt += 1
        for tb in range(ntb):
            pi = state["pset"] % len(psum_sets)
            state["pset"] = (pi + 1) % len(psum_sets)
            pset = psum_sets[pi]
            for gi, (W, col0, ncols, KC) in enumerate(job):
                ws, dst = slot_of[n + gi]
                pb = pset[gi]

                def mm(e, dst=dst, pb=pb, KC=KC, gi=gi, tb=tb, ncols=ncols):
                    ins = None
                    for kc in range(KC):
                        ins = e.matmul(pb.t[0:ncols, 0:tbw], dst[:, kc, :], rhs_fn(gi, kc, tb),
                                       start=(kc == 0), stop=(kc == KC - 1))
                    return ins
                S.op("pe", mm, r=[ws.k + "_a", ws.k + "_b"] + list(rhs_keys(gi, tb)), w=[pb.k])
            epilogue(ji, tb, pset)
        n += len(job)


def sched_barrier(S):
    engs = [e for e in S.ENGS]
    toks = [("c", f, S.cnt[f]) for f in S.ENGS if S.cnt[f] > 0]
    dtoks = []
    for q in S.dsem:
        for i, cur in enumerate(S.dcur[q]):
            if cur > 0:
                dtoks.append(("d", q, i, cur))
    for e in engs:
        for t in toks:
            if t[1] != e:
                S._wait(e, t)
        for t in dtoks:
            S._wait(e, t)
    S.bufs = {}


def phase_norm(c, x_src, gcol, out_mode, out_dst=None):
    S = c.S
    xv = x_src.rearrange("(kc p) t -> p kc t", p=128)
    xin = c.R44.t[:, 0:16 * TB * 2].bitcast(F32).rearrange("p (kc t) -> p kc t", t=TB)
    hT = c.hT
    for tb in range(NTB):
        t0 = tb * TB
        for hf in range(2):
            S.dma("sp", xin[:, hf * 8:(hf + 1) * 8, :], xv[:, hf * 8:(hf + 1) * 8, t0:t0 + TB],
                  w=["xin%d" % hf])
        pb = c.psb[tb % 2]
        for kc in range(KC_D):
            sq = c.stg_bf[kc % 3]
            S.op("act", lambda e, sq=sq, kc=kc: e.activation(out=sq.t[:, :], in_=xin[:, kc, :], func=AF.Square),
                 r=["xin%d" % (kc // 8)], w=[sq.k])
            S.op("pe", lambda e, sq=sq, kc=kc, pb=pb: e.matmul(pb.t[:, :], c.ones_bf.t[:, :], sq.t[:, :],
                                                                start=(kc == 0), stop=(kc == KC_D - 1)),
                 r=[sq.k, "const"], w=[pb.k])
        rt = c.stg_f32[0]
        S.op("act", lambda e, pb=pb: e.activation(out=rt.t[:, :], in_=pb.t[:, :], func=AF.Sqrt,
                                                  scale=1.0 / D, bias=c.eps_col.t[:, 0:1]),
             r=[pb.k, "const"], w=[rt.k])
        rs = c.stg_f32[1]
        S.op("dve", lambda e: e.reciprocal(out=rs.t[:, :], in_=rt.t[:, :]), r=[rt.k], w=[rs.k])
        for kc in range(KC_D):
            if out_mode == "h":
                S.op("dve", lambda e, kc=kc: e.scalar_tensor_tensor(
                    out=hT[:, kc, t0:t0 + TB], in0=xin[:, kc, :], scalar=gcol[:, kc:kc + 1], in1=rs.t[:, :],
                    op0=ALU.mult, op1=ALU.mult),
                    r=["xin%d" % (kc // 8), rs.k, "vecs"], w=["hT%d" % tb])
            else:
                ot = c.stg_f32[2 + kc % 2]
                S.op("dve", lambda e, kc=kc, ot=ot: e.scalar_tensor_tensor(
                    out=ot.t[:, :], in0=xin[:, kc, :], scalar=gcol[:, kc:kc + 1], in1=rs.t[:, :],
                    op0=ALU.mult, op1=ALU.mult),
                    r=["xin%d" % (kc // 8), rs.k, "vecs"], w=[ot.k])
                S.dma("sp", out_dst[kc * 128:(kc + 1) * 128, t0:t0 + TB], ot.t[:, :], r=[ot.k])


def phase_ffn(c, w_in, w_out, x_src, x_dst):
    S = c.S
    NJ = DFF // 128
    hT = c.hT
    actd = c.act_d
    jobs = [[(w_in, j * 128, 128, KC_D), (w_in, DFF + j * 128, 128, KC_D)] for j in range(NJ)]
    cnt = [0]

    def epi_a(ji, tb, pset):
        i = cnt[0]
        cnt[0] += 1
        sg = c.stg_f32[i % 2]
        S.op("act", lambda e: e.activation(out=sg.t[:, :], in_=pset[0].t[:, :], func=AF.Silu),
             r=[pset[0].k], w=[sg.k])
        ab = c.stg_bf[i % 3]
        S.op("dve", lambda e: e.tensor_tensor(out=ab.t[:, :], in0=pset[1].t[:, :], in1=sg.t[:, :], op=ALU.mult),
             r=[pset[1].k, sg.k], w=[ab.k])
        S.dma("sp", actd[ji * 128:(ji + 1) * 128, tb * TB:(tb + 1) * TB], ab.t[:, :], r=[ab.k],
              w=[("act", ji, tb)])

    stream_linear(c, jobs, lambda gi, kc, tb: hT[:, kc, tb * TB:(tb + 1) * TB],
                  lambda gi, tb: ["hT%d" % tb], NTB, epi_a, c.wslots,
                  [[c.psb[0], c.psb[1]], [c.psb[2], c.psb[3]], [c.psb[4], c.psb[5]], [c.psb[6], c.psb[7]]])
    sched_barrier(S)
    av = actd.rearrange("(kc p) t -> p kc t", p=128)
    blks = [c.R64.t[:, 0:NJ * TB].rearrange("p (kc t) -> p kc t", t=TB),
            c.R44.t[:, 0:NJ * TB].rearrange("p (kc t) -> p kc t", t=TB)]
    bkeys = ["ablkA", "ablkB"]
    for tb in range(NTB):
        blk = blks[tb % 2]
        bk = bkeys[tb % 2]
        t0 = tb * TB
        for q4 in range(4):
            S.dma("sp", blk[:, q4 * 11:(q4 + 1) * 11, :], av[:, q4 * 11:(q4 + 1) * 11, t0:t0 + TB], w=[bk + str(q4)])
        jobs = [[(w_out, dc * 128, 128, NJ)] for dc in range(KC_D)]
        cnt2 = [0]

        def epi_b(ji, tb_unused, pset, tb=tb, t0=t0):
            i = cnt2[0]
            cnt2[0] += 1
            xt = c.stg_f32[i % 2]
            S.dma("sp", xt.t[:, :], x_src[ji * 128:(ji + 1) * 128, t0:t0 + TB], w=[xt.k])
            xo = c.stg_f32[2 + i % 2]
            S.op("dve", lambda e: e.scalar_tensor_tensor(out=xo.t[:, :], in0=pset[0].t[:, :], scalar=0.5,
                                                         in1=xt.t[:, :], op0=ALU.mult, op1=ALU.add),
                 r=[pset[0].k, xt.k], w=[xo.k])
            S.dma("sp", x_dst[ji * 128:(ji + 1) * 128, t0:t0 + TB], xo.t[:, :], r=[xo.k])

        stream_linear(c, jobs, lambda gi, kc, tb_, blk=blk: blk[:, kc, :],
                      lambda gi, tb_, bk=bk: [bk + str(q) for q in range(4)], 1, epi_b, c.wslots,
                      [[c.psb[i]] for i in range(8)])
    sched_barrier(S)


V_FFN1, V_MIX, V_FFN2, V_MEMN = 0, 16, 32, 48
V_SSDN = 64
V_CONVW = 96
V_CONVB = 288
V_DTB, V_ALOG, V_DSKIP = 336, 337, 338
V_FINAL = 339
V_DBC = 360
NV = 424


def pack_vecs(inp, l):
    v = np.zeros((128, NV), np.float32)

    def col(vec):
        return np.ascontiguousarray(np.asarray(vec, np.float32).reshape(-1, 128).T)
    v[:, V_FFN1:V_FFN1 + 16] = col(inp["ffn1_norm"][l])
    v[:, V_MIX:V_MIX + 16] = col(inp["mix_norm"][l])
    v[:, V_FFN2:V_FFN2 + 16] = col(inp["ffn2_norm"][l])
    v[:, V_MEMN:V_MEMN + 16] = col(inp["mem_norm"][l])
    v[:, V_SSDN:V_SSDN + 32] = col(inp["ssd_norm"][l])
    for k in range(4):
        v[:, V_CONVW + k * 48:V_CONVW + (k + 1) * 48] = col(inp["conv_w"][l][k])
    v[:, V_CONVB:V_CONVB + 48] = col(inp["conv_b"][l])
    v[0:64, V_DTB] = inp["dt_bias"][l]
    v[0:64, V_ALOG] = inp["a_log"][l]
    v[0:64, V_DSKIP] = inp["d_skip"][l]
    v[:, V_FINAL:V_FINAL + 16] = col(inp["final_norm"])
    v[:, V_DBC:V_DBC + 64] = np.asarray(inp["d_skip"][l], np.float32)[None, :]
    return v


C_IDENT, C_ONES = 0, 128
NCONST = 256


def make_consts():
    cst = np.zeros((128, NCONST), np.float32)
    cst[:, C_IDENT:C_IDENT + 128] = np.eye(128, dtype=np.float32)
    cst[:, C_ONES:C_ONES + 128] = 1.0
    return cst


def build(plan, dbg=None):
    nc = bass.Bass("TRN2", target_bir_lowering=False)
    es = ExitStack()
    with es:
        c = make_ctx(nc, es)
        S = c.S
        dt = nc.dram_tensor
        xT = dt("xT", [D, L], F32, kind="ExternalInput").ap()
        memT = dt("memT", [D, MEM], F32, kind="ExternalInput").ap()
        pos = dt("pos", [128, L], I32, kind="ExternalInput").ap()
        c.consts2_d = dt("consts2", [128, NCONST2], F32, kind="ExternalInput").ap()
        vecs = dt("vecs", [DEPTH, 128, NV], F32, kind="ExternalInput").ap()
        consts = dt("consts", [128, NCONST], F32, kind="ExternalInput").ap()
        W = {}
        for name, shp in (("w_ffn1_in", [D, 2 * DFF]), ("w_ffn1_out", [DFF, D]), ("w_ffn2_in", [D, 2 * DFF]),
                          ("w_ffn2_out", [DFF, D]), ("w_in", [D, IN_WIDTH]), ("w_mem_kv", [D, 2 * D]),
                          ("w_br_ssd", [SSD_INNER, D]), ("w_br_dsa", [D, D]), ("w_br_mem", [D, D]), ("w_out", [D, D])):
            W[name] = dt(name, [DEPTH, shp[0] + 1, shp[1]], F32, kind="ExternalInput").ap()[:, 0:shp[0], :]
        outT = dt("outT", [D, L], F32, kind="ExternalOutput").ap()
        c.xres = dt("xres", [D, L], F32, kind="Internal").ap()
        c.act_d = dt("act_d", [DFF, L], BF16, kind="Internal").ap()
        c.P = {}
        for name, off, width, dt_ in P_SPECS:
            c.P[name] = dt("P_" + name, [width, L], dt_, kind="Internal").ap()
        c.yssd = dt("yssd", [SSD_INNER, L], BF16, kind="Internal").ap()
        c.ydsa = dt("ydsa", [D, L], BF16, kind="Internal").ap()
        c.ymem = dt("ymem", [D, L], BF16, kind="Internal").ap()
        c.pos = pos
        c.xc_d = dt("xc_d", [CONV_CH, L], BF16, kind="Internal").ap()
        c.acs_d = dt("acs_d", [NH_SSD, L], F32, kind="Internal").ap()
        c.V_l = None
        c.consts_d = consts

        c.R64 = c.sb([128, 32768], BF16, "R64")
        c.R44 = c.sb([128, 22528], BF16, "R44")
        c.hT = c.R64.t[:, :].rearrange("p (kc t) -> p kc t", t=L)
        c.WS = c.sb([128, 6 * 5632], BF16, "WS")
        c.wslots = [Buf(c.WS.t[:, i * 5632:(i + 1) * 5632], "wslot%d" % i) for i in range(6)]
        c.stg_f32 = [c.sb([128, TB], F32, "stgf%d" % i) for i in range(4)]
        c.stg_bf = [c.sb([128, TB], BF16, "stgb%d" % i) for i in range(3)]
        c.vecs = [c.sb([128, NV], F32, "vecs%d" % l) for l in range(DEPTH)]
        c.cst = c.sb([128, NCONST], F32, "cst")
        c.ones_bf = c.sb([128, 128], BF16, "ones_bf")
        c.ident_bf = c.sb([128, 128], BF16, "ident_bf")
        c.eps_col = c.sb([128, 1], F32, "eps_col")
        c.psb = [c.ps("ps%d" % i) for i in range(8)]

        for l in range(DEPTH):
            S.dma("sp", c.vecs[l].t[:, :], vecs[l], w=["vecs"])
        S.dma("sp", c.cst.t[:, :], consts, w=["cst"])
        S.op("dve", lambda e: e.tensor_copy(out=c.ones_bf.t[:, :], in_=c.cst.t[:, C_ONES:C_ONES + 128]), r=["cst"], w=["const"])
        S.op("dve", lambda e: e.tensor_copy(out=c.ident_bf.t[:, :], in_=c.cst.t[:, C_IDENT:C_IDENT + 128]), r=["cst"], w=["const"])
        S.op("dve", lambda e: e.memset(c.eps_col.t[:, :], EPS), w=["const"])
        sched_barrier(S)

        x_cur = xT
        for l in range(DEPTH):
            V = c.vecs[l].t
            if ("ffn1", l) in plan:
                phase_norm(c, x_cur, V[:, V_FFN1:V_FFN1 + 16], "h")
                sched_barrier(S)
                phase_ffn(c, W["w_ffn1_in"][l], W["w_ffn1_out"][l], x_cur, c.xres)
                x_cur = c.xres
            if ("inproj", l) in plan:
                phase_norm(c, x_cur, V[:, V_MIX:V_MIX + 16], "h")
                sched_barrier(S)
                phase_inproj(c, W["w_in"][l])
            if ("mem", l) in plan:
                phase_mem(c, memT, V[:, V_MEMN:V_MEMN + 16], W["w_mem_kv"][l])
            if ("dsa", l) in plan:
                phase_dsa(c, l)
            if ("ssd", l) in plan:
                phase_ssd(c, l, V)
            if ("merge", l) in plan:
                phase_merge(c, W["w_br_ssd"][l], W["w_br_dsa"][l], W["w_br_mem"][l], W["w_out"][l], x_cur, c.xres)
                x_cur = c.xres
            if ("ffn2", l) in plan:
                phase_norm(c, x_cur, V[:, V_FFN2:V_FFN2 + 16], "h")
                sched_barrier(S)
                phase_ffn(c, W["w_ffn2_in"][l], W["w_ffn2_out"][l], x_cur, c.xres)
                x_cur = c.xres
        if "final" in plan:
            phase_norm(c, x_cur, c.vecs[0].t[:, V_FINAL:V_FINAL + 16], "out", outT)
        elif dbg is not None:
            if dbg == "xres":
                src, nrow, sdt = x_cur, D, F32
            else:
                src, nrow, sdt = dbg
            for kc in range((nrow + 127) // 128):
                nr = min(128, nrow - kc * 128)
                for tb in range(NTB):
                    st = c.stg_f32[(kc * NTB + tb) % 4]
                    if sdt == F32:
                        S.dma("sp", st.t[0:nr, :], src(c)[kc * 128:kc * 128 + nr, tb * TB:(tb + 1) * TB] if callable(src) else src[kc * 128:kc * 128 + nr, tb * TB:(tb + 1) * TB], w=[st.k])
                    else:
                        sb_ = c.stg_bf[(kc * NTB + tb) % 3]
                        S.dma("sp", sb_.t[0:nr, :], src(c)[kc * 128:kc * 128 + nr, tb * TB:(tb + 1) * TB], w=[sb_.k])
                        S.op("dve", lambda e, st=st, sb_=sb_, nr=nr: e.tensor_copy(out=st.t[0:nr, :], in_=sb_.t[0:nr, :]), r=[sb_.k], w=[st.k])
                    S.dma("sp", outT[kc * 128:kc * 128 + nr, tb * TB:(tb + 1) * TB], st.t[0:nr, :], r=[st.k])
        S.finish("sp")
        print("sched: inst=%d waits=%d cnt=%s" % (S.n_inst, S.n_wait, S.cnt))
    return nc


W_NAMES = ("w_ffn1_in", "w_ffn1_out", "w_ffn2_in", "w_ffn2_out", "w_in", "w_mem_kv", "w_br_ssd", "w_br_dsa", "w_br_mem", "w_out")


def core_inputs(inp, b, vec, cst):
    im = {"xT": np.ascontiguousarray(inp["x"][b].T), "memT": np.ascontiguousarray(inp["mem"][b].T),
          "pos": np.ascontiguousarray(np.broadcast_to(inp["positions"][b].reshape(1, L).astype(np.int32), (128, L))), "vecs": vec, "consts": cst,
          "consts2": make_consts2()}
    for n in W_NAMES:
        w = inp[n]
        p = np.empty((w.shape[0], w.shape[1] + 1, w.shape[2]), np.float32)
        p[:, :w.shape[1], :] = w
        p[:, w.shape[1], :] = float(b)
        im[n] = p
    return im


P_SPECS = [("z", O_Z, 4096, BF16), ("xbc", O_XBC, 6144, BF16), ("dt", O_DT, 64, F32), ("q", O_Q, 2048, BF16),
           ("k", O_K, 128, BF16), ("v", O_V, 128, BF16), ("qi", O_QI, 1024, BF16), ("ki", O_KI, 64, BF16),
           ("wi", O_WI, 16, F32), ("qm", O_QM, 2048, BF16), ("g", O_G, 6144, BF16)]


def phase_inproj(c, w_in):
    S = c.S
    hT = c.hT
    jobs = []
    meta = []
    for name, off, width, dt_ in P_SPECS:
        for f0 in range(0, width, 128):
            nco = min(128, width - f0)
            jobs.append([(w_in, off + f0, nco, KC_D)])
            meta.append((name, f0, nco, dt_))
    cnt = [0]

    def epi(ji, tb, pset):
        name, f0, nco, dt_ = meta[ji]
        i = cnt[0]
        cnt[0] += 1
        if dt_ == F32:
            st = c.stg_f32[i % 4]
        else:
            st = c.stg_bf[i % 3]
        if i % 2 == 0:
            S.op("act", lambda e: e.copy(out=st.t[0:nco, :], in_=pset[0].t[0:nco, :]), r=[pset[0].k], w=[st.k])
        else:
            S.op("dve", lambda e: e.tensor_copy(out=st.t[0:nco, :], in_=pset[0].t[0:nco, :]), r=[pset[0].k], w=[st.k])
        S.dma("sp", c.P[name][f0:f0 + nco, tb * TB:(tb + 1) * TB], st.t[0:nco, :], r=[st.k], w=[("P", name, f0, tb)])

    stream_linear(c, jobs, lambda gi, kc, tb: hT[:, kc, tb * TB:(tb + 1) * TB],
                  lambda gi, tb: ["hT%d" % tb], NTB, epi, c.wslots, [[c.psb[i]] for i in range(8)])
    sched_barrier(S)


def phase_mem(c, memT, gcol, w_kv):
    S = c.S
    r = c.R44.t
    mem_in = r[:, 0:8192].bitcast(F32).rearrange("p (kc t) -> p kc t", t=MEM)
    mnT = r[:, 8192:12288].rearrange("p (kc t) -> p kc t", t=MEM)
    kmT = r[:, 12288:16384].rearrange("p (kc t) -> p kc t", t=MEM)
    vm = r[:, 16384:20480].rearrange("p (mt d) -> p mt d", d=D)
    memv = memT.rearrange("(kc p) t -> p kc t", p=128)
    S.dma("sp", mem_in, memv, w=["mem_in"])
    pb = c.psb[0]
    for kc in range(KC_D):
        sq = c.stg_bf[kc % 3]
        S.op("act", lambda e, sq=sq, kc=kc: e.activation(out=sq.t[:, 0:MEM], in_=mem_in[:, kc, :], func=AF.Square),
             r=["mem_in"], w=[sq.k])
        S.op("pe", lambda e, sq=sq, kc=kc: e.matmul(pb.t[:, 0:MEM], c.ones_bf.t[:, :], sq.t[:, 0:MEM],
                                                    start=(kc == 0), stop=(kc == KC_D - 1)), r=[sq.k], w=[pb.k])
    rt, rs = c.stg_f32[0], c.stg_f32[1]
    S.op("act", lambda e: e.activation(out=rt.t[:, 0:MEM], in_=pb.t[:, 0:MEM], func=AF.Sqrt, scale=1.0 / D,
                                       bias=c.eps_col.t[:, 0:1]), r=[pb.k], w=[rt.k])
    S.op("dve", lambda e: e.reciprocal(out=rs.t[:, 0:MEM], in_=rt.t[:, 0:MEM]), r=[rt.k], w=[rs.k])
    for kc in range(KC_D):
        S.op("dve", lambda e, kc=kc: e.scalar_tensor_tensor(out=mnT[:, kc, :], in0=mem_in[:, kc, :],
                                                             scalar=gcol[:, kc:kc + 1], in1=rs.t[:, 0:MEM],
                                                             op0=ALU.mult, op1=ALU.mult),
             r=["mem_in", rs.k], w=["mnT"])
    jobs = [[(w_kv, fc * 128, 128, KC_D)] for fc in range(16)]

    def epi_k(ji, tb, pset):
        S.op("act", lambda e: e.copy(out=kmT[:, ji, :], in_=pset[0].t[:, 0:MEM]), r=[pset[0].k], w=["kmT"])
    stream_linear(c, jobs, lambda gi, kc, tb: mnT[:, kc, :], lambda gi, tb: ["mnT"], 1, epi_k, c.wslots,
                  [[c.psb[i]] for i in range(1, 5)], tbw=MEM)
    wv = w_kv.rearrange("(kc p) f -> p kc f", p=128)
    for cb in range(8):
        ws = c.wslots[cb % 6]
        dst = ws.t[:, 0:4096].rearrange("p (kc f) -> p kc f", f=256)
        S.dma("pool", dst[:, 0:8, :], wv[:, 0:8, D + cb * 256:D + (cb + 1) * 256], w=[ws.k + "_a"])
        S.dma("pool", dst[:, 8:16, :], wv[:, 8:16, D + cb * 256:D + (cb + 1) * 256], w=[ws.k + "_b"])
        for mt in range(2):
            pbv = c.psb[5 + mt]

            def mm(e, dst=dst, pbv=pbv, mt=mt):
                ins = None
                for kc in range(KC_D):
                    ins = e.matmul(pbv.t[:, 0:256], mnT[:, kc, mt * 128:(mt + 1) * 128], dst[:, kc, :],
                                   start=(kc == 0), stop=(kc == KC_D - 1))
                return ins
            S.op("pe", mm, r=[ws.k + "_a", ws.k + "_b", "mnT"], w=[pbv.k])
            S.op("dve", lambda e, pbv=pbv, mt=mt, cb=cb: e.tensor_copy(out=vm[:, mt, cb * 256:(cb + 1) * 256], in_=pbv.t[:, 0:256]),
                 r=[pbv.k], w=["vm"])
    sched_barrier(S)
    qmv = c.P["qm"].rearrange("(kc p) t -> p kc t", p=128)
    qblk = c.R64.t[:, 0:16 * TB].rearrange("p (kc t) -> p kc t", t=TB)
    pT = [c.R64.t[:, 16 * TB + i * TB:16 * TB + (i + 1) * TB] for i in range(2)]
    scale = 512.0 ** -0.5
    for tb in range(NTB):
        t0 = tb * TB
        S.dma("sp", qblk, qmv[:, :, t0:t0 + TB], w=["qblk"])
        for hd in range(4):
            for mt in range(2):
                pl = c.psb[mt]

                def mmq(e, pl=pl, hd=hd, mt=mt):
                    ins = None
                    for j in range(4):
                        ins = e.matmul(pl.t[:, :], kmT[:, hd * 4 + j, mt * 128:(mt + 1) * 128], qblk[:, hd * 4 + j, :],
                                       start=(j == 0), stop=(j == 3))
                    return ins
                S.op("pe", mmq, r=["kmT", "qblk"], w=[pl.k])
                S.op("act", lambda e, pl=pl, mt=mt: e.activation(out=pT[mt], in_=pl.t[:, :], func=AF.Exp, scale=scale),
                     r=[pl.k], w=["pT%d" % mt])
            prs = c.psb[2]

            def mmrs(e):
                e.matmul(prs.t[:, :], c.ones_bf.t[:, :], pT[0], start=True, stop=False)
                return e.matmul(prs.t[:, :], c.ones_bf.t[:, :], pT[1], start=False, stop=True)
            S.op("pe", mmrs, r=["pT0", "pT1"], w=[prs.k])
            rinv = c.stg_f32[0]
            S.op("dve", lambda e: e.reciprocal(out=rinv.t[:, :], in_=prs.t[:, :]), r=[prs.k], w=[rinv.k])
            for j in range(4):
                po = c.psb[3 + j]
                ch = hd * 4 + j

                def mmo(e, po=po, ch=ch):
                    e.matmul(po.t[:, :], vm[:, 0, ch * 128:(ch + 1) * 128], pT[0], start=True, stop=False)
                    return e.matmul(po.t[:, :], vm[:, 1, ch * 128:(ch + 1) * 128], pT[1], start=False, stop=True)
                S.op("pe", mmo, r=["pT0", "pT1", "vm"], w=[po.k])
                st = c.stg_bf[j % 3]
                S.op("dve", lambda e, po=po, st=st: e.tensor_tensor(out=st.t[:, :], in0=po.t[:, :], in1=rinv.t[:, :], op=ALU.mult),
                     r=[po.k, rinv.k], w=[st.k])
                S.dma("sp", c.ymem[ch * 128:(ch + 1) * 128, t0:t0 + TB], st.t[:, :], r=[st.k], w=[("ymem", ch, tb)])
    sched_barrier(S)


def phase_merge(c, w_ssd, w_dsa, w_memw, w_out, x_src, x_dst):
    S = c.S
    mT = c.R64.t[:, :].rearrange("p (kc t) -> p kc t", t=L)
    ysv = c.yssd.rearrange("(kc p) t -> p kc t", p=128)
    ydv = c.ydsa.rearrange("(kc p) t -> p kc t", p=128)
    ymv = c.ymem.rearrange("(kc p) t -> p kc t", p=128)
    gv = c.P["g"]
    TBM = 256
    ys = c.R44.t[:, 0:32 * TBM].rearrange("p (kc t) -> p kc t", t=TBM)
    yd = c.R44.t[:, 32 * TBM:48 * TBM].rearrange("p (kc t) -> p kc t", t=TBM)
    ym = c.R44.t[:, 48 * TBM:64 * TBM].rearrange("p (kc t) -> p kc t", t=TBM)
    gt = [c.R44.t[:, 64 * TBM + i * TBM:64 * TBM + (i + 1) * TBM] for i in range(6)]
    for tb in range(L // TBM):
        t0 = tb * TBM
        S.dma("sp", ys[:, 0:16, :], ysv[:, 0:16, t0:t0 + TBM], w=["ys0"])
        S.dma("sp", ys[:, 16:32, :], ysv[:, 16:32, t0:t0 + TBM], w=["ys1"])
        S.dma("sp", yd, ydv[:, :, t0:t0 + TBM], w=["yd"])
        S.dma("sp", ym, ymv[:, :, t0:t0 + TBM], w=["ym"])
        jobs = [[(w_ssd, dc * 128, 128, 32), (w_dsa, dc * 128, 128, 16), (w_memw, dc * 128, 128, 16)] for dc in range(KC_D)]
        cnt = [0]

        def rhs_fn(gi, kc, tb_):
            return (ys, yd, ym)[gi][:, kc, :]

        def rhs_keys(gi, tb_):
            return (["ys0", "ys1"], ["yd"], ["ym"])[gi]

        def epi(ji, tb_, pset, t0=t0):
            i = cnt[0]
            cnt[0] += 1
            acc = c.stg_f32[i % 2]
            for bi in range(3):
                g = gt[(i % 2) * 3 + bi]
                gk = "gt%d" % ((i % 2) * 3 + bi)
                S.dma("sp", g, gv[bi * D + ji * 128:bi * D + (ji + 1) * 128, t0:t0 + TBM], w=[gk])
                sg = c.stg_f32[2 + (bi % 2)]
                S.op("act", lambda e, g=g, sg=sg: e.activation(out=sg.t[:, 0:TBM], in_=g, func=AF.Sigmoid), r=[gk], w=[sg.k])
                if bi == 0:
                    S.op("dve", lambda e, sg=sg: e.tensor_tensor(out=acc.t[:, 0:TBM], in0=pset[0].t[:, 0:TBM], in1=sg.t[:, 0:TBM], op=ALU.mult),
                         r=[pset[0].k, sg.k], w=[acc.k])
                else:
                    S.op("dve", lambda e, sg=sg, bi=bi: e.tensor_tensor(out=sg.t[:, 0:TBM], in0=pset[bi].t[:, 0:TBM], in1=sg.t[:, 0:TBM], op=ALU.mult),
                         r=[pset[bi].k, sg.k], w=[sg.k])
                    if bi == 1:
                        S.op("dve", lambda e, sg=sg: e.tensor_tensor(out=acc.t[:, 0:TBM], in0=acc.t[:, 0:TBM], in1=sg.t[:, 0:TBM], op=ALU.add),
                             r=[acc.k, sg.k], w=[acc.k])
                    else:
                        S.op("dve", lambda e, sg=sg: e.tensor_tensor(out=mT[:, ji, t0:t0 + TBM], in0=acc.t[:, 0:TBM], in1=sg.t[:, 0:TBM], op=ALU.add),
                             r=[acc.k, sg.k], w=["mT%d" % (t0 // TB)])

        stream_linear(c, jobs, rhs_fn, rhs_keys, 1, epi, c.wslots,
                      [[c.psb[0], c.psb[1], c.psb[2]], [c.psb[3], c.psb[4], c.psb[5]]], tbw=TBM)
    sched_barrier(S)
    jobs = [[(w_out, dc * 128, 128, KC_D)] for dc in range(KC_D)]
    cnt2 = [0]

    def epi_o(ji, tb, pset):
        i = cnt2[0]
        cnt2[0] += 1
        t0 = tb * TB
        xt = c.stg_f32[i % 2]
        S.dma("sp", xt.t[:, :], x_src[ji * 128:(ji + 1) * 128, t0:t0 + TB], w=[xt.k])
        xo = c.stg_f32[2 + i % 2]
        S.op("dve", lambda e: e.tensor_tensor(out=xo.t[:, :], in0=pset[0].t[:, :], in1=xt.t[:, :], op=ALU.add),
             r=[pset[0].k, xt.k], w=[xo.k])
        S.dma("sp", x_dst[ji * 128:(ji + 1) * 128, t0:t0 + TB], xo.t[:, :], r=[xo.k])
    stream_linear(c, jobs, lambda gi, kc, tb: mT[:, kc, tb * TB:(tb + 1) * TB], lambda gi, tb: ["mT%d" % tb], NTB, epi_o,
                  c.wslots, [[c.psb[i]] for i in range(8)])
    sched_barrier(S)


C_INVA, C_INVI, C_SGNA, C_SGNI = 256, 257, 258, 259
C_PA, C_PI, C_CAUS = 264, 392, 520
C_U = 648
NCONST2 = 776
TWO_PI = 6.283185307179586
CW1 = 6.28125
CW2 = TWO_PI - CW1
PI_F = 3.1415925


def make_consts2():
    cst = np.zeros((128, NCONST2), np.float32)
    cst[:, 0:NCONST] = make_consts()
    th = np.float32(500000.0)
    inv_a = np.power(th, -(np.arange(0, 32, 2, dtype=np.float32) / np.float32(32))).astype(np.float32)
    inv_i = np.power(th, -(np.arange(0, 16, 2, dtype=np.float32) / np.float32(16))).astype(np.float32)
    for d_ in range(128):
        if d_ < 32:
            cst[d_, C_INVA] = inv_a[d_ % 16]
            cst[d_, C_SGNA] = -1.0 if d_ < 16 else 1.0
        e = d_ % 64
        if e < 16:
            cst[d_, C_INVI] = inv_i[e % 8]
            cst[d_, C_SGNI] = -1.0 if e < 8 else 1.0
    for dp in range(32):
        dsrc = dp + 16 if dp < 16 else dp - 16
        cst[dsrc, C_PA + dp] = 1.0
    for blk in (0, 64):
        for ep in range(16):
            esrc = ep + 8 if ep < 8 else ep - 8
            cst[blk + esrc, C_PI + blk + ep] = 1.0
    q = np.arange(128)[:, None]
    k = np.arange(128)[None, :]
    cst[:, C_CAUS:C_CAUS + 128] = np.where(k <= q, 0.0, -1e30).astype(np.float32)
    cst[:, C_U:C_U + 128] = (q <= k).astype(np.float32)
    return cst


def _sin_table(c, out_bf, ang, sgncol, tmp_u, tmp_n, tmp_r, tag):
    S = c.S
    ni = tmp_n.bitcast(I32)
    S.op("dve", lambda e: e.tensor_scalar(out=tmp_u, in0=ang, scalar1=1.0 / TWO_PI, scalar2=None, op0=ALU.mult), r=[tag + "ang"], w=[tag + "u"])
    S.op("dve", lambda e: e.tensor_copy(out=ni, in_=tmp_u), r=[tag + "u"], w=[tag + "n"])
    S.op("dve", lambda e: e.tensor_copy(out=tmp_u, in_=ni), r=[tag + "n"], w=[tag + "u"])
    S.op("dve", lambda e: e.scalar_tensor_tensor(out=tmp_r, in0=tmp_u, scalar=-CW1, in1=ang, op0=ALU.mult, op1=ALU.add),
         r=[tag + "u", tag + "ang"], w=[tag + "r"])
    S.op("dve", lambda e: e.scalar_tensor_tensor(out=tmp_r, in0=tmp_u, scalar=-CW2, in1=tmp_r, op0=ALU.mult, op1=ALU.add),
         r=[tag + "u", tag + "r"], w=[tag + "r"])
    S.op("dve", lambda e: e.tensor_scalar(out=tmp_u, in0=tmp_r, scalar1=PI_F, scalar2=None, op0=ALU.is_gt), r=[tag + "r"], w=[tag + "u"])
    S.op("dve", lambda e: e.scalar_tensor_tensor(out=tmp_r, in0=tmp_u, scalar=-TWO_PI, in1=tmp_r, op0=ALU.mult, op1=ALU.add),
         r=[tag + "u", tag + "r"], w=[tag + "r"])
    S.op("dve", lambda e: e.tensor_scalar(out=tmp_u, in0=tmp_r, scalar1=-PI_F, scalar2=None, op0=ALU.is_lt), r=[tag + "r"], w=[tag + "u"])
    S.op("dve", lambda e: e.scalar_tensor_tensor(out=tmp_r, in0=tmp_u, scalar=TWO_PI, in1=tmp_r, op0=ALU.mult, op1=ALU.add),
         r=[tag + "u", tag + "r"], w=[tag + "r"])
    S.op("dve", lambda e: e.tensor_scalar(out=tmp_r, in0=tmp_r, scalar1=PI_F, scalar2=-PI_F, op0=ALU.min, op1=ALU.max), r=[tag + "r"], w=[tag + "r"])
    if sgncol is None:
        S.op("act", lambda e: e.activation(out=out_bf, in_=tmp_r, func=AF.Sin), r=[tag + "r"], w=["tables"])
    else:
        S.op("act", lambda e: e.activation(out=out_bf, in_=tmp_r, func=AF.Sin, scale=sgncol), r=[tag + "r"], w=["tables"])


def _rope(c, X, ncols, cosT, sinT, PT, xkeys, pbank, i):
    S = c.S
    S.op("pe", lambda e: e.matmul(pbank.t[:, 0:ncols], PT, X, start=True, stop=True), r=list(xkeys) + ["dsaconst"], w=[pbank.k])
    t1 = c.stg_f32[i % 2]
    t2 = c.stg_f32[2 + i % 2]
    S.op("dve", lambda e: e.tensor_tensor(out=t1.t[:, 0:ncols], in0=X, in1=cosT, op=ALU.mult), r=list(xkeys) + ["tables"], w=[t1.k])
    S.op("dve", lambda e: e.tensor_tensor(out=t2.t[:, 0:ncols], in0=pbank.t[:, 0:ncols], in1=sinT, op=ALU.mult), r=[pbank.k, "tables"], w=[t2.k])
    S.op("pool", lambda e: e.tensor_tensor(out=X, in0=t1.t[:, 0:ncols], in1=t2.t[:, 0:ncols], op=ALU.add), r=[t1.k, t2.k], w=list(xkeys))


def phase_dsa(c, l):
    S = c.S
    R6, R4 = c.R64.t, c.R44.t
    qblk = R6[:, 0:8192].rearrange("p (h t) -> p h t", t=TB)
    selT = R6[:, 8192:16384].rearrange("p (k t) -> p k t", t=TB)
    qiblk = R6[:, 16384:20480].rearrange("p (h t) -> p h t", t=TB)
    cosA, sinA, cosI, sinI = (R6[:, 20480 + i * 2048:20480 + (i + 1) * 2048] for i in range(4))
    krT = R6[:, 28672:30720]
    kir2 = R6[:, 30720:32768]
    acc = R4[:, 0:4096].bitcast(F32)
    work = R4[:, 4096:8192].bitcast(F32)
    sel01 = R4[:, 8192:10240]
    v_tok = R4[:, 10240:12288].rearrange("p (k d) -> p k d", d=128)
    wi_tok = R4[:, 12288:12800].bitcast(F32).rearrange("p (q h) -> p q h", h=16)
    m8 = R4[:, 12800:12816].bitcast(F32)
    pTs = [R4[:, 13312 + i * 512:13312 + (i + 1) * 512] for i in range(6)]
    tmp3 = R4[:, 8192:12288].bitcast(F32)
    cst2 = c.wslots[0].t[:, 0:2 * NCONST2].bitcast(F32)
    PA_bf = c.wslots[1].t[:, 0:128]
    PI_bf = c.wslots[1].t[:, 128:256]
    posi = c.wslots[2].t[:, 0:4096].bitcast(I32)
    vT_sb = c.wslots[3].t[:, 0:2048]
    wiT_sb = c.wslots[4].t[:, 0:4096].bitcast(F32)
    ident_f = c.cst.t[:, C_IDENT:C_IDENT + 128]

    S.dma("sp", cst2, c.consts2_d, w=["cst2"])
    S.dma("sp", posi, c.pos, w=["posi"])
    S.op("dve", lambda e: e.tensor_copy(out=PA_bf, in_=cst2[:, C_PA:C_PA + 128]), r=["cst2"], w=["dsaconst"])
    S.op("dve", lambda e: e.tensor_copy(out=PI_bf, in_=cst2[:, C_PI:C_PI + 128]), r=["cst2"], w=["dsaconst"])
    S.op("dve", lambda e: e.tensor_copy(out=acc, in_=posi), r=["posi"], w=["posf"])
    for tname, invc, sgnc, cosT, sinT in (("A", C_INVA, C_SGNA, cosA, sinA), ("I", C_INVI, C_SGNI, cosI, sinI)):
        S.op("dve", lambda e, invc=invc: e.tensor_scalar(out=work, in0=acc, scalar1=cst2[:, invc:invc + 1], scalar2=None, op0=ALU.mult),
             r=["posf", "cst2"], w=[tname + "sang"])
        u_t = tmp3
        r_t = c.wslots[5].t[:, 0:4096].bitcast(F32)
        _sin_table(c, sinT, work, cst2[:, sgnc:sgnc + 1], u_t, u_t, r_t, tname + "s")
        S.op("dve", lambda e: e.tensor_scalar(out=work, in0=work, scalar1=1.5707963267948966, scalar2=None, op0=ALU.add),
             r=[tname + "sang", tname + "sr", tname + "su"], w=[tname + "cang"])
        _sin_table(c, cosT, work, None, u_t, u_t, r_t, tname + "c")
        sched_barrier(S)
    S.dma("sp", krT, c.P["k"], w=["krT"])
    S.dma("sp", kir2[0:64, :], c.P["ki"], w=["kir2"])
    S.dma("sp", kir2[64:128, :], c.P["ki"], w=["kir2b"])
    S.dma("sp", vT_sb, c.P["v"], w=["vT"])
    S.dma("sp", wiT_sb[0:16, :], c.P["wi"], w=["wiT"])
    for tb in range(NTB):
        sl = slice(tb * TB, (tb + 1) * TB)
        _rope(c, krT[:, sl], TB, cosA[:, sl], sinA[:, sl], PA_bf, ["krT"], c.psb[tb % 4], tb)
    for tb in range(NTB):
        sl = slice(tb * TB, (tb + 1) * TB)
        _rope(c, kir2[:, sl], TB, cosI[:, sl], sinI[:, sl], PI_bf, ["kir2", "kir2b"], c.psb[4 + tb % 4], tb)
    for half in range(2):
        pb = c.psb[half]
        pbv = pb.t[:, :].bitcast(BF16)

        def tr(e, half=half, pbv=pbv):
            ins = None
            for j in range(8):
                kt = half * 8 + j
                ins = e.transpose(pbv[:, j * 128:(j + 1) * 128], vT_sb[:, kt * 128:(kt + 1) * 128], c.ident_bf.t[:, :])
            return ins
        S.op("pe", tr, r=["vT", "const"], w=[pb.k])
        S.op("act", lambda e, half=half, pbv=pbv: e.copy(out=v_tok[:, half * 8:(half + 1) * 8, :],
                                                       in_=pbv[:, 0:1024].rearrange("p (k d) -> p k d", d=128)),
             r=[pb.k], w=["v_tok"])
    pbw = c.psb[2]

    def trw(e):
        ins = None
        for qt in range(16):
            ins = e.transpose(pbw.t[:, qt * 16:(qt + 1) * 16], wiT_sb[0:16, qt * 128:(qt + 1) * 128], ident_f[0:16, 0:16])
        return ins
    S.op("pe", trw, r=["wiT", "cst"], w=[pbw.k])
    S.op("act", lambda e: e.activation(out=wi_tok, in_=pbw.t[:, 0:256].rearrange("p (q h) -> p q h", h=16), func=AF.Copy,
                                       scale=0.25 * 0.125), r=[pbw.k], w=["wi_tok"])
    sched_barrier(S)

    WSv = c.WS.t
    accs = [acc, WSv[:, 11264:15360].bitcast(F32)]
    works = [work, WSv[:, 15360:19456].bitcast(F32)]
    m8s = [m8, WSv[:, 19456:19472].bitcast(F32)]
    qv = c.P["q"].rearrange("(h p) t -> p h t", p=128)
    qiv = c.P["qi"].rearrange("(h p) t -> p h t", p=128)
    scale = 128.0 ** -0.5
    NEG = -1e30
    for b in range(NTB):
        t0 = b * TB
        S.dma("sp", qblk[:, 0:8, :], qv[:, 0:8, t0:t0 + TB], w=["qblk0"])
        S.dma("sp", qblk[:, 8:16, :], qv[:, 8:16, t0:t0 + TB], w=["qblk1"])
        S.dma("sp", qiblk, qiv[:, :, t0:t0 + TB], w=["qiblk"])
        for h in range(16):
            _rope(c, qblk[:, h, :], TB, cosA[:, t0:t0 + TB], sinA[:, t0:t0 + TB], PA_bf, ["qblk%d" % (h // 8)], c.psb[h % 4], h)
        for ch in range(8):
            _rope(c, qiblk[:, ch, :], TB, cosI[:, t0:t0 + TB], sinI[:, t0:t0 + TB], PI_bf, ["qiblk"], c.psb[4 + ch % 4], ch)
        S.op("pool", lambda e, b=b: e.memset(selT[:, 4 * b:4 * b + 4, :], 0.0), w=["selT"])
        for pair in range(2):
            tiles = []
            for sub in range(2):
                qi_ = pair * 2 + sub
                qt = 4 * b + qi_
                nk = 128 * (qt + 1)
                nkb = (nk + TB - 1) // TB
                accX, workX, m8X, tg = accs[sub], works[sub], m8s[sub], "t%d" % sub
                cnt = 0
                for kb in range(nkb):
                    w_ = min(TB, nk - kb * TB)
                    for h in range(16):
                        chn, hf = h // 2, h % 2
                        pb = c.psb[cnt % 4]
                        tmp = c.stg_f32[cnt % 4]
                        cnt += 1
                        S.op("pe", lambda e, pb=pb, chn=chn, hf=hf, kb=kb, w_=w_, qi_=qi_: e.matmul(
                            pb.t[:, 0:w_], qiblk[hf * 64:(hf + 1) * 64, chn, qi_ * 128:(qi_ + 1) * 128],
                            kir2[hf * 64:(hf + 1) * 64, kb * TB:kb * TB + w_], start=True, stop=True),
                            r=["qiblk", "kir2", "kir2b"], w=[pb.k])
                        if h == 0:
                            S.op("dve", lambda e, pb=pb, kb=kb, w_=w_, qt=qt, h=h, accX=accX: e.tensor_scalar(
                                out=accX[:, kb * TB:kb * TB + w_], in0=pb.t[:, 0:w_], scalar1=0.0, scalar2=wi_tok[:, qt, h:h + 1],
                                op0=ALU.max, op1=ALU.mult), r=[pb.k, "wi_tok"], w=[tg + "acc%d" % kb])
                        else:
                            S.op("dve", lambda e, pb=pb, tmp=tmp, w_=w_, qt=qt, h=h: e.tensor_scalar(
                                out=tmp.t[:, 0:w_], in0=pb.t[:, 0:w_], scalar1=0.0, scalar2=wi_tok[:, qt, h:h + 1],
                                op0=ALU.max, op1=ALU.mult), r=[pb.k, "wi_tok"], w=[tmp.k])
                            S.op("pool", lambda e, tmp=tmp, kb=kb, w_=w_, accX=accX: e.tensor_tensor(
                                out=accX[:, kb * TB:kb * TB + w_], in0=accX[:, kb * TB:kb * TB + w_], in1=tmp.t[:, 0:w_], op=ALU.add),
                                r=[tmp.k, tg + "acc%d" % kb], w=[tg + "acc%d" % kb])
                acck = [tg + "acc%d" % kb for kb in range(nkb)]
                S.op("pool", lambda e, nk=nk, accX=accX: e.tensor_tensor(out=accX[:, nk - 128:nk], in0=accX[:, nk - 128:nk],
                                                                        in1=cst2[:, C_CAUS:C_CAUS + 128], op=ALU.add),
                     r=acck + ["cst2"], w=acck)
                tiles.append((qi_, qt, nk, accX, workX, m8X, tg, acck))
            for rd in range(32):
                for (qi_, qt, nk, accX, workX, m8X, tg, acck) in tiles:
                    if qt < 2:
                        if rd == 0:
                            S.op("dve", lambda e, m8X=m8X: e.memset(m8X, -1e29), w=[tg + "m8"])
                        continue
                    src = accX if rd == 0 else workX
                    sk = acck if rd == 0 else [tg + "work"]
                    S.op("dve", lambda e, src=src, nk=nk, m8X=m8X: e.max(out=m8X, in_=src[:, 0:nk]), r=sk, w=[tg + "m8"])
                    if rd < 31:
                        S.op("dve", lambda e, src=src, nk=nk, m8X=m8X, workX=workX: e.match_replace(
                            out=workX[:, 0:nk], in_to_replace=m8X, in_values=src[:, 0:nk], imm_value=NEG),
                            r=sk + [tg + "m8"], w=[tg + "work"])
            for (qi_, qt, nk, accX, workX, m8X, tg, acck) in tiles:
                S.op("dve", lambda e, nk=nk, accX=accX, m8X=m8X: e.tensor_scalar(out=sel01[:, 0:nk], in0=accX[:, 0:nk], scalar1=m8X[:, 7:8],
                                                                               scalar2=None, op0=ALU.is_ge), r=acck + [tg + "m8"], w=["sel01"])
                for k0 in range(0, qt + 1, 8):
                    n_ = min(8, qt + 1 - k0)
                    pb = c.psb[4 + (k0 // 8) % 2]
                    pbv = pb.t[:, :].bitcast(BF16)

                    def trs(e, k0=k0, n_=n_, pbv=pbv):
                        ins = None
                        for j in range(n_):
                            ins = e.transpose(pbv[:, j * 128:(j + 1) * 128], sel01[:, (k0 + j) * 128:(k0 + j + 1) * 128], c.ident_bf.t[:, :])
                        return ins
                    S.op("pe", trs, r=["sel01", "const"], w=[pb.k])
                    S.op("act", lambda e, k0=k0, n_=n_, pbv=pbv, qi_=qi_: e.copy(
                        out=selT[:, k0:k0 + n_, qi_ * 128:(qi_ + 1) * 128],
                        in_=pbv[:, 0:n_ * 128].rearrange("p (k d) -> p k d", d=128)), r=[pb.k], w=["selT"])
        nkt = 4 * (b + 1)
        for h in range(16):
            po = c.psb[3 + h % 2]
            prs = c.psb[5 + h % 2]
            qk = "qblk%d" % (h // 8)
            pend = []

            def pv(kt, pTm, pk, po=po, prs=prs, nkt=nkt):
                S.op("pe", lambda e: e.matmul(po.t[:, :], v_tok[:, kt, :], pTm, start=(kt == 0), stop=(kt == nkt - 1)),
                     r=[pk, "v_tok"], w=[po.k])
                S.op("pe", lambda e: e.matmul(prs.t[:, :], c.ones_bf.t[:, :], pTm, start=(kt == 0), stop=(kt == nkt - 1)),
                     r=[pk, "const"], w=[prs.k])
            for kt in range(nkt):
                pl = c.psb[kt % 3]
                S.op("pe", lambda e, pl=pl, kt=kt, h=h: e.matmul(pl.t[:, :], krT[:, kt * 128:(kt + 1) * 128], qblk[:, h, :],
                                                                start=True, stop=True), r=["krT", qk], w=[pl.k])
                pT = pTs[kt % 3]
                pTm = pTs[3 + kt % 3]
                S.op("act", lambda e, pl=pl, pT=pT: e.activation(out=pT, in_=pl.t[:, :], func=AF.Exp, scale=scale),
                     r=[pl.k], w=["pT%d" % (kt % 3)])
                S.op("dve", lambda e, pT=pT, pTm=pTm, kt=kt: e.tensor_tensor(out=pTm, in0=pT, in1=selT[:, kt, :], op=ALU.mult),
                     r=["pT%d" % (kt % 3), "selT"], w=["pTm%d" % (kt % 3)])
                pend.append((kt, pTm, "pTm%d" % (kt % 3)))
                if len(pend) > 2:
                    pv(*pend.pop(0))
            while pend:
                pv(*pend.pop(0))
            rinv = c.stg_f32[h % 2]
            S.op("dve", lambda e, prs=prs, rinv=rinv: e.reciprocal(out=rinv.t[:, :], in_=prs.t[:, :]), r=[prs.k], w=[rinv.k])
            st = c.stg_bf[h % 3]
            S.op("dve", lambda e, po=po, rinv=rinv, st=st: e.tensor_tensor(out=st.t[:, :], in0=po.t[:, :], in1=rinv.t[:, :], op=ALU.mult),
                 r=[po.k, rinv.k], w=[st.k])
            S.dma("sp", c.ydsa[h * 128:(h + 1) * 128, t0:t0 + TB], st.t[:, :], r=[st.k], w=[("ydsa", h, b)])
    sched_barrier(S)


def phase_ssd(c, l, V):
    S = c.S
    R6, R4 = c.R64.t, c.R44.t
    ident_f = c.cst.t[:, C_IDENT:C_IDENT + 128]
    xraw = [R6[:, i * 2560:i * 2560 + 2051] for i in range(2)]
    xout = [R6[:, 8192 + i * 2048:8192 + (i + 1) * 2048] for i in range(2)]
    diag = [R4[:, i * 512:(i + 1) * 512].rearrange("p (k c) -> p k c", c=128) for i in range(2)]
    for i in range(2):
        S.op("dve", lambda e, i=i: e.memset(xraw[i][:, 0:3], 0.0), w=["xraw%d" % i])
    for cc in range(48):
        i = cc % 2
        xr, xo, dg = xraw[i], xout[i], diag[i]
        S.dma("sp", xr[:, 3:2051], c.P["xbc"][cc * 128:(cc + 1) * 128, :], w=["xraw%d" % i])
        for k in range(4):
            S.op("dve", lambda e, k=k, dg=dg, cc=cc: e.tensor_scalar(
                out=dg[:, k, :], in0=c.ident_bf.t[:, :], scalar1=V[:, V_CONVW + k * 48 + cc:V_CONVW + k * 48 + cc + 1],
                scalar2=None, op0=ALU.mult), r=["const", "vecs"], w=["diag%d" % i])
        for tb in range(NTB):
            pb = c.psb[(cc * NTB + tb) % 8]

            def mm(e, pb=pb, dg=dg, xr=xr, tb=tb):
                ins = None
                for k in range(4):
                    ins = e.matmul(pb.t[:, :], dg[:, k, :], xr[:, tb * TB + k:tb * TB + k + TB], start=(k == 0), stop=(k == 3))
                return ins
            S.op("pe", mm, r=["xraw%d" % i, "diag%d" % i], w=[pb.k])
            S.op("act", lambda e, pb=pb, xo=xo, tb=tb, cc=cc: e.activation(
                out=xo[:, tb * TB:(tb + 1) * TB], in_=pb.t[:, :], func=AF.Silu, bias=V[:, V_CONVB + cc:V_CONVB + cc + 1]),
                r=[pb.k, "vecs"], w=["xout%d" % i])
        S.dma("sp", c.xc_d[cc * 128:(cc + 1) * 128, :], xo, r=["xout%d" % i], w=[("xc", cc)])
    sched_barrier(S)

    AcsT = c.wslots[0].t[:, 0:4096].bitcast(F32)
    dtT = c.wslots[1].t[:, 0:4096].bitcast(F32)
    tA = c.wslots[2].t[:, 0:4096].bitcast(F32)
    tokw = c.wslots[3].t[:, 0:4096].bitcast(F32)
    dt_tok = tokw[:, 0:1024].rearrange("p (c h) -> p c h", h=64)
    Acs_tok = tokw[:, 1024:2048].rearrange("p (c h) -> p c h", h=64)
    tok2 = c.wslots[4].t[:, 0:4096].bitcast(F32)
    expA_tok = tok2[:, 0:1024].rearrange("p (c h) -> p c h", h=64)
    dA_tok = tok2[:, 1024:2048].rearrange("p (c h) -> p c h", h=64)
    acol = c.stg_f32[3].t[:, 0:1]
    cst2 = c.stg_f32[2].t[:, 0:128]
    S.dma("sp", cst2, c.consts2_d[:, C_U:C_U + 128], w=["Uf"])
    S.dma("sp", dtT[0:64, :], c.P["dt"], w=["dtT"])
    one_col = c.cst.t[0:64, C_ONES:C_ONES + 1]
    S.op("dve", lambda e: e.tensor_scalar(out=dtT[0:64, :], in0=dtT[0:64, :], scalar1=V[0:64, V_DTB:V_DTB + 1], scalar2=None, op0=ALU.add),
         r=["dtT", "vecs"], w=["dtT"])
    S.op("act", lambda e: e.activation(out=tA[0:64, :], in_=dtT[0:64, :], func=AF.Abs), r=["dtT"], w=["tA"])
    S.op("act", lambda e: e.activation(out=tA[0:64, :], in_=tA[0:64, :], func=AF.Exp, scale=-1.0), r=["tA"], w=["tA"])
    S.op("act", lambda e: e.activation(out=tA[0:64, :], in_=tA[0:64, :], func=AF.Ln, bias=one_col), r=["tA", "cst"], w=["tA"])
    S.op("dve", lambda e: e.scalar_tensor_tensor(out=dtT[0:64, :], in0=dtT[0:64, :], scalar=0.0, in1=tA[0:64, :], op0=ALU.max, op1=ALU.add),
         r=["dtT", "tA"], w=["dtT"])
    S.op("act", lambda e: e.activation(out=acol[0:64, :], in_=V[0:64, V_ALOG:V_ALOG + 1], func=AF.Exp), r=["vecs"], w=["acol"])
    S.op("dve", lambda e: e.tensor_scalar(out=tA[0:64, :], in0=dtT[0:64, :], scalar1=acol[0:64, :], scalar2=-1.0, op0=ALU.mult, op1=ALU.mult),
         r=["dtT", "acol"], w=["tA"])
    for src, dst, nm in ((dtT, dt_tok, "dt_tok"), (tA, dA_tok, "dA_tok")):
        for half in range(2):
            pb = c.psb[half]

            def tr(e, src=src, half=half, pb=pb):
                ins = None
                for j in range(8):
                    cch = half * 8 + j
                    ins = e.transpose(pb.t[:, j * 64:(j + 1) * 64], src[0:64, cch * 128:(cch + 1) * 128], ident_f[0:64, 0:64])
                return ins
            S.op("pe", tr, r=["dtT", "tA", "cst"], w=[pb.k])
            S.op("dve", lambda e, dst=dst, half=half, pb=pb: e.tensor_copy(
                out=dst[:, half * 8:(half + 1) * 8, :], in_=pb.t[:, :].rearrange("p (c h) -> p c h", h=64)), r=[pb.k], w=[nm])
    for half in range(2):
        pb = c.psb[2 + half]

        def cs(e, half=half, pb=pb):
            ins = None
            for j in range(8):
                cch = half * 8 + j
                ins = e.matmul(pb.t[:, j * 64:(j + 1) * 64], cst2, dA_tok[:, cch, :], start=True, stop=True)
            return ins
        S.op("pe", cs, r=["dA_tok", "Uf"], w=[pb.k])
        S.op("dve", lambda e, half=half, pb=pb: e.tensor_copy(out=Acs_tok[:, half * 8:(half + 1) * 8, :],
                                                           in_=pb.t[:, :].rearrange("p (c h) -> p c h", h=64)), r=[pb.k], w=["Acs_tok"])
    for q4 in range(4):
        pb = c.psb[4 + q4]

        def cs2(e, q4=q4, pb=pb):
            ins = None
            for j in range(4):
                cch = q4 * 4 + j
                ins = e.matmul(pb.t[0:64, j * 128:(j + 1) * 128], dA_tok[:, cch, :], cst2, start=True, stop=True)
            return ins
        S.op("pe", cs2, r=["dA_tok", "Uf"], w=[pb.k])
        S.op("dve", lambda e, q4=q4, pb=pb: e.tensor_copy(out=AcsT[0:64, q4 * TB:(q4 + 1) * TB], in_=pb.t[0:64, :]), r=[pb.k], w=["AcsT"])
    S.op("act", lambda e: e.activation(out=tok2[:, 0:1024], in_=tokw[:, 1024:2048], func=AF.Exp), r=["Acs_tok"], w=["expA_tok"])
    S.dma("sp", c.acs_d, AcsT[0:64, :], r=["AcsT"], w=["acs_d"])
    Dfull = c.wslots[1].t[:, 0:4096]
    sched_barrier(S)
    for h in range(NH_SSD):
        S.op("dve", lambda e, h=h: e.tensor_scalar(out=Dfull[:, h * 64:(h + 1) * 64], in0=c.cst.t[:, C_ONES:C_ONES + 64],
                                                   scalar1=V[:, V_DBC + h:V_DBC + h + 1], scalar2=None, op0=ALU.mult),
             r=["cst", "vecs"], w=["Dfull"])

    H = R6[:, 0:8192].bitcast(F32)
    Hbf = R6[:, 8192:12288]
    xs_tok = R6[:, 12288:16384]
    xw = R6[:, 16384:20480]
    y_tok = R6[:, 20480:24576]
    zc = R6[:, 24576:28672].rearrange("p (cc t) -> p cc t", t=128)
    xsT = R6[:, 28672:32768].rearrange("p (cc t) -> p cc t", t=128)
    ygf = R4[:, 0:8192].bitcast(F32).rearrange("p (cc t) -> p cc t", t=128)
    yout = R4[:, 8192:12288].rearrange("p (cc t) -> p cc t", t=128)
    BT = R4[:, 12288:13312].rearrange("p (g t) -> p g t", t=128)
    CT = R4[:, 13312:14336].rearrange("p (g t) -> p g t", t=128)
    B_tok = R4[:, 14336:15360].rearrange("p (g t) -> p g t", t=128)
    cbTm = R4[:, 15360:17408].bitcast(F32).rearrange("p (g t) -> p g t", t=128)
    Et = [R4[:, 17408 + i * 1024:17408 + (i + 1) * 1024].bitcast(F32) for i in range(3)]
    Mp = [R4[:, 20480 + i * 512:20480 + (i + 1) * 512] for i in range(3)]
    WSv = c.WS.t
    A_rows = WSv[:, 0:5632].bitcast(F32)
    xdt = WSv[:, 9728:13824]
    mask4 = WSv[:, 13824:14848].bitcast(F32)
    negones = WSv[:, 14848:15104].bitcast(F32)
    w64 = WSv[:, 15104:15232].bitcast(F32)
    dec = WSv[:, 15232:15360].bitcast(F32)
    Uf = cst2
    for j in range(4):
        S.op("dve", lambda e, j=j: e.tensor_scalar(out=mask4[:, j * 128:(j + 1) * 128], in0=Uf, scalar1=-1.0, scalar2=30000.0,
                                                   op0=ALU.add, op1=ALU.mult), r=["Uf"], w=["mask4"])
    S.op("dve", lambda e: e.memset(negones, -1.0), w=["negones"])
    pgroups = [(0, 0, 22), (32, 22, 44), (64, 44, 64)]
    batches = []
    for pbase, hs, he in pgroups:
        for h0 in range(hs, he, 4):
            batches.append((pbase, hs, h0, min(4, he - h0)))
    xcv = c.xc_d.rearrange("(cc p) t -> p cc t", p=128)
    zv = c.P["z"].rearrange("(cc p) t -> p cc t", p=128)
    yv = c.yssd.rearrange("(cc p) t -> p cc t", p=128)
    S.op("dve", lambda e: e.memset(H, 0.0), w=["H%d" % g for g in range(8)])
    S.op("dve", lambda e: e.memset(Hbf, 0.0), w=["Hbf%d" % g for g in range(8)])
    NCH = L // 128
    for ch in range(NCH):
        t0 = ch * 128
        last = (ch == NCH - 1)
        S.dma("sp", xsT[:, 0:16, :], xcv[:, 0:16, t0:t0 + 128], w=["xsT0"])
        S.dma("sp", xsT[:, 16:32, :], xcv[:, 16:32, t0:t0 + 128], w=["xsT1"])
        S.dma("sp", BT, xcv[:, 32:40, t0:t0 + 128], w=["BT"])
        S.dma("sp", CT, xcv[:, 40:48, t0:t0 + 128], w=["CT"])
        S.dma("sp", zc, zv[:, :, t0:t0 + 128], w=["zc"])
        for pbase, hs, he in pgroups:
            S.dma("sp", A_rows[pbase:pbase + 1, 0:(he - hs) * 128].rearrange("o (h t) -> o h t", t=128),
                  c.acs_d[hs:he, t0:t0 + 128].rearrange("(o h) t -> o h t", o=1), r=["acs_d"], w=["A_rows"])
        pb7 = c.psb[7]
        pb7v = pb7.t[:, :].bitcast(BF16)
        for q8 in range(4):
            def trx(e, q8=q8):
                ins = None
                for j in range(8):
                    ins = e.transpose(pb7v[:, j * 128:(j + 1) * 128], xsT[:, q8 * 8 + j, :], c.ident_bf.t[:, :])
                return ins
            S.op("pe", trx, r=["xsT0", "xsT1", "const"], w=[pb7.k])
            S.op("act", lambda e, q8=q8: e.copy(out=xs_tok[:, q8 * 1024:(q8 + 1) * 1024], in_=pb7v[:, 0:1024]), r=[pb7.k], w=["xs_tok"])

        def trb(e):
            ins = None
            for g in range(8):
                ins = e.transpose(pb7v[:, g * 128:(g + 1) * 128], BT[:, g, :], c.ident_bf.t[:, :])
            return ins
        S.op("pe", trb, r=["BT", "const"], w=[pb7.k])
        S.op("act", lambda e: e.copy(out=B_tok, in_=pb7v[:, 0:1024].rearrange("p (g t) -> p g t", t=128)), r=[pb7.k], w=["B_tok"])
        S.op("pool", lambda e: e.tensor_tensor(out=xdt.rearrange("p (h d) -> p h d", d=64), in0=xs_tok.rearrange("p (h d) -> p h d", d=64),
                                               in1=dt_tok[:, ch, :].unsqueeze(2).to_broadcast([128, 64, 64]), op=ALU.mult),
             r=["xs_tok", "dt_tok"], w=["xdt"])
        for half in range(2):
            pb = c.psb[2 + half]

            def mcb(e, half=half, pb=pb):
                ins = None
                for j in range(4):
                    g = half * 4 + j
                    ins = e.matmul(pb.t[:, j * 128:(j + 1) * 128], BT[:, g, :], CT[:, g, :], start=True, stop=True)
                return ins
            S.op("pe", mcb, r=["BT", "CT"], w=[pb.k])
            S.op("dve", lambda e, half=half, pb=pb: e.tensor_tensor(
                out=cbTm[:, half * 4:(half + 1) * 4, :], in0=pb.t[:, :].rearrange("p (g t) -> p g t", t=128),
                in1=Uf.unsqueeze(1).to_broadcast([128, 4, 128]), op=ALU.mult), r=[pb.k, "Uf"], w=["cbTm"])
        S.op("act", lambda e: e.activation(out=zc, in_=zc, func=AF.Silu), r=["zc"], w=["zc"])
        bi = 0
        for pbase, hs, h0, nh in batches:
            pT1 = c.psb[bi % 2]
            E, M = Et[bi % 3], Mp[bi % 3]
            ek, mk = "E%d" % (bi % 3), "Mp%d" % (bi % 3)
            bi += 1

            def mseg(e, pT1=pT1, pbase=pbase, hs=hs, h0=h0, nh=nh):
                e.matmul(pT1.t[:, 0:nh * 128], c.cst.t[pbase:pbase + 1, C_ONES:C_ONES + 128],
                         A_rows[pbase:pbase + 1, (h0 - hs) * 128:(h0 - hs + nh) * 128], start=True, stop=False)
                ins_last = [None]
                for i in range(nh):
                    ins_last[0] = e.matmul(pT1.t[:, i * 128:(i + 1) * 128], A_rows[pbase:pbase + 1, (h0 - hs + i) * 128:(h0 - hs + i + 1) * 128],
                                           negones[pbase:pbase + 1, :], start=False, stop=(i == nh - 1))
                return ins_last[0]
            S.op("pe", mseg, r=["A_rows", "cst", "negones"], w=[pT1.k])
            S.op("dve", lambda e, E=E, pT1=pT1, nh=nh: e.tensor_scalar(out=E[:, 0:nh * 128], in0=pT1.t[:, 0:nh * 128], scalar1=0.0, scalar2=None,
                                                                       op0=ALU.min), r=[pT1.k], w=[ek])
            S.op("act", lambda e, E=E, nh=nh: e.activation(out=E[:, 0:nh * 128], in_=E[:, 0:nh * 128], func=AF.Exp), r=[ek], w=[ek])
            if not last:
                S.op("pool", lambda e, E=E, h0=h0, nh=nh: e.tensor_copy(
                    out=w64[:, h0:h0 + nh].rearrange("p (h o) -> p h o", o=1),
                    in_=E[:, 0:nh * 128].rearrange("p (h t) -> p h t", t=128)[:, :, 127:128]), r=[ek], w=["w64"])
            i = 0
            while i < nh:
                g = (h0 + i) // 8
                j = i
                while j < nh and (h0 + j) // 8 == g:
                    j += 1
                n_ = j - i
                S.op("dve", lambda e, E=E, M=M, i=i, n_=n_, g=g: e.tensor_tensor(
                    out=M[:, i * 128:(i + n_) * 128].rearrange("p (h t) -> p h t", t=128),
                    in0=E[:, i * 128:(i + n_) * 128].rearrange("p (h t) -> p h t", t=128),
                    in1=cbTm[:, g:g + 1, :].to_broadcast([128, n_, 128]), op=ALU.mult), r=[ek, "cbTm"], w=[mk])
                i = j
            for i in range(nh):
                h = h0 + i
                g = h // 8
                po1 = c.psb[4]
                S.op("pe", lambda e, i=i, h=h, M=M, po1=po1: e.matmul(po1.t[:, (h % 8) * 64:(h % 8 + 1) * 64], M[:, i * 128:(i + 1) * 128],
                                                                     xdt[:, h * 64:(h + 1) * 64], start=True, stop=True),
                     r=[mk, "xdt"], w=[po1.k + "_h%d" % (h % 8)])
                if h % 8 == 7:
                    gs = slice(g * 512, (g + 1) * 512)
                    po1k = [po1.k + "_h%d" % r_ for r_ in range(8)]
                    po2 = c.psb[5]
                    S.op("pe", lambda e, g=g, gs=gs, po2=po2: e.matmul(po2.t[:, :], CT[:, g, :], Hbf[:, gs], start=True, stop=True),
                         r=["CT", "Hbf%d" % g], w=[po2.k])
                    xd = c.stg_f32[0]
                    S.op("pool", lambda e, gs=gs, xd=xd: e.tensor_tensor(out=xd.t[:, :], in0=xs_tok[:, gs], in1=Dfull[:, gs], op=ALU.mult),
                         r=["xs_tok", "Dfull"], w=[xd.k])
                    sb1 = c.stg_f32[1]
                    S.op("dve", lambda e, po1=po1, xd=xd, sb1=sb1: e.tensor_tensor(out=sb1.t[:, :], in0=po1.t[:, :], in1=xd.t[:, :], op=ALU.add),
                         r=po1k + [xd.k], w=[sb1.k] + po1k)
                    S.op("dve", lambda e, gs=gs, g=g, po2=po2: e.tensor_tensor(
                        out=y_tok[:, gs].rearrange("p (h d) -> p h d", d=64), in0=po2.t[:, :].rearrange("p (h d) -> p h d", d=64),
                        in1=expA_tok[:, ch, g * 8:(g + 1) * 8].unsqueeze(2).to_broadcast([128, 8, 64]), op=ALU.mult),
                        r=[po2.k, "expA_tok"], w=["y_tok%d" % g])
                    S.op("dve", lambda e, gs=gs, sb1=sb1: e.tensor_tensor(out=y_tok[:, gs], in0=y_tok[:, gs], in1=sb1.t[:, :], op=ALU.add),
                         r=["y_tok%d" % g, sb1.k], w=["y_tok%d" % g])
                    if not last:
                        S.op("pool", lambda e, gs=gs, g=g: e.tensor_tensor(
                            out=xw[:, gs].rearrange("p (h d) -> p h d", d=64), in0=xdt[:, gs].rearrange("p (h d) -> p h d", d=64),
                            in1=w64[:, g * 8:(g + 1) * 8].unsqueeze(2).to_broadcast([128, 8, 64]), op=ALU.mult),
                            r=["xdt", "w64"], w=["xw%d" % g])
                        pS = c.psb[6]
                        S.op("pe", lambda e, g=g, gs=gs, pS=pS: e.matmul(pS.t[:, :], B_tok[:, g, :], xw[:, gs], start=True, stop=True),
                             r=["B_tok", "xw%d" % g], w=[pS.k])
                        S.op("pool", lambda e, g=g: e.tensor_tensor(out=dec[:, g * 8:(g + 1) * 8], in0=w64[:, g * 8:(g + 1) * 8],
                                                                    in1=expA_tok[:, ch, g * 8:(g + 1) * 8], op=ALU.mult),
                             r=["w64", "expA_tok"], w=["dec"])
                        S.op("pool", lambda e, gs=gs, g=g: e.tensor_tensor(
                            out=H[:, gs].rearrange("p (h d) -> p h d", d=64), in0=H[:, gs].rearrange("p (h d) -> p h d", d=64),
                            in1=dec[:, g * 8:(g + 1) * 8].unsqueeze(2).to_broadcast([128, 8, 64]), op=ALU.mult),
                            r=["dec", "H%d" % g], w=["H%d" % g])
                        S.op("dve", lambda e, gs=gs, pS=pS: e.tensor_tensor(out=H[:, gs], in0=H[:, gs], in1=pS.t[:, :], op=ALU.add),
                             r=[pS.k, "H%d" % g], w=["H%d" % g])
                        S.op("act", lambda e, gs=gs: e.copy(out=Hbf[:, gs], in_=H[:, gs]), r=["H%d" % g], w=["Hbf%d" % g])
        sq = xw.rearrange("p (cc t) -> p cc t", t=128)
        ytk = ["y_tok%d" % g for g in range(8)]
        for q8 in range(4):
            def try_(e, q8=q8):
                ins = None
                for j in range(8):
                    cc = q8 * 8 + j
                    ins = e.transpose(pb7v[:, j * 128:(j + 1) * 128], y_tok[:, cc * 128:(cc + 1) * 128], c.ident_bf.t[:, :])
                return ins
            S.op("pe", try_, r=ytk + ["const"], w=[pb7.k])
            S.op("dve", lambda e, q8=q8: e.tensor_tensor(out=ygf[:, q8 * 8:(q8 + 1) * 8, :],
                                                       in0=pb7v[:, 0:1024].rearrange("p (cc t) -> p cc t", t=128),
                                                       in1=zc[:, q8 * 8:(q8 + 1) * 8, :], op=ALU.mult), r=[pb7.k, "zc"], w=["ygf"])
        xwk = ["xw%d" % g for g in range(8)]
        S.op("act", lambda e: e.activation(out=sq, in_=ygf, func=AF.Square), r=["ygf"], w=xwk)
        pn = c.psb[6]

        def mmn(e):
            ins = None
            for cc in range(32):
                ins = e.matmul(pn.t[:, 0:128], c.ones_bf.t[:, :], sq[:, cc, :], start=(cc == 0), stop=(cc == 31))
            return ins
        S.op("pe", mmn, r=xwk + ["const"], w=[pn.k])
        rt = c.stg_f32[2].t[:, 128:256]
        rs = c.stg_f32[2].t[:, 256:384]
        S.op("act", lambda e: e.activation(out=rt, in_=pn.t[:, 0:128], func=AF.Sqrt, scale=1.0 / SSD_INNER, bias=c.eps_col.t[:, 0:1]),
             r=[pn.k], w=["rt"])
        S.op("dve", lambda e: e.reciprocal(out=rs, in_=rt), r=["rt"], w=["rs"])
        for cc in range(32):
            S.op("dve", lambda e, cc=cc: e.scalar_tensor_tensor(out=yout[:, cc, :], in0=ygf[:, cc, :], scalar=V[:, V_SSDN + cc:V_SSDN + cc + 1],
                                                                 in1=rs, op0=ALU.mult, op1=ALU.mult), r=["ygf", "rs", "vecs"], w=["yout"])
        S.dma("sp", yv[:, 0:16, t0:t0 + 128], yout[:, 0:16, :], r=["yout"], w=[("yssd", ch, 0)])
        S.dma("sp", yv[:, 16:32, t0:t0 + 128], yout[:, 16:32, :], r=["yout"], w=[("yssd", ch, 1)])
    sched_barrier(S)


FULL_PLAN = []
for _l in range(DEPTH):
    FULL_PLAN += [("ffn1", _l), ("inproj", _l), ("mem", _l), ("dsa", _l), ("ssd", _l), ("merge", _l), ("ffn2", _l)]
FULL_PLAN.append("final")
_NC_CACHE = {}


def kernel(**inputs):
    inp = {k: np.asarray(v) for k, v in inputs.items()}
    B = inp["x"].shape[0]
    if "nc" not in _NC_CACHE:
        _NC_CACHE["nc"] = build(FULL_PLAN)
    nc = _NC_CACHE["nc"]
    vec = np.stack([pack_vecs(inp, l) for l in range(DEPTH)])
    cst = make_consts()
    in_maps = [core_inputs(inp, b, vec, cst) for b in range(B)]
    res = run_bass_kernel_spmd(nc, in_maps, core_ids=list(range(B)))
    out = np.stack([np.ascontiguousarray(res.results[b]["outT"].T) for b in range(B)]).astype(np.float32)
    return out
```

```python
from contextlib import ExitStack
import numpy as np
import concourse.bass as bass
import concourse.mybir as mybir
from concourse.bass_utils import run_bass_kernel_spmd

F32 = mybir.dt.float32
BF16 = mybir.dt.bfloat16
I32 = mybir.dt.int32
AF = mybir.ActivationFunctionType
ALU = mybir.AluOpType
AX = mybir.AxisListType

D = 2048
L = 2048
DEPTH = 2
DFF = 5632
MEM = 256
EPS = 1e-6
SSD_INNER = 4096
CONV_CH = 6144
NH_SSD = 64
IN_SPLITS = (4096, 6144, 64, 2048, 128, 128, 1024, 64, 16, 2048, 6144)
IN_OFF = [0]
for _s in IN_SPLITS:
    IN_OFF.append(IN_OFF[-1] + _s)
IN_WIDTH = IN_OFF[-1]
(O_Z, O_XBC, O_DT, O_Q, O_K, O_V, O_QI, O_KI, O_WI, O_QM, O_G) = IN_OFF[:11]
TB = 512
NTB = L // TB
KC_D = D // 128


class Sched:
    ENGS = ("pe", "act", "dve", "pool", "sp")

    def __init__(self, nc, es, n_dma_sems=24):
        self.nc = nc
        self.eng = dict(pe=nc.tensor, act=nc.scalar, dve=nc.vector, pool=nc.gpsimd, sp=nc.sync)
        self.sem = {e: es.enter_context(nc.semaphore("sem_" + e)) for e in self.ENGS}
        self.cnt = {e: 0 for e in self.ENGS}
        self.seen = {e: {} for e in self.ENGS}
        self.snap = {}
        self.dsem = {}
        self.dcur = {}
        self.drr = {}
        for q in ("sp", "pool", "act"):
            n = n_dma_sems if q != "act" else 8
            self.dsem[q] = [es.enter_context(nc.semaphore("dsem_%s_%d" % (q, i))) for i in range(n)]
            self.dcur[q] = [0] * n
            self.drr[q] = 0
        self.bufs = {}
        self.n_wait = 0
        self.n_inst = 0

    def _wait(self, e, tok):
        if tok is None:
            return
        kind = tok[0]
        seen = self.seen[e]
        if kind == "c":
            _, f, c = tok
            if seen.get(f, 0) >= c:
                return
            if f == e and e == "pe":
                return
            self.eng[e].wait_ge(self.sem[f], c)
            self.n_wait += 1
            seen[f] = c
            sn = self.snap.get((f, c))
            if sn is not None:
                for g, v in zip(self.ENGS, sn):
                    if v > seen.get(g, 0):
                        seen[g] = v
        else:
            _, q, i, v = tok
            key = (q, i)
            if seen.get(key, 0) >= v:
                return
            self.eng[e].wait_ge(self.dsem[q][i], v)
            self.n_wait += 1
            seen[key] = v

    def _deps(self, e, r, w):
        for k in r:
            st = self.bufs.get(k)
            if st is not None:
                self._wait(e, st[0])
        for k in w:
            st = self.bufs.get(k)
            if st is not None:
                self._wait(e, st[0])
                for t in st[1]:
                    if t[0] == "c" and t[1] == e:
                        continue
                    self._wait(e, t)

    def _commit(self, tok, r, w):
        for k in r:
            st = self.bufs.setdefault(k, [None, []])
            st[1] = [t for t in st[1] if not (t[0] == tok[0] and t[1] == tok[1] and (t[0] == "c" or t[2] == tok[2]))]
            st[1].append(tok)
        for k in w:
            self.bufs[k] = [tok, []]

    def op(self, e, fn, r=(), w=()):
        self._deps(e, r, w)
        ins = fn(self.eng[e])
        self.cnt[e] += 1
        c = self.cnt[e]
        ins.then_inc(self.sem[e], 1)
        self.n_inst += 1
        sn = self.seen[e]
        self.snap[(e, c)] = tuple(c if g == e else sn.get(g, 0) for g in self.ENGS)
        tok = ("c", e, c)
        self._commit(tok, r, w)
        return tok

    def dma(self, q, out, in_, r=(), w=(), **kw):
        self._deps(q, r, w)
        i = self.drr[q]
        self.drr[q] = (i + 1) % len(self.dsem[q])
        cur = self.dcur[q][i]
        if cur > 0:
            self._wait(q, ("d", q, i, cur))
        self.eng[q].dma_start(out=out, in_=in_, **kw).then_inc(self.dsem[q][i], 16)
        self.dcur[q][i] = cur + 16
        tok = ("d", q, i, cur + 16)
        self._commit(tok, r, w)
        return tok

    def finish(self, e="sp"):
        for f in self.ENGS:
            if self.cnt[f] > 0 and f != e:
                self._wait(e, ("c", f, self.cnt[f]))
        for q in self.dsem:
            for i, cur in enumerate(self.dcur[q]):
                if cur > 0:
                    self._wait(e, ("d", q, i, cur))


class Buf:
    def __init__(self, t, key):
        self.t = t
        self.k = key

    def __getitem__(self, idx):
        return self.t[idx]


class Ctx:
    pass


def make_ctx(nc, es):
    c = Ctx()
    c.nc = nc
    c.es = es
    c.S = Sched(nc, es)
    c.nbuf = 0

    def sb(shape, dt, name=None):
        c.nbuf += 1
        name = name or ("sb%d" % c.nbuf)
        t = es.enter_context(nc.sbuf_tensor(name, list(shape), dt))
        return Buf(t, name)

    def ps(name, shape=(128, 512), dt=F32):
        t = es.enter_context(nc.psum_tensor(name, list(shape), dt))
        return Buf(t, name)
    c.sb = sb
    c.ps = ps
    return c


def stream_linear(c, jobs, rhs_fn, rhs_keys, ntb, epilogue, wslots, psum_sets, tbw=TB):
    S = c.S
    nslot = len(wslots)
    state = c.__dict__.setdefault("_lin_state", {"slot": 0, "pset": 0})
    flat = []
    for ji, job in enumerate(jobs):
        for gi, g in enumerate(job):
            flat.append((ji, gi, g))
    PREF = nslot - 1
    slot_of = {}

    def issue(n):
        ji, gi, (W, col0, ncols, KC) = flat[n]
        si = state["slot"]
        state["slot"] = (si + 1) % nslot
        ws = wslots[si]
        wv = W.rearrange("(kc p) f -> p kc f", p=128)
        dst = ws.t[:, 0:KC * ncols].rearrange("p (kc f) -> p kc f", f=ncols)
        half = KC // 2
        S.dma("pool", dst[:, 0:half, :], wv[:, 0:half, col0:col0 + ncols], w=[ws.k + "_a"])
        S.dma("pool", dst[:, half:KC, :], wv[:, half:KC, col0:col0 + ncols], w=[ws.k + "_b"])
        slot_of[n] = (ws, dst)

    nxt = 0
    n = 0
    for ji, job in enumerate(jobs):
        while nxt < len(flat) and nxt < n + nslot:
            issue(nxt)
            nxt += 1
        for tb in range(ntb):
            pi = state["pset"] % len(psum_sets)
            state["pset"] = (pi + 1) % len(psum_sets)
            pset = psum_sets[pi]
            for gi, (W, col0, ncols, KC) in enumerate(job):
                ws, dst = slot_of[n + gi]
                pb = pset[gi]

                def mm(e, dst=dst, pb=pb, KC=KC, gi=gi, tb=tb, ncols=ncols):
                    ins = None
                    for kc in range(KC):
                        ins = e.matmul(pb.t[0:ncols, 0:tbw], dst[:, kc, :], rhs_fn(gi, kc, tb),
                                       start=(kc == 0), stop=(kc == KC - 1))
                    return ins
                S.op("pe", mm, r=[ws.k + "_a", ws.k + "_b"] + list(rhs_keys(gi, tb)), w=[pb.k])
            epilogue(ji, tb, pset)
        n += len(job)


def sched_barrier(S):
    engs = [e for e in S.ENGS]
    toks = [("c", f, S.cnt[f]) for f in S.ENGS if S.cnt[f] > 0]
    dtoks = []
    for q in S.dsem:
        for i, cur in enumerate(S.dcur[q]):
            if cur > 0:
                dtoks.append(("d", q, i, cur))
    for e in engs:
        for t in toks:
            if t[1] != e:
                S._wait(e, t)
        for t in dtoks:
            S._wait(e, t)
    S.bufs = {}


def phase_norm(c, x_src, gcol, out_mode, out_dst=None):
    S = c.S
    xv = x_src.rearrange("(kc p) t -> p kc t", p=128)
    xin = c.R44.t[:, 0:16 * TB * 2].bitcast(F32).rearrange("p (kc t) -> p kc t", t=TB)
    hT = c.hT
    for tb in range(NTB):
        t0 = tb * TB
        for hf in range(2):
            S.dma("sp", xin[:, hf * 8:(hf + 1) * 8, :], xv[:, hf * 8:(hf + 1) * 8, t0:t0 + TB],
                  w=["xin%d" % hf])
        pb = c.psb[tb % 2]
        for kc in range(KC_D):
            sq = c.stg_bf[kc % 3]
            S.op("act", lambda e, sq=sq, kc=kc: e.activation(out=sq.t[:, :], in_=xin[:, kc, :], func=AF.Square),
                 r=["xin%d" % (kc // 8)], w=[sq.k])
            S.op("pe", lambda e, sq=sq, kc=kc, pb=pb: e.matmul(pb.t[:, :], c.ones_bf.t[:, :], sq.t[:, :],
                                                                start=(kc == 0), stop=(kc == KC_D - 1)),
                 r=[sq.k, "const"], w=[pb.k])
        rt = c.stg_f32[0]
        S.op("act", lambda e, pb=pb: e.activation(out=rt.t[:, :], in_=pb.t[:, :], func=AF.Sqrt,
                                                  scale=1.0 / D, bias=c.eps_col.t[:, 0:1]),
             r=[pb.k, "const"], w=[rt.k])
        rs = c.stg_f32[1]
        S.op("dve", lambda e: e.reciprocal(out=rs.t[:, :], in_=rt.t[:, :]), r=[rt.k], w=[rs.k])
        for kc in range(KC_D):
            if out_mode == "h":
                S.op("dve", lambda e, kc=kc: e.scalar_tensor_tensor(
                    out=hT[:, kc, t0:t0 + TB], in0=xin[:, kc, :], scalar=gcol[:, kc:kc + 1], in1=rs.t[:, :],
                    op0=ALU.mult, op1=ALU.mult),
                    r=["xin%d" % (kc // 8), rs.k, "vecs"], w=["hT%d" % tb])
            else:
                ot = c.stg_f32[2 + kc % 2]
                S.op("dve", lambda e, kc=kc, ot=ot: e.scalar_tensor_tensor(
                    out=ot.t[:, :], in0=xin[:, kc, :], scalar=gcol[:, kc:kc + 1], in1=rs.t[:, :],
                    op0=ALU.mult, op1=ALU.mult),
                    r=["xin%d" % (kc // 8), rs.k, "vecs"], w=[ot.k])
                S.dma("sp", out_dst[kc * 128:(kc + 1) * 128, t0:t0 + TB], ot.t[:, :], r=[ot.k])


def phase_ffn(c, w_in, w_out, x_src, x_dst):
    S = c.S
    NJ = DFF // 128
    hT = c.hT
    actd = c.act_d
    jobs = [[(w_in, j * 128, 128, KC_D), (w_in, DFF + j * 128, 128, KC_D)] for j in range(NJ)]
    cnt = [0]

    def epi_a(ji, tb, pset):
        i = cnt[0]
        cnt[0] += 1
        sg = c.stg_f32[i % 2]
        S.op("act", lambda e: e.activation(out=sg.t[:, :], in_=pset[0].t[:, :], func=AF.Silu),
             r=[pset[0].k], w=[sg.k])
        ab = c.stg_bf[i % 3]
        S.op("dve", lambda e: e.tensor_tensor(out=ab.t[:, :], in0=pset[1].t[:, :], in1=sg.t[:, :], op=ALU.mult),
             r=[pset[1].k, sg.k], w=[ab.k])
        S.dma("sp", actd[ji * 128:(ji + 1) * 128, tb * TB:(tb + 1) * TB], ab.t[:, :], r=[ab.k],
              w=[("act", ji, tb)])

    stream_linear(c, jobs, lambda gi, kc, tb: hT[:, kc, tb * TB:(tb + 1) * TB],
                  lambda gi, tb: ["hT%d" % tb], NTB, epi_a, c.wslots,
                  [[c.psb[0], c.psb[1]], [c.psb[2], c.psb[3]], [c.psb[4], c.psb[5]], [c.psb[6], c.psb[7]]])
    sched_barrier(S)
    av = actd.rearrange("(kc p) t -> p kc t", p=128)
    blks = [c.R64.t[:, 0:NJ * TB].rearrange("p (kc t) -> p kc t", t=TB),
            c.R44.t[:, 0:NJ * TB].rearrange("p (kc t) -> p kc t", t=TB)]
    bkeys = ["ablkA", "ablkB"]
    for tb in range(NTB):
        blk = blks[tb % 2]
        bk = bkeys[tb % 2]
        t0 = tb * TB
        for q4 in range(4):
            S.dma("sp", blk[:, q4 * 11:(q4 + 1) * 11, :], av[:, q4 * 11:(q4 + 1) * 11, t0:t0 + TB], w=[bk + str(q4)])
        jobs = [[(w_out, dc * 128, 128, NJ)] for dc in range(KC_D)]
        cnt2 = [0]

        def epi_b(ji, tb_unused, pset, tb=tb, t0=t0):
            i = cnt2[0]
            cnt2[0] += 1
            xt = c.stg_f32[i % 2]
            S.dma("sp", xt.t[:, :], x_src[ji * 128:(ji + 1) * 128, t0:t0 + TB], w=[xt.k])
            xo = c.stg_f32[2 + i % 2]
            S.op("dve", lambda e: e.scalar_tensor_tensor(out=xo.t[:, :], in0=pset[0].t[:, :], scalar=0.5,
                                                         in1=xt.t[:, :], op0=ALU.mult, op1=ALU.add),
                 r=[pset[0].k, xt.k], w=[xo.k])
            S.dma("sp", x_dst[ji * 128:(ji + 1) * 128, t0:t0 + TB], xo.t[:, :], r=[xo.k])

        stream_linear(c, jobs, lambda gi, kc, tb_, blk=blk: blk[:, kc, :],
                      lambda gi, tb_, bk=bk: [bk + str(q) for q in range(4)], 1, epi_b, c.wslots,
                      [[c.psb[i]] for i in range(8)])
    sched_barrier(S)


V_FFN1, V_MIX, V_FFN2, V_MEMN = 0, 16, 32, 48
V_SSDN = 64
V_CONVW = 96
V_CONVB = 288
V_DTB, V_ALOG, V_DSKIP = 336, 337, 338
V_FINAL = 339
V_DBC = 360
NV = 424


def pack_vecs(inp, l):
    v = np.zeros((128, NV), np.float32)

    def col(vec):
        return np.ascontiguousarray(np.asarray(vec, np.float32).reshape(-1, 128).T)
    v[:, V_FFN1:V_FFN1 + 16] = col(inp["ffn1_norm"][l])
    v[:, V_MIX:V_MIX + 16] = col(inp["mix_norm"][l])
    v[:, V_FFN2:V_FFN2 + 16] = col(inp["ffn2_norm"][l])
    v[:, V_MEMN:V_MEMN + 16] = col(inp["mem_norm"][l])
    v[:, V_SSDN:V_SSDN + 32] = col(inp["ssd_norm"][l])
    for k in range(4):
        v[:, V_CONVW + k * 48:V_CONVW + (k + 1) * 48] = col(inp["conv_w"][l][k])
    v[:, V_CONVB:V_CONVB + 48] = col(inp["conv_b"][l])
    v[0:64, V_DTB] = inp["dt_bias"][l]
    v[0:64, V_ALOG] = inp["a_log"][l]
    v[0:64, V_DSKIP] = inp["d_skip"][l]
    v[:, V_FINAL:V_FINAL + 16] = col(inp["final_norm"])
    v[:, V_DBC:V_DBC + 64] = np.asarray(inp["d_skip"][l], np.float32)[None, :]
    return v


C_IDENT, C_ONES = 0, 128
NCONST = 256


def make_consts():
    cst = np.zeros((128, NCONST), np.float32)
    cst[:, C_IDENT:C_IDENT + 128] = np.eye(128, dtype=np.float32)
    cst[:, C_ONES:C_ONES + 128] = 1.0
    return cst


def build(plan, dbg=None):
    nc = bass.Bass("TRN2", target_bir_lowering=False)
    es = ExitStack()
    with es:
        c = make_ctx(nc, es)
        S = c.S
        dt = nc.dram_tensor
        xT = dt("xT", [D, L], F32, kind="ExternalInput").ap()
        memT = dt("memT", [D, MEM], F32, kind="ExternalInput").ap()
        pos = dt("pos", [128, L], I32, kind="ExternalInput").ap()
        c.consts2_d = dt("consts2", [128, NCONST2], F32, kind="ExternalInput").ap()
        vecs = dt("vecs", [DEPTH, 128, NV], F32, kind="ExternalInput").ap()
        consts = dt("consts", [128, NCONST], F32, kind="ExternalInput").ap()
        W = {}
        for name, shp in (("w_ffn1_in", [D, 2 * DFF]), ("w_ffn1_out", [DFF, D]), ("w_ffn2_in", [D, 2 * DFF]),
                          ("w_ffn2_out", [DFF, D]), ("w_in", [D, IN_WIDTH]), ("w_mem_kv", [D, 2 * D]),
                          ("w_br_ssd", [SSD_INNER, D]), ("w_br_dsa", [D, D]), ("w_br_mem", [D, D]), ("w_out", [D, D])):
            W[name] = dt(name, [DEPTH, shp[0] + 1, shp[1]], F32, kind="ExternalInput").ap()[:, 0:shp[0], :]
        outT = dt("outT", [D, L], F32, kind="ExternalOutput").ap()
        c.xres = dt("xres", [D, L], F32, kind="Internal").ap()
        c.act_d = dt("act_d", [DFF, L], BF16, kind="Internal").ap()
        c.P = {}
        for name, off, width, dt_ in P_SPECS:
            c.P[name] = dt("P_" + name, [width, L], dt_, kind="Internal").ap()
        c.yssd = dt("yssd", [SSD_INNER, L], BF16, kind="Internal").ap()
        c.ydsa = dt("ydsa", [D, L], BF16, kind="Internal").ap()
        c.ymem = dt("ymem", [D, L], BF16, kind="Internal").ap()
        c.pos = pos
        c.xc_d = dt("xc_d", [CONV_CH, L], BF16, kind="Internal").ap()
        c.acs_d = dt("acs_d", [NH_SSD, L], F32, kind="Internal").ap()
        c.V_l = None
        c.consts_d = consts

        c.R64 = c.sb([128, 32768], BF16, "R64")
        c.R44 = c.sb([128, 22528], BF16, "R44")
        c.hT = c.R64.t[:, :].rearrange("p (kc t) -> p kc t", t=L)
        c.WS = c.sb([128, 6 * 5632], BF16, "WS")
        c.wslots = [Buf(c.WS.t[:, i * 5632:(i + 1) * 5632], "wslot%d" % i) for i in range(6)]
        c.stg_f32 = [c.sb([128, TB], F32, "stgf%d" % i) for i in range(4)]
        c.stg_bf = [c.sb([128, TB], BF16, "stgb%d" % i) for i in range(3)]
        c.vecs = [c.sb([128, NV], F32, "vecs%d" % l) for l in range(DEPTH)]
        c.cst = c.sb([128, NCONST], F32, "cst")
        c.ones_bf = c.sb([128, 128], BF16, "ones_bf")
        c.ident_bf = c.sb([128, 128], BF16, "ident_bf")
        c.eps_col = c.sb([128, 1], F32, "eps_col")
        c.psb = [c.ps("ps%d" % i) for i in range(8)]

        for l in range(DEPTH):
            S.dma("sp", c.vecs[l].t[:, :], vecs[l], w=["vecs"])
        S.dma("sp", c.cst.t[:, :], consts, w=["cst"])
        S.op("dve", lambda e: e.tensor_copy(out=c.ones_bf.t[:, :], in_=c.cst.t[:, C_ONES:C_ONES + 128]), r=["cst"], w=["const"])
        S.op("dve", lambda e: e.tensor_copy(out=c.ident_bf.t[:, :], in_=c.cst.t[:, C_IDENT:C_IDENT + 128]), r=["cst"], w=["const"])
        S.op("dve", lambda e: e.memset(c.eps_col.t[:, :], EPS), w=["const"])
        sched_barrier(S)

        x_cur = xT
        for l in range(DEPTH):
            V = c.vecs[l].t
            if ("ffn1", l) in plan:
                phase_norm(c, x_cur, V[:, V_FFN1:V_FFN1 + 16], "h")
                sched_barrier(S)
                phase_ffn(c, W["w_ffn1_in"][l], W["w_ffn1_out"][l], x_cur, c.xres)
                x_cur = c.xres
            if ("inproj", l) in plan:
                phase_norm(c, x_cur, V[:, V_MIX:V_MIX + 16], "h")
                sched_barrier(S)
                phase_inproj(c, W["w_in"][l])
            if ("mem", l) in plan:
                phase_mem(c, memT, V[:, V_MEMN:V_MEMN + 16], W["w_mem_kv"][l])
            if ("dsa", l) in plan:
                phase_dsa(c, l)
            if ("ssd", l) in plan:
                phase_ssd(c, l, V)
            if ("merge", l) in plan:
                phase_merge(c, W["w_br_ssd"][l], W["w_br_dsa"][l], W["w_br_mem"][l], W["w_out"][l], x_cur, c.xres)
                x_cur = c.xres
            if ("ffn2", l) in plan:
                phase_norm(c, x_cur, V[:, V_FFN2:V_FFN2 + 16], "h")
                sched_barrier(S)
                phase_ffn(c, W["w_ffn2_in"][l], W["w_ffn2_out"][l], x_cur, c.xres)
                x_cur = c.xres
        if "final" in plan:
            phase_norm(c, x_cur, c.vecs[0].t[:, V_FINAL:V_FINAL + 16], "out", outT)
        elif dbg is not None:
            if dbg == "xres":
                src, nrow, sdt = x_cur, D, F32
            else:
                src, nrow, sdt = dbg
            for kc in range((nrow + 127) // 128):
                nr = min(128, nrow - kc * 128)
                for tb in range(NTB):
                    st = c.stg_f32[(kc * NTB + tb) % 4]
                    if sdt == F32:
                        S.dma("sp", st.t[0:nr, :], src(c)[kc * 128:kc * 128 + nr, tb * TB:(tb + 1) * TB] if callable(src) else src[kc * 128:kc * 128 + nr, tb * TB:(tb + 1) * TB], w=[st.k])
                    else:
                        sb_ = c.stg_bf[(kc * NTB + tb) % 3]
                        S.dma("sp", sb_.t[0:nr, :], src(c)[kc * 128:kc * 128 + nr, tb * TB:(tb + 1) * TB], w=[sb_.k])
                        S.op("dve", lambda e, st=st, sb_=sb_, nr=nr: e.tensor_copy(out=st.t[0:nr, :], in_=sb_.t[0:nr, :]), r=[sb_.k], w=[st.k])
                    S.dma("sp", outT[kc * 128:kc * 128 + nr, tb * TB:(tb + 1) * TB], st.t[0:nr, :], r=[st.k])
        S.finish("sp")
        print("sched: inst=%d waits=%d cnt=%s" % (S.n_inst, S.n_wait, S.cnt))
    return nc


W_NAMES = ("w_ffn1_in", "w_ffn1_out", "w_ffn2_in", "w_ffn2_out", "w_in", "w_mem_kv", "w_br_ssd", "w_br_dsa", "w_br_mem", "w_out")


def core_inputs(inp, b, vec, cst):
    im = {"xT": np.ascontiguousarray(inp["x"][b].T), "memT": np.ascontiguousarray(inp["mem"][b].T),
          "pos": np.ascontiguousarray(np.broadcast_to(inp["positions"][b].reshape(1, L).astype(np.int32), (128, L))), "vecs": vec, "consts": cst,
          "consts2": make_consts2()}
    for n in W_NAMES:
        w = inp[n]
        p = np.empty((w.shape[0], w.shape[1] + 1, w.shape[2]), np.float32)
        p[:, :w.shape[1], :] = w
        p[:, w.shape[1], :] = float(b)
        im[n] = p
    return im


P_SPECS = [("z", O_Z, 4096, BF16), ("xbc", O_XBC, 6144, BF16), ("dt", O_DT, 64, F32), ("q", O_Q, 2048, BF16),
           ("k", O_K, 128, BF16), ("v", O_V, 128, BF16), ("qi", O_QI, 1024, BF16), ("ki", O_KI, 64, BF16),
           ("wi", O_WI, 16, F32), ("qm", O_QM, 2048, BF16), ("g", O_G, 6144, BF16)]


def phase_inproj(c, w_in):
    S = c.S
    hT = c.hT
    jobs = []
    meta = []
    for name, off, width, dt_ in P_SPECS:
        for f0 in range(0, width, 128):
            nco = min(128, width - f0)
            jobs.append([(w_in, off + f0, nco, KC_D)])
            meta.append((name, f0, nco, dt_))
    cnt = [0]

    def epi(ji, tb, pset):
        name, f0, nco, dt_ = meta[ji]
        i = cnt[0]
        cnt[0] += 1
        if dt_ == F32:
            st = c.stg_f32[i % 4]
        else:
            st = c.stg_bf[i % 3]
        if i % 2 == 0:
            S.op("act", lambda e: e.copy(out=st.t[0:nco, :], in_=pset[0].t[0:nco, :]), r=[pset[0].k], w=[st.k])
        else:
            S.op("dve", lambda e: e.tensor_copy(out=st.t[0:nco, :], in_=pset[0].t[0:nco, :]), r=[pset[0].k], w=[st.k])
        S.dma("sp", c.P[name][f0:f0 + nco, tb * TB:(tb + 1) * TB], st.t[0:nco, :], r=[st.k], w=[("P", name, f0, tb)])

    stream_linear(c, jobs, lambda gi, kc, tb: hT[:, kc, tb * TB:(tb + 1) * TB],
                  lambda gi, tb: ["hT%d" % tb], NTB, epi, c.wslots, [[c.psb[i]] for i in range(8)])
    sched_barrier(S)


def phase_mem(c, memT, gcol, w_kv):
    S = c.S
    r = c.R44.t
    mem_in = r[:, 0:8192].bitcast(F32).rearrange("p (kc t) -> p kc t", t=MEM)
    mnT = r[:, 8192:12288].rearrange("p (kc t) -> p kc t", t=MEM)
    kmT = r[:, 12288:16384].rearrange("p (kc t) -> p kc t", t=MEM)
    vm = r[:, 16384:20480].rearrange("p (mt d) -> p mt d", d=D)
    memv = memT.rearrange("(kc p) t -> p kc t", p=128)
    S.dma("sp", mem_in, memv, w=["mem_in"])
    pb = c.psb[0]
    for kc in range(KC_D):
        sq = c.stg_bf[kc % 3]
        S.op("act", lambda e, sq=sq, kc=kc: e.activation(out=sq.t[:, 0:MEM], in_=mem_in[:, kc, :], func=AF.Square),
             r=["mem_in"], w=[sq.k])
        S.op("pe", lambda e, sq=sq, kc=kc: e.matmul(pb.t[:, 0:MEM], c.ones_bf.t[:, :], sq.t[:, 0:MEM],
                                                    start=(kc == 0), stop=(kc == KC_D - 1)), r=[sq.k], w=[pb.k])
    rt, rs = c.stg_f32[0], c.stg_f32[1]
    S.op("act", lambda e: e.activation(out=rt.t[:, 0:MEM], in_=pb.t[:, 0:MEM], func=AF.Sqrt, scale=1.0 / D,
                                       bias=c.eps_col.t[:, 0:1]), r=[pb.k], w=[rt.k])
    S.op("dve", lambda e: e.reciprocal(out=rs.t[:, 0:MEM], in_=rt.t[:, 0:MEM]), r=[rt.k], w=[rs.k])
    for kc in range(KC_D):
        S.op("dve", lambda e, kc=kc: e.scalar_tensor_tensor(out=mnT[:, kc, :], in0=mem_in[:, kc, :],
                                                             scalar=gcol[:, kc:kc + 1], in1=rs.t[:, 0:MEM],
                                                             op0=ALU.mult, op1=ALU.mult),
             r=["mem_in", rs.k], w=["mnT"])
    jobs = [[(w_kv, fc * 128, 128, KC_D)] for fc in range(16)]

    def epi_k(ji, tb, pset):
        S.op("act", lambda e: e.copy(out=kmT[:, ji, :], in_=pset[0].t[:, 0:MEM]), r=[pset[0].k], w=["kmT"])
    stream_linear(c, jobs, lambda gi, kc, tb: mnT[:, kc, :], lambda gi, tb: ["mnT"], 1, epi_k, c.wslots,
                  [[c.psb[i]] for i in range(1, 5)], tbw=MEM)
    wv = w_kv.rearrange("(kc p) f -> p kc f", p=128)
    for cb in range(8):
        ws = c.wslots[cb % 6]
        dst = ws.t[:, 0:4096].rearrange("p (kc f) -> p kc f", f=256)
        S.dma("pool", dst[:, 0:8, :], wv[:, 0:8, D + cb * 256:D + (cb + 1) * 256], w=[ws.k + "_a"])
        S.dma("pool", dst[:, 8:16, :], wv[:, 8:16, D + cb * 256:D + (cb + 1) * 256], w=[ws.k + "_b"])
        for mt in range(2):
            pbv = c.psb[5 + mt]

            def mm(e, dst=dst, pbv=pbv, mt=mt):
                ins = None
                for kc in range(KC_D):
                    ins = e.matmul(pbv.t[:, 0:256], mnT[:, kc, mt * 128:(mt + 1) * 128], dst[:, kc, :],
                                   start=(kc == 0), stop=(kc == KC_D - 1))
                return ins
            S.op("pe", mm, r=[ws.k + "_a", ws.k + "_b", "mnT"], w=[pbv.k])
            S.op("dve", lambda e, pbv=pbv, mt=mt, cb=cb: e.tensor_copy(out=vm[:, mt, cb * 256:(cb + 1) * 256], in_=pbv.t[:, 0:256]),
                 r=[pbv.k], w=["vm"])
    sched_barrier(S)
    qmv = c.P["qm"].rearrange("(kc p) t -> p kc t", p=128)
    qblk = c.R64.t[:, 0:16 * TB].rearrange("p (kc t) -> p kc t", t=TB)
    pT = [c.R64.t[:, 16 * TB + i * TB:16 * TB + (i + 1) * TB] for i in range(2)]
    scale = 512.0 ** -0.5
    for tb in range(NTB):
        t0 = tb * TB
        S.dma("sp", qblk, qmv[:, :, t0:t0 + TB], w=["qblk"])
        for hd in range(4):
            for mt in range(2):
                pl = c.psb[mt]

                def mmq(e, pl=pl, hd=hd, mt=mt):
                    ins = None
                    for j in range(4):
                        ins = e.matmul(pl.t[:, :], kmT[:, hd * 4 + j, mt * 128:(mt + 1) * 128], qblk[:, hd * 4 + j, :],
                                       start=(j == 0), stop=(j == 3))
                    return ins
                S.op("pe", mmq, r=["kmT", "qblk"], w=[pl.k])
                S.op("act", lambda e, pl=pl, mt=mt: e.activation(out=pT[mt], in_=pl.t[:, :], func=AF.Exp, scale=scale),
                     r=[pl.k], w=["pT%d" % mt])
            prs = c.psb[2]

            def mmrs(e):
                e.matmul(prs.t[:, :], c.ones_bf.t[:, :], pT[0], start=True, stop=False)
                return e.matmul(prs.t[:, :], c.ones_bf.t[:, :], pT[1], start=False, stop=True)
            S.op("pe", mmrs, r=["pT0", "pT1"], w=[prs.k])
            rinv = c.stg_f32[0]
            S.op("dve", lambda e: e.reciprocal(out=rinv.t[:, :], in_=prs.t[:, :]), r=[prs.k], w=[rinv.k])
            for j in range(4):
                po = c.psb[3 + j]
                ch = hd * 4 + j

                def mmo(e, po=po, ch=ch):
                    e.matmul(po.t[:, :], vm[:, 0, ch * 128:(ch + 1) * 128], pT[0], start=True, stop=False)
                    return e.matmul(po.t[:, :], vm[:, 1, ch * 128:(ch + 1) * 128], pT[1], start=False, stop=True)
                S.op("pe", mmo, r=["pT0", "pT1", "vm"], w=[po.k])
                st = c.stg_bf[j % 3]
                S.op("dve", lambda e, po=po, st=st: e.tensor_tensor(out=st.t[:, :], in0=po.t[:, :], in1=rinv.t[:, :], op=ALU.mult),
                     r=[po.k, rinv.k], w=[st.k])
                S.dma("sp", c.ymem[ch * 128:(ch + 1) * 128, t0:t0 + TB], st.t[:, :], r=[st.k], w=[("ymem", ch, tb)])
    sched_barrier(S)


def phase_merge(c, w_ssd, w_dsa, w_memw, w_out, x_src, x_dst):
    S = c.S
    mT = c.R64.t[:, :].rearrange("p (kc t) -> p kc t", t=L)
    ysv = c.yssd.rearrange("(kc p) t -> p kc t", p=128)
    ydv = c.ydsa.rearrange("(kc p) t -> p kc t", p=128)
    ymv = c.ymem.rearrange("(kc p) t -> p kc t", p=128)
    gv = c.P["g"]
    TBM = 256
    ys = c.R44.t[:, 0:32 * TBM].rearrange("p (kc t) -> p kc t", t=TBM)
    yd = c.R44.t[:, 32 * TBM:48 * TBM].rearrange("p (kc t) -> p kc t", t=TBM)
    ym = c.R44.t[:, 48 * TBM:64 * TBM].rearrange("p (kc t) -> p kc t", t=TBM)
    gt = [c.R44.t[:, 64 * TBM + i * TBM:64 * TBM + (i + 1) * TBM] for i in range(6)]
    for tb in range(L // TBM):
        t0 = tb * TBM
        S.dma("sp", ys[:, 0:16, :], ysv[:, 0:16, t0:t0 + TBM], w=["ys0"])
        S.dma("sp", ys[:, 16:32, :], ysv[:, 16:32, t0:t0 + TBM], w=["ys1"])
        S.dma("sp", yd, ydv[:, :, t0:t0 + TBM], w=["yd"])
        S.dma("sp", ym, ymv[:, :, t0:t0 + TBM], w=["ym"])
        jobs = [[(w_ssd, dc * 128, 128, 32), (w_dsa, dc * 128, 128, 16), (w_memw, dc * 128, 128, 16)] for dc in range(KC_D)]
        cnt = [0]

        def rhs_fn(gi, kc, tb_):
            return (ys, yd, ym)[gi][:, kc, :]

        def rhs_keys(gi, tb_):
            return (["ys0", "ys1"], ["yd"], ["ym"])[gi]

        def epi(ji, tb_, pset, t0=t0):
            i = cnt[0]
            cnt[0] += 1
            acc = c.stg_f32[i % 2]
            for bi in range(3):
                g = gt[(i % 2) * 3 + bi]
                gk = "gt%d" % ((i % 2) * 3 + bi)
                S.dma("sp", g, gv[bi * D + ji * 128:bi * D + (ji + 1) * 128, t0:t0 + TBM], w=[gk])
                sg = c.stg_f32[2 + (bi % 2)]
                S.op("act", lambda e, g=g, sg=sg: e.activation(out=sg.t[:, 0:TBM], in_=g, func=AF.Sigmoid), r=[gk], w=[sg.k])
                if bi == 0:
                    S.op("dve", lambda e, sg=sg: e.tensor_tensor(out=acc.t[:, 0:TBM], in0=pset[0].t[:, 0:TBM], in1=sg.t[:, 0:TBM], op=ALU.mult),
                         r=[pset[0].k, sg.k], w=[acc.k])
                else:
                    S.op("dve", lambda e, sg=sg, bi=bi: e.tensor_tensor(out=sg.t[:, 0:TBM], in0=pset[bi].t[:, 0:TBM], in1=sg.t[:, 0:TBM], op=ALU.mult),
                         r=[pset[bi].k, sg.k], w=[sg.k])
                    if bi == 1:
                        S.op("dve", lambda e, sg=sg: e.tensor_tensor(out=acc.t[:, 0:TBM], in0=acc.t[:, 0:TBM], in1=sg.t[:, 0:TBM], op=ALU.add),
                             r=[acc.k, sg.k], w=[acc.k])
                    else:
                        S.op("dve", lambda e, sg=sg: e.tensor_tensor(out=mT[:, ji, t0:t0 + TBM], in0=acc.t[:, 0:TBM], in1=sg.t[:, 0:TBM], op=ALU.add),
                             r=[acc.k, sg.k], w=["mT%d" % (t0 // TB)])

        stream_linear(c, jobs, rhs_fn, rhs_keys, 1, epi, c.wslots,
                      [[c.psb[0], c.psb[1], c.psb[2]], [c.psb[3], c.psb[4], c.psb[5]]], tbw=TBM)
    sched_barrier(S)
    jobs = [[(w_out, dc * 128, 128, KC_D)] for dc in range(KC_D)]
    cnt2 = [0]

    def epi_o(ji, tb, pset):
        i = cnt2[0]
        cnt2[0] += 1
        t0 = tb * TB
        xt = c.stg_f32[i % 2]
        S.dma("sp", xt.t[:, :], x_src[ji * 128:(ji + 1) * 128, t0:t0 + TB], w=[xt.k])
        xo = c.stg_f32[2 + i % 2]
        S.op("dve", lambda e: e.tensor_tensor(out=xo.t[:, :], in0=pset[0].t[:, :], in1=xt.t[:, :], op=ALU.add),
             r=[pset[0].k, xt.k], w=[xo.k])
        S.dma("sp", x_dst[ji * 128:(ji + 1) * 128, t0:t0 + TB], xo.t[:, :], r=[xo.k])
    stream_linear(c, jobs, lambda gi, kc, tb: mT[:, kc, tb * TB:(tb + 1) * TB], lambda gi, tb: ["mT%d" % tb], NTB, epi_o,
                  c.wslots, [[c.psb[i]] for i in range(8)])
    sched_barrier(S)


C_INVA, C_INVI, C_SGNA, C_SGNI = 256, 257, 258, 259
C_PA, C_PI, C_CAUS = 264, 392, 520
C_U = 648
NCONST2 = 776
TWO_PI = 6.283185307179586
CW1 = 6.28125
CW2 = TWO_PI - CW1
PI_F = 3.1415925


def make_consts2():
    cst = np.zeros((128, NCONST2), np.float32)
    cst[:, 0:NCONST] = make_consts()
    th = np.float32(500000.0)
    inv_a = np.power(th, -(np.arange(0, 32, 2, dtype=np.float32) / np.float32(32))).astype(np.float32)
    inv_i = np.power(th, -(np.arange(0, 16, 2, dtype=np.float32) / np.float32(16))).astype(np.float32)
    for d_ in range(128):
        if d_ < 32:
            cst[d_, C_INVA] = inv_a[d_ % 16]
            cst[d_, C_SGNA] = -1.0 if d_ < 16 else 1.0
        e = d_ % 64
        if e < 16:
            cst[d_, C_INVI] = inv_i[e % 8]
            cst[d_, C_SGNI] = -1.0 if e < 8 else 1.0
    for dp in range(32):
        dsrc = dp + 16 if dp < 16 else dp - 16
        cst[dsrc, C_PA + dp] = 1.0
    for blk in (0, 64):
        for ep in range(16):
            esrc = ep + 8 if ep < 8 else ep - 8
            cst[blk + esrc, C_PI + blk + ep] = 1.0
    q = np.arange(128)[:, None]
    k = np.arange(128)[None, :]
    cst[:, C_CAUS:C_CAUS + 128] = np.where(k <= q, 0.0, -1e30).astype(np.float32)
    cst[:, C_U:C_U + 128] = (q <= k).astype(np.float32)
    return cst


def _sin_table(c, out_bf, ang, sgncol, tmp_u, tmp_n, tmp_r, tag):
    S = c.S
    ni = tmp_n.bitcast(I32)
    S.op("dve", lambda e: e.tensor_scalar(out=tmp_u, in0=ang, scalar1=1.0 / TWO_PI, scalar2=None, op0=ALU.mult), r=[tag + "ang"], w=[tag + "u"])
    S.op("dve", lambda e: e.tensor_copy(out=ni, in_=tmp_u), r=[tag + "u"], w=[tag + "n"])
    S.op("dve", lambda e: e.tensor_copy(out=tmp_u, in_=ni), r=[tag + "n"], w=[tag + "u"])
    S.op("dve", lambda e: e.scalar_tensor_tensor(out=tmp_r, in0=tmp_u, scalar=-CW1, in1=ang, op0=ALU.mult, op1=ALU.add),
         r=[tag + "u", tag + "ang"], w=[tag + "r"])
    S.op("dve", lambda e: e.scalar_tensor_tensor(out=tmp_r, in0=tmp_u, scalar=-CW2, in1=tmp_r, op0=ALU.mult, op1=ALU.add),
         r=[tag + "u", tag + "r"], w=[tag + "r"])
    S.op("dve", lambda e: e.tensor_scalar(out=tmp_u, in0=tmp_r, scalar1=PI_F, scalar2=None, op0=ALU.is_gt), r=[tag + "r"], w=[tag + "u"])
    S.op("dve", lambda e: e.scalar_tensor_tensor(out=tmp_r, in0=tmp_u, scalar=-TWO_PI, in1=tmp_r, op0=ALU.mult, op1=ALU.add),
         r=[tag + "u", tag + "r"], w=[tag + "r"])
    S.op("dve", lambda e: e.tensor_scalar(out=tmp_u, in0=tmp_r, scalar1=-PI_F, scalar2=None, op0=ALU.is_lt), r=[tag + "r"], w=[tag + "u"])
    S.op("dve", lambda e: e.scalar_tensor_tensor(out=tmp_r, in0=tmp_u, scalar=TWO_PI, in1=tmp_r, op0=ALU.mult, op1=ALU.add),
         r=[tag + "u", tag + "r"], w=[tag + "r"])
    S.op("dve", lambda e: e.tensor_scalar(out=tmp_r, in0=tmp_r, scalar1=PI_F, scalar2=-PI_F, op0=ALU.min, op1=ALU.max), r=[tag + "r"], w=[tag + "r"])
    if sgncol is None:
        S.op("act", lambda e: e.activation(out=out_bf, in_=tmp_r, func=AF.Sin), r=[tag + "r"], w=["tables"])
    else:
        S.op("act", lambda e: e.activation(out=out_bf, in_=tmp_r, func=AF.Sin, scale=sgncol), r=[tag + "r"], w=["tables"])


def _rope(c, X, ncols, cosT, sinT, PT, xkeys, pbank, i):
    S = c.S
    S.op("pe", lambda e: e.matmul(pbank.t[:, 0:ncols], PT, X, start=True, stop=True), r=list(xkeys) + ["dsaconst"], w=[pbank.k])
    t1 = c.stg_f32[i % 2]
    t2 = c.stg_f32[2 + i % 2]
    S.op("dve", lambda e: e.tensor_tensor(out=t1.t[:, 0:ncols], in0=X, in1=cosT, op=ALU.mult), r=list(xkeys) + ["tables"], w=[t1.k])
    S.op("dve", lambda e: e.tensor_tensor(out=t2.t[:, 0:ncols], in0=pbank.t[:, 0:ncols], in1=sinT, op=ALU.mult), r=[pbank.k, "tables"], w=[t2.k])
    S.op("pool", lambda e: e.tensor_tensor(out=X, in0=t1.t[:, 0:ncols], in1=t2.t[:, 0:ncols], op=ALU.add), r=[t1.k, t2.k], w=list(xkeys))


def phase_dsa(c, l):
    S = c.S
    R6, R4 = c.R64.t, c.R44.t
    qblk = R6[:, 0:8192].rearrange("p (h t) -> p h t", t=TB)
    selT = R6[:, 8192:16384].rearrange("p (k t) -> p k t", t=TB)
    qiblk = R6[:, 16384:20480].rearrange("p (h t) -> p h t", t=TB)
    cosA, sinA, cosI, sinI = (R6[:, 20480 + i * 2048:20480 + (i + 1) * 2048] for i in range(4))
    krT = R6[:, 28672:30720]
    kir2 = R6[:, 30720:32768]
    acc = R4[:, 0:4096].bitcast(F32)
    work = R4[:, 4096:8192].bitcast(F32)
    sel01 = R4[:, 8192:10240]
    v_tok = R4[:, 10240:12288].rearrange("p (k d) -> p k d", d=128)
    wi_tok = R4[:, 12288:12800].bitcast(F32).rearrange("p (q h) -> p q h", h=16)
    m8 = R4[:, 12800:12816].bitcast(F32)
    pTs = [R4[:, 13312 + i * 512:13312 + (i + 1) * 512] for i in range(6)]
    tmp3 = R4[:, 8192:12288].bitcast(F32)
    cst2 = c.wslots[0].t[:, 0:2 * NCONST2].bitcast(F32)
    PA_bf = c.wslots[1].t[:, 0:128]
    PI_bf = c.wslots[1].t[:, 128:256]
    posi = c.wslots[2].t[:, 0:4096].bitcast(I32)
    vT_sb = c.wslots[3].t[:, 0:2048]
    wiT_sb = c.wslots[4].t[:, 0:4096].bitcast(F32)
    ident_f = c.cst.t[:, C_IDENT:C_IDENT + 128]

    S.dma("sp", cst2, c.consts2_d, w=["cst2"])
    S.dma("sp", posi, c.pos, w=["posi"])
    S.op("dve", lambda e: e.tensor_copy(out=PA_bf, in_=cst2[:, C_PA:C_PA + 128]), r=["cst2"], w=["dsaconst"])
    S.op("dve", lambda e: e.tensor_copy(out=PI_bf, in_=cst2[:, C_PI:C_PI + 128]), r=["cst2"], w=["dsaconst"])
    S.op("dve", lambda e: e.tensor_copy(out=acc, in_=posi), r=["posi"], w=["posf"])
    for tname, invc, sgnc, cosT, sinT in (("A", C_INVA, C_SGNA, cosA, sinA), ("I", C_INVI, C_SGNI, cosI, sinI)):
        S.op("dve", lambda e, invc=invc: e.tensor_scalar(out=work, in0=acc, scalar1=cst2[:, invc:invc + 1], scalar2=None, op0=ALU.mult),
             r=["posf", "cst2"], w=[tname + "sang"])
        u_t = tmp3
        r_t = c.wslots[5].t[:, 0:4096].bitcast(F32)
        _sin_table(c, sinT, work, cst2[:, sgnc:sgnc + 1], u_t, u_t, r_t, tname + "s")
        S.op("dve", lambda e: e.tensor_scalar(out=work, in0=work, scalar1=1.5707963267948966, scalar2=None, op0=ALU.add),
             r=[tname + "sang", tname + "sr", tname + "su"], w=[tname + "cang"])
        _sin_table(c, cosT, work, None, u_t, u_t, r_t, tname + "c")
        sched_barrier(S)
    S.dma("sp", krT, c.P["k"], w=["krT"])
    S.dma("sp", kir2[0:64, :], c.P["ki"], w=["kir2"])
    S.dma("sp", kir2[64:128, :], c.P["ki"], w=["kir2b"])
    S.dma("sp", vT_sb, c.P["v"], w=["vT"])
    S.dma("sp", wiT_sb[0:16, :], c.P["wi"], w=["wiT"])
    for tb in range(NTB):
        sl = slice(tb * TB, (tb + 1) * TB)
        _rope(c, krT[:, sl], TB, cosA[:, sl], sinA[:, sl], PA_bf, ["krT"], c.psb[tb % 4], tb)
    for tb in range(NTB):
        sl = slice(tb * TB, (tb + 1) * TB)
        _rope(c, kir2[:, sl], TB, cosI[:, sl], sinI[:, sl], PI_bf, ["kir2", "kir2b"], c.psb[4 + tb % 4], tb)
    for half in range(2):
        pb = c.psb[half]
        pbv = pb.t[:, :].bitcast(BF16)

        def tr(e, half=half, pbv=pbv):
            ins = None
            for j in range(8):
                kt = half * 8 + j
                ins = e.transpose(pbv[:, j * 128:(j + 1) * 128], vT_sb[:, kt * 128:(kt + 1) * 128], c.ident_bf.t[:, :])
            return ins
        S.op("pe", tr, r=["vT", "const"], w=[pb.k])
        S.op("act", lambda e, half=half, pbv=pbv: e.copy(out=v_tok[:, half * 8:(half + 1) * 8, :],
                                                       in_=pbv[:, 0:1024].rearrange("p (k d) -> p k d", d=128)),
             r=[pb.k], w=["v_tok"])
    pbw = c.psb[2]

    def trw(e):
        ins = None
        for qt in range(16):
            ins = e.transpose(pbw.t[:, qt * 16:(qt + 1) * 16], wiT_sb[0:16, qt * 128:(qt + 1) * 128], ident_f[0:16, 0:16])
        return ins
    S.op("pe", trw, r=["wiT", "cst"], w=[pbw.k])
    S.op("act", lambda e: e.activation(out=wi_tok, in_=pbw.t[:, 0:256].rearrange("p (q h) -> p q h", h=16), func=AF.Copy,
                                       scale=0.25 * 0.125), r=[pbw.k], w=["wi_tok"])
    sched_barrier(S)

    WSv = c.WS.t
    accs = [acc, WSv[:, 11264:15360].bitcast(F32)]
    works = [work, WSv[:, 15360:19456].bitcast(F32)]
    m8s = [m8, WSv[:, 19456:19472].bitcast(F32)]
    qv = c.P["q"].rearrange("(h p) t -> p h t", p=128)
    qiv = c.P["qi"].rearrange("(h p) t -> p h t", p=128)
    scale = 128.0 ** -0.5
    NEG = -1e30
    for b in range(NTB):
        t0 = b * TB
        S.dma("sp", qblk[:, 0:8, :], qv[:, 0:8, t0:t0 + TB], w=["qblk0"])
        S.dma("sp", qblk[:, 8:16, :], qv[:, 8:16, t0:t0 + TB], w=["qblk1"])
        S.dma("sp", qiblk, qiv[:, :, t0:t0 + TB], w=["qiblk"])
        for h in range(16):
            _rope(c, qblk[:, h, :], TB, cosA[:, t0:t0 + TB], sinA[:, t0:t0 + TB], PA_bf, ["qblk%d" % (h // 8)], c.psb[h % 4], h)
        for ch in range(8):
            _rope(c, qiblk[:, ch, :], TB, cosI[:, t0:t0 + TB], sinI[:, t0:t0 + TB], PI_bf, ["qiblk"], c.psb[4 + ch % 4], ch)
        S.op("pool", lambda e, b=b: e.memset(selT[:, 4 * b:4 * b + 4, :], 0.0), w=["selT"])
        for pair in range(2):
            tiles = []
            for sub in range(2):
                qi_ = pair * 2 + sub
                qt = 4 * b + qi_
                nk = 128 * (qt + 1)
                nkb = (nk + TB - 1) // TB
                accX, workX, m8X, tg = accs[sub], works[sub], m8s[sub], "t%d" % sub
                cnt = 0
                for kb in range(nkb):
                    w_ = min(TB, nk - kb * TB)
                    for h in range(16):
                        chn, hf = h // 2, h % 2
                        pb = c.psb[cnt % 4]
                        tmp = c.stg_f32[cnt % 4]
                        cnt += 1
                        S.op("pe", lambda e, pb=pb, chn=chn, hf=hf, kb=kb, w_=w_, qi_=qi_: e.matmul(
                            pb.t[:, 0:w_], qiblk[hf * 64:(hf + 1) * 64, chn, qi_ * 128:(qi_ + 1) * 128],
                            kir2[hf * 64:(hf + 1) * 64, kb * TB:kb * TB + w_], start=True, stop=True),
                            r=["qiblk", "kir2", "kir2b"], w=[pb.k])
                        if h == 0:
                            S.op("dve", lambda e, pb=pb, kb=kb, w_=w_, qt=qt, h=h, accX=accX: e.tensor_scalar(
                                out=accX[:, kb * TB:kb * TB + w_], in0=pb.t[:, 0:w_], scalar1=0.0, scalar2=wi_tok[:, qt, h:h + 1],
                                op0=ALU.max, op1=ALU.mult), r=[pb.k, "wi_tok"], w=[tg + "acc%d" % kb])
                        else:
                            S.op("dve", lambda e, pb=pb, tmp=tmp, w_=w_, qt=qt, h=h: e.tensor_scalar(
                                out=tmp.t[:, 0:w_], in0=pb.t[:, 0:w_], scalar1=0.0, scalar2=wi_tok[:, qt, h:h + 1],
                                op0=ALU.max, op1=ALU.mult), r=[pb.k, "wi_tok"], w=[tmp.k])
                            S.op("pool", lambda e, tmp=tmp, kb=kb, w_=w_, accX=accX: e.tensor_tensor(
                                out=accX[:, kb * TB:kb * TB + w_], in0=accX[:, kb * TB:kb * TB + w_], in1=tmp.t[:, 0:w_], op=ALU.add),
                                r=[tmp.k, tg + "acc%d" % kb], w=[tg + "acc%d" % kb])
                acck = [tg + "acc%d" % kb for kb in range(nkb)]
                S.op("pool", lambda e, nk=nk, accX=accX: e.tensor_tensor(out=accX[:, nk - 128:nk], in0=accX[:, nk - 128:nk],
                                                                        in1=cst2[:, C_CAUS:C_CAUS + 128], op=ALU.add),
                     r=acck + ["cst2"], w=acck)
                tiles.append((qi_, qt, nk, accX, workX, m8X, tg, acck))
            for rd in range(32):
                for (qi_, qt, nk, accX, workX, m8X, tg, acck) in tiles:
                    if qt < 2:
                        if rd == 0:
                            S.op("dve", lambda e, m8X=m8X: e.memset(m8X, -1e29), w=[tg + "m8"])
                        continue
                    src = accX if rd == 0 else workX
                    sk = acck if rd == 0 else [tg + "work"]
                    S.op("dve", lambda e, src=src, nk=nk, m8X=m8X: e.max(out=m8X, in_=src[:, 0:nk]), r=sk, w=[tg + "m8"])
                    if rd < 31:
                        S.op("dve", lambda e, src=src, nk=nk, m8X=m8X, workX=workX: e.match_replace(
                            out=workX[:, 0:nk], in_to_replace=m8X, in_values=src[:, 0:nk], imm_value=NEG),
                            r=sk + [tg + "m8"], w=[tg + "work"])
            for (qi_, qt, nk, accX, workX, m8X, tg, acck) in tiles:
                S.op("dve", lambda e, nk=nk, accX=accX, m8X=m8X: e.tensor_scalar(out=sel01[:, 0:nk], in0=accX[:, 0:nk], scalar1=m8X[:, 7:8],
                                                                               scalar2=None, op0=ALU.is_ge), r=acck + [tg + "m8"], w=["sel01"])
                for k0 in range(0, qt + 1, 8):
                    n_ = min(8, qt + 1 - k0)
                    pb = c.psb[4 + (k0 // 8) % 2]
                    pbv = pb.t[:, :].bitcast(BF16)

                    def trs(e, k0=k0, n_=n_, pbv=pbv):
                        ins = None
                        for j in range(n_):
                            ins = e.transpose(pbv[:, j * 128:(j + 1) * 128], sel01[:, (k0 + j) * 128:(k0 + j + 1) * 128], c.ident_bf.t[:, :])
                        return ins
                    S.op("pe", trs, r=["sel01", "const"], w=[pb.k])
                    S.op("act", lambda e, k0=k0, n_=n_, pbv=pbv, qi_=qi_: e.copy(
                        out=selT[:, k0:k0 + n_, qi_ * 128:(qi_ + 1) * 128],
                        in_=pbv[:, 0:n_ * 128].rearrange("p (k d) -> p k d", d=128)), r=[pb.k], w=["selT"])
        nkt = 4 * (b + 1)
        for h in range(16):
            po = c.psb[3 + h % 2]
            prs = c.psb[5 + h % 2]
            qk = "qblk%d" % (h // 8)
            pend = []

            def pv(kt, pTm, pk, po=po, prs=prs, nkt=nkt):
                S.op("pe", lambda e: e.matmul(po.t[:, :], v_tok[:, kt, :], pTm, start=(kt == 0), stop=(kt == nkt - 1)),
                     r=[pk, "v_tok"], w=[po.k])
                S.op("pe", lambda e: e.matmul(prs.t[:, :], c.ones_bf.t[:, :], pTm, start=(kt == 0), stop=(kt == nkt - 1)),
                     r=[pk, "const"], w=[prs.k])
            for kt in range(nkt):
                pl = c.psb[kt % 3]
                S.op("pe", lambda e, pl=pl, kt=kt, h=h: e.matmul(pl.t[:, :], krT[:, kt * 128:(kt + 1) * 128], qblk[:, h, :],
                                                                start=True, stop=True), r=["krT", qk], w=[pl.k])
                pT = pTs[kt % 3]
                pTm = pTs[3 + kt % 3]
                S.op("act", lambda e, pl=pl, pT=pT: e.activation(out=pT, in_=pl.t[:, :], func=AF.Exp, scale=scale),
                     r=[pl.k], w=["pT%d" % (kt % 3)])
                S.op("dve", lambda e, pT=pT, pTm=pTm, kt=kt: e.tensor_tensor(out=pTm, in0=pT, in1=selT[:, kt, :], op=ALU.mult),
                     r=["pT%d" % (kt % 3), "selT"], w=["pTm%d" % (kt % 3)])
                pend.append((kt, pTm, "pTm%d" % (kt % 3)))
                if len(pend) > 2:
                    pv(*pend.pop(0))
            while pend:
                pv(*pend.pop(0))
            rinv = c.stg_f32[h % 2]
            S.op("dve", lambda e, prs=prs, rinv=rinv: e.reciprocal(out=rinv.t[:, :], in_=prs.t[:, :]), r=[prs.k], w=[rinv.k])
            st = c.stg_bf[h % 3]
            S.op("dve", lambda e, po=po, rinv=rinv, st=st: e.tensor_tensor(out=st.t[:, :], in0=po.t[:, :], in1=rinv.t[:, :], op=ALU.mult),
                 r=[po.k, rinv.k], w=[st.k])
            S.dma("sp", c.ydsa[h * 128:(h + 1) * 128, t0:t0 + TB], st.t[:, :], r=[st.k], w=[("ydsa", h, b)])
    sched_barrier(S)


def phase_ssd(c, l, V):
    S = c.S
    R6, R4 = c.R64.t, c.R44.t
    ident_f = c.cst.t[:, C_IDENT:C_IDENT + 128]
    xraw = [R6[:, i * 2560:i * 2560 + 2051] for i in range(2)]
    xout = [R6[:, 8192 + i * 2048:8192 + (i + 1) * 2048] for i in range(2)]
    diag = [R4[:, i * 512:(i + 1) * 512].rearrange("p (k c) -> p k c", c=128) for i in range(2)]
    for i in range(2):
        S.op("dve", lambda e, i=i: e.memset(xraw[i][:, 0:3], 0.0), w=["xraw%d" % i])
    for cc in range(48):
        i = cc % 2
        xr, xo, dg = xraw[i], xout[i], diag[i]
        S.dma("sp", xr[:, 3:2051], c.P["xbc"][cc * 128:(cc + 1) * 128, :], w=["xraw%d" % i])
        for k in range(4):
            S.op("dve", lambda e, k=k, dg=dg, cc=cc: e.tensor_scalar(
                out=dg[:, k, :], in0=c.ident_bf.t[:, :], scalar1=V[:, V_CONVW + k * 48 + cc:V_CONVW + k * 48 + cc + 1],
                scalar2=None, op0=ALU.mult), r=["const", "vecs"], w=["diag%d" % i])
        for tb in range(NTB):
            pb = c.psb[(cc * NTB + tb) % 8]

            def mm(e, pb=pb, dg=dg, xr=xr, tb=tb):
                ins = None
                for k in range(4):
                    ins = e.matmul(pb.t[:, :], dg[:, k, :], xr[:, tb * TB + k:tb * TB + k + TB], start=(k == 0), stop=(k == 3))
                return ins
            S.op("pe", mm, r=["xraw%d" % i, "diag%d" % i], w=[pb.k])
            S.op("act", lambda e, pb=pb, xo=xo, tb=tb, cc=cc: e.activation(
                out=xo[:, tb * TB:(tb + 1) * TB], in_=pb.t[:, :], func=AF.Silu, bias=V[:, V_CONVB + cc:V_CONVB + cc + 1]),
                r=[pb.k, "vecs"], w=["xout%d" % i])
        S.dma("sp", c.xc_d[cc * 128:(cc + 1) * 128, :], xo, r=["xout%d" % i], w=[("xc", cc)])
    sched_barrier(S)

    AcsT = c.wslots[0].t[:, 0:4096].bitcast(F32)
    dtT = c.wslots[1].t[:, 0:4096].bitcast(F32)
    tA = c.wslots[2].t[:, 0:4096].bitcast(F32)
    tokw = c.wslots[3].t[:, 0:4096].bitcast(F32)
    dt_tok = tokw[:, 0:1024].rearrange("p (c h) -> p c h", h=64)
    Acs_tok = tokw[:, 1024:2048].rearrange("p (c h) -> p c h", h=64)
    tok2 = c.wslots[4].t[:, 0:4096].bitcast(F32)
    expA_tok = tok2[:, 0:1024].rearrange("p (c h) -> p c h", h=64)
    dA_tok = tok2[:, 1024:2048].rearrange("p (c h) -> p c h", h=64)
    acol = c.stg_f32[3].t[:, 0:1]
    cst2 = c.stg_f32[2].t[:, 0:128]
    S.dma("sp", cst2, c.consts2_d[:, C_U:C_U + 128], w=["Uf"])
    S.dma("sp", dtT[0:64, :], c.P["dt"], w=["dtT"])
    one_col = c.cst.t[0:64, C_ONES:C_ONES + 1]
    S.op("dve", lambda e: e.tensor_scalar(out=dtT[0:64, :], in0=dtT[0:64, :], scalar1=V[0:64, V_DTB:V_DTB + 1], scalar2=None, op0=ALU.add),
         r=["dtT", "vecs"], w=["dtT"])
    S.op("act", lambda e: e.activation(out=tA[0:64, :], in_=dtT[0:64, :], func=AF.Abs), r=["dtT"], w=["tA"])
    S.op("act", lambda e: e.activation(out=tA[0:64, :], in_=tA[0:64, :], func=AF.Exp, scale=-1.0), r=["tA"], w=["tA"])
    S.op("act", lambda e: e.activation(out=tA[0:64, :], in_=tA[0:64, :], func=AF.Ln, bias=one_col), r=["tA", "cst"], w=["tA"])
    S.op("dve", lambda e: e.scalar_tensor_tensor(out=dtT[0:64, :], in0=dtT[0:64, :], scalar=0.0, in1=tA[0:64, :], op0=ALU.max, op1=ALU.add),
         r=["dtT", "tA"], w=["dtT"])
    S.op("act", lambda e: e.activation(out=acol[0:64, :], in_=V[0:64, V_ALOG:V_ALOG + 1], func=AF.Exp), r=["vecs"], w=["acol"])
    S.op("dve", lambda e: e.tensor_scalar(out=tA[0:64, :], in0=dtT[0:64, :], scalar1=acol[0:64, :], scalar2=-1.0, op0=ALU.mult, op1=ALU.mult),
         r=["dtT", "acol"], w=["tA"])
    for src, dst, nm in ((dtT, dt_tok, "dt_tok"), (tA, dA_tok, "dA_tok")):
        for half in range(2):
            pb = c.psb[half]

            def tr(e, src=src, half=half, pb=pb):
                ins = None
                for j in range(8):
                    cch = half * 8 + j
                    ins = e.transpose(pb.t[:, j * 64:(j + 1) * 64], src[0:64, cch * 128:(cch + 1) * 128], ident_f[0:64, 0:64])
                return ins
            S.op("pe", tr, r=["dtT", "tA", "cst"], w=[pb.k])
            S.op("dve", lambda e, dst=dst, half=half, pb=pb: e.tensor_copy(
                out=dst[:, half * 8:(half + 1) * 8, :], in_=pb.t[:, :].rearrange("p (c h) -> p c h", h=64)), r=[pb.k], w=[nm])
    for half in range(2):
        pb = c.psb[2 + half]

        def cs(e, half=half, pb=pb):
            ins = None
            for j in range(8):
                cch = half * 8 + j
                ins = e.matmul(pb.t[:, j * 64:(j + 1) * 64], cst2, dA_tok[:, cch, :], start=True, stop=True)
            return ins
        S.op("pe", cs, r=["dA_tok", "Uf"], w=[pb.k])
        S.op("dve", lambda e, half=half, pb=pb: e.tensor_copy(out=Acs_tok[:, half * 8:(half + 1) * 8, :],
                                                           in_=pb.t[:, :].rearrange("p (c h) -> p c h", h=64)), r=[pb.k], w=["Acs_tok"])
    for q4 in range(4):
        pb = c.psb[4 + q4]

        def cs2(e, q4=q4, pb=pb):
            ins = None
            for j in range(4):
                cch = q4 * 4 + j
                ins = e.matmul(pb.t[0:64, j * 128:(j + 1) * 128], dA_tok[:, cch, :], cst2, start=True, stop=True)
            return ins
        S.op("pe", cs2, r=["dA_tok", "Uf"], w=[pb.k])
        S.op("dve", lambda e, q4=q4, pb=pb: e.tensor_copy(out=AcsT[0:64, q4 * TB:(q4 + 1) * TB], in_=pb.t[0:64, :]), r=[pb.k], w=["AcsT"])
    S.op("act", lambda e: e.activation(out=tok2[:, 0:1024], in_=tokw[:, 1024:2048], func=AF.Exp), r=["Acs_tok"], w=["expA_tok"])
    S.dma("sp", c.acs_d, AcsT[0:64, :], r=["AcsT"], w=["acs_d"])
    Dfull = c.wslots[1].t[:, 0:4096]
    sched_barrier(S)
    for h in range(NH_SSD):
        S.op("dve", lambda e, h=h: e.tensor_scalar(out=Dfull[:, h * 64:(h + 1) * 64], in0=c.cst.t[:, C_ONES:C_ONES + 64],
                                                   scalar1=V[:, V_DBC + h:V_DBC + h + 1], scalar2=None, op0=ALU.mult),
             r=["cst", "vecs"], w=["Dfull"])

    H = R6[:, 0:8192].bitcast(F32)
    Hbf = R6[:, 8192:12288]
    xs_tok = R6[:, 12288:16384]
    xw = R6[:, 16384:20480]
    y_tok = R6[:, 20480:24576]
    zc = R6[:, 24576:28672].rearrange("p (cc t) -> p cc t", t=128)
    xsT = R6[:, 28672:32768].rearrange("p (cc t) -> p cc t", t=128)
    ygf = R4[:, 0:8192].bitcast(F32).rearrange("p (cc t) -> p cc t", t=128)
    yout = R4[:, 8192:12288].rearrange("p (cc t) -> p cc t", t=128)
    BT = R4[:, 12288:13312].rearrange("p (g t) -> p g t", t=128)
    CT = R4[:, 13312:14336].rearrange("p (g t) -> p g t", t=128)
    B_tok = R4[:, 14336:15360].rearrange("p (g t) -> p g t", t=128)
    cbTm = R4[:, 15360:17408].bitcast(F32).rearrange("p (g t) -> p g t", t=128)
    Et = [R4[:, 17408 + i * 1024:17408 + (i + 1) * 1024].bitcast(F32) for i in range(3)]
    Mp = [R4[:, 20480 + i * 512:20480 + (i + 1) * 512] for i in range(3)]
    WSv = c.WS.t
    A_rows = WSv[:, 0:5632].bitcast(F32)
    xdt = WSv[:, 9728:13824]
    mask4 = WSv[:, 13824:14848].bitcast(F32)
    negones = WSv[:, 14848:15104].bitcast(F32)
    w64 = WSv[:, 15104:15232].bitcast(F32)
    dec = WSv[:, 15232:15360].bitcast(F32)
    Uf = cst2
    for j in range(4):
        S.op("dve", lambda e, j=j: e.tensor_scalar(out=mask4[:, j * 128:(j + 1) * 128], in0=Uf, scalar1=-1.0, scalar2=30000.0,
                                                   op0=ALU.add, op1=ALU.mult), r=["Uf"], w=["mask4"])
    S.op("dve", lambda e: e.memset(negones, -1.0), w=["negones"])
    pgroups = [(0, 0, 22), (32, 22, 44), (64, 44, 64)]
    batches = []
    for pbase, hs, he in pgroups:
        for h0 in range(hs, he, 4):
            batches.append((pbase, hs, h0, min(4, he - h0)))
    xcv = c.xc_d.rearrange("(cc p) t -> p cc t", p=128)
    zv = c.P["z"].rearrange("(cc p) t -> p cc t", p=128)
    yv = c.yssd.rearrange("(cc p) t -> p cc t", p=128)
    S.op("dve", lambda e: e.memset(H, 0.0), w=["H%d" % g for g in range(8)])
    S.op("dve", lambda e: e.memset(Hbf, 0.0), w=["Hbf%d" % g for g in range(8)])
    NCH = L // 128
    for ch in range(NCH):
        t0 = ch * 128
        last = (ch == NCH - 1)
        S.dma("sp", xsT[:, 0:16, :], xcv[:, 0:16, t0:t0 + 128], w=["xsT0"])
        S.dma("sp", xsT[:, 16:32, :], xcv[:, 16:32, t0:t0 + 128], w=["xsT1"])
        S.dma("sp", BT, xcv[:, 32:40, t0:t0 + 128], w=["BT"])
        S.dma("sp", CT, xcv[:, 40:48, t0:t0 + 128], w=["CT"])
        S.dma("sp", zc, zv[:, :, t0:t0 + 128], w=["zc"])
        for pbase, hs, he in pgroups:
            S.dma("sp", A_rows[pbase:pbase + 1, 0:(he - hs) * 128].rearrange("o (h t) -> o h t", t=128),
                  c.acs_d[hs:he, t0:t0 + 128].rearrange("(o h) t -> o h t", o=1), r=["acs_d"], w=["A_rows"])
        pb7 = c.psb[7]
        pb7v = pb7.t[:, :].bitcast(BF16)
        for q8 in range(4):
            def trx(e, q8=q8):
                ins = None
                for j in range(8):
                    ins = e.transpose(pb7v[:, j * 128:(j + 1) * 128], xsT[:, q8 * 8 + j, :], c.ident_bf.t[:, :])
                return ins
            S.op("pe", trx, r=["xsT0", "xsT1", "const"], w=[pb7.k])
            S.op("act", lambda e, q8=q8: e.copy(out=xs_tok[:, q8 * 1024:(q8 + 1) * 1024], in_=pb7v[:, 0:1024]), r=[pb7.k], w=["xs_tok"])

        def trb(e):
            ins = None
            for g in range(8):
                ins = e.transpose(pb7v[:, g * 128:(g + 1) * 128], BT[:, g, :], c.ident_bf.t[:, :])
            return ins
        S.op("pe", trb, r=["BT", "const"], w=[pb7.k])
        S.op("act", lambda e: e.copy(out=B_tok, in_=pb7v[:, 0:1024].rearrange("p (g t) -> p g t", t=128)), r=[pb7.k], w=["B_tok"])
        S.op("pool", lambda e: e.tensor_tensor(out=xdt.rearrange("p (h d) -> p h d", d=64), in0=xs_tok.rearrange("p (h d) -> p h d", d=64),
                                               in1=dt_tok[:, ch, :].unsqueeze(2).to_broadcast([128, 64, 64]), op=ALU.mult),
             r=["xs_tok", "dt_tok"], w=["xdt"])
        for half in range(2):
            pb = c.psb[2 + half]

            def mcb(e, half=half, pb=pb):
                ins = None
                for j in range(4):
                    g = half * 4 + j
                    ins = e.matmul(pb.t[:, j * 128:(j + 1) * 128], BT[:, g, :], CT[:, g, :], start=True, stop=True)
                return ins
            S.op("pe", mcb, r=["BT", "CT"], w=[pb.k])
            S.op("dve", lambda e, half=half, pb=pb: e.tensor_tensor(
                out=cbTm[:, half * 4:(half + 1) * 4, :], in0=pb.t[:, :].rearrange("p (g t) -> p g t", t=128),
                in1=Uf.unsqueeze(1).to_broadcast([128, 4, 128]), op=ALU.mult), r=[pb.k, "Uf"], w=["cbTm"])
        S.op("act", lambda e: e.activation(out=zc, in_=zc, func=AF.Silu), r=["zc"], w=["zc"])
        def st1(bi):
            pbase, hs, h0, nh = batches[bi]
            pT1 = c.psb[bi % 2]

            def mseg(e):
                e.matmul(pT1.t[:, 0:nh * 128], c.cst.t[pbase:pbase + 1, C_ONES:C_ONES + 128],
                         A_rows[pbase:pbase + 1, (h0 - hs) * 128:(h0 - hs + nh) * 128], start=True, stop=False)
                ins = None
                for i in range(nh):
                    ins = e.matmul(pT1.t[:, i * 128:(i + 1) * 128], A_rows[pbase:pbase + 1, (h0 - hs + i) * 128:(h0 - hs + i + 1) * 128],
                                   negones[pbase:pbase + 1, :], start=False, stop=(i == nh - 1))
                return ins
            S.op("pe", mseg, r=["A_rows", "cst", "negones"], w=[pT1.k])

        def st2(bi):
            pbase, hs, h0, nh = batches[bi]
            pT1 = c.psb[bi % 2]
            E = Et[bi % 3]
            ek = "E%d" % (bi % 3)
            S.op("dve", lambda e: e.tensor_scalar(out=E[:, 0:nh * 128], in0=pT1.t[:, 0:nh * 128], scalar1=0.0, scalar2=None, op0=ALU.min),
                 r=[pT1.k], w=[ek])
            S.op("act", lambda e: e.activation(out=E[:, 0:nh * 128], in_=E[:, 0:nh * 128], func=AF.Exp), r=[ek], w=[ek])
            if not last:
                S.op("pool", lambda e: e.tensor_copy(
                    out=w64[:, h0:h0 + nh].rearrange("p (h o) -> p h o", o=1),
                    in_=E[:, 0:nh * 128].rearrange("p (h t) -> p h t", t=128)[:, :, 127:128]), r=[ek], w=["w64"])

        def st3(bi):
            pbase, hs, h0, nh = batches[bi]
            E, M = Et[bi % 3], Mp[bi % 3]
            ek, mk = "E%d" % (bi % 3), "Mp%d" % (bi % 3)
            i = 0
            while i < nh:
                g = (h0 + i) // 8
                j = i
                while j < nh and (h0 + j) // 8 == g:
                    j += 1
                n_ = j - i
                S.op("dve", lambda e, i=i, n_=n_, g=g: e.tensor_tensor(
                    out=M[:, i * 128:(i + n_) * 128].rearrange("p (h t) -> p h t", t=128),
                    in0=E[:, i * 128:(i + n_) * 128].rearrange("p (h t) -> p h t", t=128),
                    in1=cbTm[:, g:g + 1, :].to_broadcast([128, n_, 128]), op=ALU.mult), r=[ek, "cbTm"], w=[mk])
                i = j
            for i in range(nh):
                h = h0 + i
                g = h // 8
                po1 = c.psb[4]
                S.op("pe", lambda e, i=i, h=h: e.matmul(po1.t[:, (h % 8) * 64:(h % 8 + 1) * 64], M[:, i * 128:(i + 1) * 128],
                                                       xdt[:, h * 64:(h + 1) * 64], start=True, stop=True),
                     r=[mk, "xdt"], w=[po1.k + "_h%d" % (h % 8)])
                if h % 8 == 7:
                    gs = slice(g * 512, (g + 1) * 512)
                    po1k = [po1.k + "_h%d" % r_ for r_ in range(8)]
                    po2 = c.psb[5]
                    S.op("pe", lambda e, g=g, gs=gs: e.matmul(po2.t[:, :], CT[:, g, :], Hbf[:, gs], start=True, stop=True),
                         r=["CT", "Hbf%d" % g], w=[po2.k])
                    xd = c.stg_f32[0]
                    S.op("pool", lambda e, gs=gs: e.tensor_tensor(out=xd.t[:, :], in0=xs_tok[:, gs], in1=Dfull[:, gs], op=ALU.mult),
                         r=["xs_tok", "Dfull"], w=[xd.k])
                    sb1 = c.stg_f32[1]
                    S.op("dve", lambda e: e.tensor_tensor(out=sb1.t[:, :], in0=po1.t[:, :], in1=xd.t[:, :], op=ALU.add),
                         r=po1k + [xd.k], w=[sb1.k] + po1k)
                    S.op("dve", lambda e, gs=gs, g=g: e.tensor_tensor(
                        out=y_tok[:, gs].rearrange("p (h d) -> p h d", d=64), in0=po2.t[:, :].rearrange("p (h d) -> p h d", d=64),
                        in1=expA_tok[:, ch, g * 8:(g + 1) * 8].unsqueeze(2).to_broadcast([128, 8, 64]), op=ALU.mult),
                        r=[po2.k, "expA_tok"], w=["y_tok%d" % g])
                    S.op("dve", lambda e, gs=gs: e.tensor_tensor(out=y_tok[:, gs], in0=y_tok[:, gs], in1=sb1.t[:, :], op=ALU.add),
                         r=["y_tok%d" % g, sb1.k], w=["y_tok%d" % g])
                    if not last:
                        S.op("pool", lambda e, gs=gs, g=g: e.tensor_tensor(
                            out=xw[:, gs].rearrange("p (h d) -> p h d", d=64), in0=xdt[:, gs].rearrange("p (h d) -> p h d", d=64),
                            in1=w64[:, g * 8:(g + 1) * 8].unsqueeze(2).to_broadcast([128, 8, 64]), op=ALU.mult),
                            r=["xdt", "w64"], w=["xw%d" % g])
                        pS = c.psb[6]
                        S.op("pe", lambda e, g=g, gs=gs: e.matmul(pS.t[:, :], B_tok[:, g, :], xw[:, gs], start=True, stop=True),
                             r=["B_tok", "xw%d" % g], w=[pS.k])
                        S.op("pool", lambda e, g=g: e.tensor_tensor(out=dec[:, g * 8:(g + 1) * 8], in0=w64[:, g * 8:(g + 1) * 8],
                                                                    in1=expA_tok[:, ch, g * 8:(g + 1) * 8], op=ALU.mult),
                             r=["w64", "expA_tok"], w=["dec"])
                        S.op("pool", lambda e, gs=gs, g=g: e.tensor_tensor(
                            out=H[:, gs].rearrange("p (h d) -> p h d", d=64), in0=H[:, gs].rearrange("p (h d) -> p h d", d=64),
                            in1=dec[:, g * 8:(g + 1) * 8].unsqueeze(2).to_broadcast([128, 8, 64]), op=ALU.mult),
                            r=["dec", "H%d" % g], w=["H%d" % g])
                        S.op("dve", lambda e, gs=gs: e.tensor_tensor(out=H[:, gs], in0=H[:, gs], in1=pS.t[:, :], op=ALU.add),
                             r=[pS.k, "H%d" % g], w=["H%d" % g])
                        S.op("act", lambda e, gs=gs: e.copy(out=Hbf[:, gs], in_=H[:, gs]), r=["H%d" % g], w=["Hbf%d" % g])

        nb_ = len(batches)
        for step in range(nb_ + 2):
            if step < nb_:
                st1(step)
            if 0 <= step - 1 < nb_:
                st2(step - 1)
            if 0 <= step - 2 < nb_:
                st3(step - 2)
        sq = xw.rearrange("p (cc t) -> p cc t", t=128)
        ytk = ["y_tok%d" % g for g in range(8)]
        for q8 in range(4):
            def try_(e, q8=q8):
                ins = None
                for j in range(8):
                    cc = q8 * 8 + j
                    ins = e.transpose(pb7v[:, j * 128:(j + 1) * 128], y_tok[:, cc * 128:(cc + 1) * 128], c.ident_bf.t[:, :])
                return ins
            S.op("pe", try_, r=ytk + ["const"], w=[pb7.k])
            S.op("dve", lambda e, q8=q8: e.tensor_tensor(out=ygf[:, q8 * 8:(q8 + 1) * 8, :],
                                                       in0=pb7v[:, 0:1024].rearrange("p (cc t) -> p cc t", t=128),
                                                       in1=zc[:, q8 * 8:(q8 + 1) * 8, :], op=ALU.mult), r=[pb7.k, "zc"], w=["ygf"])
        xwk = ["xw%d" % g for g in range(8)]
        S.op("act", lambda e: e.activation(out=sq, in_=ygf, func=AF.Square), r=["ygf"], w=xwk)
        pn = c.psb[6]

        def mmn(e):
            ins = None
            for cc in range(32):
                ins = e.matmul(pn.t[:, 0:128], c.ones_bf.t[:, :], sq[:, cc, :], start=(cc == 0), stop=(cc == 31))
            return ins
        S.op("pe", mmn, r=xwk + ["const"], w=[pn.k])
        rt = c.stg_f32[2].t[:, 128:256]
        rs = c.stg_f32[2].t[:, 256:384]
        S.op("act", lambda e: e.activation(out=rt, in_=pn.t[:, 0:128], func=AF.Sqrt, scale=1.0 / SSD_INNER, bias=c.eps_col.t[:, 0:1]),
             r=[pn.k], w=["rt"])
        S.op("dve", lambda e: e.reciprocal(out=rs, in_=rt), r=["rt"], w=["rs"])
        for cc in range(32):
            S.op("dve", lambda e, cc=cc: e.scalar_tensor_tensor(out=yout[:, cc, :], in0=ygf[:, cc, :], scalar=V[:, V_SSDN + cc:V_SSDN + cc + 1],
                                                                 in1=rs, op0=ALU.mult, op1=ALU.mult), r=["ygf", "rs", "vecs"], w=["yout"])
        S.dma("sp", yv[:, 0:16, t0:t0 + 128], yout[:, 0:16, :], r=["yout"], w=[("yssd", ch, 0)])
        S.dma("sp", yv[:, 16:32, t0:t0 + 128], yout[:, 16:32, :], r=["yout"], w=[("yssd", ch, 1)])
    sched_barrier(S)


FULL_PLAN = []
for _l in range(DEPTH):
    FULL_PLAN += [("ffn1", _l), ("inproj", _l), ("mem", _l), ("dsa", _l), ("ssd", _l), ("merge", _l), ("ffn2", _l)]
FULL_PLAN.append("final")
_NC_CACHE = {}


def kernel(**inputs):
    inp = {k: np.asarray(v) for k, v in inputs.items()}
    B = inp["x"].shape[0]
    if "nc" not in _NC_CACHE:
        _NC_CACHE["nc"] = build(FULL_PLAN)
    nc = _NC_CACHE["nc"]
    vec = np.stack([pack_vecs(inp, l) for l in range(DEPTH)])
    cst = make_consts()
    in_maps = [core_inputs(inp, b, vec, cst) for b in range(B)]
    res = run_bass_kernel_spmd(nc, in_maps, core_ids=list(range(B)))
    out = np.stack([np.ascontiguousarray(res.results[b]["outT"].T) for b in range(B)]).astype(np.float32)
    return out
```

```python
from contextlib import ExitStack
import numpy as np
import concourse.bass as bass
import concourse.mybir as mybir
from concourse.bass_utils import run_bass_kernel_spmd

F32 = mybir.dt.float32
BF16 = mybir.dt.bfloat16
I32 = mybir.dt.int32
AF = mybir.ActivationFunctionType
ALU = mybir.AluOpType
AX = mybir.AxisListType

D = 2048
L = 2048
DEPTH = 2
DFF = 5632
MEM = 256
EPS = 1e-6
SSD_INNER = 4096
CONV_CH = 6144
NH_SSD = 64
IN_SPLITS = (4096, 6144, 64, 2048, 128, 128, 1024, 64, 16, 2048, 6144)
IN_OFF = [0]
for _s in IN_SPLITS:
    IN_OFF.append(IN_OFF[-1] + _s)
IN_WIDTH = IN_OFF[-1]
(O_Z, O_XBC, O_DT, O_Q, O_K, O_V, O_QI, O_KI, O_WI, O_QM, O_G) = IN_OFF[:11]
TB = 512
NTB = L // TB
KC_D = D // 128


class Sched:
    ENGS = ("pe", "act", "dve", "pool", "sp")

    def __init__(self, nc, es, n_dma_sems=24):
        self.nc = nc
        self.eng = dict(pe=nc.tensor, act=nc.scalar, dve=nc.vector, pool=nc.gpsimd, sp=nc.sync)
        self.sem = {e: es.enter_context(nc.semaphore("sem_" + e)) for e in self.ENGS}
        self.cnt = {e: 0 for e in self.ENGS}
        self.seen = {e: {} for e in self.ENGS}
        self.snap = {}
        self.dsem = {}
        self.dcur = {}
        self.drr = {}
        for q in ("sp", "pool", "act"):
            n = n_dma_sems if q != "act" else 8
            self.dsem[q] = [es.enter_context(nc.semaphore("dsem_%s_%d" % (q, i))) for i in range(n)]
            self.dcur[q] = [0] * n
            self.drr[q] = 0
        self.bufs = {}
        self.n_wait = 0
        self.n_inst = 0

    def _wait(self, e, tok):
        if tok is None:
            return
        kind = tok[0]
        seen = self.seen[e]
        if kind == "c":
            _, f, c = tok
            if seen.get(f, 0) >= c:
                return
            if f == e and e == "pe":
                return
            self.eng[e].wait_ge(self.sem[f], c)
            self.n_wait += 1
            seen[f] = c
            sn = self.snap.get((f, c))
            if sn is not None:
                for g, v in zip(self.ENGS, sn):
                    if v > seen.get(g, 0):
                        seen[g] = v
        else:
            _, q, i, v = tok
            key = (q, i)
            if seen.get(key, 0) >= v:
                return
            self.eng[e].wait_ge(self.dsem[q][i], v)
            self.n_wait += 1
            seen[key] = v

    def _deps(self, e, r, w):
        for k in r:
            st = self.bufs.get(k)
            if st is not None:
                self._wait(e, st[0])
        for k in w:
            st = self.bufs.get(k)
            if st is not None:
                self._wait(e, st[0])
                for t in st[1]:
                    if t[0] == "c" and t[1] == e:
                        continue
                    self._wait(e, t)

    def _commit(self, tok, r, w):
        for k in r:
            st = self.bufs.setdefault(k, [None, []])
            st[1] = [t for t in st[1] if not (t[0] == tok[0] and t[1] == tok[1] and (t[0] == "c" or t[2] == tok[2]))]
            st[1].append(tok)
        for k in w:
            self.bufs[k] = [tok, []]

    def op(self, e, fn, r=(), w=()):
        self._deps(e, r, w)
        ins = fn(self.eng[e])
        self.cnt[e] += 1
        c = self.cnt[e]
        ins.then_inc(self.sem[e], 1)
        self.n_inst += 1
        sn = self.seen[e]
        self.snap[(e, c)] = tuple(c if g == e else sn.get(g, 0) for g in self.ENGS)
        tok = ("c", e, c)
        self._commit(tok, r, w)
        return tok

    def dma(self, q, out, in_, r=(), w=(), **kw):
        self._deps(q, r, w)
        i = self.drr[q]
        self.drr[q] = (i + 1) % len(self.dsem[q])
        cur = self.dcur[q][i]
        if cur > 0:
            self._wait(q, ("d", q, i, cur))
        self.eng[q].dma_start(out=out, in_=in_, **kw).then_inc(self.dsem[q][i], 16)
        self.dcur[q][i] = cur + 16
        tok = ("d", q, i, cur + 16)
        self._commit(tok, r, w)
        return tok

    def finish(self, e="sp"):
        for f in self.ENGS:
            if self.cnt[f] > 0 and f != e:
                self._wait(e, ("c", f, self.cnt[f]))
        for q in self.dsem:
            for i, cur in enumerate(self.dcur[q]):
                if cur > 0:
                    self._wait(e, ("d", q, i, cur))


class Buf:
    def __init__(self, t, key):
        self.t = t
        self.k = key

    def __getitem__(self, idx):
        return self.t[idx]


class Ctx:
    pass


def make_ctx(nc, es):
    c = Ctx()
    c.nc = nc
    c.es = es
    c.S = Sched(nc, es)
    c.nbuf = 0

    def sb(shape, dt, name=None):
        c.nbuf += 1
        name = name or ("sb%d" % c.nbuf)
        t = es.enter_context(nc.sbuf_tensor(name, list(shape), dt))
        return Buf(t, name)

    def ps(name, shape=(128, 512), dt=F32):
        t = es.enter_context(nc.psum_tensor(name, list(shape), dt))
        return Buf(t, name)
    c.sb = sb
    c.ps = ps
    return c


def stream_linear(c, jobs, rhs_fn, rhs_keys, ntb, epilogue, wslots, psum_sets, tbw=TB):
    S = c.S
    nslot = len(wslots)
    state = c.__dict__.setdefault("_lin_state", {"slot": 0, "pset": 0})
    flat = []
    for ji, job in enumerate(jobs):
        for gi, g in enumerate(job):
            flat.append((ji, gi, g))
    PREF = nslot - 1
    slot_of = {}

    def issue(n):
        ji, gi, (W, col0, ncols, KC) = flat[n]
        si = state["slot"]
        state["slot"] = (si + 1) % nslot
        ws = wslots[si]
        wv = W.rearrange("(kc p) f -> p kc f", p=128)
        dst = ws.t[:, 0:KC * ncols].rearrange("p (kc f) -> p kc f", f=ncols)
        half = KC // 2
        S.dma("pool", dst[:, 0:half, :], wv[:, 0:half, col0:col0 + ncols], w=[ws.k + "_a"])
        S.dma("pool", dst[:, half:KC, :], wv[:, half:KC, col0:col0 + ncols], w=[ws.k + "_b"])
        slot_of[n] = (ws, dst)

    nxt = 0
    n = 0
    for ji, job in enumerate(jobs):
        while nxt < len(flat) and nxt < n + nslot:
            issue(nxt)
            nxt += 1
        for tb in range(ntb):
            pi = state["pset"] % len(psum_sets)
            state["pset"] = (pi + 1) % len(psum_sets)
            pset = psum_sets[pi]
            for gi, (W, col0, ncols, KC) in enumerate(job):
                ws, dst = slot_of[n + gi]
                pb = pset[gi]

                def mm(e, dst=dst, pb=pb, KC=KC, gi=gi, tb=tb, ncols=ncols):
                    ins = None
                    for kc in range(KC):
                        ins = e.matmul(pb.t[0:ncols, 0:tbw], dst[:, kc, :], rhs_fn(gi, kc, tb),
                                       start=(kc == 0), stop=(kc == KC - 1))
                    return ins
                S.op("pe", mm, r=[ws.k + "_a", ws.k + "_b"] + list(rhs_keys(gi, tb)), w=[pb.k])
            epilogue(ji, tb, pset)
        n += len(job)


def sched_barrier(S):
    engs = [e for e in S.ENGS]
    toks = [("c", f, S.cnt[f]) for f in S.ENGS if S.cnt[f] > 0]
    dtoks = []
    for q in S.dsem:
        for i, cur in enumerate(S.dcur[q]):
            if cur > 0:
                dtoks.append(("d", q, i, cur))
    for e in engs:
        for t in toks:
            if t[1] != e:
                S._wait(e, t)
        for t in dtoks:
            S._wait(e, t)
    S.bufs = {}


def phase_norm(c, x_src, gcol, out_mode, out_dst=None):
    S = c.S
    xv = x_src.rearrange("(kc p) t -> p kc t", p=128)
    xin = c.R44.t[:, 0:16 * TB * 2].bitcast(F32).rearrange("p (kc t) -> p kc t", t=TB)
    hT = c.hT
    for tb in range(NTB):
        t0 = tb * TB
        for hf in range(2):
            S.dma("sp", xin[:, hf * 8:(hf + 1) * 8, :], xv[:, hf * 8:(hf + 1) * 8, t0:t0 + TB],
                  w=["xin%d" % hf])
        pb = c.psb[tb % 2]
        for kc in range(KC_D):
            sq = c.stg_bf[kc % 3]
            S.op("act", lambda e, sq=sq, kc=kc: e.activation(out=sq.t[:, :], in_=xin[:, kc, :], func=AF.Square),
                 r=["xin%d" % (kc // 8)], w=[sq.k])
            S.op("pe", lambda e, sq=sq, kc=kc, pb=pb: e.matmul(pb.t[:, :], c.ones_bf.t[:, :], sq.t[:, :],
                                                                start=(kc == 0), stop=(kc == KC_D - 1)),
                 r=[sq.k, "const"], w=[pb.k])
        rt = c.stg_f32[0]
        S.op("act", lambda e, pb=pb: e.activation(out=rt.t[:, :], in_=pb.t[:, :], func=AF.Sqrt,
                                                  scale=1.0 / D, bias=c.eps_col.t[:, 0:1]),
             r=[pb.k, "const"], w=[rt.k])
        rs = c.stg_f32[1]
        S.op("dve", lambda e: e.reciprocal(out=rs.t[:, :], in_=rt.t[:, :]), r=[rt.k], w=[rs.k])
        for kc in range(KC_D):
            if out_mode == "h":
                S.op("dve", lambda e, kc=kc: e.scalar_tensor_tensor(
                    out=hT[:, kc, t0:t0 + TB], in0=xin[:, kc, :], scalar=gcol[:, kc:kc + 1], in1=rs.t[:, :],
                    op0=ALU.mult, op1=ALU.mult),
                    r=["xin%d" % (kc // 8), rs.k, "vecs"], w=["hT%d" % tb])
            else:
                ot = c.stg_f32[2 + kc % 2]
                S.op("dve", lambda e, kc=kc, ot=ot: e.scalar_tensor_tensor(
                    out=ot.t[:, :], in0=xin[:, kc, :], scalar=gcol[:, kc:kc + 1], in1=rs.t[:, :],
                    op0=ALU.mult, op1=ALU.mult),
                    r=["xin%d" % (kc // 8), rs.k, "vecs"], w=[ot.k])
                S.dma("sp", out_dst[kc * 128:(kc + 1) * 128, t0:t0 + TB], ot.t[:, :], r=[ot.k])


def phase_ffn(c, w_in, w_out, x_src, x_dst):
    S = c.S
    NJ = DFF // 128
    hT = c.hT
    actd = c.act_d
    jobs = [[(w_in, j * 128, 128, KC_D), (w_in, DFF + j * 128, 128, KC_D)] for j in range(NJ)]
    cnt = [0]

    def epi_a(ji, tb, pset):
        i = cnt[0]
        cnt[0] += 1
        sg = c.stg_f32[i % 2]
        S.op("act", lambda e: e.activation(out=sg.t[:, :], in_=pset[0].t[:, :], func=AF.Silu),
             r=[pset[0].k], w=[sg.k])
        ab = c.stg_bf[i % 3]
        S.op("dve", lambda e: e.tensor_tensor(out=ab.t[:, :], in0=pset[1].t[:, :], in1=sg.t[:, :], op=ALU.mult),
             r=[pset[1].k, sg.k], w=[ab.k])
        S.dma("sp", actd[ji * 128:(ji + 1) * 128, tb * TB:(tb + 1) * TB], ab.t[:, :], r=[ab.k],
              w=[("act", ji, tb)])

    stream_linear(c, jobs, lambda gi, kc, tb: hT[:, kc, tb * TB:(tb + 1) * TB],
                  lambda gi, tb: ["hT%d" % tb], NTB, epi_a, c.wslots,
                  [[c.psb[0], c.psb[1]], [c.psb[2], c.psb[3]], [c.psb[4], c.psb[5]], [c.psb[6], c.psb[7]]])
    sched_barrier(S)
    av = actd.rearrange("(kc p) t -> p kc t", p=128)
    blks = [c.R64.t[:, 0:NJ * TB].rearrange("p (kc t) -> p kc t", t=TB),
            c.R44.t[:, 0:NJ * TB].rearrange("p (kc t) -> p kc t", t=TB)]
    bkeys = ["ablkA", "ablkB"]
    for tb in range(NTB):
        blk = blks[tb % 2]
        bk = bkeys[tb % 2]
        t0 = tb * TB
        for q4 in range(4):
            S.dma("sp", blk[:, q4 * 11:(q4 + 1) * 11, :], av[:, q4 * 11:(q4 + 1) * 11, t0:t0 + TB], w=[bk + str(q4)])
        jobs = [[(w_out, dc * 128, 128, NJ)] for dc in range(KC_D)]
        cnt2 = [0]

        def epi_b(ji, tb_unused, pset, tb=tb, t0=t0):
            i = cnt2[0]
            cnt2[0] += 1
            xt = c.stg_f32[i % 2]
            S.dma("sp", xt.t[:, :], x_src[ji * 128:(ji + 1) * 128, t0:t0 + TB], w=[xt.k])
            xo = c.stg_f32[2 + i % 2]
            S.op("dve", lambda e: e.scalar_tensor_tensor(out=xo.t[:, :], in0=pset[0].t[:, :], scalar=0.5,
                                                         in1=xt.t[:, :], op0=ALU.mult, op1=ALU.add),
                 r=[pset[0].k, xt.k], w=[xo.k])
            S.dma("sp", x_dst[ji * 128:(ji + 1) * 128, t0:t0 + TB], xo.t[:, :], r=[xo.k])

        stream_linear(c, jobs, lambda gi, kc, tb_, blk=blk: blk[:, kc, :],
                      lambda gi, tb_, bk=bk: [bk + str(q) for q in range(4)], 1, epi_b, c.wslots,
                      [[c.psb[i]] for i in range(8)])
    sched_barrier(S)


V_FFN1, V_MIX, V_FFN2, V_MEMN = 0, 16, 32, 48
V_SSDN = 64
V_CONVW = 96
V_CONVB = 288
V_DTB, V_ALOG, V_DSKIP = 336, 337, 338
V_FINAL = 339
V_DBC = 360
NV = 424


def pack_vecs(inp, l):
    v = np.zeros((128, NV), np.float32)

    def col(vec):
        return np.ascontiguousarray(np.asarray(vec, np.float32).reshape(-1, 128).T)
    v[:, V_FFN1:V_FFN1 + 16] = col(inp["ffn1_norm"][l])
    v[:, V_MIX:V_MIX + 16] = col(inp["mix_norm"][l])
    v[:, V_FFN2:V_FFN2 + 16] = col(inp["ffn2_norm"][l])
    v[:, V_MEMN:V_MEMN + 16] = col(inp["mem_norm"][l])
    v[:, V_SSDN:V_SSDN + 32] = col(inp["ssd_norm"][l])
    for k in range(4):
        v[:, V_CONVW + k * 48:V_CONVW + (k + 1) * 48] = col(inp["conv_w"][l][k])
    v[:, V_CONVB:V_CONVB + 48] = col(inp["conv_b"][l])
    v[0:64, V_DTB] = inp["dt_bias"][l]
    v[0:64, V_ALOG] = inp["a_log"][l]
    v[0:64, V_DSKIP] = inp["d_skip"][l]
    v[:, V_FINAL:V_FINAL + 16] = col(inp["final_norm"])
    v[:, V_DBC:V_DBC + 64] = np.asarray(inp["d_skip"][l], np.float32)[None, :]
    return v


C_IDENT, C_ONES = 0, 128
NCONST = 256


def make_consts():
    cst = np.zeros((128, NCONST), np.float32)
    cst[:, C_IDENT:C_IDENT + 128] = np.eye(128, dtype=np.float32)
    cst[:, C_ONES:C_ONES + 128] = 1.0
    return cst


def build(plan, dbg=None):
    nc = bass.Bass("TRN2", target_bir_lowering=False)
    es = ExitStack()
    with es:
        c = make_ctx(nc, es)
        S = c.S
        dt = nc.dram_tensor
        xT = dt("xT", [D, L], F32, kind="ExternalInput").ap()
        memT = dt("memT", [D, MEM], F32, kind="ExternalInput").ap()
        pos = dt("pos", [128, L], I32, kind="ExternalInput").ap()
        c.consts2_d = dt("consts2", [128, NCONST2], F32, kind="ExternalInput").ap()
        vecs = dt("vecs", [DEPTH, 128, NV], F32, kind="ExternalInput").ap()
        consts = dt("consts", [128, NCONST], F32, kind="ExternalInput").ap()
        W = {}
        for name, shp in (("w_ffn1_in", [D, 2 * DFF]), ("w_ffn1_out", [DFF, D]), ("w_ffn2_in", [D, 2 * DFF]),
                          ("w_ffn2_out", [DFF, D]), ("w_in", [D, IN_WIDTH]), ("w_mem_kv", [D, 2 * D]),
                          ("w_br_ssd", [SSD_INNER, D]), ("w_br_dsa", [D, D]), ("w_br_mem", [D, D]), ("w_out", [D, D])):
            W[name] = dt(name, [DEPTH, shp[0] + 1, shp[1]], F32, kind="ExternalInput").ap()[:, 0:shp[0], :]
        outT = dt("outT", [D, L], F32, kind="ExternalOutput").ap()
        c.xres = dt("xres", [D, L], F32, kind="Internal").ap()
        c.act_d = dt("act_d", [DFF, L], BF16, kind="Internal").ap()
        c.P = {}
        for name, off, width, dt_ in P_SPECS:
            c.P[name] = dt("P_" + name, [width, L], dt_, kind="Internal").ap()
        c.yssd = dt("yssd", [SSD_INNER, L], BF16, kind="Internal").ap()
        c.ydsa = dt("ydsa", [D, L], BF16, kind="Internal").ap()
        c.ymem = dt("ymem", [D, L], BF16, kind="Internal").ap()
        c.pos = pos
        c.xc_d = dt("xc_d", [CONV_CH, L], BF16, kind="Internal").ap()
        c.acs_d = dt("acs_d", [NH_SSD, L], F32, kind="Internal").ap()
        c.V_l = None
        c.consts_d = consts

        c.R64 = c.sb([128, 32768], BF16, "R64")
        c.R44 = c.sb([128, 22528], BF16, "R44")
        c.hT = c.R64.t[:, :].rearrange("p (kc t) -> p kc t", t=L)
        c.WS = c.sb([128, 6 * 5632], BF16, "WS")
        c.wslots = [Buf(c.WS.t[:, i * 5632:(i + 1) * 5632], "wslot%d" % i) for i in range(6)]
        c.stg_f32 = [c.sb([128, TB], F32, "stgf%d" % i) for i in range(4)]
        c.stg_bf = [c.sb([128, TB], BF16, "stgb%d" % i) for i in range(3)]
        c.vecs = [c.sb([128, NV], F32, "vecs%d" % l) for l in range(DEPTH)]
        c.cst = c.sb([128, NCONST], F32, "cst")
        c.ones_bf = c.sb([128, 128], BF16, "ones_bf")
        c.ident_bf = c.sb([128, 128], BF16, "ident_bf")
        c.eps_col = c.sb([128, 1], F32, "eps_col")
        c.psb = [c.ps("ps%d" % i) for i in range(8)]

        for l in range(DEPTH):
            S.dma("sp", c.vecs[l].t[:, :], vecs[l], w=["vecs"])
        S.dma("sp", c.cst.t[:, :], consts, w=["cst"])
        S.op("dve", lambda e: e.tensor_copy(out=c.ones_bf.t[:, :], in_=c.cst.t[:, C_ONES:C_ONES + 128]), r=["cst"], w=["const"])
        S.op("dve", lambda e: e.tensor_copy(out=c.ident_bf.t[:, :], in_=c.cst.t[:, C_IDENT:C_IDENT + 128]), r=["cst"], w=["const"])
        S.op("dve", lambda e: e.memset(c.eps_col.t[:, :], EPS), w=["const"])
        sched_barrier(S)

        x_cur = xT
        for l in range(DEPTH):
            V = c.vecs[l].t
            if ("ffn1", l) in plan:
                phase_norm(c, x_cur, V[:, V_FFN1:V_FFN1 + 16], "h")
                phase_ffn(c, W["w_ffn1_in"][l], W["w_ffn1_out"][l], x_cur, c.xres)
                x_cur = c.xres
            if ("inproj", l) in plan:
                phase_norm(c, x_cur, V[:, V_MIX:V_MIX + 16], "h")
                phase_inproj(c, W["w_in"][l])
            if ("mem", l) in plan:
                phase_mem(c, memT, V[:, V_MEMN:V_MEMN + 16], W["w_mem_kv"][l])
            if ("dsa", l) in plan:
                phase_dsa(c, l)
            if ("ssd", l) in plan:
                phase_ssd(c, l, V)
            if ("merge", l) in plan:
                phase_merge(c, W["w_br_ssd"][l], W["w_br_dsa"][l], W["w_br_mem"][l], W["w_out"][l], x_cur, c.xres)
                x_cur = c.xres
            if ("ffn2", l) in plan:
                phase_norm(c, x_cur, V[:, V_FFN2:V_FFN2 + 16], "h")
                phase_ffn(c, W["w_ffn2_in"][l], W["w_ffn2_out"][l], x_cur, c.xres)
                x_cur = c.xres
        if "final" in plan:
            phase_norm(c, x_cur, c.vecs[0].t[:, V_FINAL:V_FINAL + 16], "out", outT)
        elif dbg is not None:
            if dbg == "xres":
                src, nrow, sdt = x_cur, D, F32
            else:
                src, nrow, sdt = dbg
            for kc in range((nrow + 127) // 128):
                nr = min(128, nrow - kc * 128)
                for tb in range(NTB):
                    st = c.stg_f32[(kc * NTB + tb) % 4]
                    if sdt == F32:
                        S.dma("sp", st.t[0:nr, :], src(c)[kc * 128:kc * 128 + nr, tb * TB:(tb + 1) * TB] if callable(src) else src[kc * 128:kc * 128 + nr, tb * TB:(tb + 1) * TB], w=[st.k])
                    else:
                        sb_ = c.stg_bf[(kc * NTB + tb) % 3]
                        S.dma("sp", sb_.t[0:nr, :], src(c)[kc * 128:kc * 128 + nr, tb * TB:(tb + 1) * TB], w=[sb_.k])
                        S.op("dve", lambda e, st=st, sb_=sb_, nr=nr: e.tensor_copy(out=st.t[0:nr, :], in_=sb_.t[0:nr, :]), r=[sb_.k], w=[st.k])
                    S.dma("sp", outT[kc * 128:kc * 128 + nr, tb * TB:(tb + 1) * TB], st.t[0:nr, :], r=[st.k])
        S.finish("sp")
        print("sched: inst=%d waits=%d cnt=%s" % (S.n_inst, S.n_wait, S.cnt))
    return nc


W_NAMES = ("w_ffn1_in", "w_ffn1_out", "w_ffn2_in", "w_ffn2_out", "w_in", "w_mem_kv", "w_br_ssd", "w_br_dsa", "w_br_mem", "w_out")


def core_inputs(inp, b, vec, cst):
    im = {"xT": np.ascontiguousarray(inp["x"][b].T), "memT": np.ascontiguousarray(inp["mem"][b].T),
          "pos": np.ascontiguousarray(np.broadcast_to(inp["positions"][b].reshape(1, L).astype(np.int32), (128, L))), "vecs": vec, "consts": cst,
          "consts2": make_consts2()}
    for n in W_NAMES:
        w = inp[n]
        p = np.empty((w.shape[0], w.shape[1] + 1, w.shape[2]), np.float32)
        p[:, :w.shape[1], :] = w
        p[:, w.shape[1], :] = float(b)
        im[n] = p
    return im


P_SPECS = [("z", O_Z, 4096, BF16), ("xbc", O_XBC, 6144, BF16), ("dt", O_DT, 64, F32), ("q", O_Q, 2048, BF16),
           ("k", O_K, 128, BF16), ("v", O_V, 128, BF16), ("qi", O_QI, 1024, BF16), ("ki", O_KI, 64, BF16),
           ("wi", O_WI, 16, F32), ("qm", O_QM, 2048, BF16), ("g", O_G, 6144, BF16)]


def phase_inproj(c, w_in):
    S = c.S
    hT = c.hT
    jobs = []
    meta = []
    for name, off, width, dt_ in P_SPECS:
        for f0 in range(0, width, 128):
            nco = min(128, width - f0)
            jobs.append([(w_in, off + f0, nco, KC_D)])
            meta.append((name, f0, nco, dt_))
    cnt = [0]

    def epi(ji, tb, pset):
        name, f0, nco, dt_ = meta[ji]
        i = cnt[0]
        cnt[0] += 1
        if dt_ == F32:
            st = c.stg_f32[i % 4]
        else:
            st = c.stg_bf[i % 3]
        if i % 2 == 0:
            S.op("act", lambda e: e.copy(out=st.t[0:nco, :], in_=pset[0].t[0:nco, :]), r=[pset[0].k], w=[st.k])
        else:
            S.op("dve", lambda e: e.tensor_copy(out=st.t[0:nco, :], in_=pset[0].t[0:nco, :]), r=[pset[0].k], w=[st.k])
        S.dma("sp", c.P[name][f0:f0 + nco, tb * TB:(tb + 1) * TB], st.t[0:nco, :], r=[st.k], w=[("P", name, f0, tb)])

    stream_linear(c, jobs, lambda gi, kc, tb: hT[:, kc, tb * TB:(tb + 1) * TB],
                  lambda gi, tb: ["hT%d" % tb], NTB, epi, c.wslots, [[c.psb[i]] for i in range(8)])
    sched_barrier(S)


def phase_mem(c, memT, gcol, w_kv):
    S = c.S
    r = c.R44.t
    mem_in = r[:, 0:8192].bitcast(F32).rearrange("p (kc t) -> p kc t", t=MEM)
    mnT = r[:, 8192:12288].rearrange("p (kc t) -> p kc t", t=MEM)
    kmT = r[:, 12288:16384].rearrange("p (kc t) -> p kc t", t=MEM)
    vm = r[:, 16384:20480].rearrange("p (mt d) -> p mt d", d=D)
    memv = memT.rearrange("(kc p) t -> p kc t", p=128)
    S.dma("sp", mem_in, memv, w=["mem_in"])
    pb = c.psb[0]
    for kc in range(KC_D):
        sq = c.stg_bf[kc % 3]
        S.op("act", lambda e, sq=sq, kc=kc: e.activation(out=sq.t[:, 0:MEM], in_=mem_in[:, kc, :], func=AF.Square),
             r=["mem_in"], w=[sq.k])
        S.op("pe", lambda e, sq=sq, kc=kc: e.matmul(pb.t[:, 0:MEM], c.ones_bf.t[:, :], sq.t[:, 0:MEM],
                                                    start=(kc == 0), stop=(kc == KC_D - 1)), r=[sq.k], w=[pb.k])
    rt, rs = c.stg_f32[0], c.stg_f32[1]
    S.op("act", lambda e: e.activation(out=rt.t[:, 0:MEM], in_=pb.t[:, 0:MEM], func=AF.Sqrt, scale=1.0 / D,
                                       bias=c.eps_col.t[:, 0:1]), r=[pb.k], w=[rt.k])
    S.op("dve", lambda e: e.reciprocal(out=rs.t[:, 0:MEM], in_=rt.t[:, 0:MEM]), r=[rt.k], w=[rs.k])
    for kc in range(KC_D):
        S.op("dve", lambda e, kc=kc: e.scalar_tensor_tensor(out=mnT[:, kc, :], in0=mem_in[:, kc, :],
                                                             scalar=gcol[:, kc:kc + 1], in1=rs.t[:, 0:MEM],
                                                             op0=ALU.mult, op1=ALU.mult),
             r=["mem_in", rs.k], w=["mnT"])
    jobs = [[(w_kv, fc * 128, 128, KC_D)] for fc in range(16)]

    def epi_k(ji, tb, pset):
        S.op("act", lambda e: e.copy(out=kmT[:, ji, :], in_=pset[0].t[:, 0:MEM]), r=[pset[0].k], w=["kmT"])
    stream_linear(c, jobs, lambda gi, kc, tb: mnT[:, kc, :], lambda gi, tb: ["mnT"], 1, epi_k, c.wslots,
                  [[c.psb[i]] for i in range(1, 5)], tbw=MEM)
    wv = w_kv.rearrange("(kc p) f -> p kc f", p=128)
    for cb in range(8):
        ws = c.wslots[cb % 6]
        dst = ws.t[:, 0:4096].rearrange("p (kc f) -> p kc f", f=256)
        S.dma("pool", dst[:, 0:8, :], wv[:, 0:8, D + cb * 256:D + (cb + 1) * 256], w=[ws.k + "_a"])
        S.dma("pool", dst[:, 8:16, :], wv[:, 8:16, D + cb * 256:D + (cb + 1) * 256], w=[ws.k + "_b"])
        for mt in range(2):
            pbv = c.psb[5 + mt]

            def mm(e, dst=dst, pbv=pbv, mt=mt):
                ins = None
                for kc in range(KC_D):
                    ins = e.matmul(pbv.t[:, 0:256], mnT[:, kc, mt * 128:(mt + 1) * 128], dst[:, kc, :],
                                   start=(kc == 0), stop=(kc == KC_D - 1))
                return ins
            S.op("pe", mm, r=[ws.k + "_a", ws.k + "_b", "mnT"], w=[pbv.k])
            S.op("dve", lambda e, pbv=pbv, mt=mt, cb=cb: e.tensor_copy(out=vm[:, mt, cb * 256:(cb + 1) * 256], in_=pbv.t[:, 0:256]),
                 r=[pbv.k], w=["vm"])
    sched_barrier(S)
    qmv = c.P["qm"].rearrange("(kc p) t -> p kc t", p=128)
    qblk = c.R64.t[:, 0:16 * TB].rearrange("p (kc t) -> p kc t", t=TB)
    pT = [c.R64.t[:, 16 * TB + i * TB:16 * TB + (i + 1) * TB] for i in range(2)]
    scale = 512.0 ** -0.5
    for tb in range(NTB):
        t0 = tb * TB
        S.dma("sp", qblk, qmv[:, :, t0:t0 + TB], w=["qblk"])
        for hd in range(4):
            for mt in range(2):
                pl = c.psb[mt]

                def mmq(e, pl=pl, hd=hd, mt=mt):
                    ins = None
                    for j in range(4):
                        ins = e.matmul(pl.t[:, :], kmT[:, hd * 4 + j, mt * 128:(mt + 1) * 128], qblk[:, hd * 4 + j, :],
                                       start=(j == 0), stop=(j == 3))
                    return ins
                S.op("pe", mmq, r=["kmT", "qblk"], w=[pl.k])
                S.op("act", lambda e, pl=pl, mt=mt: e.activation(out=pT[mt], in_=pl.t[:, :], func=AF.Exp, scale=scale),
                     r=[pl.k], w=["pT%d" % mt])
            prs = c.psb[2]

            def mmrs(e):
                e.matmul(prs.t[:, :], c.ones_bf.t[:, :], pT[0], start=True, stop=False)
                return e.matmul(prs.t[:, :], c.ones_bf.t[:, :], pT[1], start=False, stop=True)
            S.op("pe", mmrs, r=["pT0", "pT1"], w=[prs.k])
            rinv = c.stg_f32[0]
            S.op("dve", lambda e: e.reciprocal(out=rinv.t[:, :], in_=prs.t[:, :]), r=[prs.k], w=[rinv.k])
            for j in range(4):
                po = c.psb[3 + j]
                ch = hd * 4 + j

                def mmo(e, po=po, ch=ch):
                    e.matmul(po.t[:, :], vm[:, 0, ch * 128:(ch + 1) * 128], pT[0], start=True, stop=False)
                    return e.matmul(po.t[:, :], vm[:, 1, ch * 128:(ch + 1) * 128], pT[1], start=False, stop=True)
                S.op("pe", mmo, r=["pT0", "pT1", "vm"], w=[po.k])
                st = c.stg_bf[j % 3]
                S.op("dve", lambda e, po=po, st=st: e.tensor_tensor(out=st.t[:, :], in0=po.t[:, :], in1=rinv.t[:, :], op=ALU.mult),
                     r=[po.k, rinv.k], w=[st.k])
                S.dma("sp", c.ymem[ch * 128:(ch + 1) * 128, t0:t0 + TB], st.t[:, :], r=[st.k], w=[("ymem", ch, tb)])
    sched_barrier(S)


def phase_merge(c, w_ssd, w_dsa, w_memw, w_out, x_src, x_dst):
    S = c.S
    mT = c.R64.t[:, :].rearrange("p (kc t) -> p kc t", t=L)
    ysv = c.yssd.rearrange("(kc p) t -> p kc t", p=128)
    ydv = c.ydsa.rearrange("(kc p) t -> p kc t", p=128)
    ymv = c.ymem.rearrange("(kc p) t -> p kc t", p=128)
    gv = c.P["g"]
    TBM = 256
    ys = c.R44.t[:, 0:32 * TBM].rearrange("p (kc t) -> p kc t", t=TBM)
    yd = c.R44.t[:, 32 * TBM:48 * TBM].rearrange("p (kc t) -> p kc t", t=TBM)
    ym = c.R44.t[:, 48 * TBM:64 * TBM].rearrange("p (kc t) -> p kc t", t=TBM)
    gt = [c.R44.t[:, 64 * TBM + i * TBM:64 * TBM + (i + 1) * TBM] for i in range(6)]
    for tb in range(L // TBM):
        t0 = tb * TBM
        S.dma("sp", ys[:, 0:16, :], ysv[:, 0:16, t0:t0 + TBM], w=["ys0"])
        S.dma("sp", ys[:, 16:32, :], ysv[:, 16:32, t0:t0 + TBM], w=["ys1"])
        S.dma("sp", yd, ydv[:, :, t0:t0 + TBM], w=["yd"])
        S.dma("sp", ym, ymv[:, :, t0:t0 + TBM], w=["ym"])
        jobs = [[(w_ssd, dc * 128, 128, 32), (w_dsa, dc * 128, 128, 16), (w_memw, dc * 128, 128, 16)] for dc in range(KC_D)]
        cnt = [0]

        def rhs_fn(gi, kc, tb_):
            return (ys, yd, ym)[gi][:, kc, :]

        def rhs_keys(gi, tb_):
            return (["ys0", "ys1"], ["yd"], ["ym"])[gi]

        def epi(ji, tb_, pset, t0=t0):
            i = cnt[0]
            cnt[0] += 1
            acc = c.stg_f32[i % 2]
            for bi in range(3):
                g = gt[(i % 2) * 3 + bi]
                gk = "gt%d" % ((i % 2) * 3 + bi)
                S.dma("sp", g, gv[bi * D + ji * 128:bi * D + (ji + 1) * 128, t0:t0 + TBM], w=[gk])
                sg = c.stg_f32[2 + (bi % 2)]
                S.op("act", lambda e, g=g, sg=sg: e.activation(out=sg.t[:, 0:TBM], in_=g, func=AF.Sigmoid), r=[gk], w=[sg.k])
                if bi == 0:
                    S.op("dve", lambda e, sg=sg: e.tensor_tensor(out=acc.t[:, 0:TBM], in0=pset[0].t[:, 0:TBM], in1=sg.t[:, 0:TBM], op=ALU.mult),
                         r=[pset[0].k, sg.k], w=[acc.k])
                else:
                    S.op("dve", lambda e, sg=sg, bi=bi: e.tensor_tensor(out=sg.t[:, 0:TBM], in0=pset[bi].t[:, 0:TBM], in1=sg.t[:, 0:TBM], op=ALU.mult),
                         r=[pset[bi].k, sg.k], w=[sg.k])
                    if bi == 1:
                        S.op("dve", lambda e, sg=sg: e.tensor_tensor(out=acc.t[:, 0:TBM], in0=acc.t[:, 0:TBM], in1=sg.t[:, 0:TBM], op=ALU.add),
                             r=[acc.k, sg.k], w=[acc.k])
                    else:
                        S.op("dve", lambda e, sg=sg: e.tensor_tensor(out=mT[:, ji, t0:t0 + TBM], in0=acc.t[:, 0:TBM], in1=sg.t[:, 0:TBM], op=ALU.add),
                             r=[acc.k, sg.k], w=["mT%d" % (t0 // TB)])

        stream_linear(c, jobs, rhs_fn, rhs_keys, 1, epi, c.wslots,
                      [[c.psb[0], c.psb[1], c.psb[2]], [c.psb[3], c.psb[4], c.psb[5]]], tbw=TBM)
    sched_barrier(S)
    jobs = [[(w_out, dc * 128, 128, KC_D)] for dc in range(KC_D)]
    cnt2 = [0]

    def epi_o(ji, tb, pset):
        i = cnt2[0]
        cnt2[0] += 1
        t0 = tb * TB
        xt = c.stg_f32[i % 2]
        S.dma("sp", xt.t[:, :], x_src[ji * 128:(ji + 1) * 128, t0:t0 + TB], w=[xt.k])
        xo = c.stg_f32[2 + i % 2]
        S.op("dve", lambda e: e.tensor_tensor(out=xo.t[:, :], in0=pset[0].t[:, :], in1=xt.t[:, :], op=ALU.add),
             r=[pset[0].k, xt.k], w=[xo.k])
        S.dma("sp", x_dst[ji * 128:(ji + 1) * 128, t0:t0 + TB], xo.t[:, :], r=[xo.k])
    stream_linear(c, jobs, lambda gi, kc, tb: mT[:, kc, tb * TB:(tb + 1) * TB], lambda gi, tb: ["mT%d" % tb], NTB, epi_o,
                  c.wslots, [[c.psb[i]] for i in range(8)])
    sched_barrier(S)


C_INVA, C_INVI, C_SGNA, C_SGNI = 256, 257, 258, 259
C_PA, C_PI, C_CAUS = 264, 392, 520
C_U = 648
NCONST2 = 776
TWO_PI = 6.283185307179586
CW1 = 6.28125
CW2 = TWO_PI - CW1
PI_F = 3.1415925


def make_consts2():
    cst = np.zeros((128, NCONST2), np.float32)
    cst[:, 0:NCONST] = make_consts()
    th = np.float32(500000.0)
    inv_a = np.power(th, -(np.arange(0, 32, 2, dtype=np.float32) / np.float32(32))).astype(np.float32)
    inv_i = np.power(th, -(np.arange(0, 16, 2, dtype=np.float32) / np.float32(16))).astype(np.float32)
    for d_ in range(128):
        if d_ < 32:
            cst[d_, C_INVA] = inv_a[d_ % 16]
            cst[d_, C_SGNA] = -1.0 if d_ < 16 else 1.0
        e = d_ % 64
        if e < 16:
            cst[d_, C_INVI] = inv_i[e % 8]
            cst[d_, C_SGNI] = -1.0 if e < 8 else 1.0
    for dp in range(32):
        dsrc = dp + 16 if dp < 16 else dp - 16
        cst[dsrc, C_PA + dp] = 1.0
    for blk in (0, 64):
        for ep in range(16):
            esrc = ep + 8 if ep < 8 else ep - 8
            cst[blk + esrc, C_PI + blk + ep] = 1.0
    q = np.arange(128)[:, None]
    k = np.arange(128)[None, :]
    cst[:, C_CAUS:C_CAUS + 128] = np.where(k <= q, 0.0, -1e30).astype(np.float32)
    cst[:, C_U:C_U + 128] = (q <= k).astype(np.float32)
    return cst


def _sin_table(c, out_bf, ang, sgncol, tmp_u, tmp_n, tmp_r, tag):
    S = c.S
    ni = tmp_n.bitcast(I32)
    S.op("dve", lambda e: e.tensor_scalar(out=tmp_u, in0=ang, scalar1=1.0 / TWO_PI, scalar2=None, op0=ALU.mult), r=[tag + "ang"], w=[tag + "u"])
    S.op("dve", lambda e: e.tensor_copy(out=ni, in_=tmp_u), r=[tag + "u"], w=[tag + "n"])
    S.op("dve", lambda e: e.tensor_copy(out=tmp_u, in_=ni), r=[tag + "n"], w=[tag + "u"])
    S.op("dve", lambda e: e.scalar_tensor_tensor(out=tmp_r, in0=tmp_u, scalar=-CW1, in1=ang, op0=ALU.mult, op1=ALU.add),
         r=[tag + "u", tag + "ang"], w=[tag + "r"])
    S.op("dve", lambda e: e.scalar_tensor_tensor(out=tmp_r, in0=tmp_u, scalar=-CW2, in1=tmp_r, op0=ALU.mult, op1=ALU.add),
         r=[tag + "u", tag + "r"], w=[tag + "r"])
    S.op("dve", lambda e: e.tensor_scalar(out=tmp_u, in0=tmp_r, scalar1=PI_F, scalar2=None, op0=ALU.is_gt), r=[tag + "r"], w=[tag + "u"])
    S.op("dve", lambda e: e.scalar_tensor_tensor(out=tmp_r, in0=tmp_u, scalar=-TWO_PI, in1=tmp_r, op0=ALU.mult, op1=ALU.add),
         r=[tag + "u", tag + "r"], w=[tag + "r"])
    S.op("dve", lambda e: e.tensor_scalar(out=tmp_u, in0=tmp_r, scalar1=-PI_F, scalar2=None, op0=ALU.is_lt), r=[tag + "r"], w=[tag + "u"])
    S.op("dve", lambda e: e.scalar_tensor_tensor(out=tmp_r, in0=tmp_u, scalar=TWO_PI, in1=tmp_r, op0=ALU.mult, op1=ALU.add),
         r=[tag + "u", tag + "r"], w=[tag + "r"])
    S.op("dve", lambda e: e.tensor_scalar(out=tmp_r, in0=tmp_r, scalar1=PI_F, scalar2=-PI_F, op0=ALU.min, op1=ALU.max), r=[tag + "r"], w=[tag + "r"])
    if sgncol is None:
        S.op("act", lambda e: e.activation(out=out_bf, in_=tmp_r, func=AF.Sin), r=[tag + "r"], w=["tables"])
    else:
        S.op("act", lambda e: e.activation(out=out_bf, in_=tmp_r, func=AF.Sin, scale=sgncol), r=[tag + "r"], w=["tables"])


def _rope(c, X, ncols, cosT, sinT, PT, xkeys, pbank, i):
    S = c.S
    S.op("pe", lambda e: e.matmul(pbank.t[:, 0:ncols], PT, X, start=True, stop=True), r=list(xkeys) + ["dsaconst"], w=[pbank.k])
    t1 = c.stg_f32[i % 2]
    t2 = c.stg_f32[2 + i % 2]
    S.op("dve", lambda e: e.tensor_tensor(out=t1.t[:, 0:ncols], in0=X, in1=cosT, op=ALU.mult), r=list(xkeys) + ["tables"], w=[t1.k])
    S.op("dve", lambda e: e.tensor_tensor(out=t2.t[:, 0:ncols], in0=pbank.t[:, 0:ncols], in1=sinT, op=ALU.mult), r=[pbank.k, "tables"], w=[t2.k])
    S.op("pool", lambda e: e.tensor_tensor(out=X, in0=t1.t[:, 0:ncols], in1=t2.t[:, 0:ncols], op=ALU.add), r=[t1.k, t2.k], w=list(xkeys))


def phase_dsa(c, l):
    S = c.S
    R6, R4 = c.R64.t, c.R44.t
    qblk = R6[:, 0:8192].rearrange("p (h t) -> p h t", t=TB)
    selT = R6[:, 8192:16384].rearrange("p (k t) -> p k t", t=TB)
    qiblk = R6[:, 16384:20480].rearrange("p (h t) -> p h t", t=TB)
    cosA, sinA, cosI, sinI = (R6[:, 20480 + i * 2048:20480 + (i + 1) * 2048] for i in range(4))
    krT = R6[:, 28672:30720]
    kir2 = R6[:, 30720:32768]
    acc = R4[:, 0:4096].bitcast(F32)
    work = R4[:, 4096:8192].bitcast(F32)
    sel01 = R4[:, 8192:10240]
    v_tok = R4[:, 10240:12288].rearrange("p (k d) -> p k d", d=128)
    wi_tok = R4[:, 12288:12800].bitcast(F32).rearrange("p (q h) -> p q h", h=16)
    m8 = R4[:, 12800:12816].bitcast(F32)
    pTs = [R4[:, 13312 + i * 512:13312 + (i + 1) * 512] for i in range(6)]
    tmp3 = R4[:, 8192:12288].bitcast(F32)
    cst2 = c.wslots[0].t[:, 0:2 * NCONST2].bitcast(F32)
    PA_bf = c.wslots[1].t[:, 0:128]
    PI_bf = c.wslots[1].t[:, 128:256]
    posi = c.wslots[2].t[:, 0:4096].bitcast(I32)
    vT_sb = c.wslots[3].t[:, 0:2048]
    wiT_sb = c.wslots[4].t[:, 0:4096].bitcast(F32)
    ident_f = c.cst.t[:, C_IDENT:C_IDENT + 128]

    S.dma("sp", cst2, c.consts2_d, w=["cst2"])
    S.dma("sp", posi, c.pos, w=["posi"])
    S.op("dve", lambda e: e.tensor_copy(out=PA_bf, in_=cst2[:, C_PA:C_PA + 128]), r=["cst2"], w=["dsaconst"])
    S.op("dve", lambda e: e.tensor_copy(out=PI_bf, in_=cst2[:, C_PI:C_PI + 128]), r=["cst2"], w=["dsaconst"])
    S.op("dve", lambda e: e.tensor_copy(out=acc, in_=posi), r=["posi"], w=["posf"])
    for tname, invc, sgnc, cosT, sinT in (("A", C_INVA, C_SGNA, cosA, sinA), ("I", C_INVI, C_SGNI, cosI, sinI)):
        S.op("dve", lambda e, invc=invc: e.tensor_scalar(out=work, in0=acc, scalar1=cst2[:, invc:invc + 1], scalar2=None, op0=ALU.mult),
             r=["posf", "cst2"], w=[tname + "sang"])
        u_t = tmp3
        r_t = c.wslots[5].t[:, 0:4096].bitcast(F32)
        _sin_table(c, sinT, work, cst2[:, sgnc:sgnc + 1], u_t, u_t, r_t, tname + "s")
        S.op("dve", lambda e: e.tensor_scalar(out=work, in0=work, scalar1=1.5707963267948966, scalar2=None, op0=ALU.add),
             r=[tname + "sang", tname + "sr", tname + "su"], w=[tname + "cang"])
        _sin_table(c, cosT, work, None, u_t, u_t, r_t, tname + "c")
        sched_barrier(S)
    S.dma("sp", krT, c.P["k"], w=["krT"])
    S.dma("sp", kir2[0:64, :], c.P["ki"], w=["kir2"])
    S.dma("sp", kir2[64:128, :], c.P["ki"], w=["kir2b"])
    S.dma("sp", vT_sb, c.P["v"], w=["vT"])
    S.dma("sp", wiT_sb[0:16, :], c.P["wi"], w=["wiT"])
    for tb in range(NTB):
        sl = slice(tb * TB, (tb + 1) * TB)
        _rope(c, krT[:, sl], TB, cosA[:, sl], sinA[:, sl], PA_bf, ["krT"], c.psb[tb % 4], tb)
    for tb in range(NTB):
        sl = slice(tb * TB, (tb + 1) * TB)
        _rope(c, kir2[:, sl], TB, cosI[:, sl], sinI[:, sl], PI_bf, ["kir2", "kir2b"], c.psb[4 + tb % 4], tb)
    for half in range(2):
        pb = c.psb[half]
        pbv = pb.t[:, :].bitcast(BF16)

        def tr(e, half=half, pbv=pbv):
            ins = None
            for j in range(8):
                kt = half * 8 + j
                ins = e.transpose(pbv[:, j * 128:(j + 1) * 128], vT_sb[:, kt * 128:(kt + 1) * 128], c.ident_bf.t[:, :])
            return ins
        S.op("pe", tr, r=["vT", "const"], w=[pb.k])
        S.op("act", lambda e, half=half, pbv=pbv: e.copy(out=v_tok[:, half * 8:(half + 1) * 8, :],
                                                       in_=pbv[:, 0:1024].rearrange("p (k d) -> p k d", d=128)),
             r=[pb.k], w=["v_tok"])
    pbw = c.psb[2]

    def trw(e):
        ins = None
        for qt in range(16):
            ins = e.transpose(pbw.t[:, qt * 16:(qt + 1) * 16], wiT_sb[0:16, qt * 128:(qt + 1) * 128], ident_f[0:16, 0:16])
        return ins
    S.op("pe", trw, r=["wiT", "cst"], w=[pbw.k])
    S.op("act", lambda e: e.activation(out=wi_tok, in_=pbw.t[:, 0:256].rearrange("p (q h) -> p q h", h=16), func=AF.Copy,
                                       scale=0.25 * 0.125), r=[pbw.k], w=["wi_tok"])
    sched_barrier(S)

    WSv = c.WS.t
    accs = [acc, WSv[:, 11264:15360].bitcast(F32)]
    works = [work, WSv[:, 15360:19456].bitcast(F32)]
    m8s = [m8, WSv[:, 19456:19472].bitcast(F32)]
    qv = c.P["q"].rearrange("(h p) t -> p h t", p=128)
    qiv = c.P["qi"].rearrange("(h p) t -> p h t", p=128)
    scale = 128.0 ** -0.5
    NEG = -1e30
    for b in range(NTB):
        t0 = b * TB
        S.dma("sp", qblk[:, 0:8, :], qv[:, 0:8, t0:t0 + TB], w=["qblk0"])
        S.dma("sp", qblk[:, 8:16, :], qv[:, 8:16, t0:t0 + TB], w=["qblk1"])
        S.dma("sp", qiblk, qiv[:, :, t0:t0 + TB], w=["qiblk"])
        for h in range(16):
            _rope(c, qblk[:, h, :], TB, cosA[:, t0:t0 + TB], sinA[:, t0:t0 + TB], PA_bf, ["qblk%d" % (h // 8)], c.psb[h % 4], h)
        for ch in range(8):
            _rope(c, qiblk[:, ch, :], TB, cosI[:, t0:t0 + TB], sinI[:, t0:t0 + TB], PI_bf, ["qiblk"], c.psb[4 + ch % 4], ch)
        S.op("pool", lambda e, b=b: e.memset(selT[:, 4 * b:4 * b + 4, :], 0.0), w=["selT"])
        for pair in range(2):
            tiles = []
            for sub in range(2):
                qi_ = pair * 2 + sub
                qt = 4 * b + qi_
                nk = 128 * (qt + 1)
                nkb = (nk + TB - 1) // TB
                accX, workX, m8X, tg = accs[sub], works[sub], m8s[sub], "t%d" % sub
                cnt = 0
                for kb in range(nkb):
                    w_ = min(TB, nk - kb * TB)
                    for h in range(16):
                        chn, hf = h // 2, h % 2
                        pb = c.psb[cnt % 4]
                        tmp = c.stg_f32[cnt % 4]
                        cnt += 1
                        S.op("pe", lambda e, pb=pb, chn=chn, hf=hf, kb=kb, w_=w_, qi_=qi_: e.matmul(
                            pb.t[:, 0:w_], qiblk[hf * 64:(hf + 1) * 64, chn, qi_ * 128:(qi_ + 1) * 128],
                            kir2[hf * 64:(hf + 1) * 64, kb * TB:kb * TB + w_], start=True, stop=True),
                            r=["qiblk", "kir2", "kir2b"], w=[pb.k])
                        if h == 0:
                            S.op("dve", lambda e, pb=pb, kb=kb, w_=w_, qt=qt, h=h, accX=accX: e.tensor_scalar(
                                out=accX[:, kb * TB:kb * TB + w_], in0=pb.t[:, 0:w_], scalar1=0.0, scalar2=wi_tok[:, qt, h:h + 1],
                                op0=ALU.max, op1=ALU.mult), r=[pb.k, "wi_tok"], w=[tg + "acc%d" % kb])
                        else:
                            S.op("act", lambda e, pb=pb, tmp=tmp, w_=w_: e.activation(out=tmp.t[:, 0:w_], in_=pb.t[:, 0:w_], func=AF.Relu),
                                 r=[pb.k], w=[tmp.k])
                            S.op("dve", lambda e, tmp=tmp, kb=kb, w_=w_, qt=qt, h=h, accX=accX: e.scalar_tensor_tensor(
                                out=accX[:, kb * TB:kb * TB + w_], in0=tmp.t[:, 0:w_], scalar=wi_tok[:, qt, h:h + 1],
                                in1=accX[:, kb * TB:kb * TB + w_], op0=ALU.mult, op1=ALU.add),
                                r=[tmp.k, "wi_tok", tg + "acc%d" % kb], w=[tg + "acc%d" % kb])
                acck = [tg + "acc%d" % kb for kb in range(nkb)]
                S.op("pool", lambda e, nk=nk, accX=accX: e.tensor_tensor(out=accX[:, nk - 128:nk], in0=accX[:, nk - 128:nk],
                                                                        in1=cst2[:, C_CAUS:C_CAUS + 128], op=ALU.add),
                     r=acck + ["cst2"], w=acck)
                tiles.append((qi_, qt, nk, accX, workX, m8X, tg, acck))
            for rd in range(32):
                for (qi_, qt, nk, accX, workX, m8X, tg, acck) in tiles:
                    if qt < 2:
                        if rd == 0:
                            S.op("dve", lambda e, m8X=m8X: e.memset(m8X, -1e29), w=[tg + "m8"])
                        continue
                    src = accX if rd == 0 else workX
                    sk = acck if rd == 0 else [tg + "work"]
                    S.op("dve", lambda e, src=src, nk=nk, m8X=m8X: e.max(out=m8X, in_=src[:, 0:nk]), r=sk, w=[tg + "m8"])
                    if rd < 31:
                        S.op("dve", lambda e, src=src, nk=nk, m8X=m8X, workX=workX: e.match_replace(
                            out=workX[:, 0:nk], in_to_replace=m8X, in_values=src[:, 0:nk], imm_value=NEG),
                            r=sk + [tg + "m8"], w=[tg + "work"])
            for (qi_, qt, nk, accX, workX, m8X, tg, acck) in tiles:
                S.op("dve", lambda e, nk=nk, accX=accX, m8X=m8X: e.tensor_scalar(out=sel01[:, 0:nk], in0=accX[:, 0:nk], scalar1=m8X[:, 7:8],
                                                                               scalar2=None, op0=ALU.is_ge), r=acck + [tg + "m8"], w=["sel01"])
                for k0 in range(0, qt + 1, 8):
                    n_ = min(8, qt + 1 - k0)
                    pb = c.psb[4 + (k0 // 8) % 2]
                    pbv = pb.t[:, :].bitcast(BF16)

                    def trs(e, k0=k0, n_=n_, pbv=pbv):
                        ins = None
                        for j in range(n_):
                            ins = e.transpose(pbv[:, j * 128:(j + 1) * 128], sel01[:, (k0 + j) * 128:(k0 + j + 1) * 128], c.ident_bf.t[:, :])
                        return ins
                    S.op("pe", trs, r=["sel01", "const"], w=[pb.k])
                    S.op("act", lambda e, k0=k0, n_=n_, pbv=pbv, qi_=qi_: e.copy(
                        out=selT[:, k0:k0 + n_, qi_ * 128:(qi_ + 1) * 128],
                        in_=pbv[:, 0:n_ * 128].rearrange("p (k d) -> p k d", d=128)), r=[pb.k], w=["selT"])
        nkt = 4 * (b + 1)
        for h in range(16):
            po = c.psb[3 + h % 2]
            prs = c.psb[5 + h % 2]
            qk = "qblk%d" % (h // 8)
            pend = []

            def pv(kt, pTm, pk, po=po, prs=prs, nkt=nkt):
                S.op("pe", lambda e: e.matmul(po.t[:, :], v_tok[:, kt, :], pTm, start=(kt == 0), stop=(kt == nkt - 1)),
                     r=[pk, "v_tok"], w=[po.k])
                S.op("pe", lambda e: e.matmul(prs.t[:, :], c.ones_bf.t[:, :], pTm, start=(kt == 0), stop=(kt == nkt - 1)),
                     r=[pk, "const"], w=[prs.k])
            for kt in range(nkt):
                pl = c.psb[kt % 3]
                S.op("pe", lambda e, pl=pl, kt=kt, h=h: e.matmul(pl.t[:, :], krT[:, kt * 128:(kt + 1) * 128], qblk[:, h, :],
                                                                start=True, stop=True), r=["krT", qk], w=[pl.k])
                pT = pTs[kt % 3]
                pTm = pTs[3 + kt % 3]
                S.op("act", lambda e, pl=pl, pT=pT: e.activation(out=pT, in_=pl.t[:, :], func=AF.Exp, scale=scale),
                     r=[pl.k], w=["pT%d" % (kt % 3)])
                S.op("dve", lambda e, pT=pT, pTm=pTm, kt=kt: e.tensor_tensor(out=pTm, in0=pT, in1=selT[:, kt, :], op=ALU.mult),
                     r=["pT%d" % (kt % 3), "selT"], w=["pTm%d" % (kt % 3)])
                pend.append((kt, pTm, "pTm%d" % (kt % 3)))
                if len(pend) > 2:
                    pv(*pend.pop(0))
            while pend:
                pv(*pend.pop(0))
            rinv = c.stg_f32[h % 2]
            S.op("dve", lambda e, prs=prs, rinv=rinv: e.reciprocal(out=rinv.t[:, :], in_=prs.t[:, :]), r=[prs.k], w=[rinv.k])
            st = c.stg_bf[h % 3]
            S.op("dve", lambda e, po=po, rinv=rinv, st=st: e.tensor_tensor(out=st.t[:, :], in0=po.t[:, :], in1=rinv.t[:, :], op=ALU.mult),
                 r=[po.k, rinv.k], w=[st.k])
            S.dma("sp", c.ydsa[h * 128:(h + 1) * 128, t0:t0 + TB], st.t[:, :], r=[st.k], w=[("ydsa", h, b)])
    sched_barrier(S)


def phase_ssd(c, l, V):
    S = c.S
    R6, R4 = c.R64.t, c.R44.t
    ident_f = c.cst.t[:, C_IDENT:C_IDENT + 128]
    xraw = [R6[:, i * 2560:i * 2560 + 2051] for i in range(2)]
    xout = [R6[:, 8192 + i * 2048:8192 + (i + 1) * 2048] for i in range(2)]
    diag = [R4[:, i * 512:(i + 1) * 512].rearrange("p (k c) -> p k c", c=128) for i in range(2)]
    for i in range(2):
        S.op("dve", lambda e, i=i: e.memset(xraw[i][:, 0:3], 0.0), w=["xraw%d" % i])
    for cc in range(48):
        i = cc % 2
        xr, xo, dg = xraw[i], xout[i], diag[i]
        S.dma("sp", xr[:, 3:2051], c.P["xbc"][cc * 128:(cc + 1) * 128, :], w=["xraw%d" % i])
        for k in range(4):
            S.op("dve", lambda e, k=k, dg=dg, cc=cc: e.tensor_scalar(
                out=dg[:, k, :], in0=c.ident_bf.t[:, :], scalar1=V[:, V_CONVW + k * 48 + cc:V_CONVW + k * 48 + cc + 1],
                scalar2=None, op0=ALU.mult), r=["const", "vecs"], w=["diag%d" % i])
        for tb in range(NTB):
            pb = c.psb[(cc * NTB + tb) % 8]

            def mm(e, pb=pb, dg=dg, xr=xr, tb=tb):
                ins = None
                for k in range(4):
                    ins = e.matmul(pb.t[:, :], dg[:, k, :], xr[:, tb * TB + k:tb * TB + k + TB], start=(k == 0), stop=(k == 3))
                return ins
            S.op("pe", mm, r=["xraw%d" % i, "diag%d" % i], w=[pb.k])
            S.op("act", lambda e, pb=pb, xo=xo, tb=tb, cc=cc: e.activation(
                out=xo[:, tb * TB:(tb + 1) * TB], in_=pb.t[:, :], func=AF.Silu, bias=V[:, V_CONVB + cc:V_CONVB + cc + 1]),
                r=[pb.k, "vecs"], w=["xout%d" % i])
        S.dma("sp", c.xc_d[cc * 128:(cc + 1) * 128, :], xo, r=["xout%d" % i], w=[("xc", cc)])
    sched_barrier(S)

    AcsT = c.wslots[0].t[:, 0:4096].bitcast(F32)
    dtT = c.wslots[1].t[:, 0:4096].bitcast(F32)
    tA = c.wslots[2].t[:, 0:4096].bitcast(F32)
    tokw = c.wslots[3].t[:, 0:4096].bitcast(F32)
    dt_tok = tokw[:, 0:1024].rearrange("p (c h) -> p c h", h=64)
    Acs_tok = tokw[:, 1024:2048].rearrange("p (c h) -> p c h", h=64)
    tok2 = c.wslots[4].t[:, 0:4096].bitcast(F32)
    expA_tok = tok2[:, 0:1024].rearrange("p (c h) -> p c h", h=64)
    dA_tok = tok2[:, 1024:2048].rearrange("p (c h) -> p c h", h=64)
    acol = c.stg_f32[3].t[:, 0:1]
    cst2 = c.stg_f32[2].t[:, 0:128]
    S.dma("sp", cst2, c.consts2_d[:, C_U:C_U + 128], w=["Uf"])
    S.dma("sp", dtT[0:64, :], c.P["dt"], w=["dtT"])
    one_col = c.cst.t[0:64, C_ONES:C_ONES + 1]
    S.op("dve", lambda e: e.tensor_scalar(out=dtT[0:64, :], in0=dtT[0:64, :], scalar1=V[0:64, V_DTB:V_DTB + 1], scalar2=None, op0=ALU.add),
         r=["dtT", "vecs"], w=["dtT"])
    S.op("act", lambda e: e.activation(out=tA[0:64, :], in_=dtT[0:64, :], func=AF.Abs), r=["dtT"], w=["tA"])
    S.op("act", lambda e: e.activation(out=tA[0:64, :], in_=tA[0:64, :], func=AF.Exp, scale=-1.0), r=["tA"], w=["tA"])
    S.op("act", lambda e: e.activation(out=tA[0:64, :], in_=tA[0:64, :], func=AF.Ln, bias=one_col), r=["tA", "cst"], w=["tA"])
    S.op("dve", lambda e: e.scalar_tensor_tensor(out=dtT[0:64, :], in0=dtT[0:64, :], scalar=0.0, in1=tA[0:64, :], op0=ALU.max, op1=ALU.add),
         r=["dtT", "tA"], w=["dtT"])
    S.op("act", lambda e: e.activation(out=acol[0:64, :], in_=V[0:64, V_ALOG:V_ALOG + 1], func=AF.Exp), r=["vecs"], w=["acol"])
    S.op("dve", lambda e: e.tensor_scalar(out=tA[0:64, :], in0=dtT[0:64, :], scalar1=acol[0:64, :], scalar2=-1.0, op0=ALU.mult, op1=ALU.mult),
         r=["dtT", "acol"], w=["tA"])
    for src, dst, nm in ((dtT, dt_tok, "dt_tok"), (tA, dA_tok, "dA_tok")):
        for half in range(2):
            pb = c.psb[half]

            def tr(e, src=src, half=half, pb=pb):
                ins = None
                for j in range(8):
                    cch = half * 8 + j
                    ins = e.transpose(pb.t[:, j * 64:(j + 1) * 64], src[0:64, cch * 128:(cch + 1) * 128], ident_f[0:64, 0:64])
                return ins
            S.op("pe", tr, r=["dtT", "tA", "cst"], w=[pb.k])
            S.op("dve", lambda e, dst=dst, half=half, pb=pb: e.tensor_copy(
                out=dst[:, half * 8:(half + 1) * 8, :], in_=pb.t[:, :].rearrange("p (c h) -> p c h", h=64)), r=[pb.k], w=[nm])
    for half in range(2):
        pb = c.psb[2 + half]

        def cs(e, half=half, pb=pb):
            ins = None
            for j in range(8):
                cch = half * 8 + j
                ins = e.matmul(pb.t[:, j * 64:(j + 1) * 64], cst2, dA_tok[:, cch, :], start=True, stop=True)
            return ins
        S.op("pe", cs, r=["dA_tok", "Uf"], w=[pb.k])
        S.op("dve", lambda e, half=half, pb=pb: e.tensor_copy(out=Acs_tok[:, half * 8:(half + 1) * 8, :],
                                                           in_=pb.t[:, :].rearrange("p (c h) -> p c h", h=64)), r=[pb.k], w=["Acs_tok"])
    for q4 in range(4):
        pb = c.psb[4 + q4]

        def cs2(e, q4=q4, pb=pb):
            ins = None
            for j in range(4):
                cch = q4 * 4 + j
                ins = e.matmul(pb.t[0:64, j * 128:(j + 1) * 128], dA_tok[:, cch, :], cst2, start=True, stop=True)
            return ins
        S.op("pe", cs2, r=["dA_tok", "Uf"], w=[pb.k])
        S.op("dve", lambda e, q4=q4, pb=pb: e.tensor_copy(out=AcsT[0:64, q4 * TB:(q4 + 1) * TB], in_=pb.t[0:64, :]), r=[pb.k], w=["AcsT"])
    S.op("act", lambda e: e.activation(out=tok2[:, 0:1024], in_=tokw[:, 1024:2048], func=AF.Exp), r=["Acs_tok"], w=["expA_tok"])
    S.dma("sp", c.acs_d, AcsT[0:64, :], r=["AcsT"], w=["acs_d"])
    Dfull = c.wslots[1].t[:, 0:4096]
    sched_barrier(S)
    for h in range(NH_SSD):
        S.op("dve", lambda e, h=h: e.tensor_scalar(out=Dfull[:, h * 64:(h + 1) * 64], in0=c.cst.t[:, C_ONES:C_ONES + 64],
                                                   scalar1=V[:, V_DBC + h:V_DBC + h + 1], scalar2=None, op0=ALU.mult),
             r=["cst", "vecs"], w=["Dfull"])

    H = R6[:, 0:8192].bitcast(F32)
    Hbf = R6[:, 8192:12288]
    xs_tok = R6[:, 12288:16384]
    xw = R6[:, 16384:20480]
    y_tok = R6[:, 20480:24576]
    zc = R6[:, 24576:28672].rearrange("p (cc t) -> p cc t", t=128)
    xsT = R6[:, 28672:32768].rearrange("p (cc t) -> p cc t", t=128)
    ygf = R4[:, 0:8192].bitcast(F32).rearrange("p (cc t) -> p cc t", t=128)
    yout = R4[:, 8192:12288].rearrange("p (cc t) -> p cc t", t=128)
    BT = R4[:, 12288:13312].rearrange("p (g t) -> p g t", t=128)
    CT = R4[:, 13312:14336].rearrange("p (g t) -> p g t", t=128)
    B_tok = R4[:, 14336:15360].rearrange("p (g t) -> p g t", t=128)
    cbTm = R4[:, 15360:17408].bitcast(F32).rearrange("p (g t) -> p g t", t=128)
    Et = [R4[:, 17408 + i * 1024:17408 + (i + 1) * 1024].bitcast(F32) for i in range(3)]
    Mp = [R4[:, 20480 + i * 512:20480 + (i + 1) * 512] for i in range(3)]
    WSv = c.WS.t
    A_rows = WSv[:, 0:5632].bitcast(F32)
    xdt = WSv[:, 9728:13824]
    mask4 = WSv[:, 13824:14848].bitcast(F32)
    negones = WSv[:, 14848:15104].bitcast(F32)
    w64 = WSv[:, 15104:15232].bitcast(F32)
    dec = WSv[:, 15232:15360].bitcast(F32)
    Uf = cst2
    for j in range(4):
        S.op("dve", lambda e, j=j: e.tensor_scalar(out=mask4[:, j * 128:(j + 1) * 128], in0=Uf, scalar1=-1.0, scalar2=30000.0,
                                                   op0=ALU.add, op1=ALU.mult), r=["Uf"], w=["mask4"])
    S.op("dve", lambda e: e.memset(negones, -1.0), w=["negones"])
    pgroups = [(0, 0, 22), (32, 22, 44), (64, 44, 64)]
    batches = []
    for pbase, hs, he in pgroups:
        for h0 in range(hs, he, 4):
            batches.append((pbase, hs, h0, min(4, he - h0)))
    xcv = c.xc_d.rearrange("(cc p) t -> p cc t", p=128)
    zv = c.P["z"].rearrange("(cc p) t -> p cc t", p=128)
    yv = c.yssd.rearrange("(cc p) t -> p cc t", p=128)
    S.op("dve", lambda e: e.memset(H, 0.0), w=["H%d" % g for g in range(8)])
    S.op("dve", lambda e: e.memset(Hbf, 0.0), w=["Hbf%d" % g for g in range(8)])
    NCH = L // 128
    pending_D = [None]
    for ch in range(NCH):
        t0 = ch * 128
        last = (ch == NCH - 1)
        S.dma("sp", xsT[:, 0:16, :], xcv[:, 0:16, t0:t0 + 128], w=["xsT0"])
        S.dma("sp", xsT[:, 16:32, :], xcv[:, 16:32, t0:t0 + 128], w=["xsT1"])
        S.dma("sp", BT, xcv[:, 32:40, t0:t0 + 128], w=["BT"])
        S.dma("sp", CT, xcv[:, 40:48, t0:t0 + 128], w=["CT"])
        for pbase, hs, he in pgroups:
            S.dma("sp", A_rows[pbase:pbase + 1, 0:(he - hs) * 128].rearrange("o (h t) -> o h t", t=128),
                  c.acs_d[hs:he, t0:t0 + 128].rearrange("(o h) t -> o h t", o=1), r=["acs_d"], w=["A_rows"])
        pb7 = c.psb[7]
        pb7v = pb7.t[:, :].bitcast(BF16)
        for q8 in range(4):
            def trx(e, q8=q8):
                ins = None
                for j in range(8):
                    ins = e.transpose(pb7v[:, j * 128:(j + 1) * 128], xsT[:, q8 * 8 + j, :], c.ident_bf.t[:, :])
                return ins
            S.op("pe", trx, r=["xsT0", "xsT1", "const"], w=[pb7.k])
            S.op("act", lambda e, q8=q8: e.copy(out=xs_tok[:, q8 * 1024:(q8 + 1) * 1024], in_=pb7v[:, 0:1024]), r=[pb7.k], w=["xs_tok"])

        def trb(e):
            ins = None
            for g in range(8):
                ins = e.transpose(pb7v[:, g * 128:(g + 1) * 128], BT[:, g, :], c.ident_bf.t[:, :])
            return ins
        S.op("pe", trb, r=["BT", "const"], w=[pb7.k])
        S.op("act", lambda e: e.copy(out=B_tok, in_=pb7v[:, 0:1024].rearrange("p (g t) -> p g t", t=128)), r=[pb7.k], w=["B_tok"])
        S.op("pool", lambda e: e.tensor_tensor(out=xdt.rearrange("p (h d) -> p h d", d=64), in0=xs_tok.rearrange("p (h d) -> p h d", d=64),
                                               in1=dt_tok[:, ch, :].unsqueeze(2).to_broadcast([128, 64, 64]), op=ALU.mult),
             r=["xs_tok", "dt_tok"], w=["xdt"])
        for half in range(2):
            pb = c.psb[2 + half]

            def mcb(e, half=half, pb=pb):
                ins = None
                for j in range(4):
                    g = half * 4 + j
                    ins = e.matmul(pb.t[:, j * 128:(j + 1) * 128], BT[:, g, :], CT[:, g, :], start=True, stop=True)
                return ins
            S.op("pe", mcb, r=["BT", "CT"], w=[pb.k])
            S.op("dve", lambda e, half=half, pb=pb: e.tensor_tensor(
                out=cbTm[:, half * 4:(half + 1) * 4, :], in0=pb.t[:, :].rearrange("p (g t) -> p g t", t=128),
                in1=Uf.unsqueeze(1).to_broadcast([128, 4, 128]), op=ALU.mult), r=[pb.k, "Uf"], w=["cbTm"])
        if pending_D[0] is not None:
            pending_D[0]()
            pending_D[0] = None
        S.dma("sp", zc, zv[:, :, t0:t0 + 128], w=["zc"])
        S.op("act", lambda e: e.activation(out=zc, in_=zc, func=AF.Silu), r=["zc"], w=["zc"])
        def st1(bi):
            pbase, hs, h0, nh = batches[bi]
            pT1 = c.psb[bi % 2]

            def mseg(e):
                e.matmul(pT1.t[:, 0:nh * 128], c.cst.t[pbase:pbase + 1, C_ONES:C_ONES + 128],
                         A_rows[pbase:pbase + 1, (h0 - hs) * 128:(h0 - hs + nh) * 128], start=True, stop=False)
                ins = None
                for i in range(nh):
                    ins = e.matmul(pT1.t[:, i * 128:(i + 1) * 128], A_rows[pbase:pbase + 1, (h0 - hs + i) * 128:(h0 - hs + i + 1) * 128],
                                   negones[pbase:pbase + 1, :], start=False, stop=(i == nh - 1))
                return ins
            S.op("pe", mseg, r=["A_rows", "cst", "negones"], w=[pT1.k])

        def st2(bi):
            pbase, hs, h0, nh = batches[bi]
            pT1 = c.psb[bi % 2]
            E = Et[bi % 3]
            ek = "E%d" % (bi % 3)
            S.op("dve", lambda e: e.tensor_scalar(out=E[:, 0:nh * 128], in0=pT1.t[:, 0:nh * 128], scalar1=0.0, scalar2=None, op0=ALU.min),
                 r=[pT1.k], w=[ek])
            S.op("act", lambda e: e.activation(out=E[:, 0:nh * 128], in_=E[:, 0:nh * 128], func=AF.Exp), r=[ek], w=[ek])
            if not last:
                S.op("pool", lambda e: e.tensor_copy(
                    out=w64[:, h0:h0 + nh].rearrange("p (h o) -> p h o", o=1),
                    in_=E[:, 0:nh * 128].rearrange("p (h t) -> p h t", t=128)[:, :, 127:128]), r=[ek], w=["w64"])

        def st3(bi):
            pbase, hs, h0, nh = batches[bi]
            E, M = Et[bi % 3], Mp[bi % 3]
            ek, mk = "E%d" % (bi % 3), "Mp%d" % (bi % 3)
            i = 0
            while i < nh:
                g = (h0 + i) // 8
                j = i
                while j < nh and (h0 + j) // 8 == g:
                    j += 1
                n_ = j - i
                S.op("dve", lambda e, i=i, n_=n_, g=g: e.tensor_tensor(
                    out=M[:, i * 128:(i + n_) * 128].rearrange("p (h t) -> p h t", t=128),
                    in0=E[:, i * 128:(i + n_) * 128].rearrange("p (h t) -> p h t", t=128),
                    in1=cbTm[:, g:g + 1, :].to_broadcast([128, n_, 128]), op=ALU.mult), r=[ek, "cbTm"], w=[mk])
                i = j
            for i in range(nh):
                h = h0 + i
                g = h // 8
                po1 = c.psb[4]
                S.op("pe", lambda e, i=i, h=h: e.matmul(po1.t[:, (h % 8) * 64:(h % 8 + 1) * 64], M[:, i * 128:(i + 1) * 128],
                                                       xdt[:, h * 64:(h + 1) * 64], start=True, stop=True),
                     r=[mk, "xdt"], w=[po1.k + "_h%d" % (h % 8)])
                if h % 8 == 7:
                    gs = slice(g * 512, (g + 1) * 512)
                    po1k = [po1.k + "_h%d" % r_ for r_ in range(8)]
                    po2 = c.psb[5]
                    S.op("pe", lambda e, g=g, gs=gs: e.matmul(po2.t[:, :], CT[:, g, :], Hbf[:, gs], start=True, stop=True),
                         r=["CT", "Hbf%d" % g], w=[po2.k])
                    xd = c.stg_f32[0]
                    S.op("pool", lambda e, gs=gs: e.tensor_tensor(out=xd.t[:, :], in0=xs_tok[:, gs], in1=Dfull[:, gs], op=ALU.mult),
                         r=["xs_tok", "Dfull"], w=[xd.k])
                    sb1 = c.stg_f32[1]
                    S.op("dve", lambda e: e.tensor_tensor(out=sb1.t[:, :], in0=po1.t[:, :], in1=xd.t[:, :], op=ALU.add),
                         r=po1k + [xd.k], w=[sb1.k] + po1k)
                    S.op("dve", lambda e, gs=gs, g=g: e.tensor_tensor(
                        out=y_tok[:, gs].rearrange("p (h d) -> p h d", d=64), in0=po2.t[:, :].rearrange("p (h d) -> p h d", d=64),
                        in1=expA_tok[:, ch, g * 8:(g + 1) * 8].unsqueeze(2).to_broadcast([128, 8, 64]), op=ALU.mult),
                        r=[po2.k, "expA_tok"], w=["y_tok%d" % g])
                    S.op("dve", lambda e, gs=gs: e.tensor_tensor(out=y_tok[:, gs], in0=y_tok[:, gs], in1=sb1.t[:, :], op=ALU.add),
                         r=["y_tok%d" % g, sb1.k], w=["y_tok%d" % g])
                    if not last:
                        S.op("pool", lambda e, gs=gs, g=g: e.tensor_tensor(
                            out=xw[:, gs].rearrange("p (h d) -> p h d", d=64), in0=xdt[:, gs].rearrange("p (h d) -> p h d", d=64),
                            in1=w64[:, g * 8:(g + 1) * 8].unsqueeze(2).to_broadcast([128, 8, 64]), op=ALU.mult),
                            r=["xdt", "w64"], w=["xw%d" % g])
                        pS = c.psb[6]
                        S.op("pe", lambda e, g=g, gs=gs: e.matmul(pS.t[:, :], B_tok[:, g, :], xw[:, gs], start=True, stop=True),
                             r=["B_tok", "xw%d" % g], w=[pS.k])
                        S.op("pool", lambda e, g=g: e.tensor_tensor(out=dec[:, g * 8:(g + 1) * 8], in0=w64[:, g * 8:(g + 1) * 8],
                                                                    in1=expA_tok[:, ch, g * 8:(g + 1) * 8], op=ALU.mult),
                             r=["w64", "expA_tok"], w=["dec"])
                        S.op("pool", lambda e, gs=gs, g=g: e.tensor_tensor(
                            out=H[:, gs].rearrange("p (h d) -> p h d", d=64), in0=H[:, gs].rearrange("p (h d) -> p h d", d=64),
                            in1=dec[:, g * 8:(g + 1) * 8].unsqueeze(2).to_broadcast([128, 8, 64]), op=ALU.mult),
                            r=["dec", "H%d" % g], w=["H%d" % g])
                        S.op("dve", lambda e, gs=gs: e.tensor_tensor(out=H[:, gs], in0=H[:, gs], in1=pS.t[:, :], op=ALU.add),
                             r=[pS.k, "H%d" % g], w=["H%d" % g])
                        S.op("act", lambda e, gs=gs: e.copy(out=Hbf[:, gs], in_=H[:, gs]), r=["H%d" % g], w=["Hbf%d" % g])

        nb_ = len(batches)
        for step in range(nb_ + 2):
            if step < nb_:
                st1(step)
            if 0 <= step - 1 < nb_:
                st2(step - 1)
            if 0 <= step - 2 < nb_:
                st3(step - 2)
        def stage_D(ch=ch, t0=t0):
            sq = xw.rearrange("p (cc t) -> p cc t", t=128)
            ytk = ["y_tok%d" % g for g in range(8)]
            for q8 in range(4):
                def try_(e, q8=q8):
                    ins = None
                    for j in range(8):
                        cc = q8 * 8 + j
                        ins = e.transpose(pb7v[:, j * 128:(j + 1) * 128], y_tok[:, cc * 128:(cc + 1) * 128], c.ident_bf.t[:, :])
                    return ins
                S.op("pe", try_, r=ytk + ["const"], w=[pb7.k])
                S.op("dve", lambda e, q8=q8: e.tensor_tensor(out=ygf[:, q8 * 8:(q8 + 1) * 8, :],
                                                           in0=pb7v[:, 0:1024].rearrange("p (cc t) -> p cc t", t=128),
                                                           in1=zc[:, q8 * 8:(q8 + 1) * 8, :], op=ALU.mult), r=[pb7.k, "zc"], w=["ygf"])
            xwk = ["xw%d" % g for g in range(8)]
            S.op("act", lambda e: e.activation(out=sq, in_=ygf, func=AF.Square), r=["ygf"], w=xwk)
            pn = c.psb[6]

            def mmn(e):
                ins = None
                for cc in range(32):
                    ins = e.matmul(pn.t[:, 0:128], c.ones_bf.t[:, :], sq[:, cc, :], start=(cc == 0), stop=(cc == 31))
                return ins
            S.op("pe", mmn, r=xwk + ["const"], w=[pn.k])
            rt = c.stg_f32[2].t[:, 128:256]
            rs = c.stg_f32[2].t[:, 256:384]
            S.op("act", lambda e: e.activation(out=rt, in_=pn.t[:, 0:128], func=AF.Sqrt, scale=1.0 / SSD_INNER, bias=c.eps_col.t[:, 0:1]),
                 r=[pn.k], w=["rt"])
            S.op("dve", lambda e: e.reciprocal(out=rs, in_=rt), r=["rt"], w=["rs"])
            for cc in range(32):
                S.op("dve", lambda e, cc=cc: e.scalar_tensor_tensor(out=yout[:, cc, :], in0=ygf[:, cc, :], scalar=V[:, V_SSDN + cc:V_SSDN + cc + 1],
                                                                     in1=rs, op0=ALU.mult, op1=ALU.mult), r=["ygf", "rs", "vecs"], w=["yout"])
            S.dma("sp", yv[:, 0:16, t0:t0 + 128], yout[:, 0:16, :], r=["yout"], w=[("yssd", ch, 0)])
            S.dma("sp", yv[:, 16:32, t0:t0 + 128], yout[:, 16:32, :], r=["yout"], w=[("yssd", ch, 1)])

        pending_D[0] = stage_D
    if pending_D[0] is not None:
        pending_D[0]()
        pending_D[0] = None
    sched_barrier(S)


FULL_PLAN = []
for _l in range(DEPTH):
    FULL_PLAN += [("ffn1", _l), ("inproj", _l), ("mem", _l), ("dsa", _l), ("ssd", _l), ("merge", _l), ("ffn2", _l)]
FULL_PLAN.append("final")
_NC_CACHE = {}


def kernel(**inputs):
    inp = {k: np.asarray(v) for k, v in inputs.items()}
    B = inp["x"].shape[0]
    if "nc" not in _NC_CACHE:
        _NC_CACHE["nc"] = build(FULL_PLAN)
    nc = _NC_CACHE["nc"]
    vec = np.stack([pack_vecs(inp, l) for l in range(DEPTH)])
    cst = make_consts()
    in_maps = [core_inputs(inp, b, vec, cst) for b in range(B)]
    res = run_bass_kernel_spmd(nc, in_maps, core_ids=list(range(B)))
    out = np.stack([np.ascontiguousarray(res.results[b]["outT"].T) for b in range(B)]).astype(np.float32)
    return out
```

```python
from contextlib import ExitStack
import numpy as np
import concourse.bass as bass
import concourse.mybir as mybir
from concourse.bass_utils import run_bass_kernel_spmd

F32 = mybir.dt.float32
BF16 = mybir.dt.bfloat16
I32 = mybir.dt.int32
AF = mybir.ActivationFunctionType
ALU = mybir.AluOpType
AX = mybir.AxisListType

D = 2048
L = 2048
DEPTH = 2
DFF = 5632
MEM = 256
EPS = 1e-6
SSD_INNER = 4096
CONV_CH = 6144
NH_SSD = 64
IN_SPLITS = (4096, 6144, 64, 2048, 128, 128, 1024, 64, 16, 2048, 6144)
IN_OFF = [0]
for _s in IN_SPLITS:
    IN_OFF.append(IN_OFF[-1] + _s)
IN_WIDTH = IN_OFF[-1]
(O_Z, O_XBC, O_DT, O_Q, O_K, O_V, O_QI, O_KI, O_WI, O_QM, O_G) = IN_OFF[:11]
TB = 512
NTB = L // TB
KC_D = D // 128


class Sched:
    ENGS = ("pe", "act", "dve", "pool", "sp")

    def __init__(self, nc, es, n_dma_sems=24):
        self.nc = nc
        self.eng = dict(pe=nc.tensor, act=nc.scalar, dve=nc.vector, pool=nc.gpsimd, sp=nc.sync)
        self.sem = {e: es.enter_context(nc.semaphore("sem_" + e)) for e in self.ENGS}
        self.cnt = {e: 0 for e in self.ENGS}
        self.seen = {e: {} for e in self.ENGS}
        self.snap = {}
        self.dsem = {}
        self.dcur = {}
        self.drr = {}
        for q in ("sp", "pool", "act"):
            n = n_dma_sems if q != "act" else 8
            self.dsem[q] = [es.enter_context(nc.semaphore("dsem_%s_%d" % (q, i))) for i in range(n)]
            self.dcur[q] = [0] * n
            self.drr[q] = 0
        self.bufs = {}
        self.n_wait = 0
        self.n_inst = 0

    def _wait(self, e, tok):
        if tok is None:
            return
        kind = tok[0]
        seen = self.seen[e]
        if kind == "c":
            _, f, c = tok
            if seen.get(f, 0) >= c:
                return
            if f == e and e == "pe":
                return
            self.eng[e].wait_ge(self.sem[f], c)
            self.n_wait += 1
            seen[f] = c
            sn = self.snap.get((f, c))
            if sn is not None:
                for g, v in zip(self.ENGS, sn):
                    if v > seen.get(g, 0):
                        seen[g] = v
        else:
            _, q, i, v = tok
            key = (q, i)
            if seen.get(key, 0) >= v:
                return
            self.eng[e].wait_ge(self.dsem[q][i], v)
            self.n_wait += 1
            seen[key] = v

    def _deps(self, e, r, w):
        for k in r:
            st = self.bufs.get(k)
            if st is not None:
                self._wait(e, st[0])
        for k in w:
            st = self.bufs.get(k)
            if st is not None:
                self._wait(e, st[0])
                for t in st[1]:
                    if t[0] == "c" and t[1] == e:
                        continue
                    self._wait(e, t)

    def _commit(self, tok, r, w):
        for k in r:
            st = self.bufs.setdefault(k, [None, []])
            st[1] = [t for t in st[1] if not (t[0] == tok[0] and t[1] == tok[1] and (t[0] == "c" or t[2] == tok[2]))]
            st[1].append(tok)
        for k in w:
            self.bufs[k] = [tok, []]

    def op(self, e, fn, r=(), w=()):
        self._deps(e, r, w)
        ins = fn(self.eng[e])
        self.cnt[e] += 1
        c = self.cnt[e]
        ins.then_inc(self.sem[e], 1)
        self.n_inst += 1
        sn = self.seen[e]
        self.snap[(e, c)] = tuple(c if g == e else sn.get(g, 0) for g in self.ENGS)
        tok = ("c", e, c)
        self._commit(tok, r, w)
        return tok

    def dma(self, q, out, in_, r=(), w=(), **kw):
        self._deps(q, r, w)
        i = self.drr[q]
        self.drr[q] = (i + 1) % len(self.dsem[q])
        cur = self.dcur[q][i]
        if cur > 0:
            self._wait(q, ("d", q, i, cur))
        self.eng[q].dma_start(out=out, in_=in_, **kw).then_inc(self.dsem[q][i], 16)
        self.dcur[q][i] = cur + 16
        tok = ("d", q, i, cur + 16)
        self._commit(tok, r, w)
        return tok

    def finish(self, e="sp"):
        for f in self.ENGS:
            if self.cnt[f] > 0 and f != e:
                self._wait(e, ("c", f, self.cnt[f]))
        for q in self.dsem:
            for i, cur in enumerate(self.dcur[q]):
                if cur > 0:
                    self._wait(e, ("d", q, i, cur))


class Buf:
    def __init__(self, t, key):
        self.t = t
        self.k = key

    def __getitem__(self, idx):
        return self.t[idx]


class Ctx:
    pass


def make_ctx(nc, es):
    c = Ctx()
    c.nc = nc
    c.es = es
    c.S = Sched(nc, es)
    c.nbuf = 0

    def sb(shape, dt, name=None):
        c.nbuf += 1
        name = name or ("sb%d" % c.nbuf)
        t = es.enter_context(nc.sbuf_tensor(name, list(shape), dt))
        return Buf(t, name)

    def ps(name, shape=(128, 512), dt=F32):
        t = es.enter_context(nc.psum_tensor(name, list(shape), dt))
        return Buf(t, name)
    c.sb = sb
    c.ps = ps
    return c


def stream_linear(c, jobs, rhs_fn, rhs_keys, ntb, epilogue, wslots, psum_sets, tbw=TB):
    S = c.S
    nslot = len(wslots)
    state = c.__dict__.setdefault("_lin_state", {"slot": 0, "pset": 0})
    flat = []
    for ji, job in enumerate(jobs):
        for gi, g in enumerate(job):
            flat.append((ji, gi, g))
    PREF = nslot - 1
    slot_of = {}

    def issue(n):
        ji, gi, (W, col0, ncols, KC) = flat[n]
        si = state["slot"]
        state["slot"] = (si + 1) % nslot
        ws = wslots[si]
        wv = W.rearrange("(kc p) f -> p kc f", p=128)
        dst = ws.t[:, 0:KC * ncols].rearrange("p (kc f) -> p kc f", f=ncols)
        half = KC // 2
        S.dma("pool", dst[:, 0:half, :], wv[:, 0:half, col0:col0 + ncols], w=[ws.k + "_a"])
        S.dma("pool", dst[:, half:KC, :], wv[:, half:KC, col0:col0 + ncols], w=[ws.k + "_b"])
        slot_of[n] = (ws, dst)

    nxt = 0
    n = 0
    for ji, job in enumerate(jobs):
        while nxt < len(flat) and nxt < n + nslot:
            issue(nxt)
            nxt += 1
        for tb in range(ntb):
            pi = state["pset"] % len(psum_sets)
            state["pset"] = (pi + 1) % len(psum_sets)
            pset = psum_sets[pi]
            for gi, (W, col0, ncols, KC) in enumerate(job):
                ws, dst = slot_of[n + gi]
                pb = pset[gi]

                def mm(e, dst=dst, pb=pb, KC=KC, gi=gi, tb=tb, ncols=ncols):
                    ins = None
                    for kc in range(KC):
                        ins = e.matmul(pb.t[0:ncols, 0:tbw], dst[:, kc, :], rhs_fn(gi, kc, tb),
                                       start=(kc == 0), stop=(kc == KC - 1))
                    return ins
                S.op("pe", mm, r=[ws.k + "_a", ws.k + "_b"] + list(rhs_keys(gi, tb)), w=[pb.k])
            epilogue(ji, tb, pset)
        n += len(job)


def sched_barrier(S):
    engs = [e for e in S.ENGS]
    toks = [("c", f, S.cnt[f]) for f in S.ENGS if S.cnt[f] > 0]
    dtoks = []
    for q in S.dsem:
        for i, cur in enumerate(S.dcur[q]):
            if cur > 0:
                dtoks.append(("d", q, i, cur))
    for e in engs:
        for t in toks:
            if t[1] != e:
                S._wait(e, t)
        for t in dtoks:
            S._wait(e, t)
    S.bufs = {}


def phase_norm(c, x_src, gcol, out_mode, out_dst=None):
    S = c.S
    xv = x_src.rearrange("(kc p) t -> p kc t", p=128)
    xin = c.R44.t[:, 0:16 * TB * 2].bitcast(F32).rearrange("p (kc t) -> p kc t", t=TB)
    hT = c.hT
    for tb in range(NTB):
        t0 = tb * TB
        for hf in range(2):
            S.dma("sp", xin[:, hf * 8:(hf + 1) * 8, :], xv[:, hf * 8:(hf + 1) * 8, t0:t0 + TB],
                  w=["xin%d" % hf])
        pb = c.psb[tb % 2]
        for kc in range(KC_D):
            sq = c.stg_bf[kc % 3]
            S.op("act", lambda e, sq=sq, kc=kc: e.activation(out=sq.t[:, :], in_=xin[:, kc, :], func=AF.Square),
                 r=["xin%d" % (kc // 8)], w=[sq.k])
            S.op("pe", lambda e, sq=sq, kc=kc, pb=pb: e.matmul(pb.t[:, :], c.ones_bf.t[:, :], sq.t[:, :],
                                                                start=(kc == 0), stop=(kc == KC_D - 1)),
                 r=[sq.k, "const"], w=[pb.k])
        rt = c.stg_f32[0]
        S.op("act", lambda e, pb=pb: e.activation(out=rt.t[:, :], in_=pb.t[:, :], func=AF.Sqrt,
                                                  scale=1.0 / D, bias=c.eps_col.t[:, 0:1]),
             r=[pb.k, "const"], w=[rt.k])
        rs = c.stg_f32[1]
        S.op("dve", lambda e: e.reciprocal(out=rs.t[:, :], in_=rt.t[:, :]), r=[rt.k], w=[rs.k])
        for kc in range(KC_D):
            if out_mode == "h":
                S.op("dve", lambda e, kc=kc: e.scalar_tensor_tensor(
                    out=hT[:, kc, t0:t0 + TB], in0=xin[:, kc, :], scalar=gcol[:, kc:kc + 1], in1=rs.t[:, :],
                    op0=ALU.mult, op1=ALU.mult),
                    r=["xin%d" % (kc // 8), rs.k, "vecs"], w=["hT%d" % tb])
            else:
                ot = c.stg_f32[2 + kc % 2]
                S.op("dve", lambda e, kc=kc, ot=ot: e.scalar_tensor_tensor(
                    out=ot.t[:, :], in0=xin[:, kc, :], scalar=gcol[:, kc:kc + 1], in1=rs.t[:, :],
                    op0=ALU.mult, op1=ALU.mult),
                    r=["xin%d" % (kc // 8), rs.k, "vecs"], w=[ot.k])
                S.dma("sp", out_dst[kc * 128:(kc + 1) * 128, t0:t0 + TB], ot.t[:, :], r=[ot.k])


def phase_ffn(c, w_in, w_out, x_src, x_dst):
    S = c.S
    NJ = DFF // 128
    hT = c.hT
    actd = c.act_d
    jobs = [[(w_in, j * 128, 128, KC_D), (w_in, DFF + j * 128, 128, KC_D)] for j in range(NJ)]
    cnt = [0]

    def epi_a(ji, tb, pset):
        i = cnt[0]
        cnt[0] += 1
        sg = c.stg_f32[i % 2]
        S.op("act", lambda e: e.activation(out=sg.t[:, :], in_=pset[0].t[:, :], func=AF.Silu),
             r=[pset[0].k], w=[sg.k])
        ab = c.stg_bf[i % 3]
        S.op("dve", lambda e: e.tensor_tensor(out=ab.t[:, :], in0=pset[1].t[:, :], in1=sg.t[:, :], op=ALU.mult),
             r=[pset[1].k, sg.k], w=[ab.k])
        S.dma("sp", actd[ji * 128:(ji + 1) * 128, tb * TB:(tb + 1) * TB], ab.t[:, :], r=[ab.k],
              w=[("act", ji, tb)])

    stream_linear(c, jobs, lambda gi, kc, tb: hT[:, kc, tb * TB:(tb + 1) * TB],
                  lambda gi, tb: ["hT%d" % tb], NTB, epi_a, c.wslots,
                  [[c.psb[0], c.psb[1]], [c.psb[2], c.psb[3]], [c.psb[4], c.psb[5]], [c.psb[6], c.psb[7]]])
    sched_barrier(S)
    av = actd.rearrange("(kc p) t -> p kc t", p=128)
    blks = [c.R64.t[:, 0:NJ * TB].rearrange("p (kc t) -> p kc t", t=TB),
            c.R44.t[:, 0:NJ * TB].rearrange("p (kc t) -> p kc t", t=TB)]
    bkeys = ["ablkA", "ablkB"]
    for tb in range(NTB):
        blk = blks[tb % 2]
        bk = bkeys[tb % 2]
        t0 = tb * TB
        for q4 in range(4):
            S.dma("sp", blk[:, q4 * 11:(q4 + 1) * 11, :], av[:, q4 * 11:(q4 + 1) * 11, t0:t0 + TB], w=[bk + str(q4)])
        jobs = [[(w_out, dc * 128, 128, NJ)] for dc in range(KC_D)]
        cnt2 = [0]

        def epi_b(ji, tb_unused, pset, tb=tb, t0=t0):
            i = cnt2[0]
            cnt2[0] += 1
            xt = c.stg_f32[i % 2]
            S.dma("sp", xt.t[:, :], x_src[ji * 128:(ji + 1) * 128, t0:t0 + TB], w=[xt.k])
            xo = c.stg_f32[2 + i % 2]
            S.op("dve", lambda e: e.scalar_tensor_tensor(out=xo.t[:, :], in0=pset[0].t[:, :], scalar=0.5,
                                                         in1=xt.t[:, :], op0=ALU.mult, op1=ALU.add),
                 r=[pset[0].k, xt.k], w=[xo.k])
            S.dma("sp", x_dst[ji * 128:(ji + 1) * 128, t0:t0 + TB], xo.t[:, :], r=[xo.k])

        stream_linear(c, jobs, lambda gi, kc, tb_, blk=blk: blk[:, kc, :],
                      lambda gi, tb_, bk=bk: [bk + str(q) for q in range(4)], 1, epi_b, c.wslots,
                      [[c.psb[i]] for i in range(8)])
    sched_barrier(S)


V_FFN1, V_MIX, V_FFN2, V_MEMN = 0, 16, 32, 48
V_SSDN = 64
V_CONVW = 96
V_CONVB = 288
V_DTB, V_ALOG, V_DSKIP = 336, 337, 338
V_FINAL = 339
V_DBC = 360
NV = 424


def pack_vecs(inp, l):
    v = np.zeros((128, NV), np.float32)

    def col(vec):
        return np.ascontiguousarray(np.asarray(vec, np.float32).reshape(-1, 128).T)
    v[:, V_FFN1:V_FFN1 + 16] = col(inp["ffn1_norm"][l])
    v[:, V_MIX:V_MIX + 16] = col(inp["mix_norm"][l])
    v[:, V_FFN2:V_FFN2 + 16] = col(inp["ffn2_norm"][l])
    v[:, V_MEMN:V_MEMN + 16] = col(inp["mem_norm"][l])
    v[:, V_SSDN:V_SSDN + 32] = col(inp["ssd_norm"][l])
    for k in range(4):
        v[:, V_CONVW + k * 48:V_CONVW + (k + 1) * 48] = col(inp["conv_w"][l][k])
    v[:, V_CONVB:V_CONVB + 48] = col(inp["conv_b"][l])
    v[0:64, V_DTB] = inp["dt_bias"][l]
    v[0:64, V_ALOG] = inp["a_log"][l]
    v[0:64, V_DSKIP] = inp["d_skip"][l]
    v[:, V_FINAL:V_FINAL + 16] = col(inp["final_norm"])
    v[:, V_DBC:V_DBC + 64] = np.asarray(inp["d_skip"][l], np.float32)[None, :]
    return v


C_IDENT, C_ONES = 0, 128
NCONST = 256


def make_consts():
    cst = np.zeros((128, NCONST), np.float32)
    cst[:, C_IDENT:C_IDENT + 128] = np.eye(128, dtype=np.float32)
    cst[:, C_ONES:C_ONES + 128] = 1.0
    return cst


def build(plan, dbg=None):
    nc = bass.Bass("TRN2", target_bir_lowering=False)
    es = ExitStack()
    with es:
        c = make_ctx(nc, es)
        S = c.S
        dt = nc.dram_tensor
        xT = dt("xT", [D, L], F32, kind="ExternalInput").ap()
        memT = dt("memT", [D, MEM], F32, kind="ExternalInput").ap()
        pos = dt("pos", [128, L], I32, kind="ExternalInput").ap()
        c.consts2_d = dt("consts2", [128, NCONST2], F32, kind="ExternalInput").ap()
        vecs = dt("vecs", [DEPTH, 128, NV], F32, kind="ExternalInput").ap()
        consts = dt("consts", [128, NCONST], F32, kind="ExternalInput").ap()
        W = {}
        for name, shp in (("w_ffn1_in", [D, 2 * DFF]), ("w_ffn1_out", [DFF, D]), ("w_ffn2_in", [D, 2 * DFF]),
                          ("w_ffn2_out", [DFF, D]), ("w_in", [D, IN_WIDTH]), ("w_mem_kv", [D, 2 * D]),
                          ("w_br_ssd", [SSD_INNER, D]), ("w_br_dsa", [D, D]), ("w_br_mem", [D, D]), ("w_out", [D, D])):
            W[name] = dt(name, [DEPTH, shp[0] + 1, shp[1]], F32, kind="ExternalInput").ap()[:, 0:shp[0], :]
        outT = dt("outT", [D, L], F32, kind="ExternalOutput").ap()
        c.xres = dt("xres", [D, L], F32, kind="Internal").ap()
        c.act_d = dt("act_d", [DFF, L], BF16, kind="Internal").ap()
        c.P = {}
        for name, off, width, dt_ in P_SPECS:
            c.P[name] = dt("P_" + name, [width, L], dt_, kind="Internal").ap()
        c.yssd = dt("yssd", [SSD_INNER, L], BF16, kind="Internal").ap()
        c.ydsa = dt("ydsa", [D, L], BF16, kind="Internal").ap()
        c.ymem = dt("ymem", [D, L], BF16, kind="Internal").ap()
        c.pos = pos
        c.xc_d = dt("xc_d", [CONV_CH, L], BF16, kind="Internal").ap()
        c.acs_d = dt("acs_d", [NH_SSD, L], F32, kind="Internal").ap()
        c.V_l = None
        c.consts_d = consts

        c.R64 = c.sb([128, 32768], BF16, "R64")
        c.R44 = c.sb([128, 22528], BF16, "R44")
        c.hT = c.R64.t[:, :].rearrange("p (kc t) -> p kc t", t=L)
        c.WS = c.sb([128, 6 * 5632], BF16, "WS")
        c.wslots = [Buf(c.WS.t[:, i * 5632:(i + 1) * 5632], "wslot%d" % i) for i in range(6)]
        c.stg_f32 = [c.sb([128, TB], F32, "stgf%d" % i) for i in range(4)]
        c.stg_bf = [c.sb([128, TB], BF16, "stgb%d" % i) for i in range(3)]
        c.vecs = [c.sb([128, NV], F32, "vecs%d" % l) for l in range(DEPTH)]
        c.cst = c.sb([128, NCONST], F32, "cst")
        c.ones_bf = c.sb([128, 128], BF16, "ones_bf")
        c.ident_bf = c.sb([128, 128], BF16, "ident_bf")
        c.eps_col = c.sb([128, 1], F32, "eps_col")
        c.psb = [c.ps("ps%d" % i) for i in range(8)]

        for l in range(DEPTH):
            S.dma("sp", c.vecs[l].t[:, :], vecs[l], w=["vecs"])
        S.dma("sp", c.cst.t[:, :], consts, w=["cst"])
        S.op("dve", lambda e: e.tensor_copy(out=c.ones_bf.t[:, :], in_=c.cst.t[:, C_ONES:C_ONES + 128]), r=["cst"], w=["const"])
        S.op("dve", lambda e: e.tensor_copy(out=c.ident_bf.t[:, :], in_=c.cst.t[:, C_IDENT:C_IDENT + 128]), r=["cst"], w=["const"])
        S.op("dve", lambda e: e.memset(c.eps_col.t[:, :], EPS), w=["const"])
        sched_barrier(S)

        x_cur = xT
        for l in range(DEPTH):
            V = c.vecs[l].t
            if ("ffn1", l) in plan:
                phase_norm(c, x_cur, V[:, V_FFN1:V_FFN1 + 16], "h")
                phase_ffn(c, W["w_ffn1_in"][l], W["w_ffn1_out"][l], x_cur, c.xres)
                x_cur = c.xres
            if ("inproj", l) in plan:
                phase_norm(c, x_cur, V[:, V_MIX:V_MIX + 16], "h")
                phase_inproj(c, W["w_in"][l])
            if ("mem", l) in plan:
                phase_mem(c, memT, V[:, V_MEMN:V_MEMN + 16], W["w_mem_kv"][l])
            if ("dsa", l) in plan:
                phase_dsa(c, l)
            if ("ssd", l) in plan:
                phase_ssd(c, l, V)
            if ("merge", l) in plan:
                phase_merge(c, W["w_br_ssd"][l], W["w_br_dsa"][l], W["w_br_mem"][l], W["w_out"][l], x_cur, c.xres)
                x_cur = c.xres
            if ("ffn2", l) in plan:
                phase_norm(c, x_cur, V[:, V_FFN2:V_FFN2 + 16], "h")
                phase_ffn(c, W["w_ffn2_in"][l], W["w_ffn2_out"][l], x_cur, c.xres)
                x_cur = c.xres
        if "final" in plan:
            phase_norm(c, x_cur, c.vecs[0].t[:, V_FINAL:V_FINAL + 16], "out", outT)
        elif dbg is not None:
            if dbg == "xres":
                src, nrow, sdt = x_cur, D, F32
            else:
                src, nrow, sdt = dbg
            for kc in range((nrow + 127) // 128):
                nr = min(128, nrow - kc * 128)
                for tb in range(NTB):
                    st = c.stg_f32[(kc * NTB + tb) % 4]
                    if sdt == F32:
                        S.dma("sp", st.t[0:nr, :], src(c)[kc * 128:kc * 128 + nr, tb * TB:(tb + 1) * TB] if callable(src) else src[kc * 128:kc * 128 + nr, tb * TB:(tb + 1) * TB], w=[st.k])
                    else:
                        sb_ = c.stg_bf[(kc * NTB + tb) % 3]
                        S.dma("sp", sb_.t[0:nr, :], src(c)[kc * 128:kc * 128 + nr, tb * TB:(tb + 1) * TB], w=[sb_.k])
                        S.op("dve", lambda e, st=st, sb_=sb_, nr=nr: e.tensor_copy(out=st.t[0:nr, :], in_=sb_.t[0:nr, :]), r=[sb_.k], w=[st.k])
                    S.dma("sp", outT[kc * 128:kc * 128 + nr, tb * TB:(tb + 1) * TB], st.t[0:nr, :], r=[st.k])
        S.finish("sp")
        print("sched: inst=%d waits=%d cnt=%s" % (S.n_inst, S.n_wait, S.cnt))
    return nc


W_NAMES = ("w_ffn1_in", "w_ffn1_out", "w_ffn2_in", "w_ffn2_out", "w_in", "w_mem_kv", "w_br_ssd", "w_br_dsa", "w_br_mem", "w_out")


def core_inputs(inp, b, vec, cst):
    im = {"xT": np.ascontiguousarray(inp["x"][b].T), "memT": np.ascontiguousarray(inp["mem"][b].T),
          "pos": np.ascontiguousarray(np.broadcast_to(inp["positions"][b].reshape(1, L).astype(np.int32), (128, L))), "vecs": vec, "consts": cst,
          "consts2": make_consts2()}
    for n in W_NAMES:
        w = inp[n]
        p = np.empty((w.shape[0], w.shape[1] + 1, w.shape[2]), np.float32)
        p[:, :w.shape[1], :] = w
        p[:, w.shape[1], :] = float(b)
        im[n] = p
    return im


P_SPECS = [("z", O_Z, 4096, BF16), ("xbc", O_XBC, 6144, BF16), ("dt", O_DT, 64, F32), ("q", O_Q, 2048, BF16),
           ("k", O_K, 128, BF16), ("v", O_V, 128, BF16), ("qi", O_QI, 1024, BF16), ("ki", O_KI, 64, BF16),
           ("wi", O_WI, 16, F32), ("qm", O_QM, 2048, BF16), ("g", O_G, 6144, BF16)]


def phase_inproj(c, w_in):
    S = c.S
    hT = c.hT
    jobs = []
    meta = []
    for name, off, width, dt_ in P_SPECS:
        for f0 in range(0, width, 128):
            nco = min(128, width - f0)
            jobs.append([(w_in, off + f0, nco, KC_D)])
            meta.append((name, f0, nco, dt_))
    cnt = [0]

    def epi(ji, tb, pset):
        name, f0, nco, dt_ = meta[ji]
        i = cnt[0]
        cnt[0] += 1
        if dt_ == F32:
            st = c.stg_f32[i % 4]
        else:
            st = c.stg_bf[i % 3]
        if i % 2 == 0:
            S.op("act", lambda e: e.copy(out=st.t[0:nco, :], in_=pset[0].t[0:nco, :]), r=[pset[0].k], w=[st.k])
        else:
            S.op("dve", lambda e: e.tensor_copy(out=st.t[0:nco, :], in_=pset[0].t[0:nco, :]), r=[pset[0].k], w=[st.k])
        S.dma("sp", c.P[name][f0:f0 + nco, tb * TB:(tb + 1) * TB], st.t[0:nco, :], r=[st.k], w=[("P", name, f0, tb)])

    stream_linear(c, jobs, lambda gi, kc, tb: hT[:, kc, tb * TB:(tb + 1) * TB],
                  lambda gi, tb: ["hT%d" % tb], NTB, epi, c.wslots, [[c.psb[i]] for i in range(8)])
    sched_barrier(S)


def phase_mem(c, memT, gcol, w_kv):
    S = c.S
    r = c.R44.t
    mem_in = r[:, 0:8192].bitcast(F32).rearrange("p (kc t) -> p kc t", t=MEM)
    mnT = r[:, 8192:12288].rearrange("p (kc t) -> p kc t", t=MEM)
    kmT = r[:, 12288:16384].rearrange("p (kc t) -> p kc t", t=MEM)
    vm = r[:, 16384:20480].rearrange("p (mt d) -> p mt d", d=D)
    memv = memT.rearrange("(kc p) t -> p kc t", p=128)
    S.dma("sp", mem_in, memv, w=["mem_in"])
    pb = c.psb[0]
    for kc in range(KC_D):
        sq = c.stg_bf[kc % 3]
        S.op("act", lambda e, sq=sq, kc=kc: e.activation(out=sq.t[:, 0:MEM], in_=mem_in[:, kc, :], func=AF.Square),
             r=["mem_in"], w=[sq.k])
        S.op("pe", lambda e, sq=sq, kc=kc: e.matmul(pb.t[:, 0:MEM], c.ones_bf.t[:, :], sq.t[:, 0:MEM],
                                                    start=(kc == 0), stop=(kc == KC_D - 1)), r=[sq.k], w=[pb.k])
    rt, rs = c.stg_f32[0], c.stg_f32[1]
    S.op("act", lambda e: e.activation(out=rt.t[:, 0:MEM], in_=pb.t[:, 0:MEM], func=AF.Sqrt, scale=1.0 / D,
                                       bias=c.eps_col.t[:, 0:1]), r=[pb.k], w=[rt.k])
    S.op("dve", lambda e: e.reciprocal(out=rs.t[:, 0:MEM], in_=rt.t[:, 0:MEM]), r=[rt.k], w=[rs.k])
    for kc in range(KC_D):
        S.op("dve", lambda e, kc=kc: e.scalar_tensor_tensor(out=mnT[:, kc, :], in0=mem_in[:, kc, :],
                                                             scalar=gcol[:, kc:kc + 1], in1=rs.t[:, 0:MEM],
                                                             op0=ALU.mult, op1=ALU.mult),
             r=["mem_in", rs.k], w=["mnT"])
    jobs = [[(w_kv, fc * 128, 128, KC_D)] for fc in range(16)]

    def epi_k(ji, tb, pset):
        S.op("act", lambda e: e.copy(out=kmT[:, ji, :], in_=pset[0].t[:, 0:MEM]), r=[pset[0].k], w=["kmT"])
    stream_linear(c, jobs, lambda gi, kc, tb: mnT[:, kc, :], lambda gi, tb: ["mnT"], 1, epi_k, c.wslots,
                  [[c.psb[i]] for i in range(1, 5)], tbw=MEM)
    wv = w_kv.rearrange("(kc p) f -> p kc f", p=128)
    for cb in range(8):
        ws = c.wslots[cb % 6]
        dst = ws.t[:, 0:4096].rearrange("p (kc f) -> p kc f", f=256)
        S.dma("pool", dst[:, 0:8, :], wv[:, 0:8, D + cb * 256:D + (cb + 1) * 256], w=[ws.k + "_a"])
        S.dma("pool", dst[:, 8:16, :], wv[:, 8:16, D + cb * 256:D + (cb + 1) * 256], w=[ws.k + "_b"])
        for mt in range(2):
            pbv = c.psb[5 + mt]

            def mm(e, dst=dst, pbv=pbv, mt=mt):
                ins = None
                for kc in range(KC_D):
                    ins = e.matmul(pbv.t[:, 0:256], mnT[:, kc, mt * 128:(mt + 1) * 128], dst[:, kc, :],
                                   start=(kc == 0), stop=(kc == KC_D - 1))
                return ins
            S.op("pe", mm, r=[ws.k + "_a", ws.k + "_b", "mnT"], w=[pbv.k])
            S.op("dve", lambda e, pbv=pbv, mt=mt, cb=cb: e.tensor_copy(out=vm[:, mt, cb * 256:(cb + 1) * 256], in_=pbv.t[:, 0:256]),
                 r=[pbv.k], w=["vm"])
    sched_barrier(S)
    qmv = c.P["qm"].rearrange("(kc p) t -> p kc t", p=128)
    qblk = c.R64.t[:, 0:16 * TB].rearrange("p (kc t) -> p kc t", t=TB)
    pT = [c.R64.t[:, 16 * TB + i * TB:16 * TB + (i + 1) * TB] for i in range(2)]
    scale = 512.0 ** -0.5
    for tb in range(NTB):
        t0 = tb * TB
        S.dma("sp", qblk, qmv[:, :, t0:t0 + TB], w=["qblk"])
        for hd in range(4):
            for mt in range(2):
                pl = c.psb[mt]

                def mmq(e, pl=pl, hd=hd, mt=mt):
                    ins = None
                    for j in range(4):
                        ins = e.matmul(pl.t[:, :], kmT[:, hd * 4 + j, mt * 128:(mt + 1) * 128], qblk[:, hd * 4 + j, :],
                                       start=(j == 0), stop=(j == 3))
                    return ins
                S.op("pe", mmq, r=["kmT", "qblk"], w=[pl.k])
                S.op("act", lambda e, pl=pl, mt=mt: e.activation(out=pT[mt], in_=pl.t[:, :], func=AF.Exp, scale=scale),
                     r=[pl.k], w=["pT%d" % mt])
            prs = c.psb[2]

            def mmrs(e):
                e.matmul(prs.t[:, :], c.ones_bf.t[:, :], pT[0], start=True, stop=False)
                return e.matmul(prs.t[:, :], c.ones_bf.t[:, :], pT[1], start=False, stop=True)
            S.op("pe", mmrs, r=["pT0", "pT1"], w=[prs.k])
            rinv = c.stg_f32[0]
            S.op("dve", lambda e: e.reciprocal(out=rinv.t[:, :], in_=prs.t[:, :]), r=[prs.k], w=[rinv.k])
            for j in range(4):
                po = c.psb[3 + j]
                ch = hd * 4 + j

                def mmo(e, po=po, ch=ch):
                    e.matmul(po.t[:, :], vm[:, 0, ch * 128:(ch + 1) * 128], pT[0], start=True, stop=False)
                    return e.matmul(po.t[:, :], vm[:, 1, ch * 128:(ch + 1) * 128], pT[1], start=False, stop=True)
                S.op("pe", mmo, r=["pT0", "pT1", "vm"], w=[po.k])
                st = c.stg_bf[j % 3]
                S.op("dve", lambda e, po=po, st=st: e.tensor_tensor(out=st.t[:, :], in0=po.t[:, :], in1=rinv.t[:, :], op=ALU.mult),
                     r=[po.k, rinv.k], w=[st.k])
                S.dma("sp", c.ymem[ch * 128:(ch + 1) * 128, t0:t0 + TB], st.t[:, :], r=[st.k], w=[("ymem", ch, tb)])
    sched_barrier(S)


def phase_merge(c, w_ssd, w_dsa, w_memw, w_out, x_src, x_dst):
    S = c.S
    mT = c.R64.t[:, :].rearrange("p (kc t) -> p kc t", t=L)
    ysv = c.yssd.rearrange("(kc p) t -> p kc t", p=128)
    ydv = c.ydsa.rearrange("(kc p) t -> p kc t", p=128)
    ymv = c.ymem.rearrange("(kc p) t -> p kc t", p=128)
    gv = c.P["g"]
    TBM = 256
    ys = c.R44.t[:, 0:32 * TBM].rearrange("p (kc t) -> p kc t", t=TBM)
    yd = c.R44.t[:, 32 * TBM:48 * TBM].rearrange("p (kc t) -> p kc t", t=TBM)
    ym = c.R44.t[:, 48 * TBM:64 * TBM].rearrange("p (kc t) -> p kc t", t=TBM)
    gt = [c.R44.t[:, 64 * TBM + i * TBM:64 * TBM + (i + 1) * TBM] for i in range(6)]
    for tb in range(L // TBM):
        t0 = tb * TBM
        S.dma("sp", ys[:, 0:16, :], ysv[:, 0:16, t0:t0 + TBM], w=["ys0"])
        S.dma("sp", ys[:, 16:32, :], ysv[:, 16:32, t0:t0 + TBM], w=["ys1"])
        S.dma("sp", yd, ydv[:, :, t0:t0 + TBM], w=["yd"])
        S.dma("sp", ym, ymv[:, :, t0:t0 + TBM], w=["ym"])
        jobs = [[(w_ssd, dc * 128, 128, 32), (w_dsa, dc * 128, 128, 16), (w_memw, dc * 128, 128, 16)] for dc in range(KC_D)]
        cnt = [0]

        def rhs_fn(gi, kc, tb_):
            return (ys, yd, ym)[gi][:, kc, :]

        def rhs_keys(gi, tb_):
            return (["ys0", "ys1"], ["yd"], ["ym"])[gi]

        def epi(ji, tb_, pset, t0=t0):
            i = cnt[0]
            cnt[0] += 1
            acc = c.stg_f32[i % 2]
            for bi in range(3):
                g = gt[(i % 2) * 3 + bi]
                gk = "gt%d" % ((i % 2) * 3 + bi)
                S.dma("sp", g, gv[bi * D + ji * 128:bi * D + (ji + 1) * 128, t0:t0 + TBM], w=[gk])
                sg = c.stg_f32[2 + (bi % 2)]
                S.op("act", lambda e, g=g, sg=sg: e.activation(out=sg.t[:, 0:TBM], in_=g, func=AF.Sigmoid), r=[gk], w=[sg.k])
                if bi == 0:
                    S.op("dve", lambda e, sg=sg: e.tensor_tensor(out=acc.t[:, 0:TBM], in0=pset[0].t[:, 0:TBM], in1=sg.t[:, 0:TBM], op=ALU.mult),
                         r=[pset[0].k, sg.k], w=[acc.k])
                else:
                    S.op("dve", lambda e, sg=sg, bi=bi: e.tensor_tensor(out=sg.t[:, 0:TBM], in0=pset[bi].t[:, 0:TBM], in1=sg.t[:, 0:TBM], op=ALU.mult),
                         r=[pset[bi].k, sg.k], w=[sg.k])
                    if bi == 1:
                        S.op("dve", lambda e, sg=sg: e.tensor_tensor(out=acc.t[:, 0:TBM], in0=acc.t[:, 0:TBM], in1=sg.t[:, 0:TBM], op=ALU.add),
                             r=[acc.k, sg.k], w=[acc.k])
                    else:
                        S.op("dve", lambda e, sg=sg: e.tensor_tensor(out=mT[:, ji, t0:t0 + TBM], in0=acc.t[:, 0:TBM], in1=sg.t[:, 0:TBM], op=ALU.add),
                             r=[acc.k, sg.k], w=["mT%d" % (t0 // TB)])

        stream_linear(c, jobs, rhs_fn, rhs_keys, 1, epi, c.wslots,
                      [[c.psb[0], c.psb[1], c.psb[2]], [c.psb[3], c.psb[4], c.psb[5]]], tbw=TBM)
    sched_barrier(S)
    jobs = [[(w_out, dc * 128, 128, KC_D)] for dc in range(KC_D)]
    cnt2 = [0]

    def epi_o(ji, tb, pset):
        i = cnt2[0]
        cnt2[0] += 1
        t0 = tb * TB
        xt = c.stg_f32[i % 2]
        S.dma("sp", xt.t[:, :], x_src[ji * 128:(ji + 1) * 128, t0:t0 + TB], w=[xt.k])
        xo = c.stg_f32[2 + i % 2]
        S.op("dve", lambda e: e.tensor_tensor(out=xo.t[:, :], in0=pset[0].t[:, :], in1=xt.t[:, :], op=ALU.add),
             r=[pset[0].k, xt.k], w=[xo.k])
        S.dma("sp", x_dst[ji * 128:(ji + 1) * 128, t0:t0 + TB], xo.t[:, :], r=[xo.k])
    stream_linear(c, jobs, lambda gi, kc, tb: mT[:, kc, tb * TB:(tb + 1) * TB], lambda gi, tb: ["mT%d" % tb], NTB, epi_o,
                  c.wslots, [[c.psb[i]] for i in range(8)])
    sched_barrier(S)


C_INVA, C_INVI, C_SGNA, C_SGNI = 256, 257, 258, 259
C_PA, C_PI, C_CAUS = 264, 392, 520
C_U = 648
NCONST2 = 776
TWO_PI = 6.283185307179586
CW1 = 6.28125
CW2 = TWO_PI - CW1
PI_F = 3.1415925


def make_consts2():
    cst = np.zeros((128, NCONST2), np.float32)
    cst[:, 0:NCONST] = make_consts()
    th = np.float32(500000.0)
    inv_a = np.power(th, -(np.arange(0, 32, 2, dtype=np.float32) / np.float32(32))).astype(np.float32)
    inv_i = np.power(th, -(np.arange(0, 16, 2, dtype=np.float32) / np.float32(16))).astype(np.float32)
    for d_ in range(128):
        if d_ < 32:
            cst[d_, C_INVA] = inv_a[d_ % 16]
            cst[d_, C_SGNA] = -1.0 if d_ < 16 else 1.0
        e = d_ % 64
        if e < 16:
            cst[d_, C_INVI] = inv_i[e % 8]
            cst[d_, C_SGNI] = -1.0 if e < 8 else 1.0
    for dp in range(32):
        dsrc = dp + 16 if dp < 16 else dp - 16
        cst[dsrc, C_PA + dp] = 1.0
    for blk in (0, 64):
        for ep in range(16):
            esrc = ep + 8 if ep < 8 else ep - 8
            cst[blk + esrc, C_PI + blk + ep] = 1.0
    q = np.arange(128)[:, None]
    k = np.arange(128)[None, :]
    cst[:, C_CAUS:C_CAUS + 128] = np.where(k <= q, 0.0, -1e30).astype(np.float32)
    cst[:, C_U:C_U + 128] = (q <= k).astype(np.float32)
    return cst


def _sin_table(c, out_bf, ang, sgncol, tmp_u, tmp_n, tmp_r, tag):
    S = c.S
    ni = tmp_n.bitcast(I32)
    S.op("dve", lambda e: e.tensor_scalar(out=tmp_u, in0=ang, scalar1=1.0 / TWO_PI, scalar2=None, op0=ALU.mult), r=[tag + "ang"], w=[tag + "u"])
    S.op("dve", lambda e: e.tensor_copy(out=ni, in_=tmp_u), r=[tag + "u"], w=[tag + "n"])
    S.op("dve", lambda e: e.tensor_copy(out=tmp_u, in_=ni), r=[tag + "n"], w=[tag + "u"])
    S.op("dve", lambda e: e.scalar_tensor_tensor(out=tmp_r, in0=tmp_u, scalar=-CW1, in1=ang, op0=ALU.mult, op1=ALU.add),
         r=[tag + "u", tag + "ang"], w=[tag + "r"])
    S.op("dve", lambda e: e.scalar_tensor_tensor(out=tmp_r, in0=tmp_u, scalar=-CW2, in1=tmp_r, op0=ALU.mult, op1=ALU.add),
         r=[tag + "u", tag + "r"], w=[tag + "r"])
    S.op("dve", lambda e: e.tensor_scalar(out=tmp_u, in0=tmp_r, scalar1=PI_F, scalar2=None, op0=ALU.is_gt), r=[tag + "r"], w=[tag + "u"])
    S.op("dve", lambda e: e.scalar_tensor_tensor(out=tmp_r, in0=tmp_u, scalar=-TWO_PI, in1=tmp_r, op0=ALU.mult, op1=ALU.add),
         r=[tag + "u", tag + "r"], w=[tag + "r"])
    S.op("dve", lambda e: e.tensor_scalar(out=tmp_u, in0=tmp_r, scalar1=-PI_F, scalar2=None, op0=ALU.is_lt), r=[tag + "r"], w=[tag + "u"])
    S.op("dve", lambda e: e.scalar_tensor_tensor(out=tmp_r, in0=tmp_u, scalar=TWO_PI, in1=tmp_r, op0=ALU.mult, op1=ALU.add),
         r=[tag + "u", tag + "r"], w=[tag + "r"])
    S.op("dve", lambda e: e.tensor_scalar(out=tmp_r, in0=tmp_r, scalar1=PI_F, scalar2=-PI_F, op0=ALU.min, op1=ALU.max), r=[tag + "r"], w=[tag + "r"])
    if sgncol is None:
        S.op("act", lambda e: e.activation(out=out_bf, in_=tmp_r, func=AF.Sin), r=[tag + "r"], w=["tables"])
    else:
        S.op("act", lambda e: e.activation(out=out_bf, in_=tmp_r, func=AF.Sin, scale=sgncol), r=[tag + "r"], w=["tables"])


def _rope(c, X, ncols, cosT, sinT, PT, xkeys, pbank, i):
    S = c.S
    S.op("pe", lambda e: e.matmul(pbank.t[:, 0:ncols], PT, X, start=True, stop=True), r=list(xkeys) + ["dsaconst"], w=[pbank.k])
    t1 = c.stg_f32[i % 2]
    t2 = c.stg_f32[2 + i % 2]
    S.op("dve", lambda e: e.tensor_tensor(out=t1.t[:, 0:ncols], in0=X, in1=cosT, op=ALU.mult), r=list(xkeys) + ["tables"], w=[t1.k])
    S.op("dve", lambda e: e.tensor_tensor(out=t2.t[:, 0:ncols], in0=pbank.t[:, 0:ncols], in1=sinT, op=ALU.mult), r=[pbank.k, "tables"], w=[t2.k])
    S.op("pool", lambda e: e.tensor_tensor(out=X, in0=t1.t[:, 0:ncols], in1=t2.t[:, 0:ncols], op=ALU.add), r=[t1.k, t2.k], w=list(xkeys))


def phase_dsa(c, l):
    S = c.S
    R6, R4 = c.R64.t, c.R44.t
    qblk = R6[:, 0:8192].rearrange("p (h t) -> p h t", t=TB)
    selT = R6[:, 8192:16384].rearrange("p (k t) -> p k t", t=TB)
    qiblk = R6[:, 16384:20480].rearrange("p (h t) -> p h t", t=TB)
    cosA, sinA, cosI, sinI = (R6[:, 20480 + i * 2048:20480 + (i + 1) * 2048] for i in range(4))
    krT = R6[:, 28672:30720]
    kir2 = R6[:, 30720:32768]
    acc = R4[:, 0:4096].bitcast(F32)
    work = R4[:, 4096:8192].bitcast(F32)
    sel01 = R4[:, 8192:10240]
    v_tok = R4[:, 10240:12288].rearrange("p (k d) -> p k d", d=128)
    wi_tok = R4[:, 12288:12800].bitcast(F32).rearrange("p (q h) -> p q h", h=16)
    m8 = R4[:, 12800:12816].bitcast(F32)
    pTs = [R4[:, 13312 + i * 512:13312 + (i + 1) * 512] for i in range(6)]
    tmp3 = R4[:, 8192:12288].bitcast(F32)
    cst2 = c.wslots[0].t[:, 0:2 * NCONST2].bitcast(F32)
    PA_bf = c.wslots[1].t[:, 0:128]
    PI_bf = c.wslots[1].t[:, 128:256]
    posi = c.wslots[2].t[:, 0:4096].bitcast(I32)
    vT_sb = c.wslots[3].t[:, 0:2048]
    wiT_sb = c.wslots[4].t[:, 0:4096].bitcast(F32)
    ident_f = c.cst.t[:, C_IDENT:C_IDENT + 128]

    S.dma("sp", cst2, c.consts2_d, w=["cst2"])
    S.dma("sp", posi, c.pos, w=["posi"])
    S.op("dve", lambda e: e.tensor_copy(out=PA_bf, in_=cst2[:, C_PA:C_PA + 128]), r=["cst2"], w=["dsaconst"])
    S.op("dve", lambda e: e.tensor_copy(out=PI_bf, in_=cst2[:, C_PI:C_PI + 128]), r=["cst2"], w=["dsaconst"])
    S.op("dve", lambda e: e.tensor_copy(out=acc, in_=posi), r=["posi"], w=["posf"])
    for tname, invc, sgnc, cosT, sinT in (("A", C_INVA, C_SGNA, cosA, sinA), ("I", C_INVI, C_SGNI, cosI, sinI)):
        S.op("dve", lambda e, invc=invc: e.tensor_scalar(out=work, in0=acc, scalar1=cst2[:, invc:invc + 1], scalar2=None, op0=ALU.mult),
             r=["posf", "cst2"], w=[tname + "sang"])
        u_t = tmp3
        r_t = c.wslots[5].t[:, 0:4096].bitcast(F32)
        _sin_table(c, sinT, work, cst2[:, sgnc:sgnc + 1], u_t, u_t, r_t, tname + "s")
        S.op("dve", lambda e: e.tensor_scalar(out=work, in0=work, scalar1=1.5707963267948966, scalar2=None, op0=ALU.add),
             r=[tname + "sang", tname + "sr", tname + "su"], w=[tname + "cang"])
        _sin_table(c, cosT, work, None, u_t, u_t, r_t, tname + "c")
        sched_barrier(S)
    S.dma("sp", krT, c.P["k"], w=["krT"])
    S.dma("sp", kir2[0:64, :], c.P["ki"], w=["kir2"])
    S.dma("sp", kir2[64:128, :], c.P["ki"], w=["kir2b"])
    S.dma("sp", vT_sb, c.P["v"], w=["vT"])
    S.dma("sp", wiT_sb[0:16, :], c.P["wi"], w=["wiT"])
    for tb in range(NTB):
        sl = slice(tb * TB, (tb + 1) * TB)
        _rope(c, krT[:, sl], TB, cosA[:, sl], sinA[:, sl], PA_bf, ["krT"], c.psb[tb % 4], tb)
    for tb in range(NTB):
        sl = slice(tb * TB, (tb + 1) * TB)
        _rope(c, kir2[:, sl], TB, cosI[:, sl], sinI[:, sl], PI_bf, ["kir2", "kir2b"], c.psb[4 + tb % 4], tb)
    for half in range(2):
        pb = c.psb[half]
        pbv = pb.t[:, :].bitcast(BF16)

        def tr(e, half=half, pbv=pbv):
            ins = None
            for j in range(8):
                kt = half * 8 + j
                ins = e.transpose(pbv[:, j * 128:(j + 1) * 128], vT_sb[:, kt * 128:(kt + 1) * 128], c.ident_bf.t[:, :])
            return ins
        S.op("pe", tr, r=["vT", "const"], w=[pb.k])
        S.op("act", lambda e, half=half, pbv=pbv: e.copy(out=v_tok[:, half * 8:(half + 1) * 8, :],
                                                       in_=pbv[:, 0:1024].rearrange("p (k d) -> p k d", d=128)),
             r=[pb.k], w=["v_tok"])
    pbw = c.psb[2]

    def trw(e):
        ins = None
        for qt in range(16):
            ins = e.transpose(pbw.t[:, qt * 16:(qt + 1) * 16], wiT_sb[0:16, qt * 128:(qt + 1) * 128], ident_f[0:16, 0:16])
        return ins
    S.op("pe", trw, r=["wiT", "cst"], w=[pbw.k])
    S.op("act", lambda e: e.activation(out=wi_tok, in_=pbw.t[:, 0:256].rearrange("p (q h) -> p q h", h=16), func=AF.Copy,
                                       scale=0.25 * 0.125), r=[pbw.k], w=["wi_tok"])
    sched_barrier(S)

    WSv = c.WS.t
    accs = [acc, WSv[:, 11264:15360].bitcast(F32)]
    works = [work, WSv[:, 15360:19456].bitcast(F32)]
    m8s = [m8, WSv[:, 19456:19472].bitcast(F32)]
    qv = c.P["q"].rearrange("(h p) t -> p h t", p=128)
    qiv = c.P["qi"].rearrange("(h p) t -> p h t", p=128)
    scale = 128.0 ** -0.5
    NEG = -1e30
    selTs = [selT, WSv[:, 19968:28160].rearrange("p (k t) -> p k t", t=TB)]

    def load_q(b):
        t0 = b * TB
        S.dma("sp", qblk[:, 0:8, :], qv[:, 0:8, t0:t0 + TB], w=["qblk0"])
        S.dma("sp", qblk[:, 8:16, :], qv[:, 8:16, t0:t0 + TB], w=["qblk1"])
        for h in range(16):
            _rope(c, qblk[:, h, :], TB, cosA[:, t0:t0 + TB], sinA[:, t0:t0 + TB], PA_bf, ["qblk%d" % (h // 8)], c.psb[h % 4], h)

    def load_qi(b):
        t0 = b * TB
        S.dma("sp", qiblk, qiv[:, :, t0:t0 + TB], w=["qiblk"])
        for ch in range(8):
            _rope(c, qiblk[:, ch, :], TB, cosI[:, t0:t0 + TB], sinI[:, t0:t0 + TB], PI_bf, ["qiblk"], c.psb[4 + ch % 4], ch)

    def att_head(b, h, meng="dve"):
        t0 = b * TB
        nkt = 4 * (b + 1)
        selTb, selk = selTs[b % 2], "selT%d" % (b % 2)
        po = c.psb[3 + h % 2]
        prs = c.psb[5 + h % 2]
        qk = "qblk%d" % (h // 8)
        pend = []

        def pv(kt, pTm, pk):
            S.op("pe", lambda e: e.matmul(po.t[:, :], v_tok[:, kt, :], pTm, start=(kt == 0), stop=(kt == nkt - 1)),
                 r=[pk, "v_tok"], w=[po.k])
            S.op("pe", lambda e: e.matmul(prs.t[:, :], c.ones_bf.t[:, :], pTm, start=(kt == 0), stop=(kt == nkt - 1)),
                 r=[pk, "const"], w=[prs.k])
        for kt in range(nkt):
            pl = c.psb[kt % 3]
            S.op("pe", lambda e, pl=pl, kt=kt: e.matmul(pl.t[:, :], krT[:, kt * 128:(kt + 1) * 128], qblk[:, h, :],
                                                       start=True, stop=True), r=["krT", qk], w=[pl.k])
            pT = pTs[kt % 3]
            pTm = pTs[3 + kt % 3]
            S.op("act", lambda e, pl=pl, pT=pT: e.activation(out=pT, in_=pl.t[:, :], func=AF.Exp, scale=scale),
                 r=[pl.k], w=["pT%d" % (kt % 3)])
            S.op(meng, lambda e, pT=pT, pTm=pTm, kt=kt: e.tensor_tensor(out=pTm, in0=pT, in1=selTb[:, kt, :], op=ALU.mult),
                 r=["pT%d" % (kt % 3), selk], w=["pTm%d" % (kt % 3)])
            pend.append((kt, pTm, "pTm%d" % (kt % 3)))
            if len(pend) > 2:
                pv(*pend.pop(0))
        while pend:
            pv(*pend.pop(0))
        rinv = c.stg_f32[h % 2]
        S.op("dve", lambda e: e.reciprocal(out=rinv.t[:, :], in_=prs.t[:, :]), r=[prs.k], w=[rinv.k])
        st = c.stg_bf[h % 3]
        S.op("dve", lambda e: e.tensor_tensor(out=st.t[:, :], in0=po.t[:, :], in1=rinv.t[:, :], op=ALU.mult),
             r=[po.k, rinv.k], w=[st.k])
        S.dma("sp", c.ydsa[h * 128:(h + 1) * 128, t0:t0 + TB], st.t[:, :], r=[st.k], w=[("ydsa", h, b)])

    def index_block(b, inject):
        selTb, selk = selTs[b % 2], "selT%d" % (b % 2)
        S.op("pool", lambda e: e.memset(selTb[:, 4 * b:4 * b + 4, :], 0.0), w=[selk])
        for pair in range(2):
            tiles = []
            for sub in range(2):
                qi_ = pair * 2 + sub
                qt = 4 * b + qi_
                nk = 128 * (qt + 1)
                nkb = (nk + TB - 1) // TB
                accX, workX, m8X, tg = accs[sub], works[sub], m8s[sub], "t%d" % sub
                cnt = 0
                for kb in range(nkb):
                    w_ = min(TB, nk - kb * TB)
                    for h in range(16):
                        chn, hf = h // 2, h % 2
                        pb = c.psb[cnt % 4]
                        tmp = c.stg_f32[cnt % 4]
                        cnt += 1
                        S.op("pe", lambda e, pb=pb, chn=chn, hf=hf, kb=kb, w_=w_, qi_=qi_: e.matmul(
                            pb.t[:, 0:w_], qiblk[hf * 64:(hf + 1) * 64, chn, qi_ * 128:(qi_ + 1) * 128],
                            kir2[hf * 64:(hf + 1) * 64, kb * TB:kb * TB + w_], start=True, stop=True),
                            r=["qiblk", "kir2", "kir2b"], w=[pb.k])
                        if h == 0:
                            S.op("dve", lambda e, pb=pb, kb=kb, w_=w_, qt=qt, h=h, accX=accX: e.tensor_scalar(
                                out=accX[:, kb * TB:kb * TB + w_], in0=pb.t[:, 0:w_], scalar1=0.0, scalar2=wi_tok[:, qt, h:h + 1],
                                op0=ALU.max, op1=ALU.mult), r=[pb.k, "wi_tok"], w=[tg + "acc%d" % kb])
                        else:
                            S.op("act", lambda e, pb=pb, tmp=tmp, w_=w_: e.activation(out=tmp.t[:, 0:w_], in_=pb.t[:, 0:w_], func=AF.Relu),
                                 r=[pb.k], w=[tmp.k])
                            S.op("dve", lambda e, tmp=tmp, kb=kb, w_=w_, qt=qt, h=h, accX=accX: e.scalar_tensor_tensor(
                                out=accX[:, kb * TB:kb * TB + w_], in0=tmp.t[:, 0:w_], scalar=wi_tok[:, qt, h:h + 1],
                                in1=accX[:, kb * TB:kb * TB + w_], op0=ALU.mult, op1=ALU.add),
                                r=[tmp.k, "wi_tok", tg + "acc%d" % kb], w=[tg + "acc%d" % kb])
                acck = [tg + "acc%d" % kb for kb in range(nkb)]
                S.op("pool", lambda e, nk=nk, accX=accX: e.tensor_tensor(out=accX[:, nk - 128:nk], in0=accX[:, nk - 128:nk],
                                                                        in1=cst2[:, C_CAUS:C_CAUS + 128], op=ALU.add),
                     r=acck + ["cst2"], w=acck)
                tiles.append((qi_, qt, nk, accX, workX, m8X, tg, acck))
            for rd in range(32):
                for (qi_, qt, nk, accX, workX, m8X, tg, acck) in tiles:
                    if qt < 2:
                        if rd == 0:
                            S.op("dve", lambda e, m8X=m8X: e.memset(m8X, -1e29), w=[tg + "m8"])
                        continue
                    src = accX if rd == 0 else workX
                    sk = acck if rd == 0 else [tg + "work"]
                    S.op("dve", lambda e, src=src, nk=nk, m8X=m8X: e.max(out=m8X, in_=src[:, 0:nk]), r=sk, w=[tg + "m8"])
                    if rd < 31:
                        S.op("dve", lambda e, src=src, nk=nk, m8X=m8X, workX=workX: e.match_replace(
                            out=workX[:, 0:nk], in_to_replace=m8X, in_values=src[:, 0:nk], imm_value=NEG),
                            r=sk + [tg + "m8"], w=[tg + "work"])
                if inject is not None and rd % 4 == 3:
                    att_head(inject, pair * 8 + rd // 4, meng="pool")
            for (qi_, qt, nk, accX, workX, m8X, tg, acck) in tiles:
                S.op("dve", lambda e, nk=nk, accX=accX, m8X=m8X: e.tensor_scalar(out=sel01[:, 0:nk], in0=accX[:, 0:nk], scalar1=m8X[:, 7:8],
                                                                               scalar2=None, op0=ALU.is_ge), r=acck + [tg + "m8"], w=["sel01"])
                for k0 in range(0, qt + 1, 8):
                    n_ = min(8, qt + 1 - k0)
                    pb = c.psb[4 + (k0 // 8) % 2]
                    pbv = pb.t[:, :].bitcast(BF16)

                    def trs(e, k0=k0, n_=n_, pbv=pbv):
                        ins = None
                        for j in range(n_):
                            ins = e.transpose(pbv[:, j * 128:(j + 1) * 128], sel01[:, (k0 + j) * 128:(k0 + j + 1) * 128], c.ident_bf.t[:, :])
                        return ins
                    S.op("pe", trs, r=["sel01", "const"], w=[pb.k])
                    S.op("act", lambda e, k0=k0, n_=n_, pbv=pbv, qi_=qi_: e.copy(
                        out=selTb[:, k0:k0 + n_, qi_ * 128:(qi_ + 1) * 128],
                        in_=pbv[:, 0:n_ * 128].rearrange("p (k d) -> p k d", d=128)), r=[pb.k], w=[selk])

    load_qi(0)
    index_block(0, None)
    for b in range(NTB):
        load_q(b)
        if b + 1 < NTB:
            load_qi(b + 1)
            index_block(b + 1, b)
        else:
            for h in range(16):
                att_head(b, h)
    sched_barrier(S)


def phase_ssd(c, l, V):
    S = c.S
    R6, R4 = c.R64.t, c.R44.t
    ident_f = c.cst.t[:, C_IDENT:C_IDENT + 128]
    xraw = [R6[:, i * 2560:i * 2560 + 2051] for i in range(2)]
    xout = [R6[:, 8192 + i * 2048:8192 + (i + 1) * 2048] for i in range(2)]
    diag = [R4[:, i * 512:(i + 1) * 512].rearrange("p (k c) -> p k c", c=128) for i in range(2)]
    for i in range(2):
        S.op("dve", lambda e, i=i: e.memset(xraw[i][:, 0:3], 0.0), w=["xraw%d" % i])
    for cc in range(48):
        i = cc % 2
        xr, xo, dg = xraw[i], xout[i], diag[i]
        S.dma("sp", xr[:, 3:2051], c.P["xbc"][cc * 128:(cc + 1) * 128, :], w=["xraw%d" % i])
        for k in range(4):
            S.op("dve", lambda e, k=k, dg=dg, cc=cc: e.tensor_scalar(
                out=dg[:, k, :], in0=c.ident_bf.t[:, :], scalar1=V[:, V_CONVW + k * 48 + cc:V_CONVW + k * 48 + cc + 1],
                scalar2=None, op0=ALU.mult), r=["const", "vecs"], w=["diag%d" % i])
        for tb in range(NTB):
            pb = c.psb[(cc * NTB + tb) % 8]

            def mm(e, pb=pb, dg=dg, xr=xr, tb=tb):
                ins = None
                for k in range(4):
                    ins = e.matmul(pb.t[:, :], dg[:, k, :], xr[:, tb * TB + k:tb * TB + k + TB], start=(k == 0), stop=(k == 3))
                return ins
            S.op("pe", mm, r=["xraw%d" % i, "diag%d" % i], w=[pb.k])
            S.op("act", lambda e, pb=pb, xo=xo, tb=tb, cc=cc: e.activation(
                out=xo[:, tb * TB:(tb + 1) * TB], in_=pb.t[:, :], func=AF.Silu, bias=V[:, V_CONVB + cc:V_CONVB + cc + 1]),
                r=[pb.k, "vecs"], w=["xout%d" % i])
        S.dma("sp", c.xc_d[cc * 128:(cc + 1) * 128, :], xo, r=["xout%d" % i], w=[("xc", cc)])
    sched_barrier(S)

    AcsT = c.wslots[0].t[:, 0:4096].bitcast(F32)
    dtT = c.wslots[1].t[:, 0:4096].bitcast(F32)
    tA = c.wslots[2].t[:, 0:4096].bitcast(F32)
    tokw = c.wslots[3].t[:, 0:4096].bitcast(F32)
    dt_tok = tokw[:, 0:1024].rearrange("p (c h) -> p c h", h=64)
    Acs_tok = tokw[:, 1024:2048].rearrange("p (c h) -> p c h", h=64)
    tok2 = c.wslots[4].t[:, 0:4096].bitcast(F32)
    expA_tok = tok2[:, 0:1024].rearrange("p (c h) -> p c h", h=64)
    dA_tok = tok2[:, 1024:2048].rearrange("p (c h) -> p c h", h=64)
    acol = c.stg_f32[3].t[:, 0:1]
    cst2 = c.stg_f32[2].t[:, 0:128]
    S.dma("sp", cst2, c.consts2_d[:, C_U:C_U + 128], w=["Uf"])
    S.dma("sp", dtT[0:64, :], c.P["dt"], w=["dtT"])
    one_col = c.cst.t[0:64, C_ONES:C_ONES + 1]
    S.op("dve", lambda e: e.tensor_scalar(out=dtT[0:64, :], in0=dtT[0:64, :], scalar1=V[0:64, V_DTB:V_DTB + 1], scalar2=None, op0=ALU.add),
         r=["dtT", "vecs"], w=["dtT"])
    S.op("act", lambda e: e.activation(out=tA[0:64, :], in_=dtT[0:64, :], func=AF.Abs), r=["dtT"], w=["tA"])
    S.op("act", lambda e: e.activation(out=tA[0:64, :], in_=tA[0:64, :], func=AF.Exp, scale=-1.0), r=["tA"], w=["tA"])
    S.op("act", lambda e: e.activation(out=tA[0:64, :], in_=tA[0:64, :], func=AF.Ln, bias=one_col), r=["tA", "cst"], w=["tA"])
    S.op("dve", lambda e: e.scalar_tensor_tensor(out=dtT[0:64, :], in0=dtT[0:64, :], scalar=0.0, in1=tA[0:64, :], op0=ALU.max, op1=ALU.add),
         r=["dtT", "tA"], w=["dtT"])
    S.op("act", lambda e: e.activation(out=acol[0:64, :], in_=V[0:64, V_ALOG:V_ALOG + 1], func=AF.Exp), r=["vecs"], w=["acol"])
    S.op("dve", lambda e: e.tensor_scalar(out=tA[0:64, :], in0=dtT[0:64, :], scalar1=acol[0:64, :], scalar2=-1.0, op0=ALU.mult, op1=ALU.mult),
         r=["dtT", "acol"], w=["tA"])
    for src, dst, nm in ((dtT, dt_tok, "dt_tok"), (tA, dA_tok, "dA_tok")):
        for half in range(2):
            pb = c.psb[half]

            def tr(e, src=src, half=half, pb=pb):
                ins = None
                for j in range(8):
                    cch = half * 8 + j
                    ins = e.transpose(pb.t[:, j * 64:(j + 1) * 64], src[0:64, cch * 128:(cch + 1) * 128], ident_f[0:64, 0:64])
                return ins
            S.op("pe", tr, r=["dtT", "tA", "cst"], w=[pb.k])
            S.op("dve", lambda e, dst=dst, half=half, pb=pb: e.tensor_copy(
                out=dst[:, half * 8:(half + 1) * 8, :], in_=pb.t[:, :].rearrange("p (c h) -> p c h", h=64)), r=[pb.k], w=[nm])
    for half in range(2):
        pb = c.psb[2 + half]

        def cs(e, half=half, pb=pb):
            ins = None
            for j in range(8):
                cch = half * 8 + j
                ins = e.matmul(pb.t[:, j * 64:(j + 1) * 64], cst2, dA_tok[:, cch, :], start=True, stop=True)
            return ins
        S.op("pe", cs, r=["dA_tok", "Uf"], w=[pb.k])
        S.op("dve", lambda e, half=half, pb=pb: e.tensor_copy(out=Acs_tok[:, half * 8:(half + 1) * 8, :],
                                                           in_=pb.t[:, :].rearrange("p (c h) -> p c h", h=64)), r=[pb.k], w=["Acs_tok"])
    for q4 in range(4):
        pb = c.psb[4 + q4]

        def cs2(e, q4=q4, pb=pb):
            ins = None
            for j in range(4):
                cch = q4 * 4 + j
                ins = e.matmul(pb.t[0:64, j * 128:(j + 1) * 128], dA_tok[:, cch, :], cst2, start=True, stop=True)
            return ins
        S.op("pe", cs2, r=["dA_tok", "Uf"], w=[pb.k])
        S.op("dve", lambda e, q4=q4, pb=pb: e.tensor_copy(out=AcsT[0:64, q4 * TB:(q4 + 1) * TB], in_=pb.t[0:64, :]), r=[pb.k], w=["AcsT"])
    S.op("act", lambda e: e.activation(out=tok2[:, 0:1024], in_=tokw[:, 1024:2048], func=AF.Exp), r=["Acs_tok"], w=["expA_tok"])
    S.dma("sp", c.acs_d, AcsT[0:64, :], r=["AcsT"], w=["acs_d"])
    Dfull = c.wslots[1].t[:, 0:4096]
    sched_barrier(S)
    for h in range(NH_SSD):
        S.op("dve", lambda e, h=h: e.tensor_scalar(out=Dfull[:, h * 64:(h + 1) * 64], in0=c.cst.t[:, C_ONES:C_ONES + 64],
                                                   scalar1=V[:, V_DBC + h:V_DBC + h + 1], scalar2=None, op0=ALU.mult),
             r=["cst", "vecs"], w=["Dfull"])

    H = R6[:, 0:8192].bitcast(F32)
    Hbf = R6[:, 8192:12288]
    xs_tok = R6[:, 12288:16384]
    xw = R6[:, 16384:20480]
    y_tok = R6[:, 20480:24576]
    zc = R6[:, 24576:28672].rearrange("p (cc t) -> p cc t", t=128)
    xsT = R6[:, 28672:32768].rearrange("p (cc t) -> p cc t", t=128)
    ygf = R4[:, 0:8192].bitcast(F32).rearrange("p (cc t) -> p cc t", t=128)
    yout = R4[:, 8192:12288].rearrange("p (cc t) -> p cc t", t=128)
    BT = R4[:, 12288:13312].rearrange("p (g t) -> p g t", t=128)
    CT = R4[:, 13312:14336].rearrange("p (g t) -> p g t", t=128)
    B_tok = R4[:, 14336:15360].rearrange("p (g t) -> p g t", t=128)
    cbTm = R4[:, 15360:17408].bitcast(F32).rearrange("p (g t) -> p g t", t=128)
    Et = [R4[:, 17408 + i * 1024:17408 + (i + 1) * 1024].bitcast(F32) for i in range(3)]
    Mp = [R4[:, 20480 + i * 512:20480 + (i + 1) * 512] for i in range(3)]
    WSv = c.WS.t
    A_rows = WSv[:, 0:5632].bitcast(F32)
    xdt = WSv[:, 9728:13824]
    mask4 = WSv[:, 13824:14848].bitcast(F32)
    negones = WSv[:, 14848:15104].bitcast(F32)
    w64 = WSv[:, 15104:15232].bitcast(F32)
    dec = WSv[:, 15232:15360].bitcast(F32)
    Uf = cst2
    for j in range(4):
        S.op("dve", lambda e, j=j: e.tensor_scalar(out=mask4[:, j * 128:(j + 1) * 128], in0=Uf, scalar1=-1.0, scalar2=30000.0,
                                                   op0=ALU.add, op1=ALU.mult), r=["Uf"], w=["mask4"])
    S.op("dve", lambda e: e.memset(negones, -1.0), w=["negones"])
    pgroups = [(0, 0, 22), (32, 22, 44), (64, 44, 64)]
    batches = []
    for pbase, hs, he in pgroups:
        for h0 in range(hs, he, 4):
            batches.append((pbase, hs, h0, min(4, he - h0)))
    xcv = c.xc_d.rearrange("(cc p) t -> p cc t", p=128)
    zv = c.P["z"].rearrange("(cc p) t -> p cc t", p=128)
    yv = c.yssd.rearrange("(cc p) t -> p cc t", p=128)
    S.op("dve", lambda e: e.memset(H, 0.0), w=["H%d" % g for g in range(8)])
    S.op("dve", lambda e: e.memset(Hbf, 0.0), w=["Hbf%d" % g for g in range(8)])
    NCH = L // 128
    pending_D = [None]
    for ch in range(NCH):
        t0 = ch * 128
        last = (ch == NCH - 1)
        S.dma("sp", xsT[:, 0:16, :], xcv[:, 0:16, t0:t0 + 128], w=["xsT0"])
        S.dma("sp", xsT[:, 16:32, :], xcv[:, 16:32, t0:t0 + 128], w=["xsT1"])
        S.dma("sp", BT, xcv[:, 32:40, t0:t0 + 128], w=["BT"])
        S.dma("sp", CT, xcv[:, 40:48, t0:t0 + 128], w=["CT"])
        for pbase, hs, he in pgroups:
            S.dma("sp", A_rows[pbase:pbase + 1, 0:(he - hs) * 128].rearrange("o (h t) -> o h t", t=128),
                  c.acs_d[hs:he, t0:t0 + 128].rearrange("(o h) t -> o h t", o=1), r=["acs_d"], w=["A_rows"])
        pb7 = c.psb[7]
        pb7v = pb7.t[:, :].bitcast(BF16)
        for q8 in range(4):
            def trx(e, q8=q8):
                ins = None
                for j in range(8):
                    ins = e.transpose(pb7v[:, j * 128:(j + 1) * 128], xsT[:, q8 * 8 + j, :], c.ident_bf.t[:, :])
                return ins
            S.op("pe", trx, r=["xsT0", "xsT1", "const"], w=[pb7.k])
            S.op("act", lambda e, q8=q8: e.copy(out=xs_tok[:, q8 * 1024:(q8 + 1) * 1024], in_=pb7v[:, 0:1024]), r=[pb7.k], w=["xs_tok"])

        def trb(e):
            ins = None
            for g in range(8):
                ins = e.transpose(pb7v[:, g * 128:(g + 1) * 128], BT[:, g, :], c.ident_bf.t[:, :])
            return ins
        S.op("pe", trb, r=["BT", "const"], w=[pb7.k])
        S.op("act", lambda e: e.copy(out=B_tok, in_=pb7v[:, 0:1024].rearrange("p (g t) -> p g t", t=128)), r=[pb7.k], w=["B_tok"])
        S.op("pool", lambda e: e.tensor_tensor(out=xdt.rearrange("p (h d) -> p h d", d=64), in0=xs_tok.rearrange("p (h d) -> p h d", d=64),
                                               in1=dt_tok[:, ch, :].unsqueeze(2).to_broadcast([128, 64, 64]), op=ALU.mult),
             r=["xs_tok", "dt_tok"], w=["xdt"])
        for half in range(2):
            pb = c.psb[2 + half]

            def mcb(e, half=half, pb=pb):
                ins = None
                for j in range(4):
                    g = half * 4 + j
                    ins = e.matmul(pb.t[:, j * 128:(j + 1) * 128], BT[:, g, :], CT[:, g, :], start=True, stop=True)
                return ins
            S.op("pe", mcb, r=["BT", "CT"], w=[pb.k])
            S.op("dve", lambda e, half=half, pb=pb: e.tensor_tensor(
                out=cbTm[:, half * 4:(half + 1) * 4, :], in0=pb.t[:, :].rearrange("p (g t) -> p g t", t=128),
                in1=Uf.unsqueeze(1).to_broadcast([128, 4, 128]), op=ALU.mult), r=[pb.k, "Uf"], w=["cbTm"])
        if pending_D[0] is not None:
            pending_D[0]()
            pending_D[0] = None
        S.dma("sp", zc, zv[:, :, t0:t0 + 128], w=["zc"])
        S.op("act", lambda e: e.activation(out=zc, in_=zc, func=AF.Silu), r=["zc"], w=["zc"])
        def st1(bi):
            pbase, hs, h0, nh = batches[bi]
            pT1 = c.psb[bi % 2]

            def mseg(e):
                e.matmul(pT1.t[:, 0:nh * 128], c.cst.t[pbase:pbase + 1, C_ONES:C_ONES + 128],
                         A_rows[pbase:pbase + 1, (h0 - hs) * 128:(h0 - hs + nh) * 128], start=True, stop=False)
                ins = None
                for i in range(nh):
                    ins = e.matmul(pT1.t[:, i * 128:(i + 1) * 128], A_rows[pbase:pbase + 1, (h0 - hs + i) * 128:(h0 - hs + i + 1) * 128],
                                   negones[pbase:pbase + 1, :], start=False, stop=(i == nh - 1))
                return ins
            S.op("pe", mseg, r=["A_rows", "cst", "negones"], w=[pT1.k])

        def st2(bi):
            pbase, hs, h0, nh = batches[bi]
            pT1 = c.psb[bi % 2]
            E = Et[bi % 3]
            ek = "E%d" % (bi % 3)
            S.op("dve", lambda e: e.tensor_scalar(out=E[:, 0:nh * 128], in0=pT1.t[:, 0:nh * 128], scalar1=0.0, scalar2=None, op0=ALU.min),
                 r=[pT1.k], w=[ek])
            S.op("act", lambda e: e.activation(out=E[:, 0:nh * 128], in_=E[:, 0:nh * 128], func=AF.Exp), r=[ek], w=[ek])
            if not last:
                S.op("pool", lambda e: e.tensor_copy(
                    out=w64[:, h0:h0 + nh].rearrange("p (h o) -> p h o", o=1),
                    in_=E[:, 0:nh * 128].rearrange("p (h t) -> p h t", t=128)[:, :, 127:128]), r=[ek], w=["w64"])

        def st3(bi):
            pbase, hs, h0, nh = batches[bi]
            E, M = Et[bi % 3], Mp[bi % 3]
            ek, mk = "E%d" % (bi % 3), "Mp%d" % (bi % 3)
            i = 0
            while i < nh:
                g = (h0 + i) // 8
                j = i
                while j < nh and (h0 + j) // 8 == g:
                    j += 1
                n_ = j - i
                S.op("dve", lambda e, i=i, n_=n_, g=g: e.tensor_tensor(
                    out=M[:, i * 128:(i + n_) * 128].rearrange("p (h t) -> p h t", t=128),
                    in0=E[:, i * 128:(i + n_) * 128].rearrange("p (h t) -> p h t", t=128),
                    in1=cbTm[:, g:g + 1, :].to_broadcast([128, n_, 128]), op=ALU.mult), r=[ek, "cbTm"], w=[mk])
                i = j
            for i in range(nh):
                h = h0 + i
                g = h // 8
                po1 = c.psb[4]
                S.op("pe", lambda e, i=i, h=h: e.matmul(po1.t[:, (h % 8) * 64:(h % 8 + 1) * 64], M[:, i * 128:(i + 1) * 128],
                                                       xdt[:, h * 64:(h + 1) * 64], start=True, stop=True),
                     r=[mk, "xdt"], w=[po1.k + "_h%d" % (h % 8)])
                if h % 8 == 7:
                    gs = slice(g * 512, (g + 1) * 512)
                    po1k = [po1.k + "_h%d" % r_ for r_ in range(8)]
                    po2 = c.psb[5]
                    S.op("pe", lambda e, g=g, gs=gs: e.matmul(po2.t[:, :], CT[:, g, :], Hbf[:, gs], start=True, stop=True),
                         r=["CT", "Hbf%d" % g], w=[po2.k])
                    xd = c.stg_f32[0]
                    S.op("pool", lambda e, gs=gs: e.tensor_tensor(out=xd.t[:, :], in0=xs_tok[:, gs], in1=Dfull[:, gs], op=ALU.mult),
                         r=["xs_tok", "Dfull"], w=[xd.k])
                    sb1 = c.stg_f32[1]
                    S.op("dve", lambda e: e.tensor_tensor(out=sb1.t[:, :], in0=po1.t[:, :], in1=xd.t[:, :], op=ALU.add),
                         r=po1k + [xd.k], w=[sb1.k] + po1k)
                    S.op("dve", lambda e, gs=gs, g=g: e.tensor_tensor(
                        out=y_tok[:, gs].rearrange("p (h d) -> p h d", d=64), in0=po2.t[:, :].rearrange("p (h d) -> p h d", d=64),
                        in1=expA_tok[:, ch, g * 8:(g + 1) * 8].unsqueeze(2).to_broadcast([128, 8, 64]), op=ALU.mult),
                        r=[po2.k, "expA_tok"], w=["y_tok%d" % g])
                    S.op("dve", lambda e, gs=gs: e.tensor_tensor(out=y_tok[:, gs], in0=y_tok[:, gs], in1=sb1.t[:, :], op=ALU.add),
                         r=["y_tok%d" % g, sb1.k], w=["y_tok%d" % g])
                    if not last:
                        S.op("pool", lambda e, gs=gs, g=g: e.tensor_tensor(
                            out=xw[:, gs].rearrange("p (h d) -> p h d", d=64), in0=xdt[:, gs].rearrange("p (h d) -> p h d", d=64),
                            in1=w64[:, g * 8:(g + 1) * 8].unsqueeze(2).to_broadcast([128, 8, 64]), op=ALU.mult),
                            r=["xdt", "w64"], w=["xw%d" % g])
                        pS = c.psb[6]
                        S.op("pe", lambda e, g=g, gs=gs: e.matmul(pS.t[:, :], B_tok[:, g, :], xw[:, gs], start=True, stop=True),
                             r=["B_tok", "xw%d" % g], w=[pS.k])
                        S.op("pool", lambda e, g=g: e.tensor_tensor(out=dec[:, g * 8:(g + 1) * 8], in0=w64[:, g * 8:(g + 1) * 8],
                                                                    in1=expA_tok[:, ch, g * 8:(g + 1) * 8], op=ALU.mult),
                             r=["w64", "expA_tok"], w=["dec"])
                        S.op("pool", lambda e, gs=gs, g=g: e.tensor_tensor(
                            out=H[:, gs].rearrange("p (h d) -> p h d", d=64), in0=H[:, gs].rearrange("p (h d) -> p h d", d=64),
                            in1=dec[:, g * 8:(g + 1) * 8].unsqueeze(2).to_broadcast([128, 8, 64]), op=ALU.mult),
                            r=["dec", "H%d" % g], w=["H%d" % g])
                        S.op("dve", lambda e, gs=gs: e.tensor_tensor(out=H[:, gs], in0=H[:, gs], in1=pS.t[:, :], op=ALU.add),
                             r=[pS.k, "H%d" % g], w=["H%d" % g])
                        S.op("act", lambda e, gs=gs: e.copy(out=Hbf[:, gs], in_=H[:, gs]), r=["H%d" % g], w=["Hbf%d" % g])

        nb_ = len(batches)
        for step in range(nb_ + 2):
            if step < nb_:
                st1(step)
            if 0 <= step - 1 < nb_:
                st2(step - 1)
            if 0 <= step - 2 < nb_:
                st3(step - 2)
        def stage_D(ch=ch, t0=t0):
            sq = xw.rearrange("p (cc t) -> p cc t", t=128)
            ytk = ["y_tok%d" % g for g in range(8)]
            for q8 in range(4):
                def try_(e, q8=q8):
                    ins = None
                    for j in range(8):
                        cc = q8 * 8 + j
                        ins = e.transpose(pb7v[:, j * 128:(j + 1) * 128], y_tok[:, cc * 128:(cc + 1) * 128], c.ident_bf.t[:, :])
                    return ins
                S.op("pe", try_, r=ytk + ["const"], w=[pb7.k])
                S.op("dve", lambda e, q8=q8: e.tensor_tensor(out=ygf[:, q8 * 8:(q8 + 1) * 8, :],
                                                           in0=pb7v[:, 0:1024].rearrange("p (cc t) -> p cc t", t=128),
                                                           in1=zc[:, q8 * 8:(q8 + 1) * 8, :], op=ALU.mult), r=[pb7.k, "zc"], w=["ygf"])
            xwk = ["xw%d" % g for g in range(8)]
            S.op("act", lambda e: e.activation(out=sq, in_=ygf, func=AF.Square), r=["ygf"], w=xwk)
            pn = c.psb[6]

            def mmn(e):
                ins = None
                for cc in range(32):
                    ins = e.matmul(pn.t[:, 0:128], c.ones_bf.t[:, :], sq[:, cc, :], start=(cc == 0), stop=(cc == 31))
                return ins
            S.op("pe", mmn, r=xwk + ["const"], w=[pn.k])
            rt = c.stg_f32[2].t[:, 128:256]
            rs = c.stg_f32[2].t[:, 256:384]
            S.op("act", lambda e: e.activation(out=rt, in_=pn.t[:, 0:128], func=AF.Sqrt, scale=1.0 / SSD_INNER, bias=c.eps_col.t[:, 0:1]),
                 r=[pn.k], w=["rt"])
            S.op("dve", lambda e: e.reciprocal(out=rs, in_=rt), r=["rt"], w=["rs"])
            for cc in range(32):
                S.op("dve", lambda e, cc=cc: e.scalar_tensor_tensor(out=yout[:, cc, :], in0=ygf[:, cc, :], scalar=V[:, V_SSDN + cc:V_SSDN + cc + 1],
                                                                     in1=rs, op0=ALU.mult, op1=ALU.mult), r=["ygf", "rs", "vecs"], w=["yout"])
            S.dma("sp", yv[:, 0:16, t0:t0 + 128], yout[:, 0:16, :], r=["yout"], w=[("yssd", ch, 0)])
            S.dma("sp", yv[:, 16:32, t0:t0 + 128], yout[:, 16:32, :], r=["yout"], w=[("yssd", ch, 1)])

        pending_D[0] = stage_D
    if pending_D[0] is not None:
        pending_D[0]()
        pending_D[0] = None
    sched_barrier(S)


FULL_PLAN = []
for _l in range(DEPTH):
    FULL_PLAN += [("ffn1", _l), ("inproj", _l), ("mem", _l), ("dsa", _l), ("ssd", _l), ("merge", _l), ("ffn2", _l)]
FULL_PLAN.append("final")
_NC_CACHE = {}


def kernel(**inputs):
    inp = {k: np.asarray(v) for k, v in inputs.items()}
    B = inp["x"].shape[0]
    if "nc" not in _NC_CACHE:
        _NC_CACHE["nc"] = build(FULL_PLAN)
    nc = _NC_CACHE["nc"]
    vec = np.stack([pack_vecs(inp, l) for l in range(DEPTH)])
    cst = make_consts()
    in_maps = [core_inputs(inp, b, vec, cst) for b in range(B)]
    res = run_bass_kernel_spmd(nc, in_maps, core_ids=list(range(B)))
    out = np.stack([np.ascontiguousarray(res.results[b]["outT"].T) for b in range(B)]).astype(np.float32)
    return out
```
